# Optimizing a Trainium2 kernel written in Bass

```python
import jax, jax.numpy as jnp
from jax import lax
import numpy as np

D_MODEL = 1024
BATCH = 8
SEQ = 2048
DEPTH = 2

CTX_LEN = 256
GRID_W = 64
ROPE_BASE = 10000.0
Q_BLOCK = 128
EPS = 1e-6

D_MIX = D_MODEL
SSD_WIDTH = D_MIX // 2
SSD_HEAD_DIM = 64
SSD_HEADS = SSD_WIDTH // SSD_HEAD_DIM
SSD_GROUPS = 2
SSD_STATE = 128
SSD_CHUNK = 128
CONV_K = 5
XBC_DIM = SSD_WIDTH + 2 * SSD_GROUPS * SSD_STATE
GQA_WIDTH = D_MIX // 4
GQA_HEAD_DIM = 64
GQA_HEADS = GQA_WIDTH // GQA_HEAD_DIM
GQA_KV_HEADS = GQA_HEADS // 2
DIFF_WIDTH = D_MIX // 4
DIFF_V_DIM = 64
DIFF_HEADS = DIFF_WIDTH // DIFF_V_DIM
DIFF_QK_DIM = DIFF_V_DIM // 2

IN_SPLITS = (
    ("xbc", XBC_DIM), ("z", SSD_WIDTH), ("dt", 2 * SSD_HEADS),
    ("gq", GQA_HEADS * GQA_HEAD_DIM), ("gk", GQA_KV_HEADS * GQA_HEAD_DIM),
    ("gv", GQA_KV_HEADS * GQA_HEAD_DIM), ("gg", GQA_WIDTH),
    ("dq", 2 * DIFF_HEADS * DIFF_QK_DIM), ("dk", 2 * DIFF_HEADS * DIFF_QK_DIM),
    ("dv", DIFF_HEADS * DIFF_V_DIM), ("dg", DIFF_WIDTH),
)
IN_COLS = sum(s for _, s in IN_SPLITS)

kernel_name = "hybrid_ssd_gqa_diffattn_dit_block"


def rmsnorm(x, g):
    xf = x.astype(jnp.float32)
    y = xf * lax.rsqrt(jnp.mean(xf * xf, axis=-1, keepdims=True) + EPS)
    return y.astype(x.dtype) * g


def split_proj(p):
    idx = [int(i) for i in np.cumsum([s for _, s in IN_SPLITS])[:-1]]
    parts = jnp.split(p, idx, axis=-1)
    return {name: part for (name, _), part in zip(IN_SPLITS, parts)}


def axial_angles(row_idx, col_idx, dim):
    quarter = dim // 4
    inv = ROPE_BASE ** (-jnp.arange(quarter, dtype=jnp.float32) / quarter)
    ang_r = row_idx.astype(jnp.float32)[:, None] * inv
    ang_c = col_idx.astype(jnp.float32)[:, None] * inv
    return jnp.concatenate([ang_r, ang_c], axis=-1)


def apply_rope(x, ang):
    half = x.shape[-1] // 2
    xf = x.astype(jnp.float32)
    x1, x2 = xf[..., :half], xf[..., half:]
    cos = jnp.cos(ang)[None, :, None, :]
    sin = jnp.sin(ang)[None, :, None, :]
    out = jnp.concatenate([x1 * cos - x2 * sin, x2 * cos + x1 * sin], axis=-1)
    return out.astype(x.dtype)


def attention(q, k, v):
    b, t, hq, d = q.shape
    g = k.shape[2]
    r = hq // g
    dv = v.shape[-1]
    nb = t // Q_BLOCK
    qb = jnp.moveaxis(q.reshape(b, nb, Q_BLOCK, g, r, d), 1, 0)
    scale = d ** -0.5

    def block(qblk):
        s = jnp.einsum('bqgrd,bsgd->bgrqs', qblk, k).astype(jnp.float32) * scale
        p = jax.nn.softmax(s, axis=-1).astype(v.dtype)
        return jnp.einsum('bgrqs,bsgd->bqgrd', p, v)

    out = lax.map(block, qb)
    return jnp.moveaxis(out, 0, 1).reshape(b, t, hq, dv)


def segsum_exp(cs):
    t = cs.shape[-1]
    diff = cs[..., :, None] - cs[..., None, :]
    mask = jnp.tril(jnp.ones((t, t), dtype=bool))
    return jnp.exp(jnp.where(mask, diff, -jnp.inf))


def ssd_scan(x, dt, a, bm, cm, init_state):
    b, l, h, p = x.shape
    nc = l // SSD_CHUNK
    f = lambda t: t.reshape((b, nc, SSD_CHUNK) + t.shape[2:])
    xd = f(x * dt[..., None])
    a_cum = jnp.cumsum(f(dt * a), axis=2)
    bc, cc = f(bm), f(cm)
    lmat = segsum_exp(jnp.moveaxis(a_cum, -1, 2))
    y_diag = jnp.einsum('bcqhn,bcshn,bchqs,bcshp->bcqhp', cc, bc, lmat, xd)
    decay_states = jnp.exp(a_cum[:, :, -1:, :] - a_cum)
    states = jnp.einsum('bcshn,bcsh,bcshp->bchpn', bc, decay_states, xd)
    chunk_cum = jnp.concatenate(
        [jnp.zeros((b, 1, h), a_cum.dtype), jnp.cumsum(a_cum[:, :, -1, :], axis=1)], axis=1)
    decay_chunk = segsum_exp(jnp.moveaxis(chunk_cum, 1, 2))
    all_states = jnp.concatenate([init_state[:, None].astype(states.dtype), states], axis=1)
    new_states = jnp.einsum('bhzy,byhpn->bzhpn', decay_chunk, all_states)
    states_in, final_state = new_states[:, :-1], new_states[:, -1]
    y_off = jnp.einsum('bcqhn,bchpn,bcqh->bcqhp', cc, states_in, jnp.exp(a_cum))
    return (y_diag + y_off).reshape(b, l, h, p), final_state


def dwconv(u, w, bias):
    out = lax.conv_general_dilated(
        u, w[:, None, :], window_strides=(1,), padding=[(CONV_K // 2, CONV_K // 2)],
        dimension_numbers=('NWC', 'WIO', 'NWC'), feature_group_count=u.shape[-1])
    return out + bias


def ssd_branch(p_l, p_c, conv_w, conv_b, a_log_f, a_log_b, dtb_f, dtb_b, d_skip, norm_g):
    a_f = -jnp.exp(a_log_f.astype(jnp.float32))
    a_b = -jnp.exp(a_log_b.astype(jnp.float32))
    rep = SSD_HEADS // SSD_GROUPS

    def prep(p):
        u = jax.nn.silu(dwconv(p["xbc"], conv_w, conv_b))
        bb, ll = u.shape[:2]
        xs, bs, cs = jnp.split(u, [SSD_WIDTH, SSD_WIDTH + SSD_GROUPS * SSD_STATE], axis=-1)
        xs = xs.reshape(bb, ll, SSD_HEADS, SSD_HEAD_DIM)
        bs = jnp.repeat(bs.reshape(bb, ll, SSD_GROUPS, SSD_STATE), rep, axis=2)
        cs = jnp.repeat(cs.reshape(bb, ll, SSD_GROUPS, SSD_STATE), rep, axis=2)
        dt_raw = p["dt"].astype(jnp.float32)
        dt_f = jax.nn.softplus(dt_raw[..., :SSD_HEADS] + dtb_f.astype(jnp.float32))
        dt_b = jax.nn.softplus(dt_raw[..., SSD_HEADS:] + dtb_b.astype(jnp.float32))
        return xs, bs, cs, dt_f, dt_b

    flip = lambda t: jnp.flip(t, axis=1)
    xc, bc, cc, dcf, dcb = prep(p_c)
    xl, bl, cl, dlf, dlb = prep(p_l)
    zeros = jnp.zeros((xc.shape[0], SSD_HEADS, SSD_HEAD_DIM, SSD_STATE), jnp.float32)
    y_cf, s_f = ssd_scan(xc, dcf, a_f, bc, cc, zeros)
    y_cb, s_b = ssd_scan(flip(xc), flip(dcb), a_b, flip(bc), flip(cc), zeros)
    y_lf, _ = ssd_scan(xl, dlf, a_f, bl, cl, s_f)
    y_lb, _ = ssd_scan(flip(xl), flip(dlb), a_b, flip(bl), flip(cl), s_b)

    def finish(yf, yb_rev, xs, z):
        y = yf + flip(yb_rev) + xs * d_skip[:, None]
        y = y.reshape(y.shape[0], y.shape[1], SSD_WIDTH)
        return rmsnorm(y * jax.nn.silu(z.astype(y.dtype)), norm_g).astype(z.dtype)

    return finish(y_lf, y_lb, xl, p_l["z"]), finish(y_cf, y_cb, xc, p_c["z"])


def gqa_branch(p_l, p_c, ang, q_g, k_g, ctx_out):
    def qkv(p):
        bb, ll = p["gq"].shape[:2]
        q = rmsnorm(p["gq"].reshape(bb, ll, GQA_HEADS, GQA_HEAD_DIM), q_g)
        k = rmsnorm(p["gk"].reshape(bb, ll, GQA_KV_HEADS, GQA_HEAD_DIM), k_g)
        v = p["gv"].reshape(bb, ll, GQA_KV_HEADS, GQA_HEAD_DIM)
        return q, k, v

    def gate(o, p):
        return o.reshape(o.shape[0], o.shape[1], GQA_WIDTH) * jax.nn.silu(p["gg"])

    q_l, k_l, v_l = qkv(p_l)
    q_c, k_c, v_c = qkv(p_c)
    q_l, k_l = apply_rope(q_l, ang), apply_rope(k_l, ang)
    k_all = jnp.concatenate([k_c, k_l], axis=1)
    v_all = jnp.concatenate([v_c, v_l], axis=1)
    y_l = gate(attention(q_l, k_all, v_all), p_l)
    y_c = gate(attention(q_c, k_c, v_c), p_c) if ctx_out else None
    return y_l, y_c


def diff_branch(p_l, p_c, ang, lam_params, norm_g, lam_init, ctx_out):
    lp = lam_params.astype(jnp.float32)
    lam = jnp.exp(jnp.sum(lp[0] * lp[1])) - jnp.exp(jnp.sum(lp[2] * lp[3])) + lam_init

    def qkv(p):
        bb, ll = p["dq"].shape[:2]
        q = p["dq"].reshape(bb, ll, 2 * DIFF_HEADS, DIFF_QK_DIM)
        k = p["dk"].reshape(bb, ll, 2 * DIFF_HEADS, DIFF_QK_DIM)
        v = p["dv"].reshape(bb, ll, DIFF_HEADS, DIFF_V_DIM)
        return q, k, v

    def diff_attend(q, k, v, p):
        o = attention(q[:, :, 0::2], k[:, :, 0::2], v) - lam * attention(q[:, :, 1::2], k[:, :, 1::2], v)
        o = (rmsnorm(o, norm_g) * (1.0 - lam_init)).astype(v.dtype)
        return o.reshape(o.shape[0], o.shape[1], DIFF_WIDTH) * jax.nn.silu(p["dg"])

    q_l, k_l, v_l = qkv(p_l)
    q_c, k_c, v_c = qkv(p_c)
    q_l, k_l = apply_rope(q_l, ang), apply_rope(k_l, ang)
    k_all = jnp.concatenate([k_c, k_l], axis=1)
    v_all = jnp.concatenate([v_c, v_l], axis=1)
    y_l = diff_attend(q_l, k_all, v_all, p_l)
    y_c = diff_attend(q_c, k_c, v_c, p_c) if ctx_out else None
    return y_l, y_c


def setup_inputs(seed: int = 0) -> dict:
    key = jax.random.key(seed)
    ks = jax.random.split(key, 24)
    nrm = jax.random.normal

    def dt_bias(k):
        dt = jnp.exp(jax.random.uniform(k, (DEPTH, SSD_HEADS), minval=float(np.log(1e-3)), maxval=float(np.log(1e-1))))
        return dt + jnp.log(-jnp.expm1(-dt))

    return {
        "x": nrm(ks[0], (BATCH, SEQ, D_MODEL), jnp.float32),
        "c": nrm(ks[1], (BATCH, D_MODEL), jnp.float32),
        "ctx": nrm(ks[2], (BATCH, CTX_LEN, D_MODEL), jnp.float32),
        "c_ctx": nrm(ks[3], (D_MODEL,), jnp.float32),
        "w_mod": nrm(ks[4], (DEPTH, D_MODEL, 3 * D_MODEL), jnp.float32) * (0.5 * D_MODEL ** -0.5),
        "b_mod": 0.01 * nrm(ks[5], (DEPTH, 3 * D_MODEL), jnp.float32),
        "g_pre": 1.0 + 0.02 * nrm(ks[6], (DEPTH, D_MODEL), jnp.float32),
        "g_post": 1.0 + 0.02 * nrm(ks[7], (DEPTH, D_MODEL), jnp.float32),
        "w_in": nrm(ks[8], (DEPTH, D_MODEL, IN_COLS), jnp.float32) * D_MODEL ** -0.5,
        "conv_w": nrm(ks[9], (DEPTH, CONV_K, XBC_DIM), jnp.float32) * CONV_K ** -0.5,
        "conv_b": 0.01 * nrm(ks[10], (DEPTH, XBC_DIM), jnp.float32),
        "a_log_fwd": jnp.log(jax.random.uniform(ks[11], (DEPTH, SSD_HEADS), minval=1.0, maxval=16.0)),
        "a_log_bwd": jnp.log(jax.random.uniform(ks[12], (DEPTH, SSD_HEADS), minval=1.0, maxval=16.0)),
        "dt_bias_fwd": dt_bias(ks[13]),
        "dt_bias_bwd": dt_bias(ks[14]),
        "d_skip": 1.0 + 0.1 * nrm(ks[15], (DEPTH, SSD_HEADS), jnp.float32),
        "ssd_norm_g": 1.0 + 0.02 * nrm(ks[16], (DEPTH, SSD_WIDTH), jnp.float32),
        "q_norm_g": 1.0 + 0.02 * nrm(ks[17], (DEPTH, GQA_HEAD_DIM), jnp.float32),
        "k_norm_g": 1.0 + 0.02 * nrm(ks[18], (DEPTH, GQA_HEAD_DIM), jnp.float32),
        "diff_lambda": 0.1 * nrm(ks[19], (DEPTH, 4, DIFF_QK_DIM), jnp.float32),
        "diff_norm_g": 1.0 + 0.02 * nrm(ks[20], (DEPTH, DIFF_V_DIM), jnp.float32),
        "w_out": nrm(ks[21], (DEPTH, D_MIX, D_MODEL), jnp.float32) * D_MIX ** -0.5,
    }


def reference(x, c, ctx, c_ctx, w_mod, b_mod, g_pre, g_post, w_in, conv_w, conv_b,
              a_log_fwd, a_log_bwd, dt_bias_fwd, dt_bias_bwd, d_skip, ssd_norm_g,
              q_norm_g, k_norm_g, diff_lambda, diff_norm_g, w_out):
    n_lat = x.shape[1]
    ROWS = n_lat // GRID_W
    row_idx = jnp.repeat(jnp.arange(ROWS), GRID_W)
    col_idx = jnp.arange(ROWS * GRID_W) % GRID_W
    ang_g = axial_angles(row_idx, col_idx, GQA_HEAD_DIM)
    ang_d = axial_angles(row_idx, col_idx, DIFF_QK_DIM)

    h, hc = x, ctx
    s_lat = jax.nn.silu(c)
    s_ctx = jax.nn.silu(c_ctx)
    for l in range(DEPTH):
        ctx_out = l < DEPTH - 1
        lam_init = 0.8 - 0.6 * float(np.exp(-0.3 * l))
        mod_l = s_lat @ w_mod[l] + b_mod[l]
        mod_c = s_ctx @ w_mod[l] + b_mod[l]
        sh_l, sc_l, gt_l = jnp.split(mod_l[:, None, :], 3, axis=-1)
        sh_c, sc_c, gt_c = jnp.split(mod_c, 3)
        u_l = rmsnorm(h, g_pre[l]) * (1 + sc_l) + sh_l
        u_c = rmsnorm(hc, g_pre[l]) * (1 + sc_c) + sh_c
        p_l = split_proj(u_l @ w_in[l])
        p_c = split_proj(u_c @ w_in[l])

        y_s_l, y_s_c = ssd_branch(p_l, p_c, conv_w[l], conv_b[l], a_log_fwd[l], a_log_bwd[l],
                                  dt_bias_fwd[l], dt_bias_bwd[l], d_skip[l], ssd_norm_g[l])
        y_g_l, y_g_c = gqa_branch(p_l, p_c, ang_g, q_norm_g[l], k_norm_g[l], ctx_out)
        y_d_l, y_d_c = diff_branch(p_l, p_c, ang_d, diff_lambda[l], diff_norm_g[l], lam_init, ctx_out)

        o_l = jnp.concatenate([y_s_l, y_g_l, y_d_l], axis=-1) @ w_out[l]
        h = h + gt_l * rmsnorm(o_l, g_post[l])
        if ctx_out:
            o_c = jnp.concatenate([y_s_c, y_g_c, y_d_c], axis=-1) @ w_out[l]
            hc = hc + gt_c * rmsnorm(o_c, g_post[l])
    return h
```

```python
import numpy as np
import ml_dtypes
from contextlib import ExitStack
import concourse.bass as bass
import concourse.mybir as mybir
from concourse.bass_utils import run_bass_kernel_spmd

F32 = mybir.dt.float32
BF16 = mybir.dt.bfloat16
AF = mybir.ActivationFunctionType
ALU = mybir.AluOpType
AX = mybir.AxisListType

NCORES = 8
D = 1024
SEQ = 2048
CTX = 256
NT = SEQ + CTX
NTILE = NT // 128
DEPTH = 2
EPS = 1e-6
INC = 3344
C_X, C_B, C_C, C_Z, C_DT = 0, 512, 768, 1024, 1536
C_GQ, C_GK, C_GV, C_GG = 1552, 1808, 1936, 2064
C_DQ, C_DK, C_DV, C_DG = 2320, 2576, 2832, 3088
R_GQ, R_GK, R_DQ, R_DK, NROT = 0, 256, 384, 640, 896
NEG = -30000.0
SAME_ENGINE_SYNC = True
S_LEVEL = [0]
BCAST_LHST = True


class Buf:
    __slots__ = ("w", "r", "name")

    def __init__(self, name=""):
        self.w = None
        self.r = {}
        self.name = name


class T:
    NDS = 8
    ROT = 20000

    def __init__(self, nc, es):
        self.nc = nc
        self.es = es
        self.E = {"pe": nc.tensor, "act": nc.scalar, "dve": nc.vector, "pool": nc.gpsimd, "sp": nc.sync}
        self.sem = {}
        self.cnt = {}
        self.nsem = 0
        for e in ("pe", "act", "dve", "pool"):
            self.sem[e] = self._newsem("s_" + e)
            self.cnt[e] = 0
        self.seen = {e: {} for e in self.E}
        self.pend = {e: ([], []) for e in self.E}
        self.dq = {}
        for q in ("sp", "pool", "act"):
            self.dq[q] = {"sems": [self._newsem(f"d_{q}{i}") for i in range(self.NDS)],
                          "vals": [0] * self.NDS, "n": 0}
        self.alltoks = {}
        self.ninstr = 0

    def _newsem(self, name):
        self.nsem += 1
        return self.es.enter_context(self.nc.semaphore(f"{name}_{self.nsem}"))

    def _wait(self, x, sem, val):
        if self.seen[x].get(sem, 0) >= val:
            return
        self.E[x].wait_ge(sem, val)
        self.seen[x][sem] = val

    def _deps(self, x, reads, writes):
        deps = {}
        for b in reads:
            if b.w is not None:
                s, v = b.w
                if deps.get(s, 0) < v:
                    deps[s] = v
        for b in writes:
            if b.w is not None:
                s, v = b.w
                if deps.get(s, 0) < v:
                    deps[s] = v
            for s, v in b.r.items():
                if deps.get(s, 0) < v:
                    deps[s] = v
        for e, (pr, pw) in self.pend.items():
            if e == x or (not pr and not pw):
                continue
            for b in writes:
                assert all(b is not o for o in pr) and all(b is not o for o in pw), f"pending conflict {b.name}"
            for b in reads:
                assert all(b is not o for o in pw), f"pending conflict {b.name}"
        own = self.sem.get(x)
        for s, v in deps.items():
            if s is own and (x == "pe" or not SAME_ENGINE_SYNC):
                continue
            self._wait(x, s, v)

    def op(self, x, fn, reads=(), writes=(), sig=True):
        self._deps(x, reads, writes)
        ins = fn(self.E[x])
        self.ninstr += 1
        pr, pw = self.pend[x]
        pr.extend(reads)
        pw.extend(writes)
        if sig:
            if self.cnt[x] >= self.ROT:
                self.sem[x] = self._newsem("s_" + x)
                self.cnt[x] = 0
            self.cnt[x] += 1
            s = self.sem[x]
            v = self.cnt[x]
            ins.then_inc(s, 1)
            self.alltoks[s] = v
            for b in pr:
                if b.r.get(s, 0) < v:
                    b.r[s] = v
            for b in pw:
                b.w = (s, v)
                b.r = {}
            self.pend[x] = ([], [])
        return ins

    def dma(self, q, out, in_, reads=(), writes=(), **kw):
        self._deps(q, reads, writes)
        st = self.dq[q]
        k = st["n"] % self.NDS
        st["n"] += 1
        sem = st["sems"][k]
        if st["vals"][k] > 0:
            self._wait(q, sem, st["vals"][k])
        ins = self.E[q].dma_start(out=out, in_=in_, **kw)
        self.ninstr += 1
        st["vals"][k] += 16
        v = st["vals"][k]
        ins.then_inc(sem, 16)
        self.alltoks[sem] = v
        for b in reads:
            if b.r.get(sem, 0) < v:
                b.r[sem] = v
        for b in writes:
            b.w = (sem, v)
            b.r = {}
        return ins

    def barrier(self):
        for e, (pr, pw) in self.pend.items():
            assert not pr and not pw, "pending unsignaled ops at barrier"
        for x in self.E:
            own = self.sem.get(x)
            for s, v in self.alltoks.items():
                if s is own and x == "pe":
                    continue
                self._wait(x, s, v)


def _rope_tables():
    rows = SEQ // 64
    row_idx = np.repeat(np.arange(rows), 64).astype(np.float32)
    col_idx = (np.arange(SEQ) % 64).astype(np.float32)

    def ang(dim):
        q = dim // 4
        inv = (np.float32(10000.0) ** (-np.arange(q, dtype=np.float32) / np.float32(q))).astype(np.float32)
        a = np.concatenate([row_idx[:, None] * inv, col_idx[:, None] * inv], axis=-1)
        return a.astype(np.float32)

    ag = ang(64)
    ad = ang(32)
    cg = np.concatenate([np.cos(ag), np.cos(ag)], axis=1).T
    sg = np.concatenate([-np.sin(ag), np.sin(ag)], axis=1).T
    cd = np.concatenate([np.cos(ad), np.cos(ad)], axis=1).T
    sd = np.concatenate([-np.sin(ad), np.sin(ad)], axis=1).T
    cd = np.tile(cd, (4, 1))
    sd = np.tile(sd, (4, 1))
    return (np.ascontiguousarray(cg, np.float32), np.ascontiguousarray(sg, np.float32),
            np.ascontiguousarray(cd, np.float32), np.ascontiguousarray(sd, np.float32))


K_ID, K_UINC, K_LINC, K_NEGF, K_NEGB, K_ONES, K_SWAP, K_BD64, K_SELW, NCONST = 0, 1, 2, 3, 4, 5, 6, 7, 8, 9


def _const_mats():
    i = np.arange(128)
    m = np.zeros((NCONST, 128, 128), np.float32)
    m[K_ID] = np.eye(128)
    m[K_UINC] = (i[:, None] <= i[None, :])
    m[K_LINC] = (i[:, None] >= i[None, :])
    m[K_NEGF] = np.where(i[None, :] < i[:, None], NEG, 0.0)
    m[K_NEGB] = np.where(i[None, :] > i[:, None], NEG, 0.0)
    m[K_ONES] = 1.0
    m[K_SWAP] = (i[:, None] == ((i[None, :] + 64) % 128))
    bd = np.zeros((128, 128), np.float32)
    bd[:64, :64] = 1.0 / 64
    bd[64:, 64:] = 1.0 / 64
    m[K_BD64] = bd
    sel = np.zeros((128, 128), np.float32)
    m[K_SELW] = sel
    return np.ascontiguousarray(m.transpose(1, 0, 2))


def _feat(v):
    v = np.asarray(v, np.float32)
    return np.ascontiguousarray(v.reshape(-1, 128).T)


class Tl:
    def __init__(self, h, name):
        self.h = h
        self.b = Buf(name)

    def __getitem__(self, k):
        return self.h[k]


def bc(ap, shape):
    return ap.broadcast_to(list(shape))


def build(dbg=(), stop_after=None, layers=DEPTH):
    nc = bass.Bass("TRN2", target_bir_lowering=False)
    dbg = set(dbg)

    def din(name, shape, dt=F32):
        return Tl(nc.dram_tensor(name, list(shape), dt, kind="ExternalInput").ap(), name)

    def dscr(name, shape, dt=BF16):
        kind = "ExternalOutput" if name in dbg else "Internal"
        return Tl(nc.dram_tensor(name, list(shape), dt, kind=kind).ap(), name)

    x_d = din("x", [SEQ, D])
    ctx_d = din("ctx", [CTX, D])
    cvec_d = din("cvec", [128, 16])
    wmod_d = din("w_mod", [DEPTH, D, 3 * D])
    bmodf_d = din("bmodf", [DEPTH, 128, 16])
    bmod_d = din("b_mod", [DEPTH, 3 * D])
    gpref_d = din("gpref", [DEPTH, 128, 8])
    gpost_d = din("g_post", [DEPTH, D])
    win_d = din("w_in", [DEPTH, D, INC])
    wrot_d = din("w_rot", [DEPTH, D, NROT])
    wout_d = din("w_out", [DEPTH, D, D])
    convw_d = din("convw_f", [DEPTH, 128, 8, 5])
    convb_d = din("convb_f", [DEPTH, 128, 8])
    alog_d = din("alog", [DEPTH, 16])
    dtb_d = din("dtb", [DEPTH, 16])
    dskip_d = din("d_skip", [DEPTH, 8])
    ssdg_d = din("ssd_norm_g", [DEPTH, 512])
    qkg_d = din("qkg_f", [DEPTH, 64, 4])
    dlam_d = din("diff_lambda", [DEPTH, 128])
    dng_d = din("dng_f", [DEPTH, 128, 1])
    cmat_d = din("cmat", [128, NCONST, 128])
    identb_d = din("identb", [128, 128], BF16)
    ropeg_d = din("rope_g", [2, 64, SEQ])
    roped_d = din("rope_d", [2, 128, SEQ])
    out_d = Tl(nc.dram_tensor("out", [SEQ, D], F32, kind="ExternalOutput").ap(), "out")

    gt_s = dscr("gt_s", [DEPTH, 2, D], F32)
    h1_s = dscr("h1_s", [NT, D], F32)
    xs_s = dscr("xs_s", [NT, 512])
    bt_s = dscr("bt_s", [NT, 256])
    bT_s = dscr("bT_s", [256, NT])
    cT_s = dscr("cT_s", [256, NT])
    zs_s = dscr("zs_s", [NT, 512])
    dt_s = dscr("dt_s", [NT, 16], F32)
    gqT_s = dscr("gqT_s", [4, 64, NT])
    gkT_s = dscr("gkT_s", [2, 64, NT])
    gv_s = dscr("gv_s", [NT, 2, 192])
    ggT_s = dscr("ggT_s", [256, NT])
    dqT_s = dscr("dqT_s", [256, NT])
    dkT_s = dscr("dkT_s", [256, NT])
    dv_s = dscr("dv_s", [NT, 4, 192])
    dgT_s = dscr("dgT_s", [256, NT])
    ycT_s = dscr("ycT_s", [D, NT])
    uT_dbg = dscr("uT_dbg", [D, NT]) if "uT_dbg" in dbg else None
    mod_dbg = dscr("mod_dbg", [128, 32], F32) if "mod_dbg" in dbg else None

    with ExitStack() as es:
        tk = T(nc, es)

        uniq = [0]

        def sb(st, name, shape, dt=F32):
            uniq[0] += 1
            name = f"{name}_s{uniq[0]}"
            return Tl(st.enter_context(nc.sbuf_tensor(name, list(shape), dt)), name)

        def ps(st, name, shape, dt=F32):
            uniq[0] += 1
            name = f"{name}_p{uniq[0]}"
            return Tl(st.enter_context(nc.psum_tensor(name, list(shape), dt)), name)

        cm = sb(es, "cm", [128, NCONST, 128])
        identb = sb(es, "identb_sb", [128, 128], BF16)
        A1L = [sb(es, f"A1_{i}", [128, 8, 2]) for i in range(DEPTH)]
        SHL = [sb(es, f"SH_{i}", [128, 8, 2]) for i in range(DEPTH)]
        Ssil = sb(es, "Ssil", [128, 8, 2])
        tk.dma("sp", cm[:], cmat_d[:], reads=[cmat_d.b], writes=[cm.b])
        tk.dma("sp", identb[:], identb_d[:], reads=[identb_d.b], writes=[identb.b])

        epsc = sb(es, "epsc", [128, 1])
        tk.op("pool", lambda e: e.memset(epsc[:], EPS), writes=[epsc.b])

        def CM(k):
            return cm[:, k, :]

        ones_t = sb(es, "ones_t", [128, 768], BF16)
        tk.op("dve", lambda e: e.memset(ones_t[:], 1.0), writes=[ones_t.b])
        for (dst, nh) in ((gv_s, 2), (dv_s, 4)):
            tk.dma("sp", dst[:].rearrange("(c p) g w -> p c (g w)", p=128),
                   bc(ones_t[:, 0:nh * 192].unsqueeze(1), [128, NTILE, nh * 192]), reads=[ones_t.b], writes=[dst.b])

        def phase_M(l, q):
            A1, SH, S = A1L[l], SHL[l], Ssil
            with ExitStack() as ph:
                bmf = sb(ph, "bmf", [128, 16])
                gpf = sb(ph, "gpf", [128, 8])
                bgt = sb(ph, "bgt", [2, D])
                modT = sb(ph, "modT", [128, 16, 2])
                gtr = sb(ph, "gtr", [2, D])
                wb = [sb(ph, f"wmb{i}", [128, 8, 512]) for i in range(2)]
                psM = ps(ph, "psM", [128, 16, 2])
                psG = [ps(ph, f"psG{i}", [2, 512]) for i in range(2)]
                if l == 0:
                    cv = sb(ph, "cv", [128, 16])
                    tk.dma(q, cv[:], cvec_d[:], reads=[cvec_d.b], writes=[cv.b])
                    for w in range(2):
                        tk.op("act", lambda e, w=w: e.activation(out=S[:, :, w], in_=cv[:, 8 * w:8 * w + 8], func=AF.Silu),
                              reads=[cv.b], writes=[S.b])
                tk.dma(q, bmf[:], bmodf_d[l], reads=[bmodf_d.b], writes=[bmf.b])
                tk.dma(q, gpf[:], gpref_d[l], reads=[gpref_d.b], writes=[gpf.b])
                tk.dma(q, bgt[:], bmod_d[l, 2 * D:3 * D].partition_broadcast(2), reads=[bmod_d.b], writes=[bgt.b])
                wv = wmod_d[l].rearrange("(kc p) n -> p kc n", p=128)
                yield
                for blk in range(6):
                    w_ = wb[blk % 2]
                    tk.dma(q, w_[:], wv[:, :, blk * 512:(blk + 1) * 512], reads=[wmod_d.b], writes=[w_.b])
                    if blk < 4:
                        for oc in range(4):
                            for kc in range(8):
                                tk.op("pe", lambda e, oc=oc, kc=kc, w_=w_, blk=blk: e.matmul(
                                    psM[:, blk * 4 + oc, :], lhsT=w_[:, kc, oc * 128:(oc + 1) * 128], rhs=S[:, kc, :],
                                    start=(kc == 0), stop=(kc == 7)),
                                    reads=[w_.b, S.b], writes=[psM.b], sig=(kc == 7 and oc == 3))
                    else:
                        g = psG[blk - 4]
                        for kc in range(8):
                            tk.op("pe", lambda e, kc=kc, w_=w_, g=g: e.matmul(
                                g[:, :], lhsT=S[:, kc, :], rhs=w_[:, kc, :], start=(kc == 0), stop=(kc == 7)),
                                reads=[w_.b, S.b], writes=[g.b], sig=(kc == 7))
                    yield
                tk.op("dve", lambda e: e.tensor_tensor(out=modT[:], in0=psM[:], in1=bc(bmf[:].unsqueeze(2), [128, 16, 2]),
                                                       op=ALU.add), reads=[psM.b, bmf.b], writes=[modT.b])
                tk.op("dve", lambda e: e.tensor_copy(out=SH[:], in_=modT[:, 0:8, :]), reads=[modT.b], writes=[SH.b])
                tk.op("dve", lambda e: e.scalar_tensor_tensor(out=A1[:], in0=modT[:, 8:16, :], scalar=1.0,
                                                              in1=bc(gpf[:].unsqueeze(2), [128, 8, 2]),
                                                              op0=ALU.add, op1=ALU.mult),
                      reads=[modT.b, gpf.b], writes=[A1.b])
                for i in range(2):
                    tk.op("dve", lambda e, i=i: e.tensor_tensor(out=gtr[:, i * 512:(i + 1) * 512], in0=psG[i][:, :],
                                                                in1=bgt[:, i * 512:(i + 1) * 512], op=ALU.add),
                          reads=[psG[i].b, bgt.b], writes=[gtr.b])
                tk.dma("sp", gt_s[l], gtr[:], reads=[gtr.b], writes=[gt_s.b])
                if mod_dbg is not None and l == 0:
                    tk.dma("sp", mod_dbg[:, 0:16], A1[:].rearrange("p a b -> p (a b)"), reads=[A1.b], writes=[mod_dbg.b])
                    tk.dma("sp", mod_dbg[:, 16:32], SH[:].rearrange("p a b -> p (a b)"), reads=[SH.b], writes=[mod_dbg.b])
                yield

        def phase_N(l, uT, mnext=None, mid_hook=None):
            A1, SH = A1L[l], SHL[l]
            with ExitStack() as ph:
                if mnext is not None:
                    next(mnext)
                ht = [sb(ph, f"ht{i}", [128, D]) for i in range(2)]
                hb = [sb(ph, f"hb{i}", [128, D], BF16) for i in range(2)]
                junk = sb(ph, "junkN", [128, D], BF16)
                ssqs = [sb(ph, f"ssq{i}", [128, 1]) for i in range(2)]
                rstds = [sb(ph, f"rstd{i}", [128, 1]) for i in range(2)]
                tmpf = [sb(ph, f"tmpf{i}", [128, 8, 128]) for i in range(2)]
                pT = [ps(ph, f"pT{i}", [128, 8, 128], BF16) for i in range(2)]
                for tt in range(NTILE):
                    k = tt % 2
                    w = 1 if tt < 2 else 0
                    if l == 0:
                        src = ctx_d if tt < 2 else x_d
                        sap = ctx_d[tt * 128:(tt + 1) * 128, :] if tt < 2 else x_d[(tt - 2) * 128:(tt - 1) * 128, :]
                    else:
                        src = h1_s
                        sap = h1_s[tt * 128:(tt + 1) * 128, :]
                    tk.dma("sp", ht[k][:], sap, reads=[src.b], writes=[ht[k].b])
                    ssq = ssqs[k]
                    rstd = rstds[k]
                    tk.op("pool", lambda e: e.memset(ssq[:], 0.0), writes=[ssq.b])
                    tk.op("act", lambda e, k=k: e.activation(out=junk[:], in_=ht[k][:], func=AF.Square, accum_out=ssq[:, 0:1]),
                          reads=[ht[k].b], writes=[junk.b, ssq.b])
                    tk.op("act", lambda e: e.activation(out=rstd[:], in_=ssq[:], func=AF.Ln, scale=1.0 / D, bias=epsc[:, 0:1]),
                          reads=[ssq.b, epsc.b], writes=[rstd.b])
                    tk.op("act", lambda e: e.activation(out=rstd[:], in_=rstd[:], func=AF.Exp, scale=-0.5),
                          reads=[rstd.b], writes=[rstd.b])
                    tk.op("act", lambda e, k=k: e.activation(out=hb[k][:], in_=ht[k][:], func=AF.Copy, scale=rstd[:, 0:1]),
                          reads=[ht[k].b, rstd.b], writes=[hb[k].b])
                    for j in range(8):
                        tk.op("pe", lambda e, k=k, j=j: e.transpose(out=pT[k][:, j, :], in_=hb[k][:, j * 128:(j + 1) * 128],
                                                                    identity=identb[:]),
                              reads=[hb[k].b, identb.b], writes=[pT[k].b], sig=(j == 7))
                    tk.op("dve", lambda e, k=k, w=w: e.tensor_tensor(out=tmpf[k][:], in0=pT[k][:],
                                                                     in1=bc(A1[:, :, w:w + 1], [128, 8, 128]), op=ALU.mult),
                          reads=[pT[k].b, A1.b], writes=[tmpf[k].b])
                    tk.op("dve", lambda e, k=k, w=w, tt=tt: e.tensor_tensor(
                        out=uT[:, :, tt * 128:(tt + 1) * 128], in0=tmpf[k][:],
                        in1=bc(SH[:, :, w:w + 1], [128, 8, 128]), op=ALU.add),
                        reads=[tmpf[k].b, SH.b], writes=[uT.b])
                    if mnext is not None and tt % 3 == 2:
                        next(mnext)
                    if mid_hook is not None and tt == 9:
                        mid_hook()
                if mnext is not None:
                    next(mnext)
                if uT_dbg is not None and l == 0:
                    tk.dma("sp", uT_dbg[:].rearrange("(kc p) t -> p kc t", p=128), uT[:], reads=[uT.b], writes=[uT_dbg.b])
                tk.barrier()

        def make_I_loader(l, wr):
            NS = 4
            winv = win_d[l].rearrange("(kc p) n -> p kc n", p=128)
            wrotv = wrot_d[l].rearrange("(kc p) n -> p kc n", p=128)
            blocks = [
                [(winv, 0, 512)],
                [(winv, 512, 512)],
                [(winv, C_Z, 512)],
                [(winv, C_DT, 16), (winv, C_GV, 128), (winv, C_DV, 256)],
                [(winv, C_GG, 256), (winv, C_DG, 256)],
                [(winv, C_GQ, 384)],
                [(wrotv, R_GQ, 384)],
                [(winv, C_DQ, 512)],
                [(wrotv, R_DQ, 512)],
            ]

            def load_block(i):
                slot = wr[i % NS]
                o = 0
                for (v, c0, n) in blocks[i]:
                    src = win_d if v is winv else wrot_d
                    tk.dma("pool", slot[:, :, o:o + n], v[:, :, c0:c0 + n], reads=[src.b], writes=[slot.b])
                    o += n
                return slot


            return load_block

        TB = [(0, 256)] + [(256 + 512 * i, 512) for i in range(4)]

        def phase_I(l, uT, wr, slots):
            with ExitStack() as ph:
                NS = 4
                cs = [sb(ph, f"cs{i}", [128, 2312]) for i in range(2)]
                accs = [sb(ph, f"acc{i}", [128, NT]) for i in range(2)]
                xb = [sb(ph, f"xb{i}", [128, NT], BF16) for i in range(2)]
                xtoks = [sb(ph, f"xtok{i}", [128, NTILE, 128], BF16) for i in range(2)]
                xti = [0]
                rg = sb(ph, "rg", [64, 2, SEQ])
                rd = sb(ph, "rd", [128, 2, SEQ])
                zb = [sb(ph, f"zb{i}", [128, 512], BF16) for i in range(2)]
                vb = [sb(ph, f"vb{i}", [128, 384], BF16) for i in range(2)]
                dts = sb(ph, "dts", [128, NTILE, 16])
                sqs = sb(ph, "sqs", [128, 512])
                rs = sb(ph, "rs", [64, 512])
                t1 = sb(ph, "t1", [128, 512])
                t2 = sb(ph, "t2", [128, 512])
                cw = sb(ph, "cw", [128, 8, 5])
                cb = sb(ph, "cb", [128, 8])
                qkg = sb(ph, "qkg", [64, 4])
                pf = [ps(ph, f"pf{i}", [128, 512]) for i in range(4)]
                pm = ps(ph, "pm", [128, 512])
                ptr = ps(ph, "ptr", [128, 4, 128], BF16)
                ptm = [ps(ph, f"ptm{i}", [128, 512]) for i in range(2)]
                pfi = [0]

                def nextpf():
                    p = pf[pfi[0] % 4]
                    pfi[0] += 1
                    return p

                tk.dma("sp", rg[:], ropeg_d[:].rearrange("a p t -> p a t"), reads=[ropeg_d.b], writes=[rg.b])
                tk.dma("sp", rd[:], roped_d[:].rearrange("a p t -> p a t"), reads=[roped_d.b], writes=[rd.b])
                tk.dma("sp", cw[:], convw_d[l], reads=[convw_d.b], writes=[cw.b])
                tk.dma("sp", cb[:], convb_d[l], reads=[convb_d.b], writes=[cb.b])
                tk.dma("sp", qkg[:], qkg_d[l], reads=[qkg_d.b], writes=[qkg.b])
                for c in cs:
                    tk.op("pool", lambda e, c=c: e.memset(c[:], 0.0), writes=[c.b])
                tk.op("pool", lambda e: e.memset(sqs[:], 0.0), writes=[sqs.b])

                load_block = make_I_loader(l, wr)

                def prefetch(i):
                    slots[i] = load_block(i)

                def fm_matmuls(psb, w_, off, m, t0, n):
                    for kc in range(8):
                        tk.op("pe", lambda e, kc=kc: e.matmul(psb[0:m, 0:n], lhsT=w_[:, kc, off:off + m],
                                                               rhs=uT[:, kc, t0:t0 + n], start=(kc == 0), stop=(kc == 7)),
                              reads=[w_.b, uT.b], writes=[psb.b], sig=(kc == 7))

                def transposes_to(xbt, dst_d, c0):
                    xtok = xtoks[xti[0] % 2]
                    xti[0] += 1
                    for g0 in range(0, NTILE, 4):
                        gn = min(4, NTILE - g0)
                        for i in range(gn):
                            tt = g0 + i
                            tk.op("pe", lambda e, i=i, tt=tt: e.transpose(out=ptr[:, i, :], in_=xbt[:, tt * 128:(tt + 1) * 128],
                                                                          identity=identb[:]),
                                  reads=[xbt.b, identb.b], writes=[ptr.b], sig=(i == gn - 1))
                        tk.op("act", lambda e, g0=g0, gn=gn: e.activation(out=xtok[:, g0:g0 + gn, :], in_=ptr[:, 0:gn, :], func=AF.Copy),
                              reads=[ptr.b], writes=[xtok.b])
                    tk.dma("sp", dst_d[:].rearrange("(tt p) f -> p tt f", p=128)[:, :, c0:c0 + 128], xtok[:],
                           reads=[xtok.b], writes=[dst_d.b])

                def A_P(j):
                    if j == 0:
                        prefetch(3)
                    if j == 4:
                        prefetch(4)
                    w_ = slots[j // 4]
                    c_ = cs[j % 2]
                    for (t0, n) in TB:
                        p = nextpf()
                        fm_matmuls(p, w_, (j % 4) * 128, 128, t0, n)
                        d0 = 2 + t0 if t0 < 256 else 262 + (t0 - 256)
                        tk.op("act", lambda e, p=p, d0=d0, n=n, c_=c_: e.activation(out=c_[:, d0:d0 + n], in_=p[:, 0:n], func=AF.Copy),
                              reads=[p.b], writes=[c_.b])

                def A_C(j):
                    c_ = cs[j % 2]
                    acc = accs[j % 2]
                    for (eng, s0, a0, n) in (("dve", 2, 0, 256), ("dve", 262, 256, 2048)):
                        tk.op(eng, lambda e, s0=s0, a0=a0, n=n: e.tensor_scalar(
                            out=acc[:, a0:a0 + n], in0=c_[:, s0 - 2:s0 - 2 + n], scalar1=cw[:, j, 0:1], scalar2=cb[:, j:j + 1],
                            op0=ALU.mult, op1=ALU.add), reads=[c_.b, cw.b, cb.b], writes=[acc.b])
                        for k in range(1, 5):
                            tk.op(eng, lambda e, s0=s0, a0=a0, n=n, k=k: e.scalar_tensor_tensor(
                                out=acc[:, a0:a0 + n], in0=c_[:, s0 - 2 + k:s0 - 2 + k + n], scalar=cw[:, j, k:k + 1],
                                in1=acc[:, a0:a0 + n], op0=ALU.mult, op1=ALU.add), reads=[c_.b, cw.b, acc.b], writes=[acc.b])

                def A_F(j):
                    acc = accs[j % 2]
                    xbt = xb[j % 2]
                    tk.op("act", lambda e: e.activation(out=xbt[:], in_=acc[:], func=AF.Silu), reads=[acc.b], writes=[xbt.b])
                    if j < 4:
                        transposes_to(xbt, xs_s, j * 128)
                    elif j < 6:
                        tk.dma("sp", bT_s[(j - 4) * 128:(j - 3) * 128, :], xbt[:], reads=[xbt.b], writes=[bT_s.b])
                        transposes_to(xbt, bt_s, (j - 4) * 128)
                    else:
                        tk.dma("sp", cT_s[(j - 6) * 128:(j - 5) * 128, :], xbt[:], reads=[xbt.b], writes=[cT_s.b])

                A_P(0)
                A_P(1)
                A_C(0)
                for j in range(8):
                    A_F(j)
                    if j + 2 < 8:
                        A_P(j + 2)
                    if j + 1 < 8:
                        A_C(j + 1)

                prefetch(5)
                wz = slots[2]
                wt = slots[3]
                for tt in range(NTILE):
                    k = tt % 2
                    for (w_, n, p) in ((wz, 512, ptm[0]), (wt, 400, ptm[1])):
                        for kc in range(8):
                            tk.op("pe", lambda e, kc=kc, w_=w_, n=n, p=p, tt=tt: e.matmul(
                                p[:, 0:n], lhsT=uT[:, kc, tt * 128:(tt + 1) * 128], rhs=w_[:, kc, 0:n],
                                start=(kc == 0), stop=(kc == 7)), reads=[w_.b, uT.b], writes=[p.b], sig=(kc == 7))
                    tk.op("act", lambda e, k=k: e.activation(out=zb[k][:], in_=ptm[0][:], func=AF.Silu),
                          reads=[ptm[0].b], writes=[zb[k].b])
                    tk.dma("sp", zs_s[tt * 128:(tt + 1) * 128, :], zb[k][:], reads=[zb[k].b], writes=[zs_s.b])
                    tk.op("dve", lambda e, tt=tt: e.tensor_copy(out=dts[:, tt, :], in_=ptm[1][:, 0:16]),
                          reads=[ptm[1].b], writes=[dts.b])
                    tk.op("dve", lambda e, k=k: e.tensor_copy(out=vb[k][:], in_=ptm[1][:, 16:400]),
                          reads=[ptm[1].b], writes=[vb[k].b])
                    tk.dma("sp", gv_s[tt * 128:(tt + 1) * 128, :, 64:128], vb[k][:, 0:128].rearrange("p (g d) -> p g d", g=2),
                           reads=[vb[k].b], writes=[gv_s.b])
                    tk.dma("sp", dv_s[tt * 128:(tt + 1) * 128, :, 64:128], vb[k][:, 128:384].rearrange("p (g d) -> p g d", g=4),
                           reads=[vb[k].b], writes=[dv_s.b])
                tk.dma("sp", dt_s[:].rearrange("(tt p) f -> p tt f", p=128), dts[:], reads=[dts.b], writes=[dt_s.b])

                prefetch(6)
                prefetch(7)
                wg = slots[4]
                for j in range(4):
                    xbt = xb[j % 2]
                    for (t0, n) in TB:
                        p = nextpf()
                        fm_matmuls(p, wg, j * 128, 128, t0, n)
                        tk.op("act", lambda e, p=p, t0=t0, n=n, xbt=xbt: e.activation(out=xbt[:, t0:t0 + n], in_=p[:, 0:n], func=AF.Silu),
                              reads=[p.b], writes=[xbt.b])
                    dst = ggT_s if j < 2 else dgT_s
                    tk.dma("sp", dst[(j % 2) * 128:(j % 2 + 1) * 128, :], xbt[:], reads=[xbt.b], writes=[dst.b])

                prefetch(8)
                wq = slots[5]
                wqr = slots[6]
                for hh in range(6):
                    gcol = 0 if hh < 4 else 2
                    xbt = xb[hh % 2]
                    for (t0, n) in TB:
                        pa = nextpf()
                        fm_matmuls(pa, wq, hh * 64, 64, t0, n)
                        lat = t0 >= 256
                        if lat:
                            pb = nextpf()
                            fm_matmuls(pb, wqr, hh * 64, 64, t0, n)
                        tk.op("act", lambda e, pa=pa, n=n: e.activation(out=sqs[0:64, 0:n], in_=pa[0:64, 0:n], func=AF.Square),
                              reads=[pa.b], writes=[sqs.b])
                        tk.op("pe", lambda e, n=n: e.matmul(pm[0:64, 0:n], lhsT=cm[:, K_BD64, 0:64], rhs=sqs[:, 0:n],
                                                            start=True, stop=True), reads=[cm.b, sqs.b], writes=[pm.b])
                        tk.op("act", lambda e, n=n: e.activation(out=rs[:, 0:n], in_=pm[0:64, 0:n], func=AF.Ln, bias=epsc[0:64, 0:1]),
                              reads=[pm.b, epsc.b], writes=[rs.b])
                        tk.op("act", lambda e, n=n: e.activation(out=rs[:, 0:n], in_=rs[:, 0:n], func=AF.Exp, scale=-0.5),
                              reads=[rs.b], writes=[rs.b])
                        if lat:
                            r0 = t0 - 256
                            tk.op("dve", lambda e, pa=pa, n=n, gcol=gcol: e.scalar_tensor_tensor(
                                out=t1[0:64, 0:n], in0=pa[0:64, 0:n], scalar=qkg[:, gcol:gcol + 1], in1=rs[:, 0:n],
                                op0=ALU.mult, op1=ALU.mult), reads=[pa.b, qkg.b, rs.b], writes=[t1.b])
                            tk.op("dve", lambda e, pb=pb, n=n, gcol=gcol: e.scalar_tensor_tensor(
                                out=t2[0:64, 0:n], in0=pb[0:64, 0:n], scalar=qkg[:, gcol + 1:gcol + 2], in1=rs[:, 0:n],
                                op0=ALU.mult, op1=ALU.mult), reads=[pb.b, qkg.b, rs.b], writes=[t2.b])
                            tk.op("pool", lambda e, n=n, r0=r0: e.tensor_tensor(out=t1[0:64, 0:n], in0=t1[0:64, 0:n],
                                                                                in1=rg[:, 0, r0:r0 + n], op=ALU.mult),
                                  reads=[t1.b, rg.b], writes=[t1.b])
                            tk.op("dve", lambda e, n=n, r0=r0: e.tensor_tensor(out=t2[0:64, 0:n], in0=t2[0:64, 0:n],
                                                                               in1=rg[:, 1, r0:r0 + n], op=ALU.mult),
                                  reads=[t2.b, rg.b], writes=[t2.b])
                            tk.op("dve", lambda e, n=n, t0=t0, xbt=xbt: e.tensor_tensor(out=xbt[0:64, t0:t0 + n], in0=t1[0:64, 0:n],
                                                                                        in1=t2[0:64, 0:n], op=ALU.add),
                                  reads=[t1.b, t2.b], writes=[xbt.b])
                        else:
                            tk.op("dve", lambda e, pa=pa, n=n, gcol=gcol, t0=t0, xbt=xbt: e.scalar_tensor_tensor(
                                out=xbt[0:64, t0:t0 + n], in0=pa[0:64, 0:n], scalar=qkg[:, gcol:gcol + 1], in1=rs[:, 0:n],
                                op0=ALU.mult, op1=ALU.mult), reads=[pa.b, qkg.b, rs.b], writes=[xbt.b])
                    dst = gqT_s[hh] if hh < 4 else gkT_s[hh - 4]
                    dstb = gqT_s.b if hh < 4 else gkT_s.b
                    tk.dma("sp", dst, xbt[0:64, :], reads=[xbt.b], writes=[dstb])

                wd = slots[7]
                wdr = slots[8]
                for j in range(4):
                    xbt = xb[j % 2]
                    for (t0, n) in TB:
                        pa = nextpf()
                        fm_matmuls(pa, wd, j * 128, 128, t0, n)
                        if t0 >= 256:
                            pb = nextpf()
                            fm_matmuls(pb, wdr, j * 128, 128, t0, n)
                            r0 = t0 - 256
                            tk.op("dve", lambda e, pa=pa, n=n, r0=r0: e.tensor_tensor(out=t1[:, 0:n], in0=pa[:, 0:n],
                                                                                      in1=rd[:, 0, r0:r0 + n], op=ALU.mult),
                                  reads=[pa.b, rd.b], writes=[t1.b])
                            tk.op("dve", lambda e, pb=pb, n=n, r0=r0: e.tensor_tensor(out=t2[:, 0:n], in0=pb[:, 0:n],
                                                                                      in1=rd[:, 1, r0:r0 + n], op=ALU.mult),
                                  reads=[pb.b, rd.b], writes=[t2.b])
                            tk.op("pool", lambda e, n=n, t0=t0, xbt=xbt: e.tensor_tensor(out=xbt[:, t0:t0 + n], in0=t1[:, 0:n],
                                                                                         in1=t2[:, 0:n], op=ALU.add),
                                  reads=[t1.b, t2.b], writes=[xbt.b])
                        else:
                            tk.op("act", lambda e, pa=pa, n=n, t0=t0, xbt=xbt: e.activation(out=xbt[:, t0:t0 + n], in_=pa[:, 0:n], func=AF.Copy),
                                  reads=[pa.b], writes=[xbt.b])
                    dst = dqT_s if j < 2 else dkT_s
                    tk.dma("sp", dst[(j % 2) * 128:(j % 2 + 1) * 128, :], xbt[:], reads=[xbt.b], writes=[dst.b])
                tk.barrier()

        def phase_S(l):
            with ExitStack() as ph:
                xs = sb(ph, "xs", [128, NTILE, 512], BF16)
                btok = sb(ph, "btok", [128, NTILE, 256], BF16)
                bT = sb(ph, "bT", [128, 2, NT], BF16)
                cT = sb(ph, "cT", [128, 2, NT], BF16)
                zs = sb(ph, "zs", [128, NTILE, 512], BF16)
                dtr = sb(ph, "dtr", [128, NTILE, 16])
                alog = sb(ph, "alog", [128, 2, 8])
                dtb = sb(ph, "dtbb", [128, 2, 8])
                dsk = sb(ph, "dsk", [128, 8])
                ng = sb(ph, "ng", [128, 512])
                Abc = sb(ph, "Abc", [128, 2, 8])
                one = sb(ph, "one", [128, 1])
                SH4 = [128, 2, NTILE, 8]
                tmp = sb(ph, "s_tmp", SH4)
                dtv = sb(ph, "dtv", SH4)
                dtA = sb(ph, "dtA", SH4)
                negcum = sb(ph, "negcum", SH4)
                ecum = sb(ph, "ecum", SH4)
                etot = sb(ph, "etot", SH4)
                dec = sb(ph, "dec", SH4)
                wdec = sb(ph, "wdec", SH4)
                lnb = sb(ph, "lnb", SH4)
                IDd = sb(ph, "IDd", [128, 8, 128], BF16)
                sinb = sb(ph, "sinb", [128, NTILE, 512], BF16)
                ycs = sb(ph, "ycs", [128, 4, NT], BF16)
                fl = lambda t: t[:].rearrange("p a b c -> p (a b c)")

                tk.dma("sp", dtr[:], dt_s[:].rearrange("(c p) f -> p c f", p=128), reads=[dt_s.b], writes=[dtr.b])
                tk.dma("sp", alog[:].rearrange("p a b -> p (a b)"), alog_d[l].partition_broadcast(128), reads=[alog_d.b], writes=[alog.b])
                tk.dma("sp", dtb[:].rearrange("p a b -> p (a b)"), dtb_d[l].partition_broadcast(128), reads=[dtb_d.b], writes=[dtb.b])
                tk.dma("sp", dsk[:], dskip_d[l].partition_broadcast(128), reads=[dskip_d.b], writes=[dsk.b])
                tk.dma("sp", xs[:], xs_s[:].rearrange("(c p) f -> p c f", p=128), reads=[xs_s.b], writes=[xs.b])
                tk.dma("sp", btok[:], bt_s[:].rearrange("(c p) f -> p c f", p=128), reads=[bt_s.b], writes=[btok.b])
                tk.dma("sp", bT[:], bT_s[:].rearrange("(g p) t -> p g t", p=128), reads=[bT_s.b], writes=[bT.b])
                tk.dma("sp", cT[:], cT_s[:].rearrange("(g p) t -> p g t", p=128), reads=[cT_s.b], writes=[cT.b])
                tk.dma("sp", zs[:], zs_s[:].rearrange("(c p) f -> p c f", p=128), reads=[zs_s.b], writes=[zs.b])
                tk.dma("sp", ng[:], ssdg_d[l].partition_broadcast(128), reads=[ssdg_d.b], writes=[ng.b])
                tk.op("pool", lambda e: e.memset(one[:], 1.0), writes=[one.b])

                if S_LEVEL[0] == -1:
                    tk.barrier()
                    return
                with ExitStack() as s1:
                    pcum = ps(s1, "pcum", [128, 2, NTILE * 8])
                    ptot = ps(s1, "ptot", [128, 2 * NTILE * 8])
                    tk.op("act", lambda e: e.activation(out=Abc[:], in_=alog[:], func=AF.Exp), reads=[alog.b], writes=[Abc.b])
                    tk.op("dve", lambda e: e.tensor_scalar(out=Abc[:], in0=Abc[:], scalar1=-1.0, scalar2=None, op0=ALU.mult),
                          reads=[Abc.b], writes=[Abc.b])
                    for d in range(2):
                        tk.op("dve", lambda e, d=d: e.tensor_tensor(out=tmp[:, d], in0=dtr[:, :, d * 8:(d + 1) * 8],
                                                                    in1=bc(dtb[:, d:d + 1, :], [128, NTILE, 8]), op=ALU.add),
                              reads=[dtr.b, dtb.b], writes=[tmp.b])
                    tk.op("act", lambda e: e.activation(out=fl(tmp), in_=fl(tmp), func=AF.Exp), reads=[tmp.b], writes=[tmp.b])
                    tk.op("act", lambda e: e.activation(out=fl(dtv), in_=fl(tmp), func=AF.Ln, bias=one[:, 0:1]),
                          reads=[tmp.b, one.b], writes=[dtv.b])
                    tk.op("act", lambda e: e.activation(out=fl(lnb), in_=fl(dtv), func=AF.Ln), reads=[dtv.b], writes=[lnb.b])
                    for h in range(8):
                        tk.op("dve", lambda e, h=h: e.tensor_scalar(out=IDd[:, h, :], in0=identb[:], scalar1=dsk[:, h:h + 1], scalar2=None,
                                                                    op0=ALU.mult), reads=[identb.b, dsk.b], writes=[IDd.b])
                    for d in range(2):
                        tk.op("dve", lambda e, d=d: e.tensor_tensor(out=dtA[:, d], in0=dtv[:, d],
                                                                    in1=bc(Abc[:, d:d + 1, :], [128, NTILE, 8]), op=ALU.mult),
                              reads=[dtv.b, Abc.b], writes=[dtA.b])
                    if S_LEVEL[0] == -2:
                        tk.barrier()
                        return
                    tk.op("pe", lambda e: e.matmul(pcum[:, 0, :], lhsT=CM(K_UINC), rhs=dtA[:, 0].rearrange("p b c -> p (b c)"),
                                                   start=True, stop=True), reads=[cm.b, dtA.b], writes=[pcum.b])
                    tk.op("pe", lambda e: e.matmul(pcum[:, 1, :], lhsT=CM(K_LINC), rhs=dtA[:, 1].rearrange("p b c -> p (b c)"),
                                                   start=True, stop=True), reads=[cm.b, dtA.b], writes=[pcum.b])
                    tk.op("pe", lambda e: e.matmul(ptot[:, :], lhsT=CM(K_ONES), rhs=fl(dtA), start=True, stop=True),
                          reads=[cm.b, dtA.b], writes=[ptot.b])
                    if S_LEVEL[0] == -3:
                        tk.barrier()
                        return
                    pcf = pcum[:].rearrange("p a n -> p (a n)")
                    tk.op("dve", lambda e: e.tensor_scalar(out=fl(negcum), in0=pcf, scalar1=-1.0, scalar2=None, op0=ALU.mult),
                          reads=[pcum.b], writes=[negcum.b])
                    tk.op("dve", lambda e: e.tensor_copy(out=fl(wdec), in_=ptot[:, :]), reads=[ptot.b], writes=[wdec.b])
                    tk.op("dve", lambda e: e.tensor_tensor(out=fl(lnb), in0=fl(lnb), in1=fl(negcum), op=ALU.add),
                          reads=[lnb.b, negcum.b], writes=[lnb.b])
                    if S_LEVEL[0] == -4:
                        tk.barrier()
                        return
                    tk.op("act", lambda e: e.activation(out=fl(ecum), in_=fl(negcum), func=AF.Exp, scale=-1.0),
                          reads=[negcum.b], writes=[ecum.b])
                    if S_LEVEL[0] == -5:
                        tk.barrier()
                        return
                    tk.op("act", lambda e: e.activation(out=fl(etot), in_=fl(wdec), func=AF.Exp), reads=[wdec.b], writes=[etot.b])
                    tk.op("dve", lambda e: e.tensor_tensor(out=fl(tmp), in0=fl(wdec), in1=fl(negcum), op=ALU.add),
                          reads=[wdec.b, negcum.b], writes=[tmp.b])
                    if S_LEVEL[0] == -6:
                        tk.barrier()
                        return
                    tk.op("act", lambda e: e.activation(out=fl(dec), in_=fl(tmp), func=AF.Exp), reads=[tmp.b], writes=[dec.b])
                    tk.op("dve", lambda e: e.tensor_tensor(out=fl(wdec), in0=fl(dtv), in1=fl(dec), op=ALU.mult),
                          reads=[dtv.b, dec.b], writes=[wdec.b])
                    tk.barrier()

                with ExitStack() as s2:
                    S = [sb(s2, f"Sst{i}", [128, 512]) for i in range(2)]
                    Sfb = sb(s2, "Sfb", [128, 512], BF16)
                    stmp = sb(s2, "stmp", [128, 512])
                    xdd = sb(s2, "xdd", [128, 512], BF16)
                    LTD = [[sb(s2, f"LT{i}g{q}", [128, 4, 128], BF16) for q in range(4)] for i in range(2)]
                    MTD = [sb(s2, f"MT{i}", [128, 16, 128], BF16) for i in range(4)]
                    xddD = [sb(s2, f"xddD{i}", [128, 512], BF16) for i in range(4)]
                    ta = sb(s2, "ta", [128, 512])
                    tb_ = sb(s2, "tb", [128, 512])
                    yz = sb(s2, "yz", [128, 512])
                    yns = [sb(s2, f"yn{i}", [128, 512], BF16) for i in range(2)]
                    junk = sb(s2, "junkS", [128, 512], BF16)
                    ssq = sb(s2, "ssqS", [128, 1])
                    rstd = sb(s2, "rstdS", [128, 1])
                    pL = [ps(s2, f"pL{i}", [128, 4, 128]) for i in range(2)]
                    pcb = ps(s2, "pcb", [128, 2, 128])
                    cbs = [sb(s2, f"cbs{i}", [128, 2, 128]) for i in range(2)]
                    py = ps(s2, "py", [128, 512])
                    pyo = [ps(s2, f"pyo{i}", [128, 512]) for i in range(2)]
                    pst = ps(s2, "pst", [128, 512])
                    pT = ps(s2, "pTS", [128, 4, 128], BF16)
                    v3 = lambda ap: ap.rearrange("p (h d) -> p h d", h=8)

                    def scaled_x(dst, c, w4, d):
                        tk.op("dve", lambda e: e.tensor_tensor(out=v3(dst[:]), in0=v3(xs[:, c, :]),
                                                               in1=bc(w4[:, d, c, :].unsqueeze(2), [128, 8, 64]), op=ALU.mult),
                              reads=[xs.b, w4.b], writes=[dst.b])

                    def state_step(c, d, xsrc):
                        for g in range(2):
                            tk.op("pe", lambda e, g=g: e.matmul(pst[:, g * 256:(g + 1) * 256], lhsT=btok[:, c, g * 128:(g + 1) * 128],
                                                                rhs=xsrc[:, g * 256:(g + 1) * 256], start=True, stop=True),
                                  reads=[btok.b, xsrc.b], writes=[pst.b], sig=(g == 1))
                        tk.op("dve", lambda e: e.tensor_tensor(out=v3(stmp[:]), in0=v3(S[d][:]),
                                                               in1=bc(etot[:, d, c, :].unsqueeze(2), [128, 8, 64]), op=ALU.mult),
                              reads=[S[d].b, etot.b], writes=[stmp.b])
                        tk.op("dve", lambda e: e.tensor_tensor(out=S[d][:], in0=pst[:, :], in1=stmp[:], op=ALU.add),
                              reads=[pst.b, stmp.b], writes=[S[d].b])

                    border = [1, 0] + list(range(NTILE - 1, 1, -1))
                    if S_LEVEL[0] == 1:
                        tk.barrier()
                        return

                    def stageA(c):
                        t0 = c * 128
                        k = c % 2
                        k3 = c % 4
                        LT, MT, cb_ = LTD[k], MTD[k3], cbs[k]

                        def pre():
                            scaled_x(xddD[k3], c, wdec, 0)
                            for g in range(2):
                                tk.op("pe", lambda e, g=g: e.matmul(pcb[:, g, :], lhsT=bT[:, g, t0:t0 + 128], rhs=cT[:, g, t0:t0 + 128],
                                                                    start=True, stop=True), reads=[bT.b, cT.b], writes=[pcb.b], sig=(g == 1))
                            tk.op("dve", lambda e: e.tensor_copy(out=cb_[:], in_=pcb[:]), reads=[pcb.b], writes=[cb_.b])

                        def grp(q4):
                            p = pL[q4 % 2]
                            d = q4 // 2
                            for i in range(4):
                                hd = q4 * 4 + i
                                lh = bc(dtA[:, d, c, hd % 8:hd % 8 + 1], [128, 128])
                                tk.op("pe", lambda e, p=p, i=i, lh=lh, d=d: e.matmul(
                                    p[:, i, :], lhsT=lh, rhs=CM(K_UINC if d == 0 else K_LINC), start=True, stop=False),
                                    reads=[dtA.b, cm.b], writes=[p.b], sig=False)
                                tk.op("pe", lambda e, p=p, i=i, d=d: e.matmul(
                                    p[:, i, :], lhsT=CM(K_ID), rhs=CM(K_NEGF if d == 0 else K_NEGB), start=False, stop=True),
                                    reads=[cm.b], writes=[p.b], sig=(i == 3))
                            for i in range(4):
                                hd = q4 * 4 + i
                                tk.op("act", lambda e, p=p, i=i, hd=hd, d=d: e.activation(
                                    out=LT[q4][:, i, :], in_=p[:, i, :], func=AF.Exp, bias=lnb[:, d, c, hd % 8:hd % 8 + 1]),
                                    reads=[p.b, lnb.b], writes=[LT[q4].b])
                            g = q4 % 2
                            tk.op("dve", lambda e, q4=q4, g=g: e.tensor_tensor(out=MT[:, q4 * 4:(q4 + 1) * 4, :], in0=LT[q4][:],
                                                                               in1=bc(cb_[:, g:g + 1, :], [128, 4, 128]), op=ALU.mult),
                                  reads=[LT[q4].b, cb_.b], writes=[MT.b])
                        return [pre] + [(lambda q4=q4: grp(q4)) for q4 in range(4)]

                    def stageB(c):
                        t0 = c * 128
                        k = c % 4
                        MT, xdd3 = MTD[k], xddD[k]
                        yn = yns[c % 2]

                        def s1():
                            for h in range(8):
                                xh = xs[:, c, h * 64:(h + 1) * 64]
                                tk.op("pe", lambda e, h=h, xh=xh: e.matmul(py[:, h * 64:(h + 1) * 64], lhsT=IDd[:, h, :], rhs=xh,
                                                                           start=True, stop=False), reads=[IDd.b, xs.b], writes=[py.b], sig=False)
                                tk.op("pe", lambda e, h=h, xh=xh: e.matmul(py[:, h * 64:(h + 1) * 64], lhsT=MT[:, h, :], rhs=xh,
                                                                           start=False, stop=False), reads=[MT.b, xs.b], writes=[py.b], sig=False)
                                tk.op("pe", lambda e, h=h, xh=xh: e.matmul(py[:, h * 64:(h + 1) * 64], lhsT=MT[:, 8 + h, :], rhs=xh,
                                                                           start=False, stop=True), reads=[MT.b, xs.b], writes=[py.b], sig=(h == 7))
                            for g in range(2):
                                tk.op("pe", lambda e, g=g: e.matmul(pyo[0][:, g * 256:(g + 1) * 256], lhsT=cT[:, g, t0:t0 + 128],
                                                                    rhs=Sfb[:, g * 256:(g + 1) * 256], start=True, stop=True),
                                      reads=[cT.b, Sfb.b], writes=[pyo[0].b], sig=(g == 1))
                            for g in range(2):
                                tk.op("pe", lambda e, g=g: e.matmul(pyo[1][:, g * 256:(g + 1) * 256], lhsT=cT[:, g, t0:t0 + 128],
                                                                    rhs=sinb[:, c, g * 256:(g + 1) * 256], start=True, stop=True),
                                      reads=[cT.b, sinb.b], writes=[pyo[1].b], sig=(g == 1))

                        def s2():
                            if c < NTILE - 1:
                                state_step(c, 0, xdd3)
                                tk.op("act", lambda e: e.activation(out=Sfb[:], in_=S[0][:], func=AF.Copy), reads=[S[0].b], writes=[Sfb.b])

                        def s3():
                            tk.op("dve", lambda e: e.tensor_tensor(out=v3(ta[:]), in0=v3(pyo[0][:, :]),
                                                                   in1=bc(ecum[:, 0, c, :].unsqueeze(2), [128, 8, 64]), op=ALU.mult),
                                  reads=[pyo[0].b, ecum.b], writes=[ta.b])
                            tk.op("dve", lambda e: e.tensor_tensor(out=v3(tb_[:]), in0=v3(pyo[1][:, :]),
                                                                   in1=bc(ecum[:, 1, c, :].unsqueeze(2), [128, 8, 64]), op=ALU.mult),
                                  reads=[pyo[1].b, ecum.b], writes=[tb_.b])

                        def s4():
                            tk.op("pool", lambda e: e.tensor_tensor(out=ta[:], in0=ta[:], in1=tb_[:], op=ALU.add),
                                  reads=[ta.b, tb_.b], writes=[ta.b])

                        def s5():
                            tk.op("dve", lambda e: e.tensor_tensor(out=yz[:], in0=py[:, :], in1=ta[:], op=ALU.add),
                                  reads=[py.b, ta.b], writes=[yz.b])

                        def s6():
                            tk.op("pool", lambda e: e.tensor_tensor(out=yz[:], in0=yz[:], in1=zs[:, c, :], op=ALU.mult),
                                  reads=[yz.b, zs.b], writes=[yz.b])
                            tk.op("pool", lambda e: e.memset(ssq[:], 0.0), writes=[ssq.b])

                        def s7():
                            tk.op("act", lambda e: e.activation(out=junk[:], in_=yz[:], func=AF.Square, accum_out=ssq[:, 0:1]),
                                  reads=[yz.b], writes=[junk.b, ssq.b])
                            tk.op("act", lambda e: e.activation(out=rstd[:], in_=ssq[:], func=AF.Ln, scale=1.0 / 512, bias=epsc[:, 0:1]),
                                  reads=[ssq.b, epsc.b], writes=[rstd.b])
                            tk.op("act", lambda e: e.activation(out=rstd[:], in_=rstd[:], func=AF.Exp, scale=-0.5),
                                  reads=[rstd.b], writes=[rstd.b])

                        def s8():
                            tk.op("dve", lambda e: e.scalar_tensor_tensor(out=yn[:], in0=yz[:], scalar=rstd[:, 0:1], in1=ng[:],
                                                                          op0=ALU.mult, op1=ALU.mult),
                                  reads=[yz.b, rstd.b, ng.b], writes=[yn.b])
                        return [s1, s2, s3, s4, s5, s6, s7, s8]

                    def stageB2(c):
                        t0 = c * 128
                        yn = yns[c % 2]
                        for j in range(4):
                            tk.op("pe", lambda e, j=j: e.transpose(out=pT[:, j, :], in_=yn[:, j * 128:(j + 1) * 128], identity=identb[:]),
                                  reads=[yn.b, identb.b], writes=[pT.b], sig=(j == 3))
                        tk.op("act", lambda e: e.activation(out=ycs[:, :, t0:t0 + 128], in_=pT[:], func=AF.Copy),
                              reads=[pT.b], writes=[ycs.b])

                    Aq = [f for c0 in range(3) for f in stageA(c0)]
                    tk.op("pool", lambda e: e.memset(S[1][:], 0.0), writes=[S[1].b])
                    tk.op("pool", lambda e: e.memset(S[0][:], 0.0), writes=[S[0].b])
                    tk.op("pool", lambda e: e.memset(Sfb[:], 0.0), writes=[Sfb.b])
                    for i, c in enumerate(border):
                        tk.op("act", lambda e, c=c: e.activation(out=sinb[:, c, :], in_=S[1][:], func=AF.Copy),
                              reads=[S[1].b], writes=[sinb.b])
                        if i == len(border) - 1:
                            break
                        scaled_x(xdd, c, wdec, 1)
                        state_step(c, 1, xdd)
                        if Aq:
                            Aq.pop(0)()
                    for f in Aq:
                        f()
                    for c in range(NTILE):
                        A = stageA(c + 3) if c + 3 < NTILE else [lambda: None] * 5
                        B = stageB(c)
                        for f in (A[0], B[0], A[1], B[1], B[2], A[2], B[3], B[4], A[3], B[5], B[6], A[4], B[7]):
                            f()
                        if c >= 1:
                            stageB2(c - 1)
                    stageB2(NTILE - 1)
                    tk.dma("sp", ycT_s[0:512, :].rearrange("(j p) t -> p j t", p=128), ycs[:], reads=[ycs.b], writes=[ycT_s.b])
                    tk.barrier()

        def attn_block(units, q0, nq, ktiles, pO, pS, PT, scale, stages=()):
            seq = [(kt, u) for kt in ktiles for u in range(len(units))]
            LAG = 2
            stages = list(stages)
            for i in range(len(seq) + LAG):
                if stages and i >= 3 and (i - 3) % 4 == 0:
                    stages.pop(0)()
                if i < len(seq):
                    kt, u = seq[i]
                    U = units[u]
                    sl = i % 3
                    tk.op("pe", lambda e, U=U, kt=kt, sl=sl: e.matmul(pS[sl][:, 0:nq], lhsT=U["k"][1](kt), rhs=U["q"][1](q0, nq),
                                                                      start=True, stop=True),
                          reads=[U["k"][0].b, U["q"][0].b], writes=[pS[sl].b])
                    tk.op("act", lambda e, sl=sl: e.activation(out=PT[sl][:, 0:nq], in_=pS[sl][:, 0:nq], func=AF.Exp, scale=scale),
                          reads=[pS[sl].b], writes=[PT[sl].b])
                j = i - LAG
                if j >= 0:
                    kt, u = seq[j]
                    U = units[u]
                    sl = j % 3
                    acc = pO[U["acc"]]
                    tk.op("pe", lambda e, U=U, kt=kt, sl=sl, acc=acc: e.matmul(acc[:, 0:nq], lhsT=U["v"][1](kt), rhs=PT[sl][:, 0:nq],
                                                                               start=(kt == ktiles[0]), stop=(kt == ktiles[-1])),
                          reads=[U["v"][0].b, PT[sl].b], writes=[acc.b])
            for st in stages:
                st()

        def qblocks(ctx_out):
            qb = [(256 + 512 * i, 512, list(range(NTILE))) for i in range(4)]
            if ctx_out:
                qb = [(0, 256, [0, 1])] + qb
            return qb

        def load_vaug(vaug, src, nh):
            sv = src[:].rearrange("(c p) g w -> p c (g w)", p=128)
            dv = vaug[:].rearrange("p c g w -> p c (g w)")
            for c0 in range(0, NTILE, 6):
                tk.dma("sp", dv[:, c0:c0 + 6, :], sv[:, c0:c0 + 6, :], reads=[src.b], writes=[vaug.b])

        def normalize_stages(pa, pb, nq, OS, SS, pR, rec):
            def s1():
                tk.op("dve", lambda e: e.tensor_copy(out=OS[0:64, 0:nq], in_=pa[0:64, 0:nq]), reads=[pa.b], writes=[OS.b])
                tk.op("dve", lambda e: e.tensor_copy(out=OS[64:128, 0:nq], in_=pb[64:128, 0:nq]), reads=[pb.b], writes=[OS.b])
                tk.op("dve", lambda e: e.tensor_copy(out=SS[0:64, 0:nq], in_=pb[0:64, 0:nq]), reads=[pb.b], writes=[SS.b])
                tk.op("dve", lambda e: e.tensor_copy(out=SS[64:128, 0:nq], in_=pa[64:128, 0:nq]), reads=[pa.b], writes=[SS.b])

            def s2():
                tk.op("pe", lambda e: e.matmul(pR[:, 0:nq], lhsT=CM(K_SWAP), rhs=SS[:, 0:nq], start=True, stop=True),
                      reads=[cm.b, SS.b], writes=[pR.b])

            def s3():
                tk.op("dve", lambda e: e.reciprocal(out=rec[:, 0:nq], in_=pR[:, 0:nq]), reads=[pR.b], writes=[rec.b])
                tk.op("dve", lambda e: e.tensor_tensor(out=OS[:, 0:nq], in0=OS[:, 0:nq], in1=rec[:, 0:nq], op=ALU.mult),
                      reads=[OS.b, rec.b], writes=[OS.b])
            return [s1, s2, s3]

        def phase_G(l, ctx_out, pS, pO, pR, PT, OS, SS, rec, yaccs, export):
            with ExitStack() as ph:
                qT = sb(ph, "qm", [128, 4, NT], BF16)
                kT = sb(ph, "kdup", [128, 2, NT], BF16)
                vaug = sb(ph, "vaugG", [128, NTILE, 2, 192], BF16)
                gg = sb(ph, "gg", [128, 2, NT], BF16)
                qz = qT[:].rearrange("p h t -> p (h t)").bitcast(F32)
                tk.op("dve", lambda e: e.memset(qz, 0.0), writes=[qT.b])
                for hf in range(2):
                    tk.dma("sp", kT[64 * hf:64 * hf + 64, :, :], gkT_s[:].rearrange("h d t -> d h t"), reads=[gkT_s.b], writes=[kT.b])
                load_vaug(vaug, gv_s, 2)
                for h in range(4):
                    tk.dma("sp", qT[64 * (h % 2):64 * (h % 2) + 64, h, :], gqT_s[h], reads=[gqT_s.b], writes=[qT.b])
                export["crit"] = [qT.b, kT.b, vaug.b]
                tk.dma("sp", gg[:], ggT_s[:].rearrange("(j p) t -> p j t", p=128), reads=[ggT_s.b], writes=[gg.b])
                yield
                pending = []
                nblk = 0
                for j in range(2):
                    yacc = yaccs[j]
                    if not ctx_out:
                        tk.op("pool", lambda e, yacc=yacc: e.memset(yacc[:, 0:256], 0.0), writes=[yacc.b])
                    qbs = qblocks(ctx_out)
                    for bi, (q0, nq, ktiles) in enumerate(qbs):
                        a0 = 2 * (nblk % 2)
                        nblk += 1
                        units = [
                            dict(q=(qT, lambda q0, nq, j=j: qT[:, 2 * j, q0:q0 + nq]), k=(kT, lambda kt, j=j: kT[:, j, kt * 128:(kt + 1) * 128]),
                                 v=(vaug, lambda kt, j=j: vaug[:, kt, j, 64:192]), acc=a0),
                            dict(q=(qT, lambda q0, nq, j=j: qT[:, 2 * j + 1, q0:q0 + nq]), k=(kT, lambda kt, j=j: kT[:, j, kt * 128:(kt + 1) * 128]),
                                 v=(vaug, lambda kt, j=j: vaug[:, kt, j, 0:128]), acc=a0 + 1),
                        ]
                        attn_block(units, q0, nq, ktiles, pO, pS, PT, 0.125, stages=pending)
                        pending = normalize_stages(pO[a0], pO[a0 + 1], nq, OS, SS, pR, rec)

                        def fin(q0=q0, nq=nq, j=j, yacc=yacc, last=(bi == len(qbs) - 1)):
                            tk.op("pool", lambda e: e.tensor_tensor(out=yacc[:, q0:q0 + nq], in0=OS[:, 0:nq],
                                                                    in1=gg[:, j, q0:q0 + nq], op=ALU.mult),
                                  reads=[OS.b, gg.b], writes=[yacc.b])
                            if last:
                                tk.dma("sp", ycT_s[512 + 128 * j:640 + 128 * j, :], yacc[:], reads=[yacc.b], writes=[ycT_s.b])
                        pending.append(fin)
                for st in pending:
                    st()
                yield

        def phase_D(l, ctx_out, pS, pO, pR, PT, OSp, SS, rec, yaccs, after):
            lam_init = 0.8 - 0.6 * float(np.exp(-0.3 * l))
            with ExitStack() as ph:
                dq = sb(ph, "dqm", [128, 8, NT], BF16)
                dk = sb(ph, "dkc", [128, 2, NT], BF16)
                vaug = sb(ph, "vaugD", [128, NTILE, 4, 192], BF16)
                dg = sb(ph, "dg", [128, 2, NT], BF16)
                OSn = sb(ph, "OSn", [128, 512])
                sq = SS
                lp = sb(ph, "lp", [128, 128])
                lpr = sb(ph, "lpr", [128, 2, 32])
                lsum = sb(ph, "lsum", [128, 2])
                neglam = sb(ph, "neglam", [128, 1])
                gsc = sb(ph, "gsc", [128, 1])
                for m in range(0, 8, 2):
                    dz = dq[:, m:m + 2, :].rearrange("p h t -> p (h t)").bitcast(F32)
                    tk.op("dve", lambda e, dz=dz: e.memset(dz, 0.0), writes=[dq.b])
                for m in range(8):
                    tk.dma("sp", dq[32 * (m % 4):32 * (m % 4) + 32, m, :], dqT_s[32 * m:32 * m + 32, :],
                           reads=[dqT_s.b] + (list(after) if m == 0 else []), writes=[dq.b])
                tk.dma("sp", dk[:], dkT_s[:].rearrange("(c p) t -> p c t", p=128), reads=[dkT_s.b], writes=[dk.b])
                tk.dma("sp", dg[:], dgT_s[:].rearrange("(j p) t -> p j t", p=128), reads=[dgT_s.b], writes=[dg.b])
                tk.dma("sp", lp[:], dlam_d[l].partition_broadcast(128), reads=[dlam_d.b], writes=[lp.b])
                tk.dma("sp", gsc[:], dng_d[l], reads=[dng_d.b], writes=[gsc.b])
                load_vaug(vaug, dv_s, 4)
                yield
                lp4 = lp[:].rearrange("p (a b c) -> p a b c", a=2, b=2)
                tk.op("dve", lambda e: e.tensor_tensor(out=lpr[:], in0=lp4[:, :, 0, :], in1=lp4[:, :, 1, :], op=ALU.mult),
                      reads=[lp.b], writes=[lpr.b])
                tk.op("dve", lambda e: e.reduce_sum(out=lsum[:], in_=lpr[:], axis=AX.X), reads=[lpr.b], writes=[lsum.b])
                tk.op("act", lambda e: e.activation(out=lsum[:], in_=lsum[:], func=AF.Exp), reads=[lsum.b], writes=[lsum.b])
                tk.op("dve", lambda e: e.tensor_tensor(out=neglam[:], in0=lsum[:, 1:2], in1=lsum[:, 0:1], op=ALU.subtract),
                      reads=[lsum.b], writes=[neglam.b])
                tk.op("dve", lambda e: e.tensor_scalar(out=neglam[:], in0=neglam[:], scalar1=-lam_init, scalar2=None, op0=ALU.add),
                      reads=[neglam.b], writes=[neglam.b])
                tk.op("dve", lambda e: e.tensor_scalar(out=gsc[:], in0=gsc[:], scalar1=1.0 - lam_init, scalar2=None, op0=ALU.mult),
                      reads=[gsc.b], writes=[gsc.b])
                pending = []
                nblk = 0
                for j in range(2):
                    yacc = yaccs[j]
                    if not ctx_out:
                        tk.op("pool", lambda e, yacc=yacc: e.memset(yacc[:, 0:256], 0.0), writes=[yacc.b])
                    qbs = qblocks(ctx_out)
                    for bi, (q0, nq, ktiles) in enumerate(qbs):
                        for sign in range(2):
                            a0 = 2 * (nblk % 2)
                            nblk += 1
                            mA = 4 * j + sign
                            mB = 4 * j + 2 + sign
                            units = [
                                dict(q=(dq, lambda q0, nq, m=mA: dq[:, m, q0:q0 + nq]), k=(dk, lambda kt, j=j: dk[:, j, kt * 128:(kt + 1) * 128]),
                                     v=(vaug, lambda kt, hh=2 * j: vaug[:, kt, hh, 64:192]), acc=a0),
                                dict(q=(dq, lambda q0, nq, m=mB: dq[:, m, q0:q0 + nq]), k=(dk, lambda kt, j=j: dk[:, j, kt * 128:(kt + 1) * 128]),
                                     v=(vaug, lambda kt, hh=2 * j + 1: vaug[:, kt, hh, 0:128]), acc=a0 + 1),
                            ]
                            attn_block(units, q0, nq, ktiles, pO, pS, PT, 32.0 ** -0.5, stages=pending)
                            if sign == 0:
                                pending = normalize_stages(pO[a0], pO[a0 + 1], nq, OSp, SS, pR, rec)
                                continue
                            pending = normalize_stages(pO[a0], pO[a0 + 1], nq, OSn, SS, pR, rec)

                            def c1(nq=nq):
                                tk.op("dve", lambda e: e.scalar_tensor_tensor(out=OSp[:, 0:nq], in0=OSn[:, 0:nq], scalar=neglam[:, 0:1],
                                                                              in1=OSp[:, 0:nq], op0=ALU.mult, op1=ALU.add),
                                      reads=[OSn.b, neglam.b, OSp.b], writes=[OSp.b])
                                tk.op("pool", lambda e: e.tensor_tensor(out=sq[:, 0:nq], in0=OSp[:, 0:nq], in1=OSp[:, 0:nq], op=ALU.mult),
                                      reads=[OSp.b], writes=[sq.b])

                            def c2(nq=nq):
                                tk.op("pe", lambda e: e.matmul(pR[:, 0:nq], lhsT=CM(K_BD64), rhs=sq[:, 0:nq], start=True, stop=True),
                                      reads=[cm.b, sq.b], writes=[pR.b])

                            def c3(nq=nq):
                                tk.op("act", lambda e: e.activation(out=rec[:, 0:nq], in_=pR[:, 0:nq], func=AF.Ln, bias=epsc[:, 0:1]),
                                      reads=[pR.b, epsc.b], writes=[rec.b])
                                tk.op("act", lambda e: e.activation(out=rec[:, 0:nq], in_=rec[:, 0:nq], func=AF.Exp, scale=-0.5),
                                      reads=[rec.b], writes=[rec.b])

                            def c4(nq=nq, q0=q0, j=j, yacc=yacc, last=(bi == len(qbs) - 1)):
                                tk.op("dve", lambda e: e.tensor_tensor(out=OSp[:, 0:nq], in0=OSp[:, 0:nq], in1=rec[:, 0:nq], op=ALU.mult),
                                      reads=[OSp.b, rec.b], writes=[OSp.b])
                                tk.op("dve", lambda e: e.scalar_tensor_tensor(
                                    out=yacc[:, q0:q0 + nq], in0=OSp[:, 0:nq], scalar=gsc[:, 0:1], in1=dg[:, j, q0:q0 + nq],
                                    op0=ALU.mult, op1=ALU.mult), reads=[OSp.b, gsc.b, dg.b], writes=[yacc.b])
                                if last:
                                    tk.dma("sp", ycT_s[768 + 128 * j:896 + 128 * j, :], yacc[:], reads=[yacc.b], writes=[ycT_s.b])
                            pending += [c1, c2, c3, c4]
                for st in pending:
                    st()
                yield

        def phase_GD(l, ctx_out, wo):
            with ExitStack() as ph:
                pS = [ps(ph, f"pSa{i}", [128, 512]) for i in range(3)]
                pO = [ps(ph, f"pOa{i}", [128, 512]) for i in range(4)]
                pR = ps(ph, "pRa", [128, 512])
                PT = [sb(ph, f"PTa{i}", [128, 512], BF16) for i in range(3)]
                OS = sb(ph, "OSa", [128, 512])
                SS = sb(ph, "SSa", [128, 512])
                rec = sb(ph, "reca", [128, 512])
                yaccs = [sb(ph, f"yacc{i}", [128, NT], BF16) for i in range(2)]
                export = {}
                g = phase_G(l, ctx_out, pS, pO, pR, PT, OS, SS, rec, yaccs, export)
                next(g)
                d = phase_D(l, ctx_out, pS, pO, pR, PT, OS, SS, rec, yaccs, export["crit"])
                next(d)
                next(g)
                wv = wout_d[l].rearrange("(kc p) n -> p kc n", p=128)
                for n in range(2):
                    tk.dma("pool", wo[:, :, n * 512:(n + 1) * 512], wv[:, :, n * 512:(n + 1) * 512], reads=[wout_d.b], writes=[wo.b])
                next(d)
                tk.barrier()
                for gen in (d, g):
                    for _ in gen:
                        pass

        def phase_O(l, ctx_out, wo):
            with ExitStack() as ph:
                ycTs = [sb(ph, f"ycT{i}", [128, 8, 768], BF16) for i in range(3)]
                G2 = [sb(ph, f"G2_{i}", [128, D]) for i in range(2)]
                gpo = sb(ph, "gpo", [128, D])
                ht = [sb(ph, f"hto{i}", [128, D]) for i in range(2)]
                on = [sb(ph, f"on{i}", [128, D]) for i in range(2)]
                junk = sb(ph, "junkO", [128, 512], BF16)
                ssqs = [sb(ph, f"ssqO{i}", [128, 2]) for i in range(2)]
                rstds = [sb(ph, f"rstdO{i}", [128, 1]) for i in range(2)]
                po = [ps(ph, f"po{i}", [128, 512]) for i in range(4)]
                tk.dma("sp", gpo[:], gpost_d[l].partition_broadcast(128), reads=[gpost_d.b], writes=[gpo.b])
                for w in range(2):
                    tk.dma("sp", G2[w][:], gt_s[l, w].partition_broadcast(128), reads=[gt_s.b], writes=[G2[w].b])
                    tk.op("dve", lambda e, w=w: e.tensor_tensor(out=G2[w][:], in0=G2[w][:], in1=gpo[:], op=ALU.mult),
                          reads=[G2[w].b, gpo.b], writes=[G2[w].b])
                ysv = ycT_s[:].rearrange("(kc p) t -> p kc t", p=128)
                for i in range(3):
                    tk.dma("sp", ycTs[i][:], ysv[:, :, i * 768:(i + 1) * 768], reads=[ycT_s.b], writes=[ycTs[i].b])
                for it, tt in enumerate(range(0 if ctx_out else 2, NTILE)):
                    k = it % 2
                    w = 1 if tt < 2 else 0
                    if l == 0:
                        src = ctx_d if tt < 2 else x_d
                        sap = ctx_d[tt * 128:(tt + 1) * 128, :] if tt < 2 else x_d[(tt - 2) * 128:(tt - 1) * 128, :]
                    else:
                        src = h1_s
                        sap = h1_s[tt * 128:(tt + 1) * 128, :]
                    tk.dma("sp", ht[k][:], sap, reads=[src.b], writes=[ht[k].b])
                    pp = po[2 * k:2 * k + 2]
                    for n in range(2):
                        for kc in range(8):
                            tk.op("pe", lambda e, n=n, kc=kc, pp=pp, tt=tt: e.matmul(
                                pp[n][:, :], lhsT=ycTs[tt // 6][:, kc, (tt % 6) * 128:(tt % 6 + 1) * 128], rhs=wo[:, kc, n * 512:(n + 1) * 512],
                                start=(kc == 0), stop=(kc == 7)), reads=[ycTs[tt // 6].b, wo.b], writes=[pp[n].b], sig=(kc == 7))
                    ssq = ssqs[k]
                    rstd = rstds[k]
                    tk.op("pool", lambda e: e.memset(ssq[:], 0.0), writes=[ssq.b])
                    for n in range(2):
                        tk.op("act", lambda e, n=n, pp=pp: e.activation(out=junk[:], in_=pp[n][:, :], func=AF.Square,
                                                                        accum_out=ssq[:, n:n + 1]),
                              reads=[pp[n].b], writes=[junk.b, ssq.b])
                    tk.op("dve", lambda e: e.tensor_tensor(out=rstd[:], in0=ssq[:, 0:1], in1=ssq[:, 1:2], op=ALU.add),
                          reads=[ssq.b], writes=[rstd.b])
                    tk.op("act", lambda e: e.activation(out=rstd[:], in_=rstd[:], func=AF.Ln, scale=1.0 / D, bias=epsc[:, 0:1]),
                          reads=[rstd.b, epsc.b], writes=[rstd.b])
                    tk.op("act", lambda e: e.activation(out=rstd[:], in_=rstd[:], func=AF.Exp, scale=-0.5),
                          reads=[rstd.b], writes=[rstd.b])
                    for n in range(2):
                        tk.op("dve", lambda e, n=n, pp=pp, k=k, w=w: e.scalar_tensor_tensor(
                            out=on[k][:, n * 512:(n + 1) * 512], in0=pp[n][:, :], scalar=rstd[:, 0:1],
                            in1=G2[w][:, n * 512:(n + 1) * 512], op0=ALU.mult, op1=ALU.mult),
                            reads=[pp[n].b, rstd.b, G2[w].b], writes=[on[k].b])
                    tk.op("dve", lambda e, k=k: e.tensor_tensor(out=on[k][:], in0=on[k][:], in1=ht[k][:], op=ALU.add),
                          reads=[on[k].b, ht[k].b], writes=[on[k].b])
                    if l == 0:
                        tk.dma("sp", h1_s[tt * 128:(tt + 1) * 128, :], on[k][:], reads=[on[k].b], writes=[h1_s.b])
                    else:
                        tk.dma("sp", out_d[(tt - 2) * 128:(tt - 1) * 128, :], on[k][:], reads=[on[k].b], writes=[out_d.b])
                tk.barrier()

        m0 = phase_M(0, "sp")
        for _ in range(8):
            next(m0)
        tk.barrier()
        for _ in m0:
            pass
        for l in range(layers):
            if stop_after == ("M", l):
                break
            with ExitStack() as ni:
                uT = sb(ni, "uT", [128, 8, NT], BF16)
                wr = [sb(ni, f"wr{i}", [128, 8, 512], BF16) for i in range(4)]
                I_load = make_I_loader(l, wr)
                I_slots = {}

                def I_pref():
                    for i in range(3):
                        I_slots[i] = I_load(i)
                mnext = phase_M(l + 1, "pool") if l + 1 < layers else None
                phase_N(l, uT, mnext=mnext, mid_hook=I_pref)
                if mnext is not None:
                    raise_if = [x for x in mnext]
                if stop_after == ("N", l):
                    break
                phase_I(l, uT, wr, I_slots)
                if stop_after == ("I", l):
                    break
            ctx_out = l < DEPTH - 1
            phase_S(l)
            if stop_after == ("S", l):
                break
            with ExitStack() as go:
                wo = sb(go, "wo", [128, 8, D], BF16)
                phase_GD(l, ctx_out, wo)
                if stop_after == ("D", l):
                    break
                phase_O(l, ctx_out, wo)
                if stop_after == ("O", l):
                    break
        tk.barrier()
    return nc


def _rot_perm(base, nheads, hd):
    idx = []
    for h in range(nheads):
        o = base + h * hd
        idx += list(range(o + hd // 2, o + hd)) + list(range(o, o + hd // 2))
    return idx


def prep_inputs(inp):
    f = lambda a: np.ascontiguousarray(np.asarray(a), dtype=np.float32)
    x, c, ctx, c_ctx = f(inp["x"]), f(inp["c"]), f(inp["ctx"]), f(inp["c_ctx"])
    w_in = f(inp["w_in"])
    perm = (_rot_perm(C_GQ, 4, 64) + _rot_perm(C_GK, 2, 64) + _rot_perm(C_DQ, 8, 32) + _rot_perm(C_DK, 8, 32))
    w_rot = np.ascontiguousarray(w_in[:, :, perm])
    b_mod = f(inp["b_mod"])
    g_pre = f(inp["g_pre"])
    conv_w = f(inp["conv_w"])
    conv_b = f(inp["conv_b"])
    qg, kg = f(inp["q_norm_g"]), f(inp["k_norm_g"])
    pq = _rot_perm(0, 1, 64)
    cg, sg, cd, sd = _rope_tables()
    shared = {
        "w_mod": f(inp["w_mod"]),
        "bmodf": np.stack([_feat(b_mod[l, :2048]) for l in range(DEPTH)]),
        "b_mod": b_mod,
        "gpref": np.stack([_feat(g_pre[l]) for l in range(DEPTH)]),
        "g_post": f(inp["g_post"]),
        "w_in": w_in,
        "w_rot": w_rot,
        "w_out": f(inp["w_out"]),
        "convw_f": np.ascontiguousarray(conv_w.reshape(DEPTH, 5, 8, 128).transpose(0, 3, 2, 1)),
        "convb_f": np.stack([_feat(conv_b[l]) for l in range(DEPTH)]),
        "alog": np.concatenate([f(inp["a_log_fwd"]), f(inp["a_log_bwd"])], axis=1),
        "dtb": np.concatenate([f(inp["dt_bias_fwd"]), f(inp["dt_bias_bwd"])], axis=1),
        "d_skip": f(inp["d_skip"]),
        "ssd_norm_g": f(inp["ssd_norm_g"]),
        "qkg_f": np.ascontiguousarray(np.stack([qg, qg[:, pq], kg, kg[:, pq]], axis=2)),
        "diff_lambda": f(inp["diff_lambda"]).reshape(DEPTH, 128),
        "dng_f": np.ascontiguousarray(np.tile(f(inp["diff_norm_g"]), (1, 2))[:, :, None]),
        "cmat": _const_mats(),
        "identb": np.eye(128, dtype=np.float32).astype(ml_dtypes.bfloat16),
        "rope_g": np.stack([cg, sg]),
        "rope_d": np.stack([cd, sd]),
    }
    cc = _feat(c_ctx)
    maps = []
    for b in range(NCORES):
        m = dict(shared)
        m["x"] = x[b]
        m["ctx"] = ctx[b]
        m["cvec"] = np.ascontiguousarray(np.concatenate([_feat(c[b]), cc], axis=1))
        maps.append(m)
    return maps


_NC_CACHE = {}


def kernel(**inputs):
    if "nc" not in _NC_CACHE:
        _NC_CACHE["nc"] = build()
    nc = _NC_CACHE["nc"]
    maps = prep_inputs(inputs)
    res = run_bass_kernel_spmd(nc, maps, core_ids=list(range(NCORES)))
    return np.stack([np.asarray(r["out"], dtype=np.float32) for r in res.results], axis=0)
```

```python
import numpy as np
import ml_dtypes
from contextlib import ExitStack
import concourse.bass as bass
import concourse.mybir as mybir
from concourse.bass_utils import run_bass_kernel_spmd

F32 = mybir.dt.float32
BF16 = mybir.dt.bfloat16
AF = mybir.ActivationFunctionType
ALU = mybir.AluOpType
AX = mybir.AxisListType

NCORES = 8
D = 1024
SEQ = 2048
CTX = 256
NT = SEQ + CTX
NTILE = NT // 128
DEPTH = 2
EPS = 1e-6
INC = 3344
C_X, C_B, C_C, C_Z, C_DT = 0, 512, 768, 1024, 1536
C_GQ, C_GK, C_GV, C_GG = 1552, 1808, 1936, 2064
C_DQ, C_DK, C_DV, C_DG = 2320, 2576, 2832, 3088
R_GQ, R_GK, R_DQ, R_DK, NROT = 0, 256, 384, 640, 896
NEG = -30000.0
SAME_ENGINE_SYNC = True
S_LEVEL = [0]
BCAST_LHST = True


class Buf:
    __slots__ = ("w", "r", "name")

    def __init__(self, name=""):
        self.w = None
        self.r = {}
        self.name = name


class T:
    NDS = 8
    ROT = 20000

    def __init__(self, nc, es):
        self.nc = nc
        self.es = es
        self.E = {"pe": nc.tensor, "act": nc.scalar, "dve": nc.vector, "pool": nc.gpsimd, "sp": nc.sync}
        self.sem = {}
        self.cnt = {}
        self.nsem = 0
        for e in ("pe", "act", "dve", "pool"):
            self.sem[e] = self._newsem("s_" + e)
            self.cnt[e] = 0
        self.seen = {e: {} for e in self.E}
        self.pend = {e: ([], []) for e in self.E}
        self.dq = {}
        for q in ("sp", "pool", "act"):
            self.dq[q] = {"sems": [self._newsem(f"d_{q}{i}") for i in range(self.NDS)],
                          "vals": [0] * self.NDS, "n": 0}
        self.alltoks = {}
        self.ninstr = 0

    def _newsem(self, name):
        self.nsem += 1
        return self.es.enter_context(self.nc.semaphore(f"{name}_{self.nsem}"))

    def _wait(self, x, sem, val):
        if self.seen[x].get(sem, 0) >= val:
            return
        self.E[x].wait_ge(sem, val)
        self.seen[x][sem] = val

    def _deps(self, x, reads, writes):
        deps = {}
        for b in reads:
            if b.w is not None:
                s, v = b.w
                if deps.get(s, 0) < v:
                    deps[s] = v
        for b in writes:
            if b.w is not None:
                s, v = b.w
                if deps.get(s, 0) < v:
                    deps[s] = v
            for s, v in b.r.items():
                if deps.get(s, 0) < v:
                    deps[s] = v
        for e, (pr, pw) in self.pend.items():
            if e == x or (not pr and not pw):
                continue
            for b in writes:
                assert all(b is not o for o in pr) and all(b is not o for o in pw), f"pending conflict {b.name}"
            for b in reads:
                assert all(b is not o for o in pw), f"pending conflict {b.name}"
        own = self.sem.get(x)
        for s, v in deps.items():
            if s is own and (x == "pe" or not SAME_ENGINE_SYNC):
                continue
            self._wait(x, s, v)

    def op(self, x, fn, reads=(), writes=(), sig=True):
        self._deps(x, reads, writes)
        ins = fn(self.E[x])
        self.ninstr += 1
        pr, pw = self.pend[x]
        pr.extend(reads)
        pw.extend(writes)
        if sig:
            if self.cnt[x] >= self.ROT:
                self.sem[x] = self._newsem("s_" + x)
                self.cnt[x] = 0
            self.cnt[x] += 1
            s = self.sem[x]
            v = self.cnt[x]
            ins.then_inc(s, 1)
            self.alltoks[s] = v
            for b in pr:
                if b.r.get(s, 0) < v:
                    b.r[s] = v
            for b in pw:
                b.w = (s, v)
                b.r = {}
            self.pend[x] = ([], [])
        return ins

    def dma(self, q, out, in_, reads=(), writes=(), **kw):
        self._deps(q, reads, writes)
        st = self.dq[q]
        k = st["n"] % self.NDS
        st["n"] += 1
        sem = st["sems"][k]
        if st["vals"][k] > 0:
            self._wait(q, sem, st["vals"][k])
        ins = self.E[q].dma_start(out=out, in_=in_, **kw)
        self.ninstr += 1
        st["vals"][k] += 16
        v = st["vals"][k]
        ins.then_inc(sem, 16)
        self.alltoks[sem] = v
        for b in reads:
            if b.r.get(sem, 0) < v:
                b.r[sem] = v
        for b in writes:
            b.w = (sem, v)
            b.r = {}
        return ins

    def barrier(self):
        for e, (pr, pw) in self.pend.items():
            assert not pr and not pw, "pending unsignaled ops at barrier"
        for x in self.E:
            own = self.sem.get(x)
            for s, v in self.alltoks.items():
                if s is own and x == "pe":
                    continue
                self._wait(x, s, v)


def _rope_tables():
    rows = SEQ // 64
    row_idx = np.repeat(np.arange(rows), 64).astype(np.float32)
    col_idx = (np.arange(SEQ) % 64).astype(np.float32)

    def ang(dim):
        q = dim // 4
        inv = (np.float32(10000.0) ** (-np.arange(q, dtype=np.float32) / np.float32(q))).astype(np.float32)
        a = np.concatenate([row_idx[:, None] * inv, col_idx[:, None] * inv], axis=-1)
        return a.astype(np.float32)

    ag = ang(64)
    ad = ang(32)
    cg = np.concatenate([np.cos(ag), np.cos(ag)], axis=1).T
    sg = np.concatenate([-np.sin(ag), np.sin(ag)], axis=1).T
    cd = np.concatenate([np.cos(ad), np.cos(ad)], axis=1).T
    sd = np.concatenate([-np.sin(ad), np.sin(ad)], axis=1).T
    cd = np.tile(cd, (4, 1))
    sd = np.tile(sd, (4, 1))
    return (np.ascontiguousarray(cg, np.float32), np.ascontiguousarray(sg, np.float32),
            np.ascontiguousarray(cd, np.float32), np.ascontiguousarray(sd, np.float32))


K_ID, K_UINC, K_LINC, K_NEGF, K_NEGB, K_ONES, K_SWAP, K_BD64, K_SELW, NCONST = 0, 1, 2, 3, 4, 5, 6, 7, 8, 9


def _const_mats():
    i = np.arange(128)
    m = np.zeros((NCONST, 128, 128), np.float32)
    m[K_ID] = np.eye(128)
    m[K_UINC] = (i[:, None] <= i[None, :])
    m[K_LINC] = (i[:, None] >= i[None, :])
    m[K_NEGF] = np.where(i[None, :] < i[:, None], NEG, 0.0)
    m[K_NEGB] = np.where(i[None, :] > i[:, None], NEG, 0.0)
    m[K_ONES] = 1.0
    m[K_SWAP] = (i[:, None] == ((i[None, :] + 64) % 128))
    bd = np.zeros((128, 128), np.float32)
    bd[:64, :64] = 1.0 / 64
    bd[64:, 64:] = 1.0 / 64
    m[K_BD64] = bd
    sel = np.zeros((128, 128), np.float32)
    m[K_SELW] = sel
    return np.ascontiguousarray(m.transpose(1, 0, 2))


def _feat(v):
    v = np.asarray(v, np.float32)
    return np.ascontiguousarray(v.reshape(-1, 128).T)


class Tl:
    def __init__(self, h, name):
        self.h = h
        self.b = Buf(name)

    def __getitem__(self, k):
        return self.h[k]


def bc(ap, shape):
    return ap.broadcast_to(list(shape))


def build(dbg=(), stop_after=None, layers=DEPTH):
    nc = bass.Bass("TRN2", target_bir_lowering=False)
    dbg = set(dbg)

    def din(name, shape, dt=F32):
        return Tl(nc.dram_tensor(name, list(shape), dt, kind="ExternalInput").ap(), name)

    def dscr(name, shape, dt=BF16):
        kind = "ExternalOutput" if name in dbg else "Internal"
        return Tl(nc.dram_tensor(name, list(shape), dt, kind=kind).ap(), name)

    x_d = din("x", [SEQ, D])
    ctx_d = din("ctx", [CTX, D])
    cvec_d = din("cvec", [128, 16])
    wmod_d = din("w_mod", [DEPTH, D, 3 * D])
    bmodf_d = din("bmodf", [DEPTH, 128, 16])
    bmod_d = din("b_mod", [DEPTH, 3 * D])
    gpref_d = din("gpref", [DEPTH, 128, 8])
    gpost_d = din("g_post", [DEPTH, D])
    win_d = din("w_in", [DEPTH, D, INC])
    wrot_d = din("w_rot", [DEPTH, D, NROT])
    wout_d = din("w_out", [DEPTH, D, D])
    convw_d = din("convw_f", [DEPTH, 128, 8, 5])
    convb_d = din("convb_f", [DEPTH, 128, 8])
    alog_d = din("alog", [DEPTH, 16])
    dtb_d = din("dtb", [DEPTH, 16])
    dskip_d = din("d_skip", [DEPTH, 8])
    ssdg_d = din("ssd_norm_g", [DEPTH, 512])
    qkg_d = din("qkg_f", [DEPTH, 64, 4])
    dlam_d = din("diff_lambda", [DEPTH, 128])
    dng_d = din("dng_f", [DEPTH, 128, 1])
    cmat_d = din("cmat", [128, NCONST, 128])
    identb_d = din("identb", [128, 128], BF16)
    ropeg_d = din("rope_g", [2, 64, SEQ])
    roped_d = din("rope_d", [2, 128, SEQ])
    out_d = Tl(nc.dram_tensor("out", [SEQ, D], F32, kind="ExternalOutput").ap(), "out")

    gt_s = dscr("gt_s", [DEPTH, 2, D], F32)
    h1_s = dscr("h1_s", [NT, D], F32)
    xs_s = dscr("xs_s", [NT, 512])
    bt_s = dscr("bt_s", [NT, 256])
    bT_s = dscr("bT_s", [256, NT])
    cT_s = dscr("cT_s", [256, NT])
    zs_s = dscr("zs_s", [NT, 512])
    dt_s = dscr("dt_s", [NT, 16], F32)
    gqT_s = dscr("gqT_s", [4, 64, NT])
    gkT_s = dscr("gkT_s", [2, 64, NT])
    gv_s = dscr("gv_s", [NT, 2, 192])
    ggT_s = dscr("ggT_s", [256, NT])
    dqT_s = dscr("dqT_s", [256, NT])
    dkT_s = dscr("dkT_s", [256, NT])
    dv_s = dscr("dv_s", [NT, 4, 192])
    dgT_s = dscr("dgT_s", [256, NT])
    ycT_s = dscr("ycT_s", [D, NT])
    uT_dbg = dscr("uT_dbg", [D, NT]) if "uT_dbg" in dbg else None
    mod_dbg = dscr("mod_dbg", [128, 32], F32) if "mod_dbg" in dbg else None

    with ExitStack() as es:
        tk = T(nc, es)

        uniq = [0]

        def sb(st, name, shape, dt=F32):
            uniq[0] += 1
            name = f"{name}_s{uniq[0]}"
            return Tl(st.enter_context(nc.sbuf_tensor(name, list(shape), dt)), name)

        def ps(st, name, shape, dt=F32):
            uniq[0] += 1
            name = f"{name}_p{uniq[0]}"
            return Tl(st.enter_context(nc.psum_tensor(name, list(shape), dt)), name)

        cm = sb(es, "cm", [128, NCONST, 128])
        identb = sb(es, "identb_sb", [128, 128], BF16)
        A1L = [sb(es, f"A1_{i}", [128, 8, 2]) for i in range(DEPTH)]
        SHL = [sb(es, f"SH_{i}", [128, 8, 2]) for i in range(DEPTH)]
        Ssil = sb(es, "Ssil", [128, 8, 2])
        tk.dma("sp", cm[:], cmat_d[:], reads=[cmat_d.b], writes=[cm.b])
        tk.dma("sp", identb[:], identb_d[:], reads=[identb_d.b], writes=[identb.b])

        epsc = sb(es, "epsc", [128, 1])
        tk.op("pool", lambda e: e.memset(epsc[:], EPS), writes=[epsc.b])

        def CM(k):
            return cm[:, k, :]

        ones_t = sb(es, "ones_t", [128, 768], BF16)
        tk.op("dve", lambda e: e.memset(ones_t[:], 1.0), writes=[ones_t.b])
        for (dst, nh) in ((gv_s, 2), (dv_s, 4)):
            tk.dma("sp", dst[:].rearrange("(c p) g w -> p c (g w)", p=128),
                   bc(ones_t[:, 0:nh * 192].unsqueeze(1), [128, NTILE, nh * 192]), reads=[ones_t.b], writes=[dst.b])

        def phase_M(l, q):
            A1, SH, S = A1L[l], SHL[l], Ssil
            with ExitStack() as ph:
                bmf = sb(ph, "bmf", [128, 16])
                gpf = sb(ph, "gpf", [128, 8])
                bgt = sb(ph, "bgt", [2, D])
                modT = sb(ph, "modT", [128, 16, 2])
                gtr = sb(ph, "gtr", [2, D])
                wb = [sb(ph, f"wmb{i}", [128, 8, 512]) for i in range(2)]
                psM = ps(ph, "psM", [128, 16, 2])
                psG = [ps(ph, f"psG{i}", [2, 512]) for i in range(2)]
                if l == 0:
                    cv = sb(ph, "cv", [128, 16])
                    tk.dma(q, cv[:], cvec_d[:], reads=[cvec_d.b], writes=[cv.b])
                    for w in range(2):
                        tk.op("act", lambda e, w=w: e.activation(out=S[:, :, w], in_=cv[:, 8 * w:8 * w + 8], func=AF.Silu),
                              reads=[cv.b], writes=[S.b])
                tk.dma(q, bmf[:], bmodf_d[l], reads=[bmodf_d.b], writes=[bmf.b])
                tk.dma(q, gpf[:], gpref_d[l], reads=[gpref_d.b], writes=[gpf.b])
                tk.dma(q, bgt[:], bmod_d[l, 2 * D:3 * D].partition_broadcast(2), reads=[bmod_d.b], writes=[bgt.b])
                wv = wmod_d[l].rearrange("(kc p) n -> p kc n", p=128)
                yield
                for blk in range(6):
                    w_ = wb[blk % 2]
                    tk.dma(q, w_[:], wv[:, :, blk * 512:(blk + 1) * 512], reads=[wmod_d.b], writes=[w_.b])
                    if blk < 4:
                        for oc in range(4):
                            for kc in range(8):
                                tk.op("pe", lambda e, oc=oc, kc=kc, w_=w_, blk=blk: e.matmul(
                                    psM[:, blk * 4 + oc, :], lhsT=w_[:, kc, oc * 128:(oc + 1) * 128], rhs=S[:, kc, :],
                                    start=(kc == 0), stop=(kc == 7)),
                                    reads=[w_.b, S.b], writes=[psM.b], sig=(kc == 7 and oc == 3))
                    else:
                        g = psG[blk - 4]
                        for kc in range(8):
                            tk.op("pe", lambda e, kc=kc, w_=w_, g=g: e.matmul(
                                g[:, :], lhsT=S[:, kc, :], rhs=w_[:, kc, :], start=(kc == 0), stop=(kc == 7)),
                                reads=[w_.b, S.b], writes=[g.b], sig=(kc == 7))
                    yield
                tk.op("dve", lambda e: e.tensor_tensor(out=modT[:], in0=psM[:], in1=bc(bmf[:].unsqueeze(2), [128, 16, 2]),
                                                       op=ALU.add), reads=[psM.b, bmf.b], writes=[modT.b])
                tk.op("dve", lambda e: e.tensor_copy(out=SH[:], in_=modT[:, 0:8, :]), reads=[modT.b], writes=[SH.b])
                tk.op("dve", lambda e: e.scalar_tensor_tensor(out=A1[:], in0=modT[:, 8:16, :], scalar=1.0,
                                                              in1=bc(gpf[:].unsqueeze(2), [128, 8, 2]),
                                                              op0=ALU.add, op1=ALU.mult),
                      reads=[modT.b, gpf.b], writes=[A1.b])
                for i in range(2):
                    tk.op("dve", lambda e, i=i: e.tensor_tensor(out=gtr[:, i * 512:(i + 1) * 512], in0=psG[i][:, :],
                                                                in1=bgt[:, i * 512:(i + 1) * 512], op=ALU.add),
                          reads=[psG[i].b, bgt.b], writes=[gtr.b])
                tk.dma("sp", gt_s[l], gtr[:], reads=[gtr.b], writes=[gt_s.b])
                if mod_dbg is not None and l == 0:
                    tk.dma("sp", mod_dbg[:, 0:16], A1[:].rearrange("p a b -> p (a b)"), reads=[A1.b], writes=[mod_dbg.b])
                    tk.dma("sp", mod_dbg[:, 16:32], SH[:].rearrange("p a b -> p (a b)"), reads=[SH.b], writes=[mod_dbg.b])
                yield

        def phase_N(l, uT, mnext=None, mid_hook=None):
            A1, SH = A1L[l], SHL[l]
            with ExitStack() as ph:
                if mnext is not None:
                    next(mnext)
                ht = [sb(ph, f"ht{i}", [128, D]) for i in range(2)]
                hb = [sb(ph, f"hb{i}", [128, D], BF16) for i in range(2)]
                junk = sb(ph, "junkN", [128, D], BF16)
                ssqs = [sb(ph, f"ssq{i}", [128, 1]) for i in range(2)]
                rstds = [sb(ph, f"rstd{i}", [128, 1]) for i in range(2)]
                tmpf = [sb(ph, f"tmpf{i}", [128, 8, 128]) for i in range(2)]
                pT = [ps(ph, f"pT{i}", [128, 8, 128], BF16) for i in range(2)]
                for tt in range(NTILE):
                    k = tt % 2
                    w = 1 if tt < 2 else 0
                    if l == 0:
                        src = ctx_d if tt < 2 else x_d
                        sap = ctx_d[tt * 128:(tt + 1) * 128, :] if tt < 2 else x_d[(tt - 2) * 128:(tt - 1) * 128, :]
                    else:
                        src = h1_s
                        sap = h1_s[tt * 128:(tt + 1) * 128, :]
                    tk.dma("sp", ht[k][:], sap, reads=[src.b], writes=[ht[k].b])
                    ssq = ssqs[k]
                    rstd = rstds[k]
                    tk.op("pool", lambda e: e.memset(ssq[:], 0.0), writes=[ssq.b])
                    tk.op("act", lambda e, k=k: e.activation(out=junk[:], in_=ht[k][:], func=AF.Square, accum_out=ssq[:, 0:1]),
                          reads=[ht[k].b], writes=[junk.b, ssq.b])
                    tk.op("act", lambda e: e.activation(out=rstd[:], in_=ssq[:], func=AF.Ln, scale=1.0 / D, bias=epsc[:, 0:1]),
                          reads=[ssq.b, epsc.b], writes=[rstd.b])
                    tk.op("act", lambda e: e.activation(out=rstd[:], in_=rstd[:], func=AF.Exp, scale=-0.5),
                          reads=[rstd.b], writes=[rstd.b])
                    tk.op("act", lambda e, k=k: e.activation(out=hb[k][:], in_=ht[k][:], func=AF.Copy, scale=rstd[:, 0:1]),
                          reads=[ht[k].b, rstd.b], writes=[hb[k].b])
                    for j in range(8):
                        tk.op("pe", lambda e, k=k, j=j: e.transpose(out=pT[k][:, j, :], in_=hb[k][:, j * 128:(j + 1) * 128],
                                                                    identity=identb[:]),
                              reads=[hb[k].b, identb.b], writes=[pT[k].b], sig=(j == 7))
                    tk.op("dve", lambda e, k=k, w=w: e.tensor_tensor(out=tmpf[k][:], in0=pT[k][:],
                                                                     in1=bc(A1[:, :, w:w + 1], [128, 8, 128]), op=ALU.mult),
                          reads=[pT[k].b, A1.b], writes=[tmpf[k].b])
                    tk.op("dve", lambda e, k=k, w=w, tt=tt: e.tensor_tensor(
                        out=uT[:, :, tt * 128:(tt + 1) * 128], in0=tmpf[k][:],
                        in1=bc(SH[:, :, w:w + 1], [128, 8, 128]), op=ALU.add),
                        reads=[tmpf[k].b, SH.b], writes=[uT.b])
                    if mnext is not None and tt % 3 == 2:
                        next(mnext)
                    if mid_hook is not None and tt == 9:
                        mid_hook()
                if mnext is not None:
                    next(mnext)
                if uT_dbg is not None and l == 0:
                    tk.dma("sp", uT_dbg[:].rearrange("(kc p) t -> p kc t", p=128), uT[:], reads=[uT.b], writes=[uT_dbg.b])
                tk.barrier()

        def make_I_loader(l, wr):
            NS = 4
            winv = win_d[l].rearrange("(kc p) n -> p kc n", p=128)
            wrotv = wrot_d[l].rearrange("(kc p) n -> p kc n", p=128)
            blocks = [
                [(winv, 0, 512)],
                [(winv, 512, 512)],
                [(winv, C_Z, 512)],
                [(winv, C_DT, 16), (winv, C_GV, 128), (winv, C_DV, 256)],
                [(winv, C_GG, 256), (winv, C_DG, 256)],
                [(winv, C_GQ, 384)],
                [(wrotv, R_GQ, 384)],
                [(winv, C_DQ, 512)],
                [(wrotv, R_DQ, 512)],
            ]

            def load_block(i):
                slot = wr[i % NS]
                o = 0
                for (v, c0, n) in blocks[i]:
                    src = win_d if v is winv else wrot_d
                    tk.dma("pool", slot[:, :, o:o + n], v[:, :, c0:c0 + n], reads=[src.b], writes=[slot.b])
                    o += n
                return slot


            return load_block

        TB = [(0, 256)] + [(256 + 512 * i, 512) for i in range(4)]

        def phase_I(l, uT, wr, slots):
            with ExitStack() as ph:
                NS = 4
                cs = [sb(ph, f"cs{i}", [128, 2312]) for i in range(2)]
                accs = [sb(ph, f"acc{i}", [128, NT]) for i in range(2)]
                xb = [sb(ph, f"xb{i}", [128, NT], BF16) for i in range(2)]
                xtoks = [sb(ph, f"xtok{i}", [128, NTILE, 128], BF16) for i in range(2)]
                xti = [0]
                rg = sb(ph, "rg", [128, 2, SEQ])
                rd = sb(ph, "rd", [128, 2, SEQ])
                zb = [sb(ph, f"zb{i}", [128, 512], BF16) for i in range(2)]
                vb = [sb(ph, f"vb{i}", [128, 384], BF16) for i in range(2)]
                dts = sb(ph, "dts", [128, NTILE, 16])
                sqs = sb(ph, "sqs", [128, 512])
                rs = sb(ph, "rs", [128, 512])
                t1 = sb(ph, "t1", [128, 512])
                t2 = sb(ph, "t2", [128, 512])
                cw = sb(ph, "cw", [128, 8, 5])
                cb = sb(ph, "cb", [128, 8])
                qkg = sb(ph, "qkg", [128, 4])
                pf = [ps(ph, f"pf{i}", [128, 512]) for i in range(4)]
                pm = ps(ph, "pm", [128, 512])
                ptr = ps(ph, "ptr", [128, 4, 128], BF16)
                ptm = [ps(ph, f"ptm{i}", [128, 512]) for i in range(2)]
                pfi = [0]

                def nextpf():
                    p = pf[pfi[0] % 4]
                    pfi[0] += 1
                    return p

                for hf in range(2):
                    tk.dma("sp", rg[64 * hf:64 * hf + 64], ropeg_d[:].rearrange("a p t -> p a t"), reads=[ropeg_d.b], writes=[rg.b])
                tk.dma("sp", rd[:], roped_d[:].rearrange("a p t -> p a t"), reads=[roped_d.b], writes=[rd.b])
                tk.dma("sp", cw[:], convw_d[l], reads=[convw_d.b], writes=[cw.b])
                tk.dma("sp", cb[:], convb_d[l], reads=[convb_d.b], writes=[cb.b])
                for hf in range(2):
                    tk.dma("sp", qkg[64 * hf:64 * hf + 64], qkg_d[l], reads=[qkg_d.b], writes=[qkg.b])
                for c in cs:
                    tk.op("pool", lambda e, c=c: e.memset(c[:], 0.0), writes=[c.b])
                tk.op("pool", lambda e: e.memset(sqs[:], 0.0), writes=[sqs.b])

                load_block = make_I_loader(l, wr)

                def prefetch(i):
                    slots[i] = load_block(i)

                def fm_matmuls(psb, w_, off, m, t0, n):
                    for kc in range(8):
                        tk.op("pe", lambda e, kc=kc: e.matmul(psb[0:m, 0:n], lhsT=w_[:, kc, off:off + m],
                                                               rhs=uT[:, kc, t0:t0 + n], start=(kc == 0), stop=(kc == 7)),
                              reads=[w_.b, uT.b], writes=[psb.b], sig=(kc == 7))

                def transposes_to(xbt, dst_d, c0):
                    xtok = xtoks[xti[0] % 2]
                    xti[0] += 1
                    for g0 in range(0, NTILE, 4):
                        gn = min(4, NTILE - g0)
                        for i in range(gn):
                            tt = g0 + i
                            tk.op("pe", lambda e, i=i, tt=tt: e.transpose(out=ptr[:, i, :], in_=xbt[:, tt * 128:(tt + 1) * 128],
                                                                          identity=identb[:]),
                                  reads=[xbt.b, identb.b], writes=[ptr.b], sig=(i == gn - 1))
                        tk.op("act", lambda e, g0=g0, gn=gn: e.activation(out=xtok[:, g0:g0 + gn, :], in_=ptr[:, 0:gn, :], func=AF.Copy),
                              reads=[ptr.b], writes=[xtok.b])
                    tk.dma("sp", dst_d[:].rearrange("(tt p) f -> p tt f", p=128)[:, :, c0:c0 + 128], xtok[:],
                           reads=[xtok.b], writes=[dst_d.b])

                def A_P(j):
                    if j == 0:
                        prefetch(3)
                    if j == 4:
                        prefetch(4)
                    w_ = slots[j // 4]
                    c_ = cs[j % 2]
                    for (t0, n) in TB:
                        p = nextpf()
                        fm_matmuls(p, w_, (j % 4) * 128, 128, t0, n)
                        d0 = 2 + t0 if t0 < 256 else 262 + (t0 - 256)
                        tk.op("act", lambda e, p=p, d0=d0, n=n, c_=c_: e.activation(out=c_[:, d0:d0 + n], in_=p[:, 0:n], func=AF.Copy),
                              reads=[p.b], writes=[c_.b])

                def A_C(j):
                    c_ = cs[j % 2]
                    acc = accs[j % 2]
                    for (eng, s0, a0, n) in (("dve", 2, 0, 256), ("dve", 262, 256, 2048)):
                        tk.op(eng, lambda e, s0=s0, a0=a0, n=n: e.tensor_scalar(
                            out=acc[:, a0:a0 + n], in0=c_[:, s0 - 2:s0 - 2 + n], scalar1=cw[:, j, 0:1], scalar2=cb[:, j:j + 1],
                            op0=ALU.mult, op1=ALU.add), reads=[c_.b, cw.b, cb.b], writes=[acc.b])
                        for k in range(1, 5):
                            tk.op(eng, lambda e, s0=s0, a0=a0, n=n, k=k: e.scalar_tensor_tensor(
                                out=acc[:, a0:a0 + n], in0=c_[:, s0 - 2 + k:s0 - 2 + k + n], scalar=cw[:, j, k:k + 1],
                                in1=acc[:, a0:a0 + n], op0=ALU.mult, op1=ALU.add), reads=[c_.b, cw.b, acc.b], writes=[acc.b])

                def A_F(j):
                    acc = accs[j % 2]
                    xbt = xb[j % 2]
                    tk.op("act", lambda e: e.activation(out=xbt[:], in_=acc[:], func=AF.Silu), reads=[acc.b], writes=[xbt.b])
                    if j < 4:
                        transposes_to(xbt, xs_s, j * 128)
                    elif j < 6:
                        tk.dma("sp", bT_s[(j - 4) * 128:(j - 3) * 128, :], xbt[:], reads=[xbt.b], writes=[bT_s.b])
                        transposes_to(xbt, bt_s, (j - 4) * 128)
                    else:
                        tk.dma("sp", cT_s[(j - 6) * 128:(j - 5) * 128, :], xbt[:], reads=[xbt.b], writes=[cT_s.b])

                A_P(0)
                A_P(1)
                A_C(0)
                for j in range(8):
                    A_F(j)
                    if j + 2 < 8:
                        A_P(j + 2)
                    if j + 1 < 8:
                        A_C(j + 1)

                prefetch(5)
                wz = slots[2]
                wt = slots[3]
                for tt in range(NTILE):
                    k = tt % 2
                    for (w_, n, p) in ((wz, 512, ptm[0]), (wt, 400, ptm[1])):
                        for kc in range(8):
                            tk.op("pe", lambda e, kc=kc, w_=w_, n=n, p=p, tt=tt: e.matmul(
                                p[:, 0:n], lhsT=uT[:, kc, tt * 128:(tt + 1) * 128], rhs=w_[:, kc, 0:n],
                                start=(kc == 0), stop=(kc == 7)), reads=[w_.b, uT.b], writes=[p.b], sig=(kc == 7))
                    tk.op("act", lambda e, k=k: e.activation(out=zb[k][:], in_=ptm[0][:], func=AF.Silu),
                          reads=[ptm[0].b], writes=[zb[k].b])
                    tk.dma("sp", zs_s[tt * 128:(tt + 1) * 128, :], zb[k][:], reads=[zb[k].b], writes=[zs_s.b])
                    tk.op("dve", lambda e, tt=tt: e.tensor_copy(out=dts[:, tt, :], in_=ptm[1][:, 0:16]),
                          reads=[ptm[1].b], writes=[dts.b])
                    tk.op("dve", lambda e, k=k: e.tensor_copy(out=vb[k][:], in_=ptm[1][:, 16:400]),
                          reads=[ptm[1].b], writes=[vb[k].b])
                    tk.dma("sp", gv_s[tt * 128:(tt + 1) * 128, :, 64:128], vb[k][:, 0:128].rearrange("p (g d) -> p g d", g=2),
                           reads=[vb[k].b], writes=[gv_s.b])
                    tk.dma("sp", dv_s[tt * 128:(tt + 1) * 128, :, 64:128], vb[k][:, 128:384].rearrange("p (g d) -> p g d", g=4),
                           reads=[vb[k].b], writes=[dv_s.b])
                tk.dma("sp", dt_s[:].rearrange("(tt p) f -> p tt f", p=128), dts[:], reads=[dts.b], writes=[dt_s.b])

                prefetch(6)
                prefetch(7)
                wg = slots[4]
                for j in range(4):
                    xbt = xb[j % 2]
                    for (t0, n) in TB:
                        p = nextpf()
                        fm_matmuls(p, wg, j * 128, 128, t0, n)
                        tk.op("act", lambda e, p=p, t0=t0, n=n, xbt=xbt: e.activation(out=xbt[:, t0:t0 + n], in_=p[:, 0:n], func=AF.Silu),
                              reads=[p.b], writes=[xbt.b])
                    dst = ggT_s if j < 2 else dgT_s
                    tk.dma("sp", dst[(j % 2) * 128:(j % 2 + 1) * 128, :], xbt[:], reads=[xbt.b], writes=[dst.b])

                prefetch(8)
                wq = slots[5]
                wqr = slots[6]
                for pp_ in range(3):
                    gcol = 0 if pp_ < 2 else 2
                    xbt = xb[pp_ % 2]
                    for (t0, n) in TB:
                        pa = nextpf()
                        fm_matmuls(pa, wq, pp_ * 128, 128, t0, n)
                        lat = t0 >= 256
                        if lat:
                            pb = nextpf()
                            fm_matmuls(pb, wqr, pp_ * 128, 128, t0, n)
                        tk.op("act", lambda e, pa=pa, n=n: e.activation(out=sqs[:, 0:n], in_=pa[:, 0:n], func=AF.Square),
                              reads=[pa.b], writes=[sqs.b])
                        tk.op("pe", lambda e, n=n: e.matmul(pm[:, 0:n], lhsT=CM(K_BD64), rhs=sqs[:, 0:n],
                                                            start=True, stop=True), reads=[cm.b, sqs.b], writes=[pm.b])
                        tk.op("act", lambda e, n=n: e.activation(out=rs[:, 0:n], in_=pm[:, 0:n], func=AF.Ln, bias=epsc[:, 0:1]),
                              reads=[pm.b, epsc.b], writes=[rs.b])
                        tk.op("act", lambda e, n=n: e.activation(out=rs[:, 0:n], in_=rs[:, 0:n], func=AF.Exp, scale=-0.5),
                              reads=[rs.b], writes=[rs.b])
                        if lat:
                            r0 = t0 - 256
                            tk.op("dve", lambda e, pa=pa, n=n, gcol=gcol: e.scalar_tensor_tensor(
                                out=t1[:, 0:n], in0=pa[:, 0:n], scalar=qkg[:, gcol:gcol + 1], in1=rs[:, 0:n],
                                op0=ALU.mult, op1=ALU.mult), reads=[pa.b, qkg.b, rs.b], writes=[t1.b])
                            tk.op("dve", lambda e, pb=pb, n=n, gcol=gcol: e.scalar_tensor_tensor(
                                out=t2[:, 0:n], in0=pb[:, 0:n], scalar=qkg[:, gcol + 1:gcol + 2], in1=rs[:, 0:n],
                                op0=ALU.mult, op1=ALU.mult), reads=[pb.b, qkg.b, rs.b], writes=[t2.b])
                            tk.op("pool", lambda e, n=n, r0=r0: e.tensor_tensor(out=t1[:, 0:n], in0=t1[:, 0:n],
                                                                                in1=rg[:, 0, r0:r0 + n], op=ALU.mult),
                                  reads=[t1.b, rg.b], writes=[t1.b])
                            tk.op("dve", lambda e, n=n, r0=r0: e.tensor_tensor(out=t2[:, 0:n], in0=t2[:, 0:n],
                                                                               in1=rg[:, 1, r0:r0 + n], op=ALU.mult),
                                  reads=[t2.b, rg.b], writes=[t2.b])
                            tk.op("dve", lambda e, n=n, t0=t0, xbt=xbt: e.tensor_tensor(out=xbt[:, t0:t0 + n], in0=t1[:, 0:n],
                                                                                        in1=t2[:, 0:n], op=ALU.add),
                                  reads=[t1.b, t2.b], writes=[xbt.b])
                        else:
                            tk.op("dve", lambda e, pa=pa, n=n, gcol=gcol, t0=t0, xbt=xbt: e.scalar_tensor_tensor(
                                out=xbt[:, t0:t0 + n], in0=pa[:, 0:n], scalar=qkg[:, gcol:gcol + 1], in1=rs[:, 0:n],
                                op0=ALU.mult, op1=ALU.mult), reads=[pa.b, qkg.b, rs.b], writes=[xbt.b])
                    if pp_ < 2:
                        dst = gqT_s[2 * pp_:2 * pp_ + 2].rearrange("h d t -> (h d) t")
                        dstb = gqT_s.b
                    else:
                        dst = gkT_s[:].rearrange("h d t -> (h d) t")
                        dstb = gkT_s.b
                    tk.dma("sp", dst, xbt[:], reads=[xbt.b], writes=[dstb])

                wd = slots[7]
                wdr = slots[8]
                for j in range(4):
                    xbt = xb[j % 2]
                    for (t0, n) in TB:
                        pa = nextpf()
                        fm_matmuls(pa, wd, j * 128, 128, t0, n)
                        if t0 >= 256:
                            pb = nextpf()
                            fm_matmuls(pb, wdr, j * 128, 128, t0, n)
                            r0 = t0 - 256
                            tk.op("dve", lambda e, pa=pa, n=n, r0=r0: e.tensor_tensor(out=t1[:, 0:n], in0=pa[:, 0:n],
                                                                                      in1=rd[:, 0, r0:r0 + n], op=ALU.mult),
                                  reads=[pa.b, rd.b], writes=[t1.b])
                            tk.op("dve", lambda e, pb=pb, n=n, r0=r0: e.tensor_tensor(out=t2[:, 0:n], in0=pb[:, 0:n],
                                                                                      in1=rd[:, 1, r0:r0 + n], op=ALU.mult),
                                  reads=[pb.b, rd.b], writes=[t2.b])
                            tk.op("pool", lambda e, n=n, t0=t0, xbt=xbt: e.tensor_tensor(out=xbt[:, t0:t0 + n], in0=t1[:, 0:n],
                                                                                         in1=t2[:, 0:n], op=ALU.add),
                                  reads=[t1.b, t2.b], writes=[xbt.b])
                        else:
                            tk.op("act", lambda e, pa=pa, n=n, t0=t0, xbt=xbt: e.activation(out=xbt[:, t0:t0 + n], in_=pa[:, 0:n], func=AF.Copy),
                                  reads=[pa.b], writes=[xbt.b])
                    dst = dqT_s if j < 2 else dkT_s
                    tk.dma("sp", dst[(j % 2) * 128:(j % 2 + 1) * 128, :], xbt[:], reads=[xbt.b], writes=[dst.b])
                tk.barrier()

        def phase_S(l):
            with ExitStack() as ph:
                xs = sb(ph, "xs", [128, NTILE, 512], BF16)
                btok = sb(ph, "btok", [128, NTILE, 256], BF16)
                bT = sb(ph, "bT", [128, 2, NT], BF16)
                cT = sb(ph, "cT", [128, 2, NT], BF16)
                zs = sb(ph, "zs", [128, NTILE, 512], BF16)
                dtr = sb(ph, "dtr", [128, NTILE, 16])
                alog = sb(ph, "alog", [128, 2, 8])
                dtb = sb(ph, "dtbb", [128, 2, 8])
                dsk = sb(ph, "dsk", [128, 8])
                ng = sb(ph, "ng", [128, 512])
                Abc = sb(ph, "Abc", [128, 2, 8])
                one = sb(ph, "one", [128, 1])
                SH4 = [128, 2, NTILE, 8]
                tmp = sb(ph, "s_tmp", SH4)
                dtv = sb(ph, "dtv", SH4)
                dtA = sb(ph, "dtA", SH4)
                negcum = sb(ph, "negcum", SH4)
                ecum = sb(ph, "ecum", SH4)
                etot = sb(ph, "etot", SH4)
                dec = sb(ph, "dec", SH4)
                wdec = sb(ph, "wdec", SH4)
                lnb = sb(ph, "lnb", SH4)
                IDd = sb(ph, "IDd", [128, 8, 128], BF16)
                sinb = sb(ph, "sinb", [128, NTILE, 512], BF16)
                ycs = sb(ph, "ycs", [128, 4, NT], BF16)
                fl = lambda t: t[:].rearrange("p a b c -> p (a b c)")

                tk.dma("sp", dtr[:], dt_s[:].rearrange("(c p) f -> p c f", p=128), reads=[dt_s.b], writes=[dtr.b])
                tk.dma("sp", alog[:].rearrange("p a b -> p (a b)"), alog_d[l].partition_broadcast(128), reads=[alog_d.b], writes=[alog.b])
                tk.dma("sp", dtb[:].rearrange("p a b -> p (a b)"), dtb_d[l].partition_broadcast(128), reads=[dtb_d.b], writes=[dtb.b])
                tk.dma("sp", dsk[:], dskip_d[l].partition_broadcast(128), reads=[dskip_d.b], writes=[dsk.b])
                tk.dma("sp", xs[:], xs_s[:].rearrange("(c p) f -> p c f", p=128), reads=[xs_s.b], writes=[xs.b])
                tk.dma("sp", btok[:], bt_s[:].rearrange("(c p) f -> p c f", p=128), reads=[bt_s.b], writes=[btok.b])
                tk.dma("sp", bT[:], bT_s[:].rearrange("(g p) t -> p g t", p=128), reads=[bT_s.b], writes=[bT.b])
                tk.dma("sp", cT[:], cT_s[:].rearrange("(g p) t -> p g t", p=128), reads=[cT_s.b], writes=[cT.b])
                tk.dma("sp", zs[:], zs_s[:].rearrange("(c p) f -> p c f", p=128), reads=[zs_s.b], writes=[zs.b])
                tk.dma("sp", ng[:], ssdg_d[l].partition_broadcast(128), reads=[ssdg_d.b], writes=[ng.b])
                tk.op("pool", lambda e: e.memset(one[:], 1.0), writes=[one.b])

                if S_LEVEL[0] == -1:
                    tk.barrier()
                    return
                with ExitStack() as s1:
                    pcum = ps(s1, "pcum", [128, 2, NTILE * 8])
                    ptot = ps(s1, "ptot", [128, 2 * NTILE * 8])
                    tk.op("act", lambda e: e.activation(out=Abc[:], in_=alog[:], func=AF.Exp), reads=[alog.b], writes=[Abc.b])
                    tk.op("dve", lambda e: e.tensor_scalar(out=Abc[:], in0=Abc[:], scalar1=-1.0, scalar2=None, op0=ALU.mult),
                          reads=[Abc.b], writes=[Abc.b])
                    for d in range(2):
                        tk.op("dve", lambda e, d=d: e.tensor_tensor(out=tmp[:, d], in0=dtr[:, :, d * 8:(d + 1) * 8],
                                                                    in1=bc(dtb[:, d:d + 1, :], [128, NTILE, 8]), op=ALU.add),
                              reads=[dtr.b, dtb.b], writes=[tmp.b])
                    tk.op("act", lambda e: e.activation(out=fl(tmp), in_=fl(tmp), func=AF.Exp), reads=[tmp.b], writes=[tmp.b])
                    tk.op("act", lambda e: e.activation(out=fl(dtv), in_=fl(tmp), func=AF.Ln, bias=one[:, 0:1]),
                          reads=[tmp.b, one.b], writes=[dtv.b])
                    tk.op("act", lambda e: e.activation(out=fl(lnb), in_=fl(dtv), func=AF.Ln), reads=[dtv.b], writes=[lnb.b])
                    for h in range(8):
                        tk.op("dve", lambda e, h=h: e.tensor_scalar(out=IDd[:, h, :], in0=identb[:], scalar1=dsk[:, h:h + 1], scalar2=None,
                                                                    op0=ALU.mult), reads=[identb.b, dsk.b], writes=[IDd.b])
                    for d in range(2):
                        tk.op("dve", lambda e, d=d: e.tensor_tensor(out=dtA[:, d], in0=dtv[:, d],
                                                                    in1=bc(Abc[:, d:d + 1, :], [128, NTILE, 8]), op=ALU.mult),
                              reads=[dtv.b, Abc.b], writes=[dtA.b])
                    if S_LEVEL[0] == -2:
                        tk.barrier()
                        return
                    tk.op("pe", lambda e: e.matmul(pcum[:, 0, :], lhsT=CM(K_UINC), rhs=dtA[:, 0].rearrange("p b c -> p (b c)"),
                                                   start=True, stop=True), reads=[cm.b, dtA.b], writes=[pcum.b])
                    tk.op("pe", lambda e: e.matmul(pcum[:, 1, :], lhsT=CM(K_LINC), rhs=dtA[:, 1].rearrange("p b c -> p (b c)"),
                                                   start=True, stop=True), reads=[cm.b, dtA.b], writes=[pcum.b])
                    tk.op("pe", lambda e: e.matmul(ptot[:, :], lhsT=CM(K_ONES), rhs=fl(dtA), start=True, stop=True),
                          reads=[cm.b, dtA.b], writes=[ptot.b])
                    if S_LEVEL[0] == -3:
                        tk.barrier()
                        return
                    pcf = pcum[:].rearrange("p a n -> p (a n)")
                    tk.op("dve", lambda e: e.tensor_scalar(out=fl(negcum), in0=pcf, scalar1=-1.0, scalar2=None, op0=ALU.mult),
                          reads=[pcum.b], writes=[negcum.b])
                    tk.op("dve", lambda e: e.tensor_copy(out=fl(wdec), in_=ptot[:, :]), reads=[ptot.b], writes=[wdec.b])
                    tk.op("dve", lambda e: e.tensor_tensor(out=fl(lnb), in0=fl(lnb), in1=fl(negcum), op=ALU.add),
                          reads=[lnb.b, negcum.b], writes=[lnb.b])
                    if S_LEVEL[0] == -4:
                        tk.barrier()
                        return
                    tk.op("act", lambda e: e.activation(out=fl(ecum), in_=fl(negcum), func=AF.Exp, scale=-1.0),
                          reads=[negcum.b], writes=[ecum.b])
                    if S_LEVEL[0] == -5:
                        tk.barrier()
                        return
                    tk.op("act", lambda e: e.activation(out=fl(etot), in_=fl(wdec), func=AF.Exp), reads=[wdec.b], writes=[etot.b])
                    tk.op("dve", lambda e: e.tensor_tensor(out=fl(tmp), in0=fl(wdec), in1=fl(negcum), op=ALU.add),
                          reads=[wdec.b, negcum.b], writes=[tmp.b])
                    if S_LEVEL[0] == -6:
                        tk.barrier()
                        return
                    tk.op("act", lambda e: e.activation(out=fl(dec), in_=fl(tmp), func=AF.Exp), reads=[tmp.b], writes=[dec.b])
                    tk.op("dve", lambda e: e.tensor_tensor(out=fl(wdec), in0=fl(dtv), in1=fl(dec), op=ALU.mult),
                          reads=[dtv.b, dec.b], writes=[wdec.b])
                    tk.barrier()

                with ExitStack() as s2:
                    S = [sb(s2, f"Sst{i}", [128, 512]) for i in range(2)]
                    Sfb = sb(s2, "Sfb", [128, 512], BF16)
                    stmp = sb(s2, "stmp", [128, 512])
                    xdd = sb(s2, "xdd", [128, 512], BF16)
                    LTD = [[sb(s2, f"LT{i}g{q}", [128, 4, 128], BF16) for q in range(4)] for i in range(2)]
                    MTD = [sb(s2, f"MT{i}", [128, 16, 128], BF16) for i in range(4)]
                    xddD = [sb(s2, f"xddD{i}", [128, 512], BF16) for i in range(4)]
                    ta = sb(s2, "ta", [128, 512])
                    tb_ = sb(s2, "tb", [128, 512])
                    yz = sb(s2, "yz", [128, 512])
                    yns = [sb(s2, f"yn{i}", [128, 512], BF16) for i in range(2)]
                    junk = sb(s2, "junkS", [128, 512], BF16)
                    ssq = sb(s2, "ssqS", [128, 1])
                    rstd = sb(s2, "rstdS", [128, 1])
                    pL = [ps(s2, f"pL{i}", [128, 4, 128]) for i in range(2)]
                    pcb = ps(s2, "pcb", [128, 2, 128])
                    cbs = [sb(s2, f"cbs{i}", [128, 2, 128]) for i in range(2)]
                    py = ps(s2, "py", [128, 512])
                    pyo = [ps(s2, f"pyo{i}", [128, 512]) for i in range(2)]
                    pst = ps(s2, "pst", [128, 512])
                    pT = ps(s2, "pTS", [128, 4, 128], BF16)
                    v3 = lambda ap: ap.rearrange("p (h d) -> p h d", h=8)

                    def scaled_x(dst, c, w4, d):
                        tk.op("dve", lambda e: e.tensor_tensor(out=v3(dst[:]), in0=v3(xs[:, c, :]),
                                                               in1=bc(w4[:, d, c, :].unsqueeze(2), [128, 8, 64]), op=ALU.mult),
                              reads=[xs.b, w4.b], writes=[dst.b])

                    def state_step(c, d, xsrc):
                        for g in range(2):
                            tk.op("pe", lambda e, g=g: e.matmul(pst[:, g * 256:(g + 1) * 256], lhsT=btok[:, c, g * 128:(g + 1) * 128],
                                                                rhs=xsrc[:, g * 256:(g + 1) * 256], start=True, stop=True),
                                  reads=[btok.b, xsrc.b], writes=[pst.b], sig=(g == 1))
                        tk.op("dve", lambda e: e.tensor_tensor(out=v3(stmp[:]), in0=v3(S[d][:]),
                                                               in1=bc(etot[:, d, c, :].unsqueeze(2), [128, 8, 64]), op=ALU.mult),
                              reads=[S[d].b, etot.b], writes=[stmp.b])
                        tk.op("dve", lambda e: e.tensor_tensor(out=S[d][:], in0=pst[:, :], in1=stmp[:], op=ALU.add),
                              reads=[pst.b, stmp.b], writes=[S[d].b])

                    border = [1, 0] + list(range(NTILE - 1, 1, -1))
                    if S_LEVEL[0] == 1:
                        tk.barrier()
                        return
                    tk.op("pool", lambda e: e.memset(S[1][:], 0.0), writes=[S[1].b])
                    tk.op("pool", lambda e: e.memset(S[0][:], 0.0), writes=[S[0].b])
                    tk.op("pool", lambda e: e.memset(Sfb[:], 0.0), writes=[Sfb.b])
                    for i, c in enumerate(border):
                        tk.op("act", lambda e, c=c: e.activation(out=sinb[:, c, :], in_=S[1][:], func=AF.Copy),
                              reads=[S[1].b], writes=[sinb.b])
                        if i == len(border) - 1:
                            break
                        scaled_x(xdd, c, wdec, 1)
                        state_step(c, 1, xdd)

                    def stageA(c):
                        t0 = c * 128
                        k = c % 2
                        k3 = c % 4
                        LT, MT, cb_ = LTD[k], MTD[k3], cbs[k]

                        def pre():
                            scaled_x(xddD[k3], c, wdec, 0)
                            for g in range(2):
                                tk.op("pe", lambda e, g=g: e.matmul(pcb[:, g, :], lhsT=bT[:, g, t0:t0 + 128], rhs=cT[:, g, t0:t0 + 128],
                                                                    start=True, stop=True), reads=[bT.b, cT.b], writes=[pcb.b], sig=(g == 1))
                            tk.op("dve", lambda e: e.tensor_copy(out=cb_[:], in_=pcb[:]), reads=[pcb.b], writes=[cb_.b])

                        def grp(q4):
                            p = pL[q4 % 2]
                            d = q4 // 2
                            for i in range(4):
                                hd = q4 * 4 + i
                                lh = bc(dtA[:, d, c, hd % 8:hd % 8 + 1], [128, 128])
                                tk.op("pe", lambda e, p=p, i=i, lh=lh, d=d: e.matmul(
                                    p[:, i, :], lhsT=lh, rhs=CM(K_UINC if d == 0 else K_LINC), start=True, stop=False),
                                    reads=[dtA.b, cm.b], writes=[p.b], sig=False)
                                tk.op("pe", lambda e, p=p, i=i, d=d: e.matmul(
                                    p[:, i, :], lhsT=CM(K_ID), rhs=CM(K_NEGF if d == 0 else K_NEGB), start=False, stop=True),
                                    reads=[cm.b], writes=[p.b], sig=(i == 3))
                            for i in range(4):
                                hd = q4 * 4 + i
                                tk.op("act", lambda e, p=p, i=i, hd=hd, d=d: e.activation(
                                    out=LT[q4][:, i, :], in_=p[:, i, :], func=AF.Exp, bias=lnb[:, d, c, hd % 8:hd % 8 + 1]),
                                    reads=[p.b, lnb.b], writes=[LT[q4].b])
                            g = q4 % 2
                            tk.op("dve", lambda e, q4=q4, g=g: e.tensor_tensor(out=MT[:, q4 * 4:(q4 + 1) * 4, :], in0=LT[q4][:],
                                                                               in1=bc(cb_[:, g:g + 1, :], [128, 4, 128]), op=ALU.mult),
                                  reads=[LT[q4].b, cb_.b], writes=[MT.b])
                        return [pre] + [(lambda q4=q4: grp(q4)) for q4 in range(4)]

                    def stageB(c):
                        t0 = c * 128
                        k = c % 4
                        MT, xdd3 = MTD[k], xddD[k]
                        yn = yns[c % 2]

                        def s1():
                            for h in range(8):
                                xh = xs[:, c, h * 64:(h + 1) * 64]
                                tk.op("pe", lambda e, h=h, xh=xh: e.matmul(py[:, h * 64:(h + 1) * 64], lhsT=IDd[:, h, :], rhs=xh,
                                                                           start=True, stop=False), reads=[IDd.b, xs.b], writes=[py.b], sig=False)
                                tk.op("pe", lambda e, h=h, xh=xh: e.matmul(py[:, h * 64:(h + 1) * 64], lhsT=MT[:, h, :], rhs=xh,
                                                                           start=False, stop=False), reads=[MT.b, xs.b], writes=[py.b], sig=False)
                                tk.op("pe", lambda e, h=h, xh=xh: e.matmul(py[:, h * 64:(h + 1) * 64], lhsT=MT[:, 8 + h, :], rhs=xh,
                                                                           start=False, stop=True), reads=[MT.b, xs.b], writes=[py.b], sig=(h == 7))
                            for g in range(2):
                                tk.op("pe", lambda e, g=g: e.matmul(pyo[0][:, g * 256:(g + 1) * 256], lhsT=cT[:, g, t0:t0 + 128],
                                                                    rhs=Sfb[:, g * 256:(g + 1) * 256], start=True, stop=True),
                                      reads=[cT.b, Sfb.b], writes=[pyo[0].b], sig=(g == 1))
                            for g in range(2):
                                tk.op("pe", lambda e, g=g: e.matmul(pyo[1][:, g * 256:(g + 1) * 256], lhsT=cT[:, g, t0:t0 + 128],
                                                                    rhs=sinb[:, c, g * 256:(g + 1) * 256], start=True, stop=True),
                                      reads=[cT.b, sinb.b], writes=[pyo[1].b], sig=(g == 1))

                        def s2():
                            if c < NTILE - 1:
                                state_step(c, 0, xdd3)
                                tk.op("act", lambda e: e.activation(out=Sfb[:], in_=S[0][:], func=AF.Copy), reads=[S[0].b], writes=[Sfb.b])

                        def s3():
                            tk.op("dve", lambda e: e.tensor_tensor(out=v3(ta[:]), in0=v3(pyo[0][:, :]),
                                                                   in1=bc(ecum[:, 0, c, :].unsqueeze(2), [128, 8, 64]), op=ALU.mult),
                                  reads=[pyo[0].b, ecum.b], writes=[ta.b])
                            tk.op("dve", lambda e: e.tensor_tensor(out=v3(tb_[:]), in0=v3(pyo[1][:, :]),
                                                                   in1=bc(ecum[:, 1, c, :].unsqueeze(2), [128, 8, 64]), op=ALU.mult),
                                  reads=[pyo[1].b, ecum.b], writes=[tb_.b])

                        def s4():
                            tk.op("pool", lambda e: e.tensor_tensor(out=ta[:], in0=ta[:], in1=tb_[:], op=ALU.add),
                                  reads=[ta.b, tb_.b], writes=[ta.b])

                        def s5():
                            tk.op("dve", lambda e: e.tensor_tensor(out=yz[:], in0=py[:, :], in1=ta[:], op=ALU.add),
                                  reads=[py.b, ta.b], writes=[yz.b])

                        def s6():
                            tk.op("pool", lambda e: e.tensor_tensor(out=yz[:], in0=yz[:], in1=zs[:, c, :], op=ALU.mult),
                                  reads=[yz.b, zs.b], writes=[yz.b])
                            tk.op("pool", lambda e: e.memset(ssq[:], 0.0), writes=[ssq.b])

                        def s7():
                            tk.op("act", lambda e: e.activation(out=junk[:], in_=yz[:], func=AF.Square, accum_out=ssq[:, 0:1]),
                                  reads=[yz.b], writes=[junk.b, ssq.b])
                            tk.op("act", lambda e: e.activation(out=rstd[:], in_=ssq[:], func=AF.Ln, scale=1.0 / 512, bias=epsc[:, 0:1]),
                                  reads=[ssq.b, epsc.b], writes=[rstd.b])
                            tk.op("act", lambda e: e.activation(out=rstd[:], in_=rstd[:], func=AF.Exp, scale=-0.5),
                                  reads=[rstd.b], writes=[rstd.b])

                        def s8():
                            tk.op("dve", lambda e: e.scalar_tensor_tensor(out=yn[:], in0=yz[:], scalar=rstd[:, 0:1], in1=ng[:],
                                                                          op0=ALU.mult, op1=ALU.mult),
                                  reads=[yz.b, rstd.b, ng.b], writes=[yn.b])
                        return [s1, s2, s3, s4, s5, s6, s7, s8]

                    def stageB2(c):
                        t0 = c * 128
                        yn = yns[c % 2]
                        for j in range(4):
                            tk.op("pe", lambda e, j=j: e.transpose(out=pT[:, j, :], in_=yn[:, j * 128:(j + 1) * 128], identity=identb[:]),
                                  reads=[yn.b, identb.b], writes=[pT.b], sig=(j == 3))
                        tk.op("act", lambda e: e.activation(out=ycs[:, :, t0:t0 + 128], in_=pT[:], func=AF.Copy),
                              reads=[pT.b], writes=[ycs.b])

                    for c0 in range(3):
                        for f in stageA(c0):
                            f()
                    for c in range(NTILE):
                        A = stageA(c + 3) if c + 3 < NTILE else [lambda: None] * 5
                        B = stageB(c)
                        for f in (A[0], B[0], A[1], B[1], B[2], A[2], B[3], B[4], A[3], B[5], B[6], A[4], B[7]):
                            f()
                        if c >= 1:
                            stageB2(c - 1)
                    stageB2(NTILE - 1)
                    tk.dma("sp", ycT_s[0:512, :].rearrange("(j p) t -> p j t", p=128), ycs[:], reads=[ycs.b], writes=[ycT_s.b])
                    tk.barrier()

        def attn_block(units, q0, nq, ktiles, pO, pS, PT, scale, stages=()):
            seq = [(kt, u) for kt in ktiles for u in range(len(units))]
            LAG = 2
            stages = list(stages)
            for i in range(len(seq) + LAG):
                if stages and i >= 3 and (i - 3) % 4 == 0:
                    stages.pop(0)()
                if i < len(seq):
                    kt, u = seq[i]
                    U = units[u]
                    sl = i % 3
                    tk.op("pe", lambda e, U=U, kt=kt, sl=sl: e.matmul(pS[sl][:, 0:nq], lhsT=U["k"][1](kt), rhs=U["q"][1](q0, nq),
                                                                      start=True, stop=True),
                          reads=[U["k"][0].b, U["q"][0].b], writes=[pS[sl].b])
                    tk.op("act", lambda e, sl=sl: e.activation(out=PT[sl][:, 0:nq], in_=pS[sl][:, 0:nq], func=AF.Exp, scale=scale),
                          reads=[pS[sl].b], writes=[PT[sl].b])
                j = i - LAG
                if j >= 0:
                    kt, u = seq[j]
                    U = units[u]
                    sl = j % 3
                    acc = pO[U["acc"]]
                    tk.op("pe", lambda e, U=U, kt=kt, sl=sl, acc=acc: e.matmul(acc[:, 0:nq], lhsT=U["v"][1](kt), rhs=PT[sl][:, 0:nq],
                                                                               start=(kt == ktiles[0]), stop=(kt == ktiles[-1])),
                          reads=[U["v"][0].b, PT[sl].b], writes=[acc.b])
            for st in stages:
                st()

        def qblocks(ctx_out):
            qb = [(256 + 512 * i, 512, list(range(NTILE))) for i in range(4)]
            if ctx_out:
                qb = [(0, 256, [0, 1])] + qb
            return qb

        def load_vaug(vaug, src, nh):
            sv = src[:].rearrange("(c p) g w -> p c (g w)", p=128)
            dv = vaug[:].rearrange("p c g w -> p c (g w)")
            for c0 in range(0, NTILE, 6):
                tk.dma("sp", dv[:, c0:c0 + 6, :], sv[:, c0:c0 + 6, :], reads=[src.b], writes=[vaug.b])

        def normalize_stages(pa, pb, nq, OS, SS, pR, rec):
            def s1():
                tk.op("dve", lambda e: e.tensor_copy(out=OS[0:64, 0:nq], in_=pa[0:64, 0:nq]), reads=[pa.b], writes=[OS.b])
                tk.op("dve", lambda e: e.tensor_copy(out=OS[64:128, 0:nq], in_=pb[64:128, 0:nq]), reads=[pb.b], writes=[OS.b])
                tk.op("dve", lambda e: e.tensor_copy(out=SS[0:64, 0:nq], in_=pb[0:64, 0:nq]), reads=[pb.b], writes=[SS.b])
                tk.op("dve", lambda e: e.tensor_copy(out=SS[64:128, 0:nq], in_=pa[64:128, 0:nq]), reads=[pa.b], writes=[SS.b])

            def s2():
                tk.op("pe", lambda e: e.matmul(pR[:, 0:nq], lhsT=CM(K_SWAP), rhs=SS[:, 0:nq], start=True, stop=True),
                      reads=[cm.b, SS.b], writes=[pR.b])

            def s3():
                tk.op("dve", lambda e: e.reciprocal(out=rec[:, 0:nq], in_=pR[:, 0:nq]), reads=[pR.b], writes=[rec.b])
                tk.op("dve", lambda e: e.tensor_tensor(out=OS[:, 0:nq], in0=OS[:, 0:nq], in1=rec[:, 0:nq], op=ALU.mult),
                      reads=[OS.b, rec.b], writes=[OS.b])
            return [s1, s2, s3]

        def phase_G(l, ctx_out, pS, pO, pR, PT, OS, SS, rec, yaccs):
            with ExitStack() as ph:
                qT = sb(ph, "qm", [128, 4, NT], BF16)
                kT = sb(ph, "kdup", [128, 2, NT], BF16)
                vaug = sb(ph, "vaugG", [128, NTILE, 2, 192], BF16)
                gg = sb(ph, "gg", [128, 2, NT], BF16)
                qz = qT[:].rearrange("p h t -> p (h t)").bitcast(F32)
                tk.op("dve", lambda e: e.memset(qz, 0.0), writes=[qT.b])
                for hf in range(2):
                    tk.dma("sp", kT[64 * hf:64 * hf + 64, :, :], gkT_s[:].rearrange("h d t -> d h t"), reads=[gkT_s.b], writes=[kT.b])
                load_vaug(vaug, gv_s, 2)
                for h in range(4):
                    tk.dma("sp", qT[64 * (h % 2):64 * (h % 2) + 64, h, :], gqT_s[h], reads=[gqT_s.b], writes=[qT.b])
                tk.dma("sp", gg[:], ggT_s[:].rearrange("(j p) t -> p j t", p=128), reads=[ggT_s.b], writes=[gg.b])
                yield
                pending = []
                nblk = 0
                for j in range(2):
                    yacc = yaccs[j]
                    if not ctx_out:
                        tk.op("pool", lambda e, yacc=yacc: e.memset(yacc[:, 0:256], 0.0), writes=[yacc.b])
                    qbs = qblocks(ctx_out)
                    for bi, (q0, nq, ktiles) in enumerate(qbs):
                        a0 = 2 * (nblk % 2)
                        nblk += 1
                        units = [
                            dict(q=(qT, lambda q0, nq, j=j: qT[:, 2 * j, q0:q0 + nq]), k=(kT, lambda kt, j=j: kT[:, j, kt * 128:(kt + 1) * 128]),
                                 v=(vaug, lambda kt, j=j: vaug[:, kt, j, 64:192]), acc=a0),
                            dict(q=(qT, lambda q0, nq, j=j: qT[:, 2 * j + 1, q0:q0 + nq]), k=(kT, lambda kt, j=j: kT[:, j, kt * 128:(kt + 1) * 128]),
                                 v=(vaug, lambda kt, j=j: vaug[:, kt, j, 0:128]), acc=a0 + 1),
                        ]
                        attn_block(units, q0, nq, ktiles, pO, pS, PT, 0.125, stages=pending)
                        pending = normalize_stages(pO[a0], pO[a0 + 1], nq, OS, SS, pR, rec)

                        def fin(q0=q0, nq=nq, j=j, yacc=yacc, last=(bi == len(qbs) - 1)):
                            tk.op("pool", lambda e: e.tensor_tensor(out=yacc[:, q0:q0 + nq], in0=OS[:, 0:nq],
                                                                    in1=gg[:, j, q0:q0 + nq], op=ALU.mult),
                                  reads=[OS.b, gg.b], writes=[yacc.b])
                            if last:
                                tk.dma("sp", ycT_s[512 + 128 * j:640 + 128 * j, :], yacc[:], reads=[yacc.b], writes=[ycT_s.b])
                        pending.append(fin)
                for st in pending:
                    st()
                yield

        def phase_D(l, ctx_out, pS, pO, pR, PT, OSp, SS, rec, yaccs):
            lam_init = 0.8 - 0.6 * float(np.exp(-0.3 * l))
            with ExitStack() as ph:
                dq = sb(ph, "dqm", [128, 8, NT], BF16)
                dk = sb(ph, "dkc", [128, 2, NT], BF16)
                vaug = sb(ph, "vaugD", [128, NTILE, 4, 192], BF16)
                dg = sb(ph, "dg", [128, 2, NT], BF16)
                OSn = sb(ph, "OSn", [128, 512])
                sq = SS
                lp = sb(ph, "lp", [128, 128])
                lpr = sb(ph, "lpr", [128, 2, 32])
                lsum = sb(ph, "lsum", [128, 2])
                neglam = sb(ph, "neglam", [128, 1])
                gsc = sb(ph, "gsc", [128, 1])
                for m in range(0, 8, 2):
                    dz = dq[:, m:m + 2, :].rearrange("p h t -> p (h t)").bitcast(F32)
                    tk.op("dve", lambda e, dz=dz: e.memset(dz, 0.0), writes=[dq.b])
                for m in range(8):
                    tk.dma("sp", dq[32 * (m % 4):32 * (m % 4) + 32, m, :], dqT_s[32 * m:32 * m + 32, :], reads=[dqT_s.b], writes=[dq.b])
                tk.dma("sp", dk[:], dkT_s[:].rearrange("(c p) t -> p c t", p=128), reads=[dkT_s.b], writes=[dk.b])
                tk.dma("sp", dg[:], dgT_s[:].rearrange("(j p) t -> p j t", p=128), reads=[dgT_s.b], writes=[dg.b])
                tk.dma("sp", lp[:], dlam_d[l].partition_broadcast(128), reads=[dlam_d.b], writes=[lp.b])
                tk.dma("sp", gsc[:], dng_d[l], reads=[dng_d.b], writes=[gsc.b])
                load_vaug(vaug, dv_s, 4)
                yield
                lp4 = lp[:].rearrange("p (a b c) -> p a b c", a=2, b=2)
                tk.op("dve", lambda e: e.tensor_tensor(out=lpr[:], in0=lp4[:, :, 0, :], in1=lp4[:, :, 1, :], op=ALU.mult),
                      reads=[lp.b], writes=[lpr.b])
                tk.op("dve", lambda e: e.reduce_sum(out=lsum[:], in_=lpr[:], axis=AX.X), reads=[lpr.b], writes=[lsum.b])
                tk.op("act", lambda e: e.activation(out=lsum[:], in_=lsum[:], func=AF.Exp), reads=[lsum.b], writes=[lsum.b])
                tk.op("dve", lambda e: e.tensor_tensor(out=neglam[:], in0=lsum[:, 1:2], in1=lsum[:, 0:1], op=ALU.subtract),
                      reads=[lsum.b], writes=[neglam.b])
                tk.op("dve", lambda e: e.tensor_scalar(out=neglam[:], in0=neglam[:], scalar1=-lam_init, scalar2=None, op0=ALU.add),
                      reads=[neglam.b], writes=[neglam.b])
                tk.op("dve", lambda e: e.tensor_scalar(out=gsc[:], in0=gsc[:], scalar1=1.0 - lam_init, scalar2=None, op0=ALU.mult),
                      reads=[gsc.b], writes=[gsc.b])
                pending = []
                nblk = 0
                for j in range(2):
                    yacc = yaccs[j]
                    if not ctx_out:
                        tk.op("pool", lambda e, yacc=yacc: e.memset(yacc[:, 0:256], 0.0), writes=[yacc.b])
                    qbs = qblocks(ctx_out)
                    for bi, (q0, nq, ktiles) in enumerate(qbs):
                        for sign in range(2):
                            a0 = 2 * (nblk % 2)
                            nblk += 1
                            mA = 4 * j + sign
                            mB = 4 * j + 2 + sign
                            units = [
                                dict(q=(dq, lambda q0, nq, m=mA: dq[:, m, q0:q0 + nq]), k=(dk, lambda kt, j=j: dk[:, j, kt * 128:(kt + 1) * 128]),
                                     v=(vaug, lambda kt, hh=2 * j: vaug[:, kt, hh, 64:192]), acc=a0),
                                dict(q=(dq, lambda q0, nq, m=mB: dq[:, m, q0:q0 + nq]), k=(dk, lambda kt, j=j: dk[:, j, kt * 128:(kt + 1) * 128]),
                                     v=(vaug, lambda kt, hh=2 * j + 1: vaug[:, kt, hh, 0:128]), acc=a0 + 1),
                            ]
                            attn_block(units, q0, nq, ktiles, pO, pS, PT, 32.0 ** -0.5, stages=pending)
                            if sign == 0:
                                pending = normalize_stages(pO[a0], pO[a0 + 1], nq, OSp, SS, pR, rec)
                                continue
                            pending = normalize_stages(pO[a0], pO[a0 + 1], nq, OSn, SS, pR, rec)

                            def c1(nq=nq):
                                tk.op("dve", lambda e: e.scalar_tensor_tensor(out=OSp[:, 0:nq], in0=OSn[:, 0:nq], scalar=neglam[:, 0:1],
                                                                              in1=OSp[:, 0:nq], op0=ALU.mult, op1=ALU.add),
                                      reads=[OSn.b, neglam.b, OSp.b], writes=[OSp.b])
                                tk.op("pool", lambda e: e.tensor_tensor(out=sq[:, 0:nq], in0=OSp[:, 0:nq], in1=OSp[:, 0:nq], op=ALU.mult),
                                      reads=[OSp.b], writes=[sq.b])

                            def c2(nq=nq):
                                tk.op("pe", lambda e: e.matmul(pR[:, 0:nq], lhsT=CM(K_BD64), rhs=sq[:, 0:nq], start=True, stop=True),
                                      reads=[cm.b, sq.b], writes=[pR.b])

                            def c3(nq=nq):
                                tk.op("act", lambda e: e.activation(out=rec[:, 0:nq], in_=pR[:, 0:nq], func=AF.Ln, bias=epsc[:, 0:1]),
                                      reads=[pR.b, epsc.b], writes=[rec.b])
                                tk.op("act", lambda e: e.activation(out=rec[:, 0:nq], in_=rec[:, 0:nq], func=AF.Exp, scale=-0.5),
                                      reads=[rec.b], writes=[rec.b])

                            def c4(nq=nq, q0=q0, j=j, yacc=yacc, last=(bi == len(qbs) - 1)):
                                tk.op("dve", lambda e: e.tensor_tensor(out=OSp[:, 0:nq], in0=OSp[:, 0:nq], in1=rec[:, 0:nq], op=ALU.mult),
                                      reads=[OSp.b, rec.b], writes=[OSp.b])
                                tk.op("dve", lambda e: e.scalar_tensor_tensor(
                                    out=yacc[:, q0:q0 + nq], in0=OSp[:, 0:nq], scalar=gsc[:, 0:1], in1=dg[:, j, q0:q0 + nq],
                                    op0=ALU.mult, op1=ALU.mult), reads=[OSp.b, gsc.b, dg.b], writes=[yacc.b])
                                if last:
                                    tk.dma("sp", ycT_s[768 + 128 * j:896 + 128 * j, :], yacc[:], reads=[yacc.b], writes=[ycT_s.b])
                            pending += [c1, c2, c3, c4]
                for st in pending:
                    st()
                yield

        def phase_GD(l, ctx_out, wo):
            wv = wout_d[l].rearrange("(kc p) n -> p kc n", p=128)
            for n in range(2):
                tk.dma("pool", wo[:, :, n * 512:(n + 1) * 512], wv[:, :, n * 512:(n + 1) * 512], reads=[wout_d.b], writes=[wo.b])
            with ExitStack() as ph:
                pS = [ps(ph, f"pSa{i}", [128, 512]) for i in range(3)]
                pO = [ps(ph, f"pOa{i}", [128, 512]) for i in range(4)]
                pR = ps(ph, "pRa", [128, 512])
                PT = [sb(ph, f"PTa{i}", [128, 512], BF16) for i in range(3)]
                OS = sb(ph, "OSa", [128, 512])
                SS = sb(ph, "SSa", [128, 512])
                rec = sb(ph, "reca", [128, 512])
                yaccs = [sb(ph, f"yacc{i}", [128, NT], BF16) for i in range(2)]
                g = phase_G(l, ctx_out, pS, pO, pR, PT, OS, SS, rec, yaccs)
                next(g)
                d = phase_D(l, ctx_out, pS, pO, pR, PT, OS, SS, rec, yaccs)
                next(d)
                next(g)
                next(d)
                tk.barrier()
                for gen in (d, g):
                    for _ in gen:
                        pass

        def phase_O(l, ctx_out, wo):
            with ExitStack() as ph:
                ycTs = [sb(ph, f"ycT{i}", [128, 8, 768], BF16) for i in range(3)]
                G2 = [sb(ph, f"G2_{i}", [128, D]) for i in range(2)]
                gpo = sb(ph, "gpo", [128, D])
                ht = [sb(ph, f"hto{i}", [128, D]) for i in range(2)]
                on = [sb(ph, f"on{i}", [128, D]) for i in range(2)]
                junk = sb(ph, "junkO", [128, 512], BF16)
                ssqs = [sb(ph, f"ssqO{i}", [128, 2]) for i in range(2)]
                rstds = [sb(ph, f"rstdO{i}", [128, 1]) for i in range(2)]
                po = [ps(ph, f"po{i}", [128, 512]) for i in range(4)]
                tk.dma("sp", gpo[:], gpost_d[l].partition_broadcast(128), reads=[gpost_d.b], writes=[gpo.b])
                for w in range(2):
                    tk.dma("sp", G2[w][:], gt_s[l, w].partition_broadcast(128), reads=[gt_s.b], writes=[G2[w].b])
                    tk.op("dve", lambda e, w=w: e.tensor_tensor(out=G2[w][:], in0=G2[w][:], in1=gpo[:], op=ALU.mult),
                          reads=[G2[w].b, gpo.b], writes=[G2[w].b])
                ysv = ycT_s[:].rearrange("(kc p) t -> p kc t", p=128)
                for i in range(3):
                    tk.dma("sp", ycTs[i][:], ysv[:, :, i * 768:(i + 1) * 768], reads=[ycT_s.b], writes=[ycTs[i].b])
                for it, tt in enumerate(range(0 if ctx_out else 2, NTILE)):
                    k = it % 2
                    w = 1 if tt < 2 else 0
                    if l == 0:
                        src = ctx_d if tt < 2 else x_d
                        sap = ctx_d[tt * 128:(tt + 1) * 128, :] if tt < 2 else x_d[(tt - 2) * 128:(tt - 1) * 128, :]
                    else:
                        src = h1_s
                        sap = h1_s[tt * 128:(tt + 1) * 128, :]
                    tk.dma("sp", ht[k][:], sap, reads=[src.b], writes=[ht[k].b])
                    pp = po[2 * k:2 * k + 2]
                    for n in range(2):
                        for kc in range(8):
                            tk.op("pe", lambda e, n=n, kc=kc, pp=pp, tt=tt: e.matmul(
                                pp[n][:, :], lhsT=ycTs[tt // 6][:, kc, (tt % 6) * 128:(tt % 6 + 1) * 128], rhs=wo[:, kc, n * 512:(n + 1) * 512],
                                start=(kc == 0), stop=(kc == 7)), reads=[ycTs[tt // 6].b, wo.b], writes=[pp[n].b], sig=(kc == 7))
                    ssq = ssqs[k]
                    rstd = rstds[k]
                    tk.op("pool", lambda e: e.memset(ssq[:], 0.0), writes=[ssq.b])
                    for n in range(2):
                        tk.op("act", lambda e, n=n, pp=pp: e.activation(out=junk[:], in_=pp[n][:, :], func=AF.Square,
                                                                        accum_out=ssq[:, n:n + 1]),
                              reads=[pp[n].b], writes=[junk.b, ssq.b])
                    tk.op("dve", lambda e: e.tensor_tensor(out=rstd[:], in0=ssq[:, 0:1], in1=ssq[:, 1:2], op=ALU.add),
                          reads=[ssq.b], writes=[rstd.b])
                    tk.op("act", lambda e: e.activation(out=rstd[:], in_=rstd[:], func=AF.Ln, scale=1.0 / D, bias=epsc[:, 0:1]),
                          reads=[rstd.b, epsc.b], writes=[rstd.b])
                    tk.op("act", lambda e: e.activation(out=rstd[:], in_=rstd[:], func=AF.Exp, scale=-0.5),
                          reads=[rstd.b], writes=[rstd.b])
                    for n in range(2):
                        tk.op("dve", lambda e, n=n, pp=pp, k=k, w=w: e.scalar_tensor_tensor(
                            out=on[k][:, n * 512:(n + 1) * 512], in0=pp[n][:, :], scalar=rstd[:, 0:1],
                            in1=G2[w][:, n * 512:(n + 1) * 512], op0=ALU.mult, op1=ALU.mult),
                            reads=[pp[n].b, rstd.b, G2[w].b], writes=[on[k].b])
                    tk.op("dve", lambda e, k=k: e.tensor_tensor(out=on[k][:], in0=on[k][:], in1=ht[k][:], op=ALU.add),
                          reads=[on[k].b, ht[k].b], writes=[on[k].b])
                    if l == 0:
                        tk.dma("sp", h1_s[tt * 128:(tt + 1) * 128, :], on[k][:], reads=[on[k].b], writes=[h1_s.b])
                    else:
                        tk.dma("sp", out_d[(tt - 2) * 128:(tt - 1) * 128, :], on[k][:], reads=[on[k].b], writes=[out_d.b])
                tk.barrier()

        m0 = phase_M(0, "sp")
        for _ in range(8):
            next(m0)
        tk.barrier()
        for _ in m0:
            pass
        for l in range(layers):
            if stop_after == ("M", l):
                break
            with ExitStack() as ni:
                uT = sb(ni, "uT", [128, 8, NT], BF16)
                wr = [sb(ni, f"wr{i}", [128, 8, 512], BF16) for i in range(4)]
                I_load = make_I_loader(l, wr)
                I_slots = {}

                def I_pref():
                    for i in range(3):
                        I_slots[i] = I_load(i)
                mnext = phase_M(l + 1, "pool") if l + 1 < layers else None
                phase_N(l, uT, mnext=mnext, mid_hook=I_pref)
                if mnext is not None:
                    raise_if = [x for x in mnext]
                if stop_after == ("N", l):
                    break
                phase_I(l, uT, wr, I_slots)
                if stop_after == ("I", l):
                    break
            ctx_out = l < DEPTH - 1
            phase_S(l)
            if stop_after == ("S", l):
                break
            with ExitStack() as go:
                wo = sb(go, "wo", [128, 8, D], BF16)
                phase_GD(l, ctx_out, wo)
                if stop_after == ("D", l):
                    break
                phase_O(l, ctx_out, wo)
                if stop_after == ("O", l):
                    break
        tk.barrier()
    return nc


def _rot_perm(base, nheads, hd):
    idx = []
    for h in range(nheads):
        o = base + h * hd
        idx += list(range(o + hd // 2, o + hd)) + list(range(o, o + hd // 2))
    return idx


def prep_inputs(inp):
    f = lambda a: np.ascontiguousarray(np.asarray(a), dtype=np.float32)
    x, c, ctx, c_ctx = f(inp["x"]), f(inp["c"]), f(inp["ctx"]), f(inp["c_ctx"])
    w_in = f(inp["w_in"])
    perm = (_rot_perm(C_GQ, 4, 64) + _rot_perm(C_GK, 2, 64) + _rot_perm(C_DQ, 8, 32) + _rot_perm(C_DK, 8, 32))
    w_rot = np.ascontiguousarray(w_in[:, :, perm])
    b_mod = f(inp["b_mod"])
    g_pre = f(inp["g_pre"])
    conv_w = f(inp["conv_w"])
    conv_b = f(inp["conv_b"])
    qg, kg = f(inp["q_norm_g"]), f(inp["k_norm_g"])
    pq = _rot_perm(0, 1, 64)
    cg, sg, cd, sd = _rope_tables()
    shared = {
        "w_mod": f(inp["w_mod"]),
        "bmodf": np.stack([_feat(b_mod[l, :2048]) for l in range(DEPTH)]),
        "b_mod": b_mod,
        "gpref": np.stack([_feat(g_pre[l]) for l in range(DEPTH)]),
        "g_post": f(inp["g_post"]),
        "w_in": w_in,
        "w_rot": w_rot,
        "w_out": f(inp["w_out"]),
        "convw_f": np.ascontiguousarray(conv_w.reshape(DEPTH, 5, 8, 128).transpose(0, 3, 2, 1)),
        "convb_f": np.stack([_feat(conv_b[l]) for l in range(DEPTH)]),
        "alog": np.concatenate([f(inp["a_log_fwd"]), f(inp["a_log_bwd"])], axis=1),
        "dtb": np.concatenate([f(inp["dt_bias_fwd"]), f(inp["dt_bias_bwd"])], axis=1),
        "d_skip": f(inp["d_skip"]),
        "ssd_norm_g": f(inp["ssd_norm_g"]),
        "qkg_f": np.ascontiguousarray(np.stack([qg, qg[:, pq], kg, kg[:, pq]], axis=2)),
        "diff_lambda": f(inp["diff_lambda"]).reshape(DEPTH, 128),
        "dng_f": np.ascontiguousarray(np.tile(f(inp["diff_norm_g"]), (1, 2))[:, :, None]),
        "cmat": _const_mats(),
        "identb": np.eye(128, dtype=np.float32).astype(ml_dtypes.bfloat16),
        "rope_g": np.stack([cg, sg]),
        "rope_d": np.stack([cd, sd]),
    }
    cc = _feat(c_ctx)
    maps = []
    for b in range(NCORES):
        m = dict(shared)
        m["x"] = x[b]
        m["ctx"] = ctx[b]
        m["cvec"] = np.ascontiguousarray(np.concatenate([_feat(c[b]), cc], axis=1))
        maps.append(m)
    return maps


_NC_CACHE = {}


def kernel(**inputs):
    if "nc" not in _NC_CACHE:
        _NC_CACHE["nc"] = build()
    nc = _NC_CACHE["nc"]
    maps = prep_inputs(inputs)
    res = run_bass_kernel_spmd(nc, maps, core_ids=list(range(NCORES)))
    return np.stack([np.asarray(r["out"], dtype=np.float32) for r in res.results], axis=0)
```

```python
import numpy as np
import ml_dtypes
from contextlib import ExitStack
import concourse.bass as bass
import concourse.mybir as mybir
from concourse.bass_utils import run_bass_kernel_spmd

F32 = mybir.dt.float32
BF16 = mybir.dt.bfloat16
AF = mybir.ActivationFunctionType
ALU = mybir.AluOpType
AX = mybir.AxisListType

NCORES = 8
D = 1024
SEQ = 2048
CTX = 256
NT = SEQ + CTX
NTILE = NT // 128
DEPTH = 2
EPS = 1e-6
INC = 3344
C_X, C_B, C_C, C_Z, C_DT = 0, 512, 768, 1024, 1536
C_GQ, C_GK, C_GV, C_GG = 1552, 1808, 1936, 2064
C_DQ, C_DK, C_DV, C_DG = 2320, 2576, 2832, 3088
R_GQ, R_GK, R_DQ, R_DK, NROT = 0, 256, 384, 640, 896
NEG = -30000.0
SAME_ENGINE_SYNC = True
S_LEVEL = [0]
BCAST_LHST = True


class Buf:
    __slots__ = ("w", "r", "name")

    def __init__(self, name=""):
        self.w = None
        self.r = {}
        self.name = name


class T:
    NDS = 8
    ROT = 20000

    def __init__(self, nc, es):
        self.nc = nc
        self.es = es
        self.E = {"pe": nc.tensor, "act": nc.scalar, "dve": nc.vector, "pool": nc.gpsimd, "sp": nc.sync}
        self.sem = {}
        self.cnt = {}
        self.nsem = 0
        for e in ("pe", "act", "dve", "pool"):
            self.sem[e] = self._newsem("s_" + e)
            self.cnt[e] = 0
        self.seen = {e: {} for e in self.E}
        self.pend = {e: ([], []) for e in self.E}
        self.dq = {}
        for q in ("sp", "pool", "act"):
            self.dq[q] = {"sems": [self._newsem(f"d_{q}{i}") for i in range(self.NDS)],
                          "vals": [0] * self.NDS, "n": 0}
        self.alltoks = {}
        self.ninstr = 0

    def _newsem(self, name):
        self.nsem += 1
        return self.es.enter_context(self.nc.semaphore(f"{name}_{self.nsem}"))

    def _wait(self, x, sem, val):
        if self.seen[x].get(sem, 0) >= val:
            return
        self.E[x].wait_ge(sem, val)
        self.seen[x][sem] = val

    def _deps(self, x, reads, writes):
        deps = {}
        for b in reads:
            if b.w is not None:
                s, v = b.w
                if deps.get(s, 0) < v:
                    deps[s] = v
        for b in writes:
            if b.w is not None:
                s, v = b.w
                if deps.get(s, 0) < v:
                    deps[s] = v
            for s, v in b.r.items():
                if deps.get(s, 0) < v:
                    deps[s] = v
        for e, (pr, pw) in self.pend.items():
            if e == x or (not pr and not pw):
                continue
            for b in writes:
                assert all(b is not o for o in pr) and all(b is not o for o in pw), f"pending conflict {b.name}"
            for b in reads:
                assert all(b is not o for o in pw), f"pending conflict {b.name}"
        own = self.sem.get(x)
        for s, v in deps.items():
            if s is own and (x == "pe" or not SAME_ENGINE_SYNC):
                continue
            self._wait(x, s, v)

    def op(self, x, fn, reads=(), writes=(), sig=True):
        self._deps(x, reads, writes)
        ins = fn(self.E[x])
        self.ninstr += 1
        pr, pw = self.pend[x]
        pr.extend(reads)
        pw.extend(writes)
        if sig:
            if self.cnt[x] >= self.ROT:
                self.sem[x] = self._newsem("s_" + x)
                self.cnt[x] = 0
            self.cnt[x] += 1
            s = self.sem[x]
            v = self.cnt[x]
            ins.then_inc(s, 1)
            self.alltoks[s] = v
            for b in pr:
                if b.r.get(s, 0) < v:
                    b.r[s] = v
            for b in pw:
                b.w = (s, v)
                b.r = {}
            self.pend[x] = ([], [])
        return ins

    def dma(self, q, out, in_, reads=(), writes=(), **kw):
        self._deps(q, reads, writes)
        st = self.dq[q]
        k = st["n"] % self.NDS
        st["n"] += 1
        sem = st["sems"][k]
        if st["vals"][k] > 0:
            self._wait(q, sem, st["vals"][k])
        ins = self.E[q].dma_start(out=out, in_=in_, **kw)
        self.ninstr += 1
        st["vals"][k] += 16
        v = st["vals"][k]
        ins.then_inc(sem, 16)
        self.alltoks[sem] = v
        for b in reads:
            if b.r.get(sem, 0) < v:
                b.r[sem] = v
        for b in writes:
            b.w = (sem, v)
            b.r = {}
        return ins

    def barrier(self):
        for e, (pr, pw) in self.pend.items():
            assert not pr and not pw, "pending unsignaled ops at barrier"
        for x in self.E:
            own = self.sem.get(x)
            for s, v in self.alltoks.items():
                if s is own and x == "pe":
                    continue
                self._wait(x, s, v)


def _rope_tables():
    rows = SEQ // 64
    row_idx = np.repeat(np.arange(rows), 64).astype(np.float32)
    col_idx = (np.arange(SEQ) % 64).astype(np.float32)

    def ang(dim):
        q = dim // 4
        inv = (np.float32(10000.0) ** (-np.arange(q, dtype=np.float32) / np.float32(q))).astype(np.float32)
        a = np.concatenate([row_idx[:, None] * inv, col_idx[:, None] * inv], axis=-1)
        return a.astype(np.float32)

    ag = ang(64)
    ad = ang(32)
    cg = np.concatenate([np.cos(ag), np.cos(ag)], axis=1).T
    sg = np.concatenate([-np.sin(ag), np.sin(ag)], axis=1).T
    cd = np.concatenate([np.cos(ad), np.cos(ad)], axis=1).T
    sd = np.concatenate([-np.sin(ad), np.sin(ad)], axis=1).T
    cd = np.tile(cd, (4, 1))
    sd = np.tile(sd, (4, 1))
    return (np.ascontiguousarray(cg, np.float32), np.ascontiguousarray(sg, np.float32),
            np.ascontiguousarray(cd, np.float32), np.ascontiguousarray(sd, np.float32))


K_ID, K_UINC, K_LINC, K_NEGF, K_NEGB, K_ONES, K_SWAP, K_BD64, K_SELW, NCONST = 0, 1, 2, 3, 4, 5, 6, 7, 8, 9


def _const_mats():
    i = np.arange(128)
    m = np.zeros((NCONST, 128, 128), np.float32)
    m[K_ID] = np.eye(128)
    m[K_UINC] = (i[:, None] <= i[None, :])
    m[K_LINC] = (i[:, None] >= i[None, :])
    m[K_NEGF] = np.where(i[None, :] < i[:, None], NEG, 0.0)
    m[K_NEGB] = np.where(i[None, :] > i[:, None], NEG, 0.0)
    m[K_ONES] = 1.0
    m[K_SWAP] = (i[:, None] == ((i[None, :] + 64) % 128))
    bd = np.zeros((128, 128), np.float32)
    bd[:64, :64] = 1.0 / 64
    bd[64:, 64:] = 1.0 / 64
    m[K_BD64] = bd
    sel = np.zeros((128, 128), np.float32)
    m[K_SELW] = sel
    return np.ascontiguousarray(m.transpose(1, 0, 2))


def _feat(v):
    v = np.asarray(v, np.float32)
    return np.ascontiguousarray(v.reshape(-1, 128).T)


class Tl:
    def __init__(self, h, name):
        self.h = h
        self.b = Buf(name)

    def __getitem__(self, k):
        return self.h[k]


def bc(ap, shape):
    return ap.broadcast_to(list(shape))


def build(dbg=(), stop_after=None, layers=DEPTH):
    nc = bass.Bass("TRN2", target_bir_lowering=False)
    dbg = set(dbg)

    def din(name, shape, dt=F32):
        return Tl(nc.dram_tensor(name, list(shape), dt, kind="ExternalInput").ap(), name)

    def dscr(name, shape, dt=BF16):
        kind = "ExternalOutput" if name in dbg else "Internal"
        return Tl(nc.dram_tensor(name, list(shape), dt, kind=kind).ap(), name)

    x_d = din("x", [SEQ, D])
    ctx_d = din("ctx", [CTX, D])
    cvec_d = din("cvec", [128, 16])
    wmod_d = din("w_mod", [DEPTH, D, 3 * D])
    bmodf_d = din("bmodf", [DEPTH, 128, 16])
    bmod_d = din("b_mod", [DEPTH, 3 * D])
    gpref_d = din("gpref", [DEPTH, 128, 8])
    gpost_d = din("g_post", [DEPTH, D])
    win_d = din("w_in", [DEPTH, D, INC])
    wrot_d = din("w_rot", [DEPTH, D, NROT])
    wout_d = din("w_out", [DEPTH, D, D])
    convw_d = din("convw_f", [DEPTH, 128, 8, 5])
    convb_d = din("convb_f", [DEPTH, 128, 8])
    alog_d = din("alog", [DEPTH, 16])
    dtb_d = din("dtb", [DEPTH, 16])
    dskip_d = din("d_skip", [DEPTH, 8])
    ssdg_d = din("ssd_norm_g", [DEPTH, 512])
    qkg_d = din("qkg_f", [DEPTH, 64, 4])
    dlam_d = din("diff_lambda", [DEPTH, 128])
    dng_d = din("dng_f", [DEPTH, 128, 1])
    cmat_d = din("cmat", [128, NCONST, 128])
    identb_d = din("identb", [128, 128], BF16)
    ropeg_d = din("rope_g", [2, 64, SEQ])
    roped_d = din("rope_d", [2, 128, SEQ])
    out_d = Tl(nc.dram_tensor("out", [SEQ, D], F32, kind="ExternalOutput").ap(), "out")

    gt_s = dscr("gt_s", [DEPTH, 2, D], F32)
    h1_s = dscr("h1_s", [NT, D], F32)
    xs_s = dscr("xs_s", [NT, 512])
    bt_s = dscr("bt_s", [NT, 256])
    bT_s = dscr("bT_s", [256, NT])
    cT_s = dscr("cT_s", [256, NT])
    zs_s = dscr("zs_s", [NT, 512])
    dt_s = dscr("dt_s", [NT, 16], F32)
    gqT_s = dscr("gqT_s", [4, 64, NT])
    gkT_s = dscr("gkT_s", [2, 64, NT])
    gv_s = dscr("gv_s", [NT, 2, 192])
    ggT_s = dscr("ggT_s", [256, NT])
    dqT_s = dscr("dqT_s", [256, NT])
    dkT_s = dscr("dkT_s", [256, NT])
    dv_s = dscr("dv_s", [NT, 4, 192])
    dgT_s = dscr("dgT_s", [256, NT])
    ycT_s = dscr("ycT_s", [D, NT])
    uT_dbg = dscr("uT_dbg", [D, NT]) if "uT_dbg" in dbg else None
    mod_dbg = dscr("mod_dbg", [128, 32], F32) if "mod_dbg" in dbg else None

    with ExitStack() as es:
        tk = T(nc, es)

        uniq = [0]

        def sb(st, name, shape, dt=F32):
            uniq[0] += 1
            name = f"{name}_s{uniq[0]}"
            return Tl(st.enter_context(nc.sbuf_tensor(name, list(shape), dt)), name)

        def ps(st, name, shape, dt=F32):
            uniq[0] += 1
            name = f"{name}_p{uniq[0]}"
            return Tl(st.enter_context(nc.psum_tensor(name, list(shape), dt)), name)

        cm = sb(es, "cm", [128, NCONST, 128])
        identb = sb(es, "identb_sb", [128, 128], BF16)
        A1L = [sb(es, f"A1_{i}", [128, 8, 2]) for i in range(DEPTH)]
        SHL = [sb(es, f"SH_{i}", [128, 8, 2]) for i in range(DEPTH)]
        Ssil = sb(es, "Ssil", [128, 8, 2])
        tk.dma("sp", cm[:], cmat_d[:], reads=[cmat_d.b], writes=[cm.b])
        tk.dma("sp", identb[:], identb_d[:], reads=[identb_d.b], writes=[identb.b])

        epsc = sb(es, "epsc", [128, 1])
        tk.op("pool", lambda e: e.memset(epsc[:], EPS), writes=[epsc.b])

        def CM(k):
            return cm[:, k, :]

        ones_t = sb(es, "ones_t", [128, 768], BF16)
        tk.op("dve", lambda e: e.memset(ones_t[:], 1.0), writes=[ones_t.b])
        for (dst, nh) in ((gv_s, 2), (dv_s, 4)):
            tk.dma("sp", dst[:].rearrange("(c p) g w -> p c (g w)", p=128),
                   bc(ones_t[:, 0:nh * 192].unsqueeze(1), [128, NTILE, nh * 192]), reads=[ones_t.b], writes=[dst.b])

        def phase_M(l, qs):
            A1, SH, S = A1L[l], SHL[l], Ssil
            q = qs[0]
            with ExitStack() as ph:
                bmf = sb(ph, "bmf", [128, 16])
                gpf = sb(ph, "gpf", [128, 8])
                bgt = sb(ph, "bgt", [2, D])
                modT = sb(ph, "modT", [128, 16, 2])
                gtr = sb(ph, "gtr", [2, D])
                wb = [[sb(ph, f"wmb{i}{h}", [128, 4, 512]) for h in range(2)] for i in range(2)]
                psM = ps(ph, "psM", [128, 16, 2])
                psG = [ps(ph, f"psG{i}", [2, 512]) for i in range(2)]
                if l == 0:
                    cv = sb(ph, "cv", [128, 16])
                    tk.dma(q, cv[:], cvec_d[:], reads=[cvec_d.b], writes=[cv.b])
                    for w in range(2):
                        tk.op("act", lambda e, w=w: e.activation(out=S[:, :, w], in_=cv[:, 8 * w:8 * w + 8], func=AF.Silu),
                              reads=[cv.b], writes=[S.b])
                tk.dma(q, bmf[:], bmodf_d[l], reads=[bmodf_d.b], writes=[bmf.b])
                tk.dma(q, gpf[:], gpref_d[l], reads=[gpref_d.b], writes=[gpf.b])
                tk.dma(q, bgt[:], bmod_d[l, 2 * D:3 * D].partition_broadcast(2), reads=[bmod_d.b], writes=[bgt.b])
                wv = wmod_d[l].rearrange("(kc p) n -> p kc n", p=128)
                yield
                for blk in range(6):
                    wh = wb[blk % 2]
                    for h in range(2):
                        tk.dma(qs[h], wh[h][:], wv[:, 4 * h:4 * h + 4, blk * 512:(blk + 1) * 512], reads=[wmod_d.b], writes=[wh[h].b])
                    if blk < 4:
                        for oc in range(4):
                            for kc in range(8):
                                w_ = wh[kc // 4]
                                tk.op("pe", lambda e, oc=oc, kc=kc, w_=w_, blk=blk: e.matmul(
                                    psM[:, blk * 4 + oc, :], lhsT=w_[:, kc % 4, oc * 128:(oc + 1) * 128], rhs=S[:, kc, :],
                                    start=(kc == 0), stop=(kc == 7)),
                                    reads=[w_.b, S.b], writes=[psM.b], sig=(kc == 7 and oc == 3))
                    else:
                        g = psG[blk - 4]
                        for kc in range(8):
                            w_ = wh[kc // 4]
                            tk.op("pe", lambda e, kc=kc, w_=w_, g=g: e.matmul(
                                g[:, :], lhsT=S[:, kc, :], rhs=w_[:, kc % 4, :], start=(kc == 0), stop=(kc == 7)),
                                reads=[w_.b, S.b], writes=[g.b], sig=(kc == 7))
                    yield
                tk.op("dve", lambda e: e.tensor_tensor(out=modT[:], in0=psM[:], in1=bc(bmf[:].unsqueeze(2), [128, 16, 2]),
                                                       op=ALU.add), reads=[psM.b, bmf.b], writes=[modT.b])
                tk.op("dve", lambda e: e.tensor_copy(out=SH[:], in_=modT[:, 0:8, :]), reads=[modT.b], writes=[SH.b])
                tk.op("dve", lambda e: e.scalar_tensor_tensor(out=A1[:], in0=modT[:, 8:16, :], scalar=1.0,
                                                              in1=bc(gpf[:].unsqueeze(2), [128, 8, 2]),
                                                              op0=ALU.add, op1=ALU.mult),
                      reads=[modT.b, gpf.b], writes=[A1.b])
                for i in range(2):
                    tk.op("dve", lambda e, i=i: e.tensor_tensor(out=gtr[:, i * 512:(i + 1) * 512], in0=psG[i][:, :],
                                                                in1=bgt[:, i * 512:(i + 1) * 512], op=ALU.add),
                          reads=[psG[i].b, bgt.b], writes=[gtr.b])
                tk.dma("sp", gt_s[l], gtr[:], reads=[gtr.b], writes=[gt_s.b])
                if mod_dbg is not None and l == 0:
                    tk.dma("sp", mod_dbg[:, 0:16], A1[:].rearrange("p a b -> p (a b)"), reads=[A1.b], writes=[mod_dbg.b])
                    tk.dma("sp", mod_dbg[:, 16:32], SH[:].rearrange("p a b -> p (a b)"), reads=[SH.b], writes=[mod_dbg.b])
                yield

        def phase_N(l, uT, mnext=None, mid_hook=None):
            A1, SH = A1L[l], SHL[l]
            with ExitStack() as ph:
                if mnext is not None:
                    next(mnext)
                ht = [sb(ph, f"ht{i}", [128, D]) for i in range(2)]
                hb = [sb(ph, f"hb{i}", [128, D], BF16) for i in range(2)]
                junk = sb(ph, "junkN", [128, D], BF16)
                ssqs = [sb(ph, f"ssq{i}", [128, 1]) for i in range(2)]
                rstds = [sb(ph, f"rstd{i}", [128, 1]) for i in range(2)]
                tmpf = [sb(ph, f"tmpf{i}", [128, 8, 128]) for i in range(2)]
                pT = [ps(ph, f"pT{i}", [128, 8, 128], BF16) for i in range(2)]
                for tt in range(NTILE):
                    k = tt % 2
                    w = 1 if tt < 2 else 0
                    if l == 0:
                        src = ctx_d if tt < 2 else x_d
                        sap = ctx_d[tt * 128:(tt + 1) * 128, :] if tt < 2 else x_d[(tt - 2) * 128:(tt - 1) * 128, :]
                    else:
                        src = h1_s
                        sap = h1_s[tt * 128:(tt + 1) * 128, :]
                    tk.dma("sp", ht[k][:], sap, reads=[src.b], writes=[ht[k].b])
                    ssq = ssqs[k]
                    rstd = rstds[k]
                    tk.op("pool", lambda e: e.memset(ssq[:], 0.0), writes=[ssq.b])
                    tk.op("act", lambda e, k=k: e.activation(out=junk[:], in_=ht[k][:], func=AF.Square, accum_out=ssq[:, 0:1]),
                          reads=[ht[k].b], writes=[junk.b, ssq.b])
                    tk.op("act", lambda e: e.activation(out=rstd[:], in_=ssq[:], func=AF.Ln, scale=1.0 / D, bias=epsc[:, 0:1]),
                          reads=[ssq.b, epsc.b], writes=[rstd.b])
                    tk.op("act", lambda e: e.activation(out=rstd[:], in_=rstd[:], func=AF.Exp, scale=-0.5),
                          reads=[rstd.b], writes=[rstd.b])
                    tk.op("act", lambda e, k=k: e.activation(out=hb[k][:], in_=ht[k][:], func=AF.Copy, scale=rstd[:, 0:1]),
                          reads=[ht[k].b, rstd.b], writes=[hb[k].b])
                    for j in range(8):
                        tk.op("pe", lambda e, k=k, j=j: e.transpose(out=pT[k][:, j, :], in_=hb[k][:, j * 128:(j + 1) * 128],
                                                                    identity=identb[:]),
                              reads=[hb[k].b, identb.b], writes=[pT[k].b], sig=(j == 7))
                    tk.op("dve", lambda e, k=k, w=w: e.tensor_tensor(out=tmpf[k][:], in0=pT[k][:],
                                                                     in1=bc(A1[:, :, w:w + 1], [128, 8, 128]), op=ALU.mult),
                          reads=[pT[k].b, A1.b], writes=[tmpf[k].b])
                    tk.op("dve", lambda e, k=k, w=w, tt=tt: e.tensor_tensor(
                        out=uT[:, :, tt * 128:(tt + 1) * 128], in0=tmpf[k][:],
                        in1=bc(SH[:, :, w:w + 1], [128, 8, 128]), op=ALU.add),
                        reads=[tmpf[k].b, SH.b], writes=[uT.b])
                    if mnext is not None and tt % 3 == 2:
                        next(mnext)
                    if mid_hook is not None and tt == 9:
                        mid_hook()
                if mnext is not None:
                    next(mnext)
                if uT_dbg is not None and l == 0:
                    tk.dma("sp", uT_dbg[:].rearrange("(kc p) t -> p kc t", p=128), uT[:], reads=[uT.b], writes=[uT_dbg.b])
                tk.barrier()

        def make_I_loader(l, wr):
            NS = 4
            winv = win_d[l].rearrange("(kc p) n -> p kc n", p=128)
            wrotv = wrot_d[l].rearrange("(kc p) n -> p kc n", p=128)
            blocks = [
                [(winv, 0, 512)],
                [(winv, 512, 512)],
                [(winv, C_Z, 512)],
                [(winv, C_DT, 16), (winv, C_GV, 128), (winv, C_DV, 256)],
                [(winv, C_GG, 256), (winv, C_DG, 256)],
                [(winv, C_GQ, 384)],
                [(wrotv, R_GQ, 384)],
                [(winv, C_DQ, 512)],
                [(wrotv, R_DQ, 512)],
            ]

            def load_block(i):
                slot = wr[i % NS]
                o = 0
                for (v, c0, n) in blocks[i]:
                    src = win_d if v is winv else wrot_d
                    tk.dma("pool", slot[:, :, o:o + n], v[:, :, c0:c0 + n], reads=[src.b], writes=[slot.b])
                    o += n
                return slot


            return load_block

        TB = [(0, 256)] + [(256 + 512 * i, 512) for i in range(4)]

        def phase_I(l, uT, wr, slots):
            with ExitStack() as ph:
                NS = 4
                cs = [sb(ph, f"cs{i}", [128, 2312]) for i in range(2)]
                accs = [sb(ph, f"acc{i}", [128, NT]) for i in range(2)]
                xb = [sb(ph, f"xb{i}", [128, NT], BF16) for i in range(2)]
                xtoks = [sb(ph, f"xtok{i}", [128, NTILE, 128], BF16) for i in range(2)]
                xti = [0]
                rg = sb(ph, "rg", [128, 2, SEQ])
                rd = sb(ph, "rd", [128, 2, SEQ])
                zb = [sb(ph, f"zb{i}", [128, 512], BF16) for i in range(2)]
                vb = [sb(ph, f"vb{i}", [128, 384], BF16) for i in range(2)]
                dts = sb(ph, "dts", [128, NTILE, 16])
                sqs = sb(ph, "sqs", [128, 512])
                rs = sb(ph, "rs", [128, 512])
                t1 = sb(ph, "t1", [128, 512])
                t2 = sb(ph, "t2", [128, 512])
                cw = sb(ph, "cw", [128, 8, 5])
                cb = sb(ph, "cb", [128, 8])
                qkg = sb(ph, "qkg", [128, 4])
                pf = [ps(ph, f"pf{i}", [128, 512]) for i in range(4)]
                pm = ps(ph, "pm", [128, 512])
                ptr = ps(ph, "ptr", [128, 4, 128], BF16)
                ptm = [ps(ph, f"ptm{i}", [128, 512]) for i in range(2)]
                pfi = [0]

                def nextpf():
                    p = pf[pfi[0] % 4]
                    pfi[0] += 1
                    return p

                for hf in range(2):
                    tk.dma("sp", rg[64 * hf:64 * hf + 64], ropeg_d[:].rearrange("a p t -> p a t"), reads=[ropeg_d.b], writes=[rg.b])
                tk.dma("sp", rd[:], roped_d[:].rearrange("a p t -> p a t"), reads=[roped_d.b], writes=[rd.b])
                tk.dma("sp", cw[:], convw_d[l], reads=[convw_d.b], writes=[cw.b])
                tk.dma("sp", cb[:], convb_d[l], reads=[convb_d.b], writes=[cb.b])
                for hf in range(2):
                    tk.dma("sp", qkg[64 * hf:64 * hf + 64], qkg_d[l], reads=[qkg_d.b], writes=[qkg.b])
                for c in cs:
                    tk.op("pool", lambda e, c=c: e.memset(c[:], 0.0), writes=[c.b])
                tk.op("pool", lambda e: e.memset(sqs[:], 0.0), writes=[sqs.b])

                load_block = make_I_loader(l, wr)

                def prefetch(i):
                    slots[i] = load_block(i)

                def fm_matmuls(psb, w_, off, m, t0, n):
                    for kc in range(8):
                        tk.op("pe", lambda e, kc=kc: e.matmul(psb[0:m, 0:n], lhsT=w_[:, kc, off:off + m],
                                                               rhs=uT[:, kc, t0:t0 + n], start=(kc == 0), stop=(kc == 7)),
                              reads=[w_.b, uT.b], writes=[psb.b], sig=(kc == 7))

                def transposes_to(xbt, dst_d, c0):
                    xtok = xtoks[xti[0] % 2]
                    xti[0] += 1
                    for g0 in range(0, NTILE, 4):
                        gn = min(4, NTILE - g0)
                        for i in range(gn):
                            tt = g0 + i
                            tk.op("pe", lambda e, i=i, tt=tt: e.transpose(out=ptr[:, i, :], in_=xbt[:, tt * 128:(tt + 1) * 128],
                                                                          identity=identb[:]),
                                  reads=[xbt.b, identb.b], writes=[ptr.b], sig=(i == gn - 1))
                        tk.op("act", lambda e, g0=g0, gn=gn: e.activation(out=xtok[:, g0:g0 + gn, :], in_=ptr[:, 0:gn, :], func=AF.Copy),
                              reads=[ptr.b], writes=[xtok.b])
                    tk.dma("sp", dst_d[:].rearrange("(tt p) f -> p tt f", p=128)[:, :, c0:c0 + 128], xtok[:],
                           reads=[xtok.b], writes=[dst_d.b])

                def A_P(j):
                    if j == 0:
                        prefetch(3)
                    if j == 4:
                        prefetch(4)
                    w_ = slots[j // 4]
                    c_ = cs[j % 2]
                    for (t0, n) in TB:
                        p = nextpf()
                        fm_matmuls(p, w_, (j % 4) * 128, 128, t0, n)
                        d0 = 2 + t0 if t0 < 256 else 262 + (t0 - 256)
                        tk.op("act", lambda e, p=p, d0=d0, n=n, c_=c_: e.activation(out=c_[:, d0:d0 + n], in_=p[:, 0:n], func=AF.Copy),
                              reads=[p.b], writes=[c_.b])

                def A_C(j):
                    c_ = cs[j % 2]
                    acc = accs[j % 2]
                    for (eng, s0, a0, n) in (("dve", 2, 0, 256), ("dve", 262, 256, 2048)):
                        tk.op(eng, lambda e, s0=s0, a0=a0, n=n: e.tensor_scalar(
                            out=acc[:, a0:a0 + n], in0=c_[:, s0 - 2:s0 - 2 + n], scalar1=cw[:, j, 0:1], scalar2=cb[:, j:j + 1],
                            op0=ALU.mult, op1=ALU.add), reads=[c_.b, cw.b, cb.b], writes=[acc.b])
                        for k in range(1, 5):
                            tk.op(eng, lambda e, s0=s0, a0=a0, n=n, k=k: e.scalar_tensor_tensor(
                                out=acc[:, a0:a0 + n], in0=c_[:, s0 - 2 + k:s0 - 2 + k + n], scalar=cw[:, j, k:k + 1],
                                in1=acc[:, a0:a0 + n], op0=ALU.mult, op1=ALU.add), reads=[c_.b, cw.b, acc.b], writes=[acc.b])

                def A_F(j):
                    acc = accs[j % 2]
                    xbt = xb[j % 2]
                    tk.op("act", lambda e: e.activation(out=xbt[:], in_=acc[:], func=AF.Silu), reads=[acc.b], writes=[xbt.b])
                    if j < 4:
                        transposes_to(xbt, xs_s, j * 128)
                    elif j < 6:
                        tk.dma("sp", bT_s[(j - 4) * 128:(j - 3) * 128, :], xbt[:], reads=[xbt.b], writes=[bT_s.b])
                        transposes_to(xbt, bt_s, (j - 4) * 128)
                    else:
                        tk.dma("sp", cT_s[(j - 6) * 128:(j - 5) * 128, :], xbt[:], reads=[xbt.b], writes=[cT_s.b])

                A_P(0)
                A_P(1)
                A_C(0)
                for j in range(8):
                    A_F(j)
                    if j + 2 < 8:
                        A_P(j + 2)
                    if j + 1 < 8:
                        A_C(j + 1)

                prefetch(5)
                wz = slots[2]
                wt = slots[3]
                for tt in range(NTILE):
                    k = tt % 2
                    for (w_, n, p) in ((wz, 512, ptm[0]), (wt, 400, ptm[1])):
                        for kc in range(8):
                            tk.op("pe", lambda e, kc=kc, w_=w_, n=n, p=p, tt=tt: e.matmul(
                                p[:, 0:n], lhsT=uT[:, kc, tt * 128:(tt + 1) * 128], rhs=w_[:, kc, 0:n],
                                start=(kc == 0), stop=(kc == 7)), reads=[w_.b, uT.b], writes=[p.b], sig=(kc == 7))
                    tk.op("act", lambda e, k=k: e.activation(out=zb[k][:], in_=ptm[0][:], func=AF.Silu),
                          reads=[ptm[0].b], writes=[zb[k].b])
                    tk.dma("sp", zs_s[tt * 128:(tt + 1) * 128, :], zb[k][:], reads=[zb[k].b], writes=[zs_s.b])
                    tk.op("dve", lambda e, tt=tt: e.tensor_copy(out=dts[:, tt, :], in_=ptm[1][:, 0:16]),
                          reads=[ptm[1].b], writes=[dts.b])
                    tk.op("dve", lambda e, k=k: e.tensor_copy(out=vb[k][:], in_=ptm[1][:, 16:400]),
                          reads=[ptm[1].b], writes=[vb[k].b])
                    tk.dma("sp", gv_s[tt * 128:(tt + 1) * 128, :, 64:128], vb[k][:, 0:128].rearrange("p (g d) -> p g d", g=2),
                           reads=[vb[k].b], writes=[gv_s.b])
                    tk.dma("sp", dv_s[tt * 128:(tt + 1) * 128, :, 64:128], vb[k][:, 128:384].rearrange("p (g d) -> p g d", g=4),
                           reads=[vb[k].b], writes=[dv_s.b])
                tk.dma("sp", dt_s[:].rearrange("(tt p) f -> p tt f", p=128), dts[:], reads=[dts.b], writes=[dt_s.b])

                prefetch(6)
                prefetch(7)
                wg = slots[4]
                for j in range(4):
                    xbt = xb[j % 2]
                    for (t0, n) in TB:
                        p = nextpf()
                        fm_matmuls(p, wg, j * 128, 128, t0, n)
                        tk.op("act", lambda e, p=p, t0=t0, n=n, xbt=xbt: e.activation(out=xbt[:, t0:t0 + n], in_=p[:, 0:n], func=AF.Silu),
                              reads=[p.b], writes=[xbt.b])
                    dst = ggT_s if j < 2 else dgT_s
                    tk.dma("sp", dst[(j % 2) * 128:(j % 2 + 1) * 128, :], xbt[:], reads=[xbt.b], writes=[dst.b])

                prefetch(8)
                wq = slots[5]
                wqr = slots[6]
                for pp_ in range(3):
                    gcol = 0 if pp_ < 2 else 2
                    xbt = xb[pp_ % 2]
                    for (t0, n) in TB:
                        pa = nextpf()
                        fm_matmuls(pa, wq, pp_ * 128, 128, t0, n)
                        lat = t0 >= 256
                        if lat:
                            pb = nextpf()
                            fm_matmuls(pb, wqr, pp_ * 128, 128, t0, n)
                        tk.op("act", lambda e, pa=pa, n=n: e.activation(out=sqs[:, 0:n], in_=pa[:, 0:n], func=AF.Square),
                              reads=[pa.b], writes=[sqs.b])
                        tk.op("pe", lambda e, n=n: e.matmul(pm[:, 0:n], lhsT=CM(K_BD64), rhs=sqs[:, 0:n],
                                                            start=True, stop=True), reads=[cm.b, sqs.b], writes=[pm.b])
                        tk.op("act", lambda e, n=n: e.activation(out=rs[:, 0:n], in_=pm[:, 0:n], func=AF.Ln, bias=epsc[:, 0:1]),
                              reads=[pm.b, epsc.b], writes=[rs.b])
                        tk.op("act", lambda e, n=n: e.activation(out=rs[:, 0:n], in_=rs[:, 0:n], func=AF.Exp, scale=-0.5),
                              reads=[rs.b], writes=[rs.b])
                        if lat:
                            r0 = t0 - 256
                            tk.op("dve", lambda e, pa=pa, n=n, gcol=gcol: e.scalar_tensor_tensor(
                                out=t1[:, 0:n], in0=pa[:, 0:n], scalar=qkg[:, gcol:gcol + 1], in1=rs[:, 0:n],
                                op0=ALU.mult, op1=ALU.mult), reads=[pa.b, qkg.b, rs.b], writes=[t1.b])
                            tk.op("dve", lambda e, pb=pb, n=n, gcol=gcol: e.scalar_tensor_tensor(
                                out=t2[:, 0:n], in0=pb[:, 0:n], scalar=qkg[:, gcol + 1:gcol + 2], in1=rs[:, 0:n],
                                op0=ALU.mult, op1=ALU.mult), reads=[pb.b, qkg.b, rs.b], writes=[t2.b])
                            tk.op("pool", lambda e, n=n, r0=r0: e.tensor_tensor(out=t1[:, 0:n], in0=t1[:, 0:n],
                                                                                in1=rg[:, 0, r0:r0 + n], op=ALU.mult),
                                  reads=[t1.b, rg.b], writes=[t1.b])
                            tk.op("dve", lambda e, n=n, r0=r0: e.tensor_tensor(out=t2[:, 0:n], in0=t2[:, 0:n],
                                                                               in1=rg[:, 1, r0:r0 + n], op=ALU.mult),
                                  reads=[t2.b, rg.b], writes=[t2.b])
                            tk.op("dve", lambda e, n=n, t0=t0, xbt=xbt: e.tensor_tensor(out=xbt[:, t0:t0 + n], in0=t1[:, 0:n],
                                                                                        in1=t2[:, 0:n], op=ALU.add),
                                  reads=[t1.b, t2.b], writes=[xbt.b])
                        else:
                            tk.op("dve", lambda e, pa=pa, n=n, gcol=gcol, t0=t0, xbt=xbt: e.scalar_tensor_tensor(
                                out=xbt[:, t0:t0 + n], in0=pa[:, 0:n], scalar=qkg[:, gcol:gcol + 1], in1=rs[:, 0:n],
                                op0=ALU.mult, op1=ALU.mult), reads=[pa.b, qkg.b, rs.b], writes=[xbt.b])
                    if pp_ < 2:
                        dst = gqT_s[2 * pp_:2 * pp_ + 2].rearrange("h d t -> (h d) t")
                        dstb = gqT_s.b
                    else:
                        dst = gkT_s[:].rearrange("h d t -> (h d) t")
                        dstb = gkT_s.b
                    tk.dma("sp", dst, xbt[:], reads=[xbt.b], writes=[dstb])

                wd = slots[7]
                wdr = slots[8]
                for j in range(4):
                    xbt = xb[j % 2]
                    for (t0, n) in TB:
                        pa = nextpf()
                        fm_matmuls(pa, wd, j * 128, 128, t0, n)
                        if t0 >= 256:
                            pb = nextpf()
                            fm_matmuls(pb, wdr, j * 128, 128, t0, n)
                            r0 = t0 - 256
                            tk.op("dve", lambda e, pa=pa, n=n, r0=r0: e.tensor_tensor(out=t1[:, 0:n], in0=pa[:, 0:n],
                                                                                      in1=rd[:, 0, r0:r0 + n], op=ALU.mult),
                                  reads=[pa.b, rd.b], writes=[t1.b])
                            tk.op("dve", lambda e, pb=pb, n=n, r0=r0: e.tensor_tensor(out=t2[:, 0:n], in0=pb[:, 0:n],
                                                                                      in1=rd[:, 1, r0:r0 + n], op=ALU.mult),
                                  reads=[pb.b, rd.b], writes=[t2.b])
                            tk.op("pool", lambda e, n=n, t0=t0, xbt=xbt: e.tensor_tensor(out=xbt[:, t0:t0 + n], in0=t1[:, 0:n],
                                                                                         in1=t2[:, 0:n], op=ALU.add),
                                  reads=[t1.b, t2.b], writes=[xbt.b])
                        else:
                            tk.op("act", lambda e, pa=pa, n=n, t0=t0, xbt=xbt: e.activation(out=xbt[:, t0:t0 + n], in_=pa[:, 0:n], func=AF.Copy),
                                  reads=[pa.b], writes=[xbt.b])
                    dst = dqT_s if j < 2 else dkT_s
                    tk.dma("sp", dst[(j % 2) * 128:(j % 2 + 1) * 128, :], xbt[:], reads=[xbt.b], writes=[dst.b])
                tk.barrier()

        def phase_S(l):
            with ExitStack() as ph:
                xs = sb(ph, "xs", [128, NTILE, 512], BF16)
                btok = sb(ph, "btok", [128, NTILE, 256], BF16)
                bT = sb(ph, "bT", [128, 2, NT], BF16)
                cT = sb(ph, "cT", [128, 2, NT], BF16)
                zs = sb(ph, "zs", [128, NTILE, 512], BF16)
                dtr = sb(ph, "dtr", [128, NTILE, 16])
                alog = sb(ph, "alog", [128, 2, 8])
                dtb = sb(ph, "dtbb", [128, 2, 8])
                dsk = sb(ph, "dsk", [128, 8])
                ng = sb(ph, "ng", [128, 512])
                Abc = sb(ph, "Abc", [128, 2, 8])
                one = sb(ph, "one", [128, 1])
                SH4 = [128, 2, NTILE, 8]
                tmp = sb(ph, "s_tmp", SH4)
                dtv = sb(ph, "dtv", SH4)
                dtA = sb(ph, "dtA", SH4)
                negcum = sb(ph, "negcum", SH4)
                ecum = sb(ph, "ecum", SH4)
                etot = sb(ph, "etot", SH4)
                dec = sb(ph, "dec", SH4)
                wdec = sb(ph, "wdec", SH4)
                lnb = sb(ph, "lnb", SH4)
                IDd = sb(ph, "IDd", [128, 8, 128], BF16)
                sinb = sb(ph, "sinb", [128, NTILE, 512], BF16)
                ycs = sb(ph, "ycs", [128, 4, NT], BF16)
                fl = lambda t: t[:].rearrange("p a b c -> p (a b c)")

                tk.dma("sp", dtr[:], dt_s[:].rearrange("(c p) f -> p c f", p=128), reads=[dt_s.b], writes=[dtr.b])
                tk.dma("sp", alog[:].rearrange("p a b -> p (a b)"), alog_d[l].partition_broadcast(128), reads=[alog_d.b], writes=[alog.b])
                tk.dma("sp", dtb[:].rearrange("p a b -> p (a b)"), dtb_d[l].partition_broadcast(128), reads=[dtb_d.b], writes=[dtb.b])
                tk.dma("sp", dsk[:], dskip_d[l].partition_broadcast(128), reads=[dskip_d.b], writes=[dsk.b])
                tk.dma("sp", xs[:], xs_s[:].rearrange("(c p) f -> p c f", p=128), reads=[xs_s.b], writes=[xs.b])
                tk.dma("sp", btok[:], bt_s[:].rearrange("(c p) f -> p c f", p=128), reads=[bt_s.b], writes=[btok.b])
                tk.dma("sp", bT[:], bT_s[:].rearrange("(g p) t -> p g t", p=128), reads=[bT_s.b], writes=[bT.b])
                tk.dma("sp", cT[:], cT_s[:].rearrange("(g p) t -> p g t", p=128), reads=[cT_s.b], writes=[cT.b])
                tk.dma("sp", zs[:], zs_s[:].rearrange("(c p) f -> p c f", p=128), reads=[zs_s.b], writes=[zs.b])
                tk.dma("sp", ng[:], ssdg_d[l].partition_broadcast(128), reads=[ssdg_d.b], writes=[ng.b])
                tk.op("pool", lambda e: e.memset(one[:], 1.0), writes=[one.b])

                if S_LEVEL[0] == -1:
                    tk.barrier()
                    return
                with ExitStack() as s1:
                    pcum = ps(s1, "pcum", [128, 2, NTILE * 8])
                    ptot = ps(s1, "ptot", [128, 2 * NTILE * 8])
                    tk.op("act", lambda e: e.activation(out=Abc[:], in_=alog[:], func=AF.Exp), reads=[alog.b], writes=[Abc.b])
                    tk.op("dve", lambda e: e.tensor_scalar(out=Abc[:], in0=Abc[:], scalar1=-1.0, scalar2=None, op0=ALU.mult),
                          reads=[Abc.b], writes=[Abc.b])
                    for d in range(2):
                        tk.op("dve", lambda e, d=d: e.tensor_tensor(out=tmp[:, d], in0=dtr[:, :, d * 8:(d + 1) * 8],
                                                                    in1=bc(dtb[:, d:d + 1, :], [128, NTILE, 8]), op=ALU.add),
                              reads=[dtr.b, dtb.b], writes=[tmp.b])
                    tk.op("act", lambda e: e.activation(out=fl(tmp), in_=fl(tmp), func=AF.Exp), reads=[tmp.b], writes=[tmp.b])
                    tk.op("act", lambda e: e.activation(out=fl(dtv), in_=fl(tmp), func=AF.Ln, bias=one[:, 0:1]),
                          reads=[tmp.b, one.b], writes=[dtv.b])
                    tk.op("act", lambda e: e.activation(out=fl(lnb), in_=fl(dtv), func=AF.Ln), reads=[dtv.b], writes=[lnb.b])
                    for h in range(8):
                        tk.op("dve", lambda e, h=h: e.tensor_scalar(out=IDd[:, h, :], in0=identb[:], scalar1=dsk[:, h:h + 1], scalar2=None,
                                                                    op0=ALU.mult), reads=[identb.b, dsk.b], writes=[IDd.b])
                    for d in range(2):
                        tk.op("dve", lambda e, d=d: e.tensor_tensor(out=dtA[:, d], in0=dtv[:, d],
                                                                    in1=bc(Abc[:, d:d + 1, :], [128, NTILE, 8]), op=ALU.mult),
                              reads=[dtv.b, Abc.b], writes=[dtA.b])
                    if S_LEVEL[0] == -2:
                        tk.barrier()
                        return
                    tk.op("pe", lambda e: e.matmul(pcum[:, 0, :], lhsT=CM(K_UINC), rhs=dtA[:, 0].rearrange("p b c -> p (b c)"),
                                                   start=True, stop=True), reads=[cm.b, dtA.b], writes=[pcum.b])
                    tk.op("pe", lambda e: e.matmul(pcum[:, 1, :], lhsT=CM(K_LINC), rhs=dtA[:, 1].rearrange("p b c -> p (b c)"),
                                                   start=True, stop=True), reads=[cm.b, dtA.b], writes=[pcum.b])
                    tk.op("pe", lambda e: e.matmul(ptot[:, :], lhsT=CM(K_ONES), rhs=fl(dtA), start=True, stop=True),
                          reads=[cm.b, dtA.b], writes=[ptot.b])
                    if S_LEVEL[0] == -3:
                        tk.barrier()
                        return
                    pcf = pcum[:].rearrange("p a n -> p (a n)")
                    tk.op("dve", lambda e: e.tensor_scalar(out=fl(negcum), in0=pcf, scalar1=-1.0, scalar2=None, op0=ALU.mult),
                          reads=[pcum.b], writes=[negcum.b])
                    tk.op("dve", lambda e: e.tensor_copy(out=fl(wdec), in_=ptot[:, :]), reads=[ptot.b], writes=[wdec.b])
                    tk.op("dve", lambda e: e.tensor_tensor(out=fl(lnb), in0=fl(lnb), in1=fl(negcum), op=ALU.add),
                          reads=[lnb.b, negcum.b], writes=[lnb.b])
                    if S_LEVEL[0] == -4:
                        tk.barrier()
                        return
                    tk.op("act", lambda e: e.activation(out=fl(ecum), in_=fl(negcum), func=AF.Exp, scale=-1.0),
                          reads=[negcum.b], writes=[ecum.b])
                    if S_LEVEL[0] == -5:
                        tk.barrier()
                        return
                    tk.op("act", lambda e: e.activation(out=fl(etot), in_=fl(wdec), func=AF.Exp), reads=[wdec.b], writes=[etot.b])
                    tk.op("dve", lambda e: e.tensor_tensor(out=fl(tmp), in0=fl(wdec), in1=fl(negcum), op=ALU.add),
                          reads=[wdec.b, negcum.b], writes=[tmp.b])
                    if S_LEVEL[0] == -6:
                        tk.barrier()
                        return
                    tk.op("act", lambda e: e.activation(out=fl(dec), in_=fl(tmp), func=AF.Exp), reads=[tmp.b], writes=[dec.b])
                    tk.op("dve", lambda e: e.tensor_tensor(out=fl(wdec), in0=fl(dtv), in1=fl(dec), op=ALU.mult),
                          reads=[dtv.b, dec.b], writes=[wdec.b])
                    tk.barrier()

                with ExitStack() as s2:
                    S = [sb(s2, f"Sst{i}", [128, 512]) for i in range(2)]
                    Sfb = sb(s2, "Sfb", [128, 512], BF16)
                    stmp = sb(s2, "stmp", [128, 512])
                    xdd = sb(s2, "xdd", [128, 512], BF16)
                    LTD = [[sb(s2, f"LT{i}g{q}", [128, 4, 128], BF16) for q in range(4)] for i in range(2)]
                    MTD = [sb(s2, f"MT{i}", [128, 16, 128], BF16) for i in range(4)]
                    xddD = [sb(s2, f"xddD{i}", [128, 512], BF16) for i in range(4)]
                    ta = sb(s2, "ta", [128, 512])
                    tb_ = sb(s2, "tb", [128, 512])
                    yz = sb(s2, "yz", [128, 512])
                    yns = [sb(s2, f"yn{i}", [128, 512], BF16) for i in range(2)]
                    junk = sb(s2, "junkS", [128, 512], BF16)
                    ssq = sb(s2, "ssqS", [128, 1])
                    rstd = sb(s2, "rstdS", [128, 1])
                    pL = [ps(s2, f"pL{i}", [128, 4, 128]) for i in range(2)]
                    pcb = ps(s2, "pcb", [128, 2, 128])
                    cbs = [sb(s2, f"cbs{i}", [128, 2, 128]) for i in range(2)]
                    py = ps(s2, "py", [128, 512])
                    pyo = [ps(s2, f"pyo{i}", [128, 512]) for i in range(2)]
                    pst = ps(s2, "pst", [128, 512])
                    pT = ps(s2, "pTS", [128, 4, 128], BF16)
                    v3 = lambda ap: ap.rearrange("p (h d) -> p h d", h=8)

                    def scaled_x(dst, c, w4, d):
                        tk.op("dve", lambda e: e.tensor_tensor(out=v3(dst[:]), in0=v3(xs[:, c, :]),
                                                               in1=bc(w4[:, d, c, :].unsqueeze(2), [128, 8, 64]), op=ALU.mult),
                              reads=[xs.b, w4.b], writes=[dst.b])

                    def state_step(c, d, xsrc):
                        for g in range(2):
                            tk.op("pe", lambda e, g=g: e.matmul(pst[:, g * 256:(g + 1) * 256], lhsT=btok[:, c, g * 128:(g + 1) * 128],
                                                                rhs=xsrc[:, g * 256:(g + 1) * 256], start=True, stop=True),
                                  reads=[btok.b, xsrc.b], writes=[pst.b], sig=(g == 1))
                        tk.op("dve", lambda e: e.tensor_tensor(out=v3(stmp[:]), in0=v3(S[d][:]),
                                                               in1=bc(etot[:, d, c, :].unsqueeze(2), [128, 8, 64]), op=ALU.mult),
                              reads=[S[d].b, etot.b], writes=[stmp.b])
                        tk.op("dve", lambda e: e.tensor_tensor(out=S[d][:], in0=pst[:, :], in1=stmp[:], op=ALU.add),
                              reads=[pst.b, stmp.b], writes=[S[d].b])

                    border = [1, 0] + list(range(NTILE - 1, 1, -1))
                    if S_LEVEL[0] == 1:
                        tk.barrier()
                        return
                    tk.op("pool", lambda e: e.memset(S[1][:], 0.0), writes=[S[1].b])
                    tk.op("pool", lambda e: e.memset(S[0][:], 0.0), writes=[S[0].b])
                    tk.op("pool", lambda e: e.memset(Sfb[:], 0.0), writes=[Sfb.b])
                    for i, c in enumerate(border):
                        tk.op("act", lambda e, c=c: e.activation(out=sinb[:, c, :], in_=S[1][:], func=AF.Copy),
                              reads=[S[1].b], writes=[sinb.b])
                        if i == len(border) - 1:
                            break
                        scaled_x(xdd, c, wdec, 1)
                        state_step(c, 1, xdd)

                    def stageA(c):
                        t0 = c * 128
                        k = c % 2
                        k3 = c % 4
                        LT, MT, cb_ = LTD[k], MTD[k3], cbs[k]

                        def pre():
                            scaled_x(xddD[k3], c, wdec, 0)
                            for g in range(2):
                                tk.op("pe", lambda e, g=g: e.matmul(pcb[:, g, :], lhsT=bT[:, g, t0:t0 + 128], rhs=cT[:, g, t0:t0 + 128],
                                                                    start=True, stop=True), reads=[bT.b, cT.b], writes=[pcb.b], sig=(g == 1))
                            tk.op("dve", lambda e: e.tensor_copy(out=cb_[:], in_=pcb[:]), reads=[pcb.b], writes=[cb_.b])

                        def grp(q4):
                            p = pL[q4 % 2]
                            d = q4 // 2
                            for i in range(4):
                                hd = q4 * 4 + i
                                lh = bc(dtA[:, d, c, hd % 8:hd % 8 + 1], [128, 128])
                                tk.op("pe", lambda e, p=p, i=i, lh=lh, d=d: e.matmul(
                                    p[:, i, :], lhsT=lh, rhs=CM(K_UINC if d == 0 else K_LINC), start=True, stop=False),
                                    reads=[dtA.b, cm.b], writes=[p.b], sig=False)
                                tk.op("pe", lambda e, p=p, i=i, d=d: e.matmul(
                                    p[:, i, :], lhsT=CM(K_ID), rhs=CM(K_NEGF if d == 0 else K_NEGB), start=False, stop=True),
                                    reads=[cm.b], writes=[p.b], sig=(i == 3))
                            for i in range(4):
                                hd = q4 * 4 + i
                                tk.op("act", lambda e, p=p, i=i, hd=hd, d=d: e.activation(
                                    out=LT[q4][:, i, :], in_=p[:, i, :], func=AF.Exp, bias=lnb[:, d, c, hd % 8:hd % 8 + 1]),
                                    reads=[p.b, lnb.b], writes=[LT[q4].b])
                            g = q4 % 2
                            tk.op("dve", lambda e, q4=q4, g=g: e.tensor_tensor(out=MT[:, q4 * 4:(q4 + 1) * 4, :], in0=LT[q4][:],
                                                                               in1=bc(cb_[:, g:g + 1, :], [128, 4, 128]), op=ALU.mult),
                                  reads=[LT[q4].b, cb_.b], writes=[MT.b])
                        return [pre] + [(lambda q4=q4: grp(q4)) for q4 in range(4)]

                    def stageB(c):
                        t0 = c * 128
                        k = c % 4
                        MT, xdd3 = MTD[k], xddD[k]
                        yn = yns[c % 2]

                        def s1():
                            for h in range(8):
                                xh = xs[:, c, h * 64:(h + 1) * 64]
                                tk.op("pe", lambda e, h=h, xh=xh: e.matmul(py[:, h * 64:(h + 1) * 64], lhsT=IDd[:, h, :], rhs=xh,
                                                                           start=True, stop=False), reads=[IDd.b, xs.b], writes=[py.b], sig=False)
                                tk.op("pe", lambda e, h=h, xh=xh: e.matmul(py[:, h * 64:(h + 1) * 64], lhsT=MT[:, h, :], rhs=xh,
                                                                           start=False, stop=False), reads=[MT.b, xs.b], writes=[py.b], sig=False)
                                tk.op("pe", lambda e, h=h, xh=xh: e.matmul(py[:, h * 64:(h + 1) * 64], lhsT=MT[:, 8 + h, :], rhs=xh,
                                                                           start=False, stop=True), reads=[MT.b, xs.b], writes=[py.b], sig=(h == 7))
                            for g in range(2):
                                tk.op("pe", lambda e, g=g: e.matmul(pyo[0][:, g * 256:(g + 1) * 256], lhsT=cT[:, g, t0:t0 + 128],
                                                                    rhs=Sfb[:, g * 256:(g + 1) * 256], start=True, stop=True),
                                      reads=[cT.b, Sfb.b], writes=[pyo[0].b], sig=(g == 1))
                            for g in range(2):
                                tk.op("pe", lambda e, g=g: e.matmul(pyo[1][:, g * 256:(g + 1) * 256], lhsT=cT[:, g, t0:t0 + 128],
                                                                    rhs=sinb[:, c, g * 256:(g + 1) * 256], start=True, stop=True),
                                      reads=[cT.b, sinb.b], writes=[pyo[1].b], sig=(g == 1))

                        def s2():
                            if c < NTILE - 1:
                                state_step(c, 0, xdd3)
                                tk.op("act", lambda e: e.activation(out=Sfb[:], in_=S[0][:], func=AF.Copy), reads=[S[0].b], writes=[Sfb.b])

                        def s3():
                            tk.op("dve", lambda e: e.tensor_tensor(out=v3(ta[:]), in0=v3(pyo[0][:, :]),
                                                                   in1=bc(ecum[:, 0, c, :].unsqueeze(2), [128, 8, 64]), op=ALU.mult),
                                  reads=[pyo[0].b, ecum.b], writes=[ta.b])
                            tk.op("dve", lambda e: e.tensor_tensor(out=v3(tb_[:]), in0=v3(pyo[1][:, :]),
                                                                   in1=bc(ecum[:, 1, c, :].unsqueeze(2), [128, 8, 64]), op=ALU.mult),
                                  reads=[pyo[1].b, ecum.b], writes=[tb_.b])

                        def s4():
                            tk.op("pool", lambda e: e.tensor_tensor(out=ta[:], in0=ta[:], in1=tb_[:], op=ALU.add),
                                  reads=[ta.b, tb_.b], writes=[ta.b])

                        def s5():
                            tk.op("dve", lambda e: e.tensor_tensor(out=yz[:], in0=py[:, :], in1=ta[:], op=ALU.add),
                                  reads=[py.b, ta.b], writes=[yz.b])

                        def s6():
                            tk.op("pool", lambda e: e.tensor_tensor(out=yz[:], in0=yz[:], in1=zs[:, c, :], op=ALU.mult),
                                  reads=[yz.b, zs.b], writes=[yz.b])
                            tk.op("pool", lambda e: e.memset(ssq[:], 0.0), writes=[ssq.b])

                        def s7():
                            tk.op("act", lambda e: e.activation(out=junk[:], in_=yz[:], func=AF.Square, accum_out=ssq[:, 0:1]),
                                  reads=[yz.b], writes=[junk.b, ssq.b])
                            tk.op("act", lambda e: e.activation(out=rstd[:], in_=ssq[:], func=AF.Ln, scale=1.0 / 512, bias=epsc[:, 0:1]),
                                  reads=[ssq.b, epsc.b], writes=[rstd.b])
                            tk.op("act", lambda e: e.activation(out=rstd[:], in_=rstd[:], func=AF.Exp, scale=-0.5),
                                  reads=[rstd.b], writes=[rstd.b])

                        def s8():
                            tk.op("dve", lambda e: e.scalar_tensor_tensor(out=yn[:], in0=yz[:], scalar=rstd[:, 0:1], in1=ng[:],
                                                                          op0=ALU.mult, op1=ALU.mult),
                                  reads=[yz.b, rstd.b, ng.b], writes=[yn.b])
                        return [s1, s2, s3, s4, s5, s6, s7, s8]

                    def stageB2(c):
                        t0 = c * 128
                        yn = yns[c % 2]
                        for j in range(4):
                            tk.op("pe", lambda e, j=j: e.transpose(out=pT[:, j, :], in_=yn[:, j * 128:(j + 1) * 128], identity=identb[:]),
                                  reads=[yn.b, identb.b], writes=[pT.b], sig=(j == 3))
                        tk.op("act", lambda e: e.activation(out=ycs[:, :, t0:t0 + 128], in_=pT[:], func=AF.Copy),
                              reads=[pT.b], writes=[ycs.b])

                    for c0 in range(3):
                        for f in stageA(c0):
                            f()
                    for c in range(NTILE):
                        A = stageA(c + 3) if c + 3 < NTILE else [lambda: None] * 5
                        B = stageB(c)
                        for f in (A[0], B[0], A[1], B[1], B[2], A[2], B[3], B[4], A[3], B[5], B[6], A[4], B[7]):
                            f()
                        if c >= 1:
                            stageB2(c - 1)
                    stageB2(NTILE - 1)
                    tk.dma("sp", ycT_s[0:512, :].rearrange("(j p) t -> p j t", p=128), ycs[:], reads=[ycs.b], writes=[ycT_s.b])
                    tk.barrier()

        def attn_block(units, q0, nq, ktiles, pO, pS, PT, scale, stages=()):
            seq = [(kt, u) for kt in ktiles for u in range(len(units))]
            LAG = 2
            stages = list(stages)
            for i in range(len(seq) + LAG):
                if stages and i >= 3 and (i - 3) % 4 == 0:
                    stages.pop(0)()
                if i < len(seq):
                    kt, u = seq[i]
                    U = units[u]
                    sl = i % 3
                    tk.op("pe", lambda e, U=U, kt=kt, sl=sl: e.matmul(pS[sl][:, 0:nq], lhsT=U["k"][1](kt), rhs=U["q"][1](q0, nq),
                                                                      start=True, stop=True),
                          reads=[U["k"][0].b, U["q"][0].b], writes=[pS[sl].b])
                    tk.op("act", lambda e, sl=sl: e.activation(out=PT[sl][:, 0:nq], in_=pS[sl][:, 0:nq], func=AF.Exp, scale=scale),
                          reads=[pS[sl].b], writes=[PT[sl].b])
                j = i - LAG
                if j >= 0:
                    kt, u = seq[j]
                    U = units[u]
                    sl = j % 3
                    acc = pO[U["acc"]]
                    tk.op("pe", lambda e, U=U, kt=kt, sl=sl, acc=acc: e.matmul(acc[:, 0:nq], lhsT=U["v"][1](kt), rhs=PT[sl][:, 0:nq],
                                                                               start=(kt == ktiles[0]), stop=(kt == ktiles[-1])),
                          reads=[U["v"][0].b, PT[sl].b], writes=[acc.b])
            for st in stages:
                st()

        def qblocks(ctx_out):
            qb = [(256 + 512 * i, 512, list(range(NTILE))) for i in range(4)]
            if ctx_out:
                qb = [(0, 256, [0, 1])] + qb
            return qb

        def load_vaug(vaug, src, nh):
            sv = src[:].rearrange("(c p) g w -> p c (g w)", p=128)
            dv = vaug[:].rearrange("p c g w -> p c (g w)")
            for c0 in range(0, NTILE, 6):
                tk.dma("sp", dv[:, c0:c0 + 6, :], sv[:, c0:c0 + 6, :], reads=[src.b], writes=[vaug.b])

        def normalize_stages(pa, pb, nq, OS, SS, pR, rec):
            def s1():
                tk.op("dve", lambda e: e.tensor_copy(out=OS[0:64, 0:nq], in_=pa[0:64, 0:nq]), reads=[pa.b], writes=[OS.b])
                tk.op("dve", lambda e: e.tensor_copy(out=OS[64:128, 0:nq], in_=pb[64:128, 0:nq]), reads=[pb.b], writes=[OS.b])
                tk.op("dve", lambda e: e.tensor_copy(out=SS[0:64, 0:nq], in_=pb[0:64, 0:nq]), reads=[pb.b], writes=[SS.b])
                tk.op("dve", lambda e: e.tensor_copy(out=SS[64:128, 0:nq], in_=pa[64:128, 0:nq]), reads=[pa.b], writes=[SS.b])

            def s2():
                tk.op("pe", lambda e: e.matmul(pR[:, 0:nq], lhsT=CM(K_SWAP), rhs=SS[:, 0:nq], start=True, stop=True),
                      reads=[cm.b, SS.b], writes=[pR.b])

            def s3():
                tk.op("dve", lambda e: e.reciprocal(out=rec[:, 0:nq], in_=pR[:, 0:nq]), reads=[pR.b], writes=[rec.b])
                tk.op("dve", lambda e: e.tensor_tensor(out=OS[:, 0:nq], in0=OS[:, 0:nq], in1=rec[:, 0:nq], op=ALU.mult),
                      reads=[OS.b, rec.b], writes=[OS.b])
            return [s1, s2, s3]

        def phase_G(l, ctx_out, pS, pO, pR, PT, OS, SS, rec, yaccs):
            with ExitStack() as ph:
                qT = sb(ph, "qm", [128, 4, NT], BF16)
                kT = sb(ph, "kdup", [128, 2, NT], BF16)
                vaug = sb(ph, "vaugG", [128, NTILE, 2, 192], BF16)
                gg = sb(ph, "gg", [128, 2, NT], BF16)
                qz = qT[:].rearrange("p h t -> p (h t)").bitcast(F32)
                tk.op("dve", lambda e: e.memset(qz, 0.0), writes=[qT.b])
                for hf in range(2):
                    tk.dma("sp", kT[64 * hf:64 * hf + 64, :, :], gkT_s[:].rearrange("h d t -> d h t"), reads=[gkT_s.b], writes=[kT.b])
                load_vaug(vaug, gv_s, 2)
                for h in range(4):
                    tk.dma("sp", qT[64 * (h % 2):64 * (h % 2) + 64, h, :], gqT_s[h], reads=[gqT_s.b], writes=[qT.b])
                tk.dma("sp", gg[:], ggT_s[:].rearrange("(j p) t -> p j t", p=128), reads=[ggT_s.b], writes=[gg.b])
                yield
                pending = []
                nblk = 0
                for j in range(2):
                    yacc = yaccs[j]
                    if not ctx_out:
                        tk.op("pool", lambda e, yacc=yacc: e.memset(yacc[:, 0:256], 0.0), writes=[yacc.b])
                    qbs = qblocks(ctx_out)
                    for bi, (q0, nq, ktiles) in enumerate(qbs):
                        a0 = 2 * (nblk % 2)
                        nblk += 1
                        units = [
                            dict(q=(qT, lambda q0, nq, j=j: qT[:, 2 * j, q0:q0 + nq]), k=(kT, lambda kt, j=j: kT[:, j, kt * 128:(kt + 1) * 128]),
                                 v=(vaug, lambda kt, j=j: vaug[:, kt, j, 64:192]), acc=a0),
                            dict(q=(qT, lambda q0, nq, j=j: qT[:, 2 * j + 1, q0:q0 + nq]), k=(kT, lambda kt, j=j: kT[:, j, kt * 128:(kt + 1) * 128]),
                                 v=(vaug, lambda kt, j=j: vaug[:, kt, j, 0:128]), acc=a0 + 1),
                        ]
                        attn_block(units, q0, nq, ktiles, pO, pS, PT, 0.125, stages=pending)
                        pending = normalize_stages(pO[a0], pO[a0 + 1], nq, OS, SS, pR, rec)

                        def fin(q0=q0, nq=nq, j=j, yacc=yacc, last=(bi == len(qbs) - 1)):
                            tk.op("pool", lambda e: e.tensor_tensor(out=yacc[:, q0:q0 + nq], in0=OS[:, 0:nq],
                                                                    in1=gg[:, j, q0:q0 + nq], op=ALU.mult),
                                  reads=[OS.b, gg.b], writes=[yacc.b])
                            if last:
                                tk.dma("sp", ycT_s[512 + 128 * j:640 + 128 * j, :], yacc[:], reads=[yacc.b], writes=[ycT_s.b])
                        pending.append(fin)
                for st in pending:
                    st()
                yield

        def phase_D(l, ctx_out, pS, pO, pR, PT, OSp, SS, rec, yaccs):
            lam_init = 0.8 - 0.6 * float(np.exp(-0.3 * l))
            with ExitStack() as ph:
                dq = sb(ph, "dqm", [128, 8, NT], BF16)
                dk = sb(ph, "dkc", [128, 2, NT], BF16)
                vaug = sb(ph, "vaugD", [128, NTILE, 4, 192], BF16)
                dg = sb(ph, "dg", [128, 2, NT], BF16)
                OSn = sb(ph, "OSn", [128, 512])
                sq = SS
                lp = sb(ph, "lp", [128, 128])
                lpr = sb(ph, "lpr", [128, 2, 32])
                lsum = sb(ph, "lsum", [128, 2])
                neglam = sb(ph, "neglam", [128, 1])
                gsc = sb(ph, "gsc", [128, 1])
                for m in range(0, 8, 2):
                    dz = dq[:, m:m + 2, :].rearrange("p h t -> p (h t)").bitcast(F32)
                    tk.op("dve", lambda e, dz=dz: e.memset(dz, 0.0), writes=[dq.b])
                for m in range(8):
                    tk.dma("sp", dq[32 * (m % 4):32 * (m % 4) + 32, m, :], dqT_s[32 * m:32 * m + 32, :], reads=[dqT_s.b], writes=[dq.b])
                tk.dma("sp", dk[:], dkT_s[:].rearrange("(c p) t -> p c t", p=128), reads=[dkT_s.b], writes=[dk.b])
                tk.dma("sp", dg[:], dgT_s[:].rearrange("(j p) t -> p j t", p=128), reads=[dgT_s.b], writes=[dg.b])
                tk.dma("sp", lp[:], dlam_d[l].partition_broadcast(128), reads=[dlam_d.b], writes=[lp.b])
                tk.dma("sp", gsc[:], dng_d[l], reads=[dng_d.b], writes=[gsc.b])
                load_vaug(vaug, dv_s, 4)
                yield
                lp4 = lp[:].rearrange("p (a b c) -> p a b c", a=2, b=2)
                tk.op("dve", lambda e: e.tensor_tensor(out=lpr[:], in0=lp4[:, :, 0, :], in1=lp4[:, :, 1, :], op=ALU.mult),
                      reads=[lp.b], writes=[lpr.b])
                tk.op("dve", lambda e: e.reduce_sum(out=lsum[:], in_=lpr[:], axis=AX.X), reads=[lpr.b], writes=[lsum.b])
                tk.op("act", lambda e: e.activation(out=lsum[:], in_=lsum[:], func=AF.Exp), reads=[lsum.b], writes=[lsum.b])
                tk.op("dve", lambda e: e.tensor_tensor(out=neglam[:], in0=lsum[:, 1:2], in1=lsum[:, 0:1], op=ALU.subtract),
                      reads=[lsum.b], writes=[neglam.b])
                tk.op("dve", lambda e: e.tensor_scalar(out=neglam[:], in0=neglam[:], scalar1=-lam_init, scalar2=None, op0=ALU.add),
                      reads=[neglam.b], writes=[neglam.b])
                tk.op("dve", lambda e: e.tensor_scalar(out=gsc[:], in0=gsc[:], scalar1=1.0 - lam_init, scalar2=None, op0=ALU.mult),
                      reads=[gsc.b], writes=[gsc.b])
                pending = []
                nblk = 0
                for j in range(2):
                    yacc = yaccs[j]
                    if not ctx_out:
                        tk.op("pool", lambda e, yacc=yacc: e.memset(yacc[:, 0:256], 0.0), writes=[yacc.b])
                    qbs = qblocks(ctx_out)
                    for bi, (q0, nq, ktiles) in enumerate(qbs):
                        for sign in range(2):
                            a0 = 2 * (nblk % 2)
                            nblk += 1
                            mA = 4 * j + sign
                            mB = 4 * j + 2 + sign
                            units = [
                                dict(q=(dq, lambda q0, nq, m=mA: dq[:, m, q0:q0 + nq]), k=(dk, lambda kt, j=j: dk[:, j, kt * 128:(kt + 1) * 128]),
                                     v=(vaug, lambda kt, hh=2 * j: vaug[:, kt, hh, 64:192]), acc=a0),
                                dict(q=(dq, lambda q0, nq, m=mB: dq[:, m, q0:q0 + nq]), k=(dk, lambda kt, j=j: dk[:, j, kt * 128:(kt + 1) * 128]),
                                     v=(vaug, lambda kt, hh=2 * j + 1: vaug[:, kt, hh, 0:128]), acc=a0 + 1),
                            ]
                            attn_block(units, q0, nq, ktiles, pO, pS, PT, 32.0 ** -0.5, stages=pending)
                            if sign == 0:
                                pending = normalize_stages(pO[a0], pO[a0 + 1], nq, OSp, SS, pR, rec)
                                continue
                            pending = normalize_stages(pO[a0], pO[a0 + 1], nq, OSn, SS, pR, rec)

                            def c1(nq=nq):
                                tk.op("dve", lambda e: e.scalar_tensor_tensor(out=OSp[:, 0:nq], in0=OSn[:, 0:nq], scalar=neglam[:, 0:1],
                                                                              in1=OSp[:, 0:nq], op0=ALU.mult, op1=ALU.add),
                                      reads=[OSn.b, neglam.b, OSp.b], writes=[OSp.b])
                                tk.op("pool", lambda e: e.tensor_tensor(out=sq[:, 0:nq], in0=OSp[:, 0:nq], in1=OSp[:, 0:nq], op=ALU.mult),
                                      reads=[OSp.b], writes=[sq.b])

                            def c2(nq=nq):
                                tk.op("pe", lambda e: e.matmul(pR[:, 0:nq], lhsT=CM(K_BD64), rhs=sq[:, 0:nq], start=True, stop=True),
                                      reads=[cm.b, sq.b], writes=[pR.b])

                            def c3(nq=nq):
                                tk.op("act", lambda e: e.activation(out=rec[:, 0:nq], in_=pR[:, 0:nq], func=AF.Ln, bias=epsc[:, 0:1]),
                                      reads=[pR.b, epsc.b], writes=[rec.b])
                                tk.op("act", lambda e: e.activation(out=rec[:, 0:nq], in_=rec[:, 0:nq], func=AF.Exp, scale=-0.5),
                                      reads=[rec.b], writes=[rec.b])

                            def c4(nq=nq, q0=q0, j=j, yacc=yacc, last=(bi == len(qbs) - 1)):
                                tk.op("dve", lambda e: e.tensor_tensor(out=OSp[:, 0:nq], in0=OSp[:, 0:nq], in1=rec[:, 0:nq], op=ALU.mult),
                                      reads=[OSp.b, rec.b], writes=[OSp.b])
                                tk.op("dve", lambda e: e.scalar_tensor_tensor(
                                    out=yacc[:, q0:q0 + nq], in0=OSp[:, 0:nq], scalar=gsc[:, 0:1], in1=dg[:, j, q0:q0 + nq],
                                    op0=ALU.mult, op1=ALU.mult), reads=[OSp.b, gsc.b, dg.b], writes=[yacc.b])
                                if last:
                                    tk.dma("sp", ycT_s[768 + 128 * j:896 + 128 * j, :], yacc[:], reads=[yacc.b], writes=[ycT_s.b])
                            pending += [c1, c2, c3, c4]
                for st in pending:
                    st()
                yield

        def phase_GD(l, ctx_out, wo):
            wv = wout_d[l].rearrange("(kc p) n -> p kc n", p=128)
            for n in range(2):
                tk.dma("pool", wo[:, :, n * 512:(n + 1) * 512], wv[:, :, n * 512:(n + 1) * 512], reads=[wout_d.b], writes=[wo.b])
            with ExitStack() as ph:
                pS = [ps(ph, f"pSa{i}", [128, 512]) for i in range(3)]
                pO = [ps(ph, f"pOa{i}", [128, 512]) for i in range(4)]
                pR = ps(ph, "pRa", [128, 512])
                PT = [sb(ph, f"PTa{i}", [128, 512], BF16) for i in range(3)]
                OS = sb(ph, "OSa", [128, 512])
                SS = sb(ph, "SSa", [128, 512])
                rec = sb(ph, "reca", [128, 512])
                yaccs = [sb(ph, f"yacc{i}", [128, NT], BF16) for i in range(2)]
                g = phase_G(l, ctx_out, pS, pO, pR, PT, OS, SS, rec, yaccs)
                next(g)
                d = phase_D(l, ctx_out, pS, pO, pR, PT, OS, SS, rec, yaccs)
                next(d)
                next(g)
                next(d)
                tk.barrier()
                for gen in (d, g):
                    for _ in gen:
                        pass

        def phase_O(l, ctx_out, wo):
            with ExitStack() as ph:
                ycTs = [sb(ph, f"ycT{i}", [128, 8, 768], BF16) for i in range(3)]
                G2 = [sb(ph, f"G2_{i}", [128, D]) for i in range(2)]
                gpo = sb(ph, "gpo", [128, D])
                ht = [sb(ph, f"hto{i}", [128, D]) for i in range(2)]
                on = [sb(ph, f"on{i}", [128, D]) for i in range(2)]
                junk = sb(ph, "junkO", [128, 512], BF16)
                ssqs = [sb(ph, f"ssqO{i}", [128, 2]) for i in range(2)]
                rstds = [sb(ph, f"rstdO{i}", [128, 1]) for i in range(2)]
                po = [ps(ph, f"po{i}", [128, 512]) for i in range(4)]
                tk.dma("sp", gpo[:], gpost_d[l].partition_broadcast(128), reads=[gpost_d.b], writes=[gpo.b])
                for w in range(2):
                    tk.dma("sp", G2[w][:], gt_s[l, w].partition_broadcast(128), reads=[gt_s.b], writes=[G2[w].b])
                    tk.op("dve", lambda e, w=w: e.tensor_tensor(out=G2[w][:], in0=G2[w][:], in1=gpo[:], op=ALU.mult),
                          reads=[G2[w].b, gpo.b], writes=[G2[w].b])
                ysv = ycT_s[:].rearrange("(kc p) t -> p kc t", p=128)
                for i in range(3):
                    tk.dma("sp", ycTs[i][:], ysv[:, :, i * 768:(i + 1) * 768], reads=[ycT_s.b], writes=[ycTs[i].b])
                for it, tt in enumerate(range(0 if ctx_out else 2, NTILE)):
                    k = it % 2
                    w = 1 if tt < 2 else 0
                    if l == 0:
                        src = ctx_d if tt < 2 else x_d
                        sap = ctx_d[tt * 128:(tt + 1) * 128, :] if tt < 2 else x_d[(tt - 2) * 128:(tt - 1) * 128, :]
                    else:
                        src = h1_s
                        sap = h1_s[tt * 128:(tt + 1) * 128, :]
                    tk.dma("sp", ht[k][:], sap, reads=[src.b], writes=[ht[k].b])
                    pp = po[2 * k:2 * k + 2]
                    for n in range(2):
                        for kc in range(8):
                            tk.op("pe", lambda e, n=n, kc=kc, pp=pp, tt=tt: e.matmul(
                                pp[n][:, :], lhsT=ycTs[tt // 6][:, kc, (tt % 6) * 128:(tt % 6 + 1) * 128], rhs=wo[:, kc, n * 512:(n + 1) * 512],
                                start=(kc == 0), stop=(kc == 7)), reads=[ycTs[tt // 6].b, wo.b], writes=[pp[n].b], sig=(kc == 7))
                    ssq = ssqs[k]
                    rstd = rstds[k]
                    tk.op("pool", lambda e: e.memset(ssq[:], 0.0), writes=[ssq.b])
                    for n in range(2):
                        tk.op("act", lambda e, n=n, pp=pp: e.activation(out=junk[:], in_=pp[n][:, :], func=AF.Square,
                                                                        accum_out=ssq[:, n:n + 1]),
                              reads=[pp[n].b], writes=[junk.b, ssq.b])
                    tk.op("dve", lambda e: e.tensor_tensor(out=rstd[:], in0=ssq[:, 0:1], in1=ssq[:, 1:2], op=ALU.add),
                          reads=[ssq.b], writes=[rstd.b])
                    tk.op("act", lambda e: e.activation(out=rstd[:], in_=rstd[:], func=AF.Ln, scale=1.0 / D, bias=epsc[:, 0:1]),
                          reads=[rstd.b, epsc.b], writes=[rstd.b])
                    tk.op("act", lambda e: e.activation(out=rstd[:], in_=rstd[:], func=AF.Exp, scale=-0.5),
                          reads=[rstd.b], writes=[rstd.b])
                    for n in range(2):
                        tk.op("dve", lambda e, n=n, pp=pp, k=k, w=w: e.scalar_tensor_tensor(
                            out=on[k][:, n * 512:(n + 1) * 512], in0=pp[n][:, :], scalar=rstd[:, 0:1],
                            in1=G2[w][:, n * 512:(n + 1) * 512], op0=ALU.mult, op1=ALU.mult),
                            reads=[pp[n].b, rstd.b, G2[w].b], writes=[on[k].b])
                    tk.op("dve", lambda e, k=k: e.tensor_tensor(out=on[k][:], in0=on[k][:], in1=ht[k][:], op=ALU.add),
                          reads=[on[k].b, ht[k].b], writes=[on[k].b])
                    if l == 0:
                        tk.dma("sp", h1_s[tt * 128:(tt + 1) * 128, :], on[k][:], reads=[on[k].b], writes=[h1_s.b])
                    else:
                        tk.dma("sp", out_d[(tt - 2) * 128:(tt - 1) * 128, :], on[k][:], reads=[on[k].b], writes=[out_d.b])
                tk.barrier()

        m0 = phase_M(0, ("sp", "act"))
        for _ in range(8):
            next(m0)
        tk.barrier()
        for _ in m0:
            pass
        for l in range(layers):
            if stop_after == ("M", l):
                break
            with ExitStack() as ni:
                uT = sb(ni, "uT", [128, 8, NT], BF16)
                wr = [sb(ni, f"wr{i}", [128, 8, 512], BF16) for i in range(4)]
                I_load = make_I_loader(l, wr)
                I_slots = {}

                def I_pref():
                    for i in range(3):
                        I_slots[i] = I_load(i)
                mnext = phase_M(l + 1, ("pool", "pool")) if l + 1 < layers else None
                phase_N(l, uT, mnext=mnext, mid_hook=I_pref)
                if mnext is not None:
                    raise_if = [x for x in mnext]
                if stop_after == ("N", l):
                    break
                phase_I(l, uT, wr, I_slots)
                if stop_after == ("I", l):
                    break
            ctx_out = l < DEPTH - 1
            phase_S(l)
            if stop_after == ("S", l):
                break
            with ExitStack() as go:
                wo = sb(go, "wo", [128, 8, D], BF16)
                phase_GD(l, ctx_out, wo)
                if stop_after == ("D", l):
                    break
                phase_O(l, ctx_out, wo)
                if stop_after == ("O", l):
                    break
        tk.barrier()
    return nc


def _rot_perm(base, nheads, hd):
    idx = []
    for h in range(nheads):
        o = base + h * hd
        idx += list(range(o + hd // 2, o + hd)) + list(range(o, o + hd // 2))
    return idx


def prep_inputs(inp):
    f = lambda a: np.ascontiguousarray(np.asarray(a), dtype=np.float32)
    x, c, ctx, c_ctx = f(inp["x"]), f(inp["c"]), f(inp["ctx"]), f(inp["c_ctx"])
    w_in = f(inp["w_in"])
    perm = (_rot_perm(C_GQ, 4, 64) + _rot_perm(C_GK, 2, 64) + _rot_perm(C_DQ, 8, 32) + _rot_perm(C_DK, 8, 32))
    w_rot = np.ascontiguousarray(w_in[:, :, perm])
    b_mod = f(inp["b_mod"])
    g_pre = f(inp["g_pre"])
    conv_w = f(inp["conv_w"])
    conv_b = f(inp["conv_b"])
    qg, kg = f(inp["q_norm_g"]), f(inp["k_norm_g"])
    pq = _rot_perm(0, 1, 64)
    cg, sg, cd, sd = _rope_tables()
    shared = {
        "w_mod": f(inp["w_mod"]),
        "bmodf": np.stack([_feat(b_mod[l, :2048]) for l in range(DEPTH)]),
        "b_mod": b_mod,
        "gpref": np.stack([_feat(g_pre[l]) for l in range(DEPTH)]),
        "g_post": f(inp["g_post"]),
        "w_in": w_in,
        "w_rot": w_rot,
        "w_out": f(inp["w_out"]),
        "convw_f": np.ascontiguousarray(conv_w.reshape(DEPTH, 5, 8, 128).transpose(0, 3, 2, 1)),
        "convb_f": np.stack([_feat(conv_b[l]) for l in range(DEPTH)]),
        "alog": np.concatenate([f(inp["a_log_fwd"]), f(inp["a_log_bwd"])], axis=1),
        "dtb": np.concatenate([f(inp["dt_bias_fwd"]), f(inp["dt_bias_bwd"])], axis=1),
        "d_skip": f(inp["d_skip"]),
        "ssd_norm_g": f(inp["ssd_norm_g"]),
        "qkg_f": np.ascontiguousarray(np.stack([qg, qg[:, pq], kg, kg[:, pq]], axis=2)),
        "diff_lambda": f(inp["diff_lambda"]).reshape(DEPTH, 128),
        "dng_f": np.ascontiguousarray(np.tile(f(inp["diff_norm_g"]), (1, 2))[:, :, None]),
        "cmat": _const_mats(),
        "identb": np.eye(128, dtype=np.float32).astype(ml_dtypes.bfloat16),
        "rope_g": np.stack([cg, sg]),
        "rope_d": np.stack([cd, sd]),
    }
    cc = _feat(c_ctx)
    maps = []
    for b in range(NCORES):
        m = dict(shared)
        m["x"] = x[b]
        m["ctx"] = ctx[b]
        m["cvec"] = np.ascontiguousarray(np.concatenate([_feat(c[b]), cc], axis=1))
        maps.append(m)
    return maps


_NC_CACHE = {}


def kernel(**inputs):
    if "nc" not in _NC_CACHE:
        _NC_CACHE["nc"] = build()
    nc = _NC_CACHE["nc"]
    maps = prep_inputs(inputs)
    res = run_bass_kernel_spmd(nc, maps, core_ids=list(range(NCORES)))
    return np.stack([np.asarray(r["out"], dtype=np.float32) for r in res.results], axis=0)
```

```python
import numpy as np
import ml_dtypes
from contextlib import ExitStack
import concourse.bass as bass
import concourse.mybir as mybir
from concourse.bass_utils import run_bass_kernel_spmd

F32 = mybir.dt.float32
BF16 = mybir.dt.bfloat16
AF = mybir.ActivationFunctionType
ALU = mybir.AluOpType
AX = mybir.AxisListType

NCORES = 8
D = 1024
SEQ = 2048
CTX = 256
NT = SEQ + CTX
NTILE = NT // 128
DEPTH = 2
EPS = 1e-6
INC = 3344
C_X, C_B, C_C, C_Z, C_DT = 0, 512, 768, 1024, 1536
C_GQ, C_GK, C_GV, C_GG = 1552, 1808, 1936, 2064
C_DQ, C_DK, C_DV, C_DG = 2320, 2576, 2832, 3088
R_GQ, R_GK, R_DQ, R_DK, NROT = 0, 256, 384, 640, 896
NEG = -30000.0
SAME_ENGINE_SYNC = True
S_LEVEL = [0]
BCAST_LHST = True


class Buf:
    __slots__ = ("w", "r", "name")

    def __init__(self, name=""):
        self.w = None
        self.r = {}
        self.name = name


class T:
    NDS = 8
    ROT = 20000

    def __init__(self, nc, es):
        self.nc = nc
        self.es = es
        self.E = {"pe": nc.tensor, "act": nc.scalar, "dve": nc.vector, "pool": nc.gpsimd, "sp": nc.sync}
        self.sem = {}
        self.cnt = {}
        self.nsem = 0
        for e in ("pe", "act", "dve", "pool"):
            self.sem[e] = self._newsem("s_" + e)
            self.cnt[e] = 0
        self.seen = {e: {} for e in self.E}
        self.pend = {e: ([], []) for e in self.E}
        self.dq = {}
        for q in ("sp", "pool", "act"):
            self.dq[q] = {"sems": [self._newsem(f"d_{q}{i}") for i in range(self.NDS)],
                          "vals": [0] * self.NDS, "n": 0}
        self.alltoks = {}
        self.ninstr = 0

    def _newsem(self, name):
        self.nsem += 1
        return self.es.enter_context(self.nc.semaphore(f"{name}_{self.nsem}"))

    def _wait(self, x, sem, val):
        if self.seen[x].get(sem, 0) >= val:
            return
        self.E[x].wait_ge(sem, val)
        self.seen[x][sem] = val

    def _deps(self, x, reads, writes):
        deps = {}
        for b in reads:
            if b.w is not None:
                s, v = b.w
                if deps.get(s, 0) < v:
                    deps[s] = v
        for b in writes:
            if b.w is not None:
                s, v = b.w
                if deps.get(s, 0) < v:
                    deps[s] = v
            for s, v in b.r.items():
                if deps.get(s, 0) < v:
                    deps[s] = v
        for e, (pr, pw) in self.pend.items():
            if e == x or (not pr and not pw):
                continue
            for b in writes:
                assert all(b is not o for o in pr) and all(b is not o for o in pw), f"pending conflict {b.name}"
            for b in reads:
                assert all(b is not o for o in pw), f"pending conflict {b.name}"
        own = self.sem.get(x)
        for s, v in deps.items():
            if s is own and (x == "pe" or not SAME_ENGINE_SYNC):
                continue
            self._wait(x, s, v)

    def op(self, x, fn, reads=(), writes=(), sig=True):
        self._deps(x, reads, writes)
        ins = fn(self.E[x])
        self.ninstr += 1
        pr, pw = self.pend[x]
        pr.extend(reads)
        pw.extend(writes)
        if sig:
            if self.cnt[x] >= self.ROT:
                self.sem[x] = self._newsem("s_" + x)
                self.cnt[x] = 0
            self.cnt[x] += 1
            s = self.sem[x]
            v = self.cnt[x]
            ins.then_inc(s, 1)
            self.alltoks[s] = v
            for b in pr:
                if b.r.get(s, 0) < v:
                    b.r[s] = v
            for b in pw:
                b.w = (s, v)
                b.r = {}
            self.pend[x] = ([], [])
        return ins

    def dma(self, q, out, in_, reads=(), writes=(), **kw):
        self._deps(q, reads, writes)
        st = self.dq[q]
        k = st["n"] % self.NDS
        st["n"] += 1
        sem = st["sems"][k]
        if st["vals"][k] > 0:
            self._wait(q, sem, st["vals"][k])
        ins = self.E[q].dma_start(out=out, in_=in_, **kw)
        self.ninstr += 1
        st["vals"][k] += 16
        v = st["vals"][k]
        ins.then_inc(sem, 16)
        self.alltoks[sem] = v
        for b in reads:
            if b.r.get(sem, 0) < v:
                b.r[sem] = v
        for b in writes:
            b.w = (sem, v)
            b.r = {}
        return ins

    def barrier(self):
        for e, (pr, pw) in self.pend.items():
            assert not pr and not pw, "pending unsignaled ops at barrier"
        for x in self.E:
            own = self.sem.get(x)
            for s, v in self.alltoks.items():
                if s is own and x == "pe":
                    continue
                self._wait(x, s, v)


def _rope_tables():
    rows = SEQ // 64
    row_idx = np.repeat(np.arange(rows), 64).astype(np.float32)
    col_idx = (np.arange(SEQ) % 64).astype(np.float32)

    def ang(dim):
        q = dim // 4
        inv = (np.float32(10000.0) ** (-np.arange(q, dtype=np.float32) / np.float32(q))).astype(np.float32)
        a = np.concatenate([row_idx[:, None] * inv, col_idx[:, None] * inv], axis=-1)
        return a.astype(np.float32)

    ag = ang(64)
    ad = ang(32)
    cg = np.concatenate([np.cos(ag), np.cos(ag)], axis=1).T
    sg = np.concatenate([-np.sin(ag), np.sin(ag)], axis=1).T
    cd = np.concatenate([np.cos(ad), np.cos(ad)], axis=1).T
    sd = np.concatenate([-np.sin(ad), np.sin(ad)], axis=1).T
    cd = np.tile(cd, (4, 1))
    sd = np.tile(sd, (4, 1))
    return (np.ascontiguousarray(cg, np.float32), np.ascontiguousarray(sg, np.float32),
            np.ascontiguousarray(cd, np.float32), np.ascontiguousarray(sd, np.float32))


K_ID, K_UINC, K_LINC, K_NEGF, K_NEGB, K_ONES, K_SWAP, K_BD64, K_SELW, NCONST = 0, 1, 2, 3, 4, 5, 6, 7, 8, 9


def _const_mats():
    i = np.arange(128)
    m = np.zeros((NCONST, 128, 128), np.float32)
    m[K_ID] = np.eye(128)
    m[K_UINC] = (i[:, None] <= i[None, :])
    m[K_LINC] = (i[:, None] >= i[None, :])
    m[K_NEGF] = np.where(i[None, :] < i[:, None], NEG, 0.0)
    m[K_NEGB] = np.where(i[None, :] > i[:, None], NEG, 0.0)
    m[K_ONES] = 1.0
    m[K_SWAP] = (i[:, None] == ((i[None, :] + 64) % 128))
    bd = np.zeros((128, 128), np.float32)
    bd[:64, :64] = 1.0 / 64
    bd[64:, 64:] = 1.0 / 64
    m[K_BD64] = bd
    sel = np.zeros((128, 128), np.float32)
    m[K_SELW] = sel
    return np.ascontiguousarray(m.transpose(1, 0, 2))


def _feat(v):
    v = np.asarray(v, np.float32)
    return np.ascontiguousarray(v.reshape(-1, 128).T)


class Tl:
    def __init__(self, h, name):
        self.h = h
        self.b = Buf(name)

    def __getitem__(self, k):
        return self.h[k]


def bc(ap, shape):
    return ap.broadcast_to(list(shape))


def build(dbg=(), stop_after=None, layers=DEPTH):
    nc = bass.Bass("TRN2", target_bir_lowering=False)
    dbg = set(dbg)

    def din(name, shape, dt=F32):
        return Tl(nc.dram_tensor(name, list(shape), dt, kind="ExternalInput").ap(), name)

    def dscr(name, shape, dt=BF16):
        kind = "ExternalOutput" if name in dbg else "Internal"
        return Tl(nc.dram_tensor(name, list(shape), dt, kind=kind).ap(), name)

    x_d = din("x", [SEQ, D])
    ctx_d = din("ctx", [CTX, D])
    cvec_d = din("cvec", [128, 16])
    wmod_d = din("w_mod", [DEPTH, D, 3 * D])
    bmodf_d = din("bmodf", [DEPTH, 128, 16])
    bmod_d = din("b_mod", [DEPTH, 3 * D])
    gpref_d = din("gpref", [DEPTH, 128, 8])
    gpost_d = din("g_post", [DEPTH, D])
    win_d = din("w_in", [DEPTH, D, INC])
    wrot_d = din("w_rot", [DEPTH, D, NROT])
    wout_d = din("w_out", [DEPTH, D, D])
    convw_d = din("convw_f", [DEPTH, 128, 8, 5])
    convb_d = din("convb_f", [DEPTH, 128, 8])
    alog_d = din("alog", [DEPTH, 16])
    dtb_d = din("dtb", [DEPTH, 16])
    dskip_d = din("d_skip", [DEPTH, 8])
    ssdg_d = din("ssd_norm_g", [DEPTH, 512])
    qkg_d = din("qkg_f", [DEPTH, 64, 4])
    dlam_d = din("diff_lambda", [DEPTH, 128])
    dng_d = din("dng_f", [DEPTH, 128, 1])
    cmat_d = din("cmat", [128, NCONST, 128])
    identb_d = din("identb", [128, 128], BF16)
    ropeg_d = din("rope_g", [2, 64, SEQ])
    roped_d = din("rope_d", [2, 128, SEQ])
    out_d = Tl(nc.dram_tensor("out", [SEQ, D], F32, kind="ExternalOutput").ap(), "out")

    gt_s = dscr("gt_s", [DEPTH, 2, D], F32)
    h1_s = dscr("h1_s", [NT, D], F32)
    xs_s = dscr("xs_s", [NT, 512])
    bt_s = dscr("bt_s", [NT, 256])
    bT_s = dscr("bT_s", [256, NT])
    cT_s = dscr("cT_s", [256, NT])
    zs_s = dscr("zs_s", [NT, 512])
    dt_s = dscr("dt_s", [NT, 16], F32)
    gqT_s = dscr("gqT_s", [4, 64, NT])
    gkT_s = dscr("gkT_s", [2, 64, NT])
    gv_s = dscr("gv_s", [NT, 2, 192])
    ggT_s = dscr("ggT_s", [256, NT])
    dqT_s = dscr("dqT_s", [256, NT])
    dkT_s = dscr("dkT_s", [256, NT])
    dv_s = dscr("dv_s", [NT, 4, 192])
    dgT_s = dscr("dgT_s", [256, NT])
    ycT_s = dscr("ycT_s", [D, NT])
    uT_dbg = dscr("uT_dbg", [D, NT]) if "uT_dbg" in dbg else None
    mod_dbg = dscr("mod_dbg", [128, 32], F32) if "mod_dbg" in dbg else None

    with ExitStack() as es:
        tk = T(nc, es)

        uniq = [0]

        def sb(st, name, shape, dt=F32):
            uniq[0] += 1
            name = f"{name}_s{uniq[0]}"
            return Tl(st.enter_context(nc.sbuf_tensor(name, list(shape), dt)), name)

        def ps(st, name, shape, dt=F32):
            uniq[0] += 1
            name = f"{name}_p{uniq[0]}"
            return Tl(st.enter_context(nc.psum_tensor(name, list(shape), dt)), name)

        cm = sb(es, "cm", [128, NCONST, 128])
        identb = sb(es, "identb_sb", [128, 128], BF16)
        A1L = [sb(es, f"A1_{i}", [128, 8, 2]) for i in range(DEPTH)]
        SHL = [sb(es, f"SH_{i}", [128, 8, 2]) for i in range(DEPTH)]
        Ssil = sb(es, "Ssil", [128, 8, 2])
        tk.dma("sp", cm[:], cmat_d[:], reads=[cmat_d.b], writes=[cm.b])
        tk.dma("sp", identb[:], identb_d[:], reads=[identb_d.b], writes=[identb.b])

        epsc = sb(es, "epsc", [128, 1])
        tk.op("pool", lambda e: e.memset(epsc[:], EPS), writes=[epsc.b])

        def CM(k):
            return cm[:, k, :]

        ones_t = sb(es, "ones_t", [128, 768], BF16)
        tk.op("dve", lambda e: e.memset(ones_t[:], 1.0), writes=[ones_t.b])
        for (dst, nh) in ((gv_s, 2), (dv_s, 4)):
            tk.dma("sp", dst[:].rearrange("(c p) g w -> p c (g w)", p=128),
                   bc(ones_t[:, 0:nh * 192].unsqueeze(1), [128, NTILE, nh * 192]), reads=[ones_t.b], writes=[dst.b])

        def phase_M(l, q):
            A1, SH, S = A1L[l], SHL[l], Ssil
            with ExitStack() as ph:
                bmf = sb(ph, "bmf", [128, 16])
                gpf = sb(ph, "gpf", [128, 8])
                bgt = sb(ph, "bgt", [2, D])
                modT = sb(ph, "modT", [128, 16, 2])
                gtr = sb(ph, "gtr", [2, D])
                wb = [sb(ph, f"wmb{i}", [128, 8, 512]) for i in range(2)]
                psM = ps(ph, "psM", [128, 16, 2])
                psG = [ps(ph, f"psG{i}", [2, 512]) for i in range(2)]
                if l == 0:
                    cv = sb(ph, "cv", [128, 16])
                    tk.dma(q, cv[:], cvec_d[:], reads=[cvec_d.b], writes=[cv.b])
                    for w in range(2):
                        tk.op("act", lambda e, w=w: e.activation(out=S[:, :, w], in_=cv[:, 8 * w:8 * w + 8], func=AF.Silu),
                              reads=[cv.b], writes=[S.b])
                tk.dma(q, bmf[:], bmodf_d[l], reads=[bmodf_d.b], writes=[bmf.b])
                tk.dma(q, gpf[:], gpref_d[l], reads=[gpref_d.b], writes=[gpf.b])
                tk.dma(q, bgt[:], bmod_d[l, 2 * D:3 * D].partition_broadcast(2), reads=[bmod_d.b], writes=[bgt.b])
                wv = wmod_d[l].rearrange("(kc p) n -> p kc n", p=128)
                yield
                for blk in range(6):
                    w_ = wb[blk % 2]
                    tk.dma(q, w_[:], wv[:, :, blk * 512:(blk + 1) * 512], reads=[wmod_d.b], writes=[w_.b])
                    if blk < 4:
                        for oc in range(4):
                            for kc in range(8):
                                tk.op("pe", lambda e, oc=oc, kc=kc, w_=w_, blk=blk: e.matmul(
                                    psM[:, blk * 4 + oc, :], lhsT=w_[:, kc, oc * 128:(oc + 1) * 128], rhs=S[:, kc, :],
                                    start=(kc == 0), stop=(kc == 7)),
                                    reads=[w_.b, S.b], writes=[psM.b], sig=(kc == 7 and oc == 3))
                    else:
                        g = psG[blk - 4]
                        for kc in range(8):
                            tk.op("pe", lambda e, kc=kc, w_=w_, g=g: e.matmul(
                                g[:, :], lhsT=S[:, kc, :], rhs=w_[:, kc, :], start=(kc == 0), stop=(kc == 7)),
                                reads=[w_.b, S.b], writes=[g.b], sig=(kc == 7))
                    yield
                tk.op("dve", lambda e: e.tensor_tensor(out=modT[:], in0=psM[:], in1=bc(bmf[:].unsqueeze(2), [128, 16, 2]),
                                                       op=ALU.add), reads=[psM.b, bmf.b], writes=[modT.b])
                tk.op("dve", lambda e: e.tensor_copy(out=SH[:], in_=modT[:, 0:8, :]), reads=[modT.b], writes=[SH.b])
                tk.op("dve", lambda e: e.scalar_tensor_tensor(out=A1[:], in0=modT[:, 8:16, :], scalar=1.0,
                                                              in1=bc(gpf[:].unsqueeze(2), [128, 8, 2]),
                                                              op0=ALU.add, op1=ALU.mult),
                      reads=[modT.b, gpf.b], writes=[A1.b])
                for i in range(2):
                    tk.op("dve", lambda e, i=i: e.tensor_tensor(out=gtr[:, i * 512:(i + 1) * 512], in0=psG[i][:, :],
                                                                in1=bgt[:, i * 512:(i + 1) * 512], op=ALU.add),
                          reads=[psG[i].b, bgt.b], writes=[gtr.b])
                tk.dma("sp", gt_s[l], gtr[:], reads=[gtr.b], writes=[gt_s.b])
                if mod_dbg is not None and l == 0:
                    tk.dma("sp", mod_dbg[:, 0:16], A1[:].rearrange("p a b -> p (a b)"), reads=[A1.b], writes=[mod_dbg.b])
                    tk.dma("sp", mod_dbg[:, 16:32], SH[:].rearrange("p a b -> p (a b)"), reads=[SH.b], writes=[mod_dbg.b])
                yield

        def phase_N(l, uT, mnext=None, mid_hook=None):
            A1, SH = A1L[l], SHL[l]
            with ExitStack() as ph:
                if mnext is not None:
                    next(mnext)
                ht = [sb(ph, f"ht{i}", [128, D]) for i in range(2)]
                hb = [sb(ph, f"hb{i}", [128, D], BF16) for i in range(2)]
                junk = sb(ph, "junkN", [128, D], BF16)
                ssqs = [sb(ph, f"ssq{i}", [128, 1]) for i in range(2)]
                rstds = [sb(ph, f"rstd{i}", [128, 1]) for i in range(2)]
                tmpf = [sb(ph, f"tmpf{i}", [128, 8, 128]) for i in range(2)]
                pT = [ps(ph, f"pT{i}", [128, 8, 128], BF16) for i in range(2)]
                for tt in range(NTILE):
                    k = tt % 2
                    w = 1 if tt < 2 else 0
                    if l == 0:
                        src = ctx_d if tt < 2 else x_d
                        sap = ctx_d[tt * 128:(tt + 1) * 128, :] if tt < 2 else x_d[(tt - 2) * 128:(tt - 1) * 128, :]
                    else:
                        src = h1_s
                        sap = h1_s[tt * 128:(tt + 1) * 128, :]
                    tk.dma("sp", ht[k][:], sap, reads=[src.b], writes=[ht[k].b])
                    ssq = ssqs[k]
                    rstd = rstds[k]
                    tk.op("pool", lambda e: e.memset(ssq[:], 0.0), writes=[ssq.b])
                    tk.op("act", lambda e, k=k: e.activation(out=junk[:], in_=ht[k][:], func=AF.Square, accum_out=ssq[:, 0:1]),
                          reads=[ht[k].b], writes=[junk.b, ssq.b])
                    tk.op("act", lambda e: e.activation(out=rstd[:], in_=ssq[:], func=AF.Ln, scale=1.0 / D, bias=epsc[:, 0:1]),
                          reads=[ssq.b, epsc.b], writes=[rstd.b])
                    tk.op("act", lambda e: e.activation(out=rstd[:], in_=rstd[:], func=AF.Exp, scale=-0.5),
                          reads=[rstd.b], writes=[rstd.b])
                    tk.op("act", lambda e, k=k: e.activation(out=hb[k][:], in_=ht[k][:], func=AF.Copy, scale=rstd[:, 0:1]),
                          reads=[ht[k].b, rstd.b], writes=[hb[k].b])
                    for j in range(8):
                        tk.op("pe", lambda e, k=k, j=j: e.transpose(out=pT[k][:, j, :], in_=hb[k][:, j * 128:(j + 1) * 128],
                                                                    identity=identb[:]),
                              reads=[hb[k].b, identb.b], writes=[pT[k].b], sig=(j == 7))
                    tk.op("dve", lambda e, k=k, w=w: e.tensor_tensor(out=tmpf[k][:], in0=pT[k][:],
                                                                     in1=bc(A1[:, :, w:w + 1], [128, 8, 128]), op=ALU.mult),
                          reads=[pT[k].b, A1.b], writes=[tmpf[k].b])
                    tk.op("dve", lambda e, k=k, w=w, tt=tt: e.tensor_tensor(
                        out=uT[:, :, tt * 128:(tt + 1) * 128], in0=tmpf[k][:],
                        in1=bc(SH[:, :, w:w + 1], [128, 8, 128]), op=ALU.add),
                        reads=[tmpf[k].b, SH.b], writes=[uT.b])
                    if mnext is not None and tt % 3 == 2:
                        next(mnext)
                    if mid_hook is not None and tt == 9:
                        mid_hook()
                if mnext is not None:
                    next(mnext)
                if uT_dbg is not None and l == 0:
                    tk.dma("sp", uT_dbg[:].rearrange("(kc p) t -> p kc t", p=128), uT[:], reads=[uT.b], writes=[uT_dbg.b])
                tk.barrier()

        def make_I_loader(l, wr):
            NS = 4
            winv = win_d[l].rearrange("(kc p) n -> p kc n", p=128)
            wrotv = wrot_d[l].rearrange("(kc p) n -> p kc n", p=128)
            blocks = [
                [(winv, 0, 512)],
                [(winv, 512, 512)],
                [(winv, C_Z, 512)],
                [(winv, C_DT, 16), (winv, C_GV, 128), (winv, C_DV, 256)],
                [(winv, C_GG, 256), (winv, C_DG, 256)],
                [(winv, C_GQ, 384)],
                [(wrotv, R_GQ, 384)],
                [(winv, C_DQ, 512)],
                [(wrotv, R_DQ, 512)],
            ]

            def load_block(i):
                slot = wr[i % NS]
                o = 0
                for (v, c0, n) in blocks[i]:
                    src = win_d if v is winv else wrot_d
                    tk.dma("pool", slot[:, :, o:o + n], v[:, :, c0:c0 + n], reads=[src.b], writes=[slot.b])
                    o += n
                return slot


            return load_block

        TB = [(0, 256)] + [(256 + 512 * i, 512) for i in range(4)]

        def phase_I(l, uT, wr, slots):
            with ExitStack() as ph:
                NS = 4
                cs = [sb(ph, f"cs{i}", [128, 2312]) for i in range(2)]
                accs = [sb(ph, f"acc{i}", [128, NT]) for i in range(2)]
                xb = [sb(ph, f"xb{i}", [128, NT], BF16) for i in range(2)]
                xtoks = [sb(ph, f"xtok{i}", [128, NTILE, 128], BF16) for i in range(2)]
                xti = [0]
                rg = sb(ph, "rg", [128, 2, SEQ])
                rd = sb(ph, "rd", [128, 2, SEQ])
                zb = [sb(ph, f"zb{i}", [128, 512], BF16) for i in range(2)]
                vb = [sb(ph, f"vb{i}", [128, 384], BF16) for i in range(2)]
                dts = sb(ph, "dts", [128, NTILE, 16])
                sqs = sb(ph, "sqs", [128, 512])
                rs = sb(ph, "rs", [128, 512])
                t1 = sb(ph, "t1", [128, 512])
                t2 = sb(ph, "t2", [128, 512])
                cw = sb(ph, "cw", [128, 8, 5])
                cb = sb(ph, "cb", [128, 8])
                qkg = sb(ph, "qkg", [128, 4])
                pf = [ps(ph, f"pf{i}", [128, 512]) for i in range(4)]
                pm = ps(ph, "pm", [128, 512])
                ptr = ps(ph, "ptr", [128, 4, 128], BF16)
                ptm = [ps(ph, f"ptm{i}", [128, 512]) for i in range(2)]
                pfi = [0]

                def nextpf():
                    p = pf[pfi[0] % 4]
                    pfi[0] += 1
                    return p

                for hf in range(2):
                    tk.dma("sp", rg[64 * hf:64 * hf + 64], ropeg_d[:].rearrange("a p t -> p a t"), reads=[ropeg_d.b], writes=[rg.b])
                tk.dma("sp", rd[:], roped_d[:].rearrange("a p t -> p a t"), reads=[roped_d.b], writes=[rd.b])
                tk.dma("sp", cw[:], convw_d[l], reads=[convw_d.b], writes=[cw.b])
                tk.dma("sp", cb[:], convb_d[l], reads=[convb_d.b], writes=[cb.b])
                for hf in range(2):
                    tk.dma("sp", qkg[64 * hf:64 * hf + 64], qkg_d[l], reads=[qkg_d.b], writes=[qkg.b])
                for c in cs:
                    tk.op("pool", lambda e, c=c: e.memset(c[:], 0.0), writes=[c.b])
                tk.op("pool", lambda e: e.memset(sqs[:], 0.0), writes=[sqs.b])

                load_block = make_I_loader(l, wr)

                def prefetch(i):
                    slots[i] = load_block(i)

                def fm_matmuls(psb, w_, off, m, t0, n):
                    for kc in range(8):
                        tk.op("pe", lambda e, kc=kc: e.matmul(psb[0:m, 0:n], lhsT=w_[:, kc, off:off + m],
                                                               rhs=uT[:, kc, t0:t0 + n], start=(kc == 0), stop=(kc == 7)),
                              reads=[w_.b, uT.b], writes=[psb.b], sig=(kc == 7))

                def transposes_to(xbt, dst_d, c0):
                    xtok = xtoks[xti[0] % 2]
                    xti[0] += 1
                    for g0 in range(0, NTILE, 4):
                        gn = min(4, NTILE - g0)
                        for i in range(gn):
                            tt = g0 + i
                            tk.op("pe", lambda e, i=i, tt=tt: e.transpose(out=ptr[:, i, :], in_=xbt[:, tt * 128:(tt + 1) * 128],
                                                                          identity=identb[:]),
                                  reads=[xbt.b, identb.b], writes=[ptr.b], sig=(i == gn - 1))
                        tk.op("act", lambda e, g0=g0, gn=gn: e.activation(out=xtok[:, g0:g0 + gn, :], in_=ptr[:, 0:gn, :], func=AF.Copy),
                              reads=[ptr.b], writes=[xtok.b])
                    tk.dma("sp", dst_d[:].rearrange("(tt p) f -> p tt f", p=128)[:, :, c0:c0 + 128], xtok[:],
                           reads=[xtok.b], writes=[dst_d.b])

                def A_P(j):
                    if j == 0:
                        prefetch(3)
                    if j == 4:
                        prefetch(4)
                    w_ = slots[j // 4]
                    c_ = cs[j % 2]
                    for (t0, n) in TB:
                        p = nextpf()
                        fm_matmuls(p, w_, (j % 4) * 128, 128, t0, n)
                        d0 = 2 + t0 if t0 < 256 else 262 + (t0 - 256)
                        tk.op("act", lambda e, p=p, d0=d0, n=n, c_=c_: e.activation(out=c_[:, d0:d0 + n], in_=p[:, 0:n], func=AF.Copy),
                              reads=[p.b], writes=[c_.b])

                def A_C(j):
                    c_ = cs[j % 2]
                    acc = accs[j % 2]
                    for (eng, s0, a0, n) in (("dve", 2, 0, 256), ("dve", 262, 256, 2048)):
                        tk.op(eng, lambda e, s0=s0, a0=a0, n=n: e.tensor_scalar(
                            out=acc[:, a0:a0 + n], in0=c_[:, s0 - 2:s0 - 2 + n], scalar1=cw[:, j, 0:1], scalar2=cb[:, j:j + 1],
                            op0=ALU.mult, op1=ALU.add), reads=[c_.b, cw.b, cb.b], writes=[acc.b])
                        for k in range(1, 5):
                            tk.op(eng, lambda e, s0=s0, a0=a0, n=n, k=k: e.scalar_tensor_tensor(
                                out=acc[:, a0:a0 + n], in0=c_[:, s0 - 2 + k:s0 - 2 + k + n], scalar=cw[:, j, k:k + 1],
                                in1=acc[:, a0:a0 + n], op0=ALU.mult, op1=ALU.add), reads=[c_.b, cw.b, acc.b], writes=[acc.b])

                def A_F(j):
                    acc = accs[j % 2]
                    xbt = xb[j % 2]
                    tk.op("act", lambda e: e.activation(out=xbt[:], in_=acc[:], func=AF.Silu), reads=[acc.b], writes=[xbt.b])
                    if j < 4:
                        transposes_to(xbt, xs_s, j * 128)
                    elif j < 6:
                        tk.dma("sp", bT_s[(j - 4) * 128:(j - 3) * 128, :], xbt[:], reads=[xbt.b], writes=[bT_s.b])
                        transposes_to(xbt, bt_s, (j - 4) * 128)
                    else:
                        tk.dma("sp", cT_s[(j - 6) * 128:(j - 5) * 128, :], xbt[:], reads=[xbt.b], writes=[cT_s.b])

                A_P(0)
                A_P(1)
                A_C(0)
                for j in range(8):
                    A_F(j)
                    if j + 2 < 8:
                        A_P(j + 2)
                    if j + 1 < 8:
                        A_C(j + 1)

                prefetch(5)
                wz = slots[2]
                wt = slots[3]
                for tt in range(NTILE):
                    k = tt % 2
                    for (w_, n, p) in ((wz, 512, ptm[0]), (wt, 400, ptm[1])):
                        for kc in range(8):
                            tk.op("pe", lambda e, kc=kc, w_=w_, n=n, p=p, tt=tt: e.matmul(
                                p[:, 0:n], lhsT=uT[:, kc, tt * 128:(tt + 1) * 128], rhs=w_[:, kc, 0:n],
                                start=(kc == 0), stop=(kc == 7)), reads=[w_.b, uT.b], writes=[p.b], sig=(kc == 7))
                    tk.op("act", lambda e, k=k: e.activation(out=zb[k][:], in_=ptm[0][:], func=AF.Silu),
                          reads=[ptm[0].b], writes=[zb[k].b])
                    tk.dma("sp", zs_s[tt * 128:(tt + 1) * 128, :], zb[k][:], reads=[zb[k].b], writes=[zs_s.b])
                    tk.op("dve", lambda e, tt=tt: e.tensor_copy(out=dts[:, tt, :], in_=ptm[1][:, 0:16]),
                          reads=[ptm[1].b], writes=[dts.b])
                    tk.op("dve", lambda e, k=k: e.tensor_copy(out=vb[k][:], in_=ptm[1][:, 16:400]),
                          reads=[ptm[1].b], writes=[vb[k].b])
                    tk.dma("sp", gv_s[tt * 128:(tt + 1) * 128, :, 64:128], vb[k][:, 0:128].rearrange("p (g d) -> p g d", g=2),
                           reads=[vb[k].b], writes=[gv_s.b])
                    tk.dma("sp", dv_s[tt * 128:(tt + 1) * 128, :, 64:128], vb[k][:, 128:384].rearrange("p (g d) -> p g d", g=4),
                           reads=[vb[k].b], writes=[dv_s.b])
                tk.dma("sp", dt_s[:].rearrange("(tt p) f -> p tt f", p=128), dts[:], reads=[dts.b], writes=[dt_s.b])

                prefetch(6)
                prefetch(7)
                wg = slots[4]
                for j in range(4):
                    xbt = xb[j % 2]
                    for (t0, n) in TB:
                        p = nextpf()
                        fm_matmuls(p, wg, j * 128, 128, t0, n)
                        tk.op("act", lambda e, p=p, t0=t0, n=n, xbt=xbt: e.activation(out=xbt[:, t0:t0 + n], in_=p[:, 0:n], func=AF.Silu),
                              reads=[p.b], writes=[xbt.b])
                    dst = ggT_s if j < 2 else dgT_s
                    tk.dma("sp", dst[(j % 2) * 128:(j % 2 + 1) * 128, :], xbt[:], reads=[xbt.b], writes=[dst.b])

                prefetch(8)
                wq = slots[5]
                wqr = slots[6]
                for pp_ in range(3):
                    gcol = 0 if pp_ < 2 else 2
                    xbt = xb[pp_ % 2]
                    for (t0, n) in TB:
                        pa = nextpf()
                        fm_matmuls(pa, wq, pp_ * 128, 128, t0, n)
                        lat = t0 >= 256
                        if lat:
                            pb = nextpf()
                            fm_matmuls(pb, wqr, pp_ * 128, 128, t0, n)
                        tk.op("act", lambda e, pa=pa, n=n: e.activation(out=sqs[:, 0:n], in_=pa[:, 0:n], func=AF.Square),
                              reads=[pa.b], writes=[sqs.b])
                        tk.op("pe", lambda e, n=n: e.matmul(pm[:, 0:n], lhsT=CM(K_BD64), rhs=sqs[:, 0:n],
                                                            start=True, stop=True), reads=[cm.b, sqs.b], writes=[pm.b])
                        tk.op("act", lambda e, n=n: e.activation(out=rs[:, 0:n], in_=pm[:, 0:n], func=AF.Ln, bias=epsc[:, 0:1]),
                              reads=[pm.b, epsc.b], writes=[rs.b])
                        tk.op("act", lambda e, n=n: e.activation(out=rs[:, 0:n], in_=rs[:, 0:n], func=AF.Exp, scale=-0.5),
                              reads=[rs.b], writes=[rs.b])
                        if lat:
                            r0 = t0 - 256
                            tk.op("dve", lambda e, pa=pa, n=n, gcol=gcol: e.scalar_tensor_tensor(
                                out=t1[:, 0:n], in0=pa[:, 0:n], scalar=qkg[:, gcol:gcol + 1], in1=rs[:, 0:n],
                                op0=ALU.mult, op1=ALU.mult), reads=[pa.b, qkg.b, rs.b], writes=[t1.b])
                            tk.op("dve", lambda e, pb=pb, n=n, gcol=gcol: e.scalar_tensor_tensor(
                                out=t2[:, 0:n], in0=pb[:, 0:n], scalar=qkg[:, gcol + 1:gcol + 2], in1=rs[:, 0:n],
                                op0=ALU.mult, op1=ALU.mult), reads=[pb.b, qkg.b, rs.b], writes=[t2.b])
                            tk.op("pool", lambda e, n=n, r0=r0: e.tensor_tensor(out=t1[:, 0:n], in0=t1[:, 0:n],
                                                                                in1=rg[:, 0, r0:r0 + n], op=ALU.mult),
                                  reads=[t1.b, rg.b], writes=[t1.b])
                            tk.op("dve", lambda e, n=n, r0=r0: e.tensor_tensor(out=t2[:, 0:n], in0=t2[:, 0:n],
                                                                               in1=rg[:, 1, r0:r0 + n], op=ALU.mult),
                                  reads=[t2.b, rg.b], writes=[t2.b])
                            tk.op("dve", lambda e, n=n, t0=t0, xbt=xbt: e.tensor_tensor(out=xbt[:, t0:t0 + n], in0=t1[:, 0:n],
                                                                                        in1=t2[:, 0:n], op=ALU.add),
                                  reads=[t1.b, t2.b], writes=[xbt.b])
                        else:
                            tk.op("dve", lambda e, pa=pa, n=n, gcol=gcol, t0=t0, xbt=xbt: e.scalar_tensor_tensor(
                                out=xbt[:, t0:t0 + n], in0=pa[:, 0:n], scalar=qkg[:, gcol:gcol + 1], in1=rs[:, 0:n],
                                op0=ALU.mult, op1=ALU.mult), reads=[pa.b, qkg.b, rs.b], writes=[xbt.b])
                    if pp_ < 2:
                        dst = gqT_s[2 * pp_:2 * pp_ + 2].rearrange("h d t -> (h d) t")
                        dstb = gqT_s.b
                    else:
                        dst = gkT_s[:].rearrange("h d t -> (h d) t")
                        dstb = gkT_s.b
                    tk.dma("sp", dst, xbt[:], reads=[xbt.b], writes=[dstb])

                wd = slots[7]
                wdr = slots[8]
                for j in range(4):
                    xbt = xb[j % 2]
                    for (t0, n) in TB:
                        pa = nextpf()
                        fm_matmuls(pa, wd, j * 128, 128, t0, n)
                        if t0 >= 256:
                            pb = nextpf()
                            fm_matmuls(pb, wdr, j * 128, 128, t0, n)
                            r0 = t0 - 256
                            tk.op("dve", lambda e, pa=pa, n=n, r0=r0: e.tensor_tensor(out=t1[:, 0:n], in0=pa[:, 0:n],
                                                                                      in1=rd[:, 0, r0:r0 + n], op=ALU.mult),
                                  reads=[pa.b, rd.b], writes=[t1.b])
                            tk.op("dve", lambda e, pb=pb, n=n, r0=r0: e.tensor_tensor(out=t2[:, 0:n], in0=pb[:, 0:n],
                                                                                      in1=rd[:, 1, r0:r0 + n], op=ALU.mult),
                                  reads=[pb.b, rd.b], writes=[t2.b])
                            tk.op("pool", lambda e, n=n, t0=t0, xbt=xbt: e.tensor_tensor(out=xbt[:, t0:t0 + n], in0=t1[:, 0:n],
                                                                                         in1=t2[:, 0:n], op=ALU.add),
                                  reads=[t1.b, t2.b], writes=[xbt.b])
                        else:
                            tk.op("act", lambda e, pa=pa, n=n, t0=t0, xbt=xbt: e.activation(out=xbt[:, t0:t0 + n], in_=pa[:, 0:n], func=AF.Copy),
                                  reads=[pa.b], writes=[xbt.b])
                    dst = dqT_s if j < 2 else dkT_s
                    tk.dma("sp", dst[(j % 2) * 128:(j % 2 + 1) * 128, :], xbt[:], reads=[xbt.b], writes=[dst.b])
                tk.barrier()

        def phase_S(l, skip_ctx=False):
            with ExitStack() as ph:
                xs = sb(ph, "xs", [128, NTILE, 512], BF16)
                btok = sb(ph, "btok", [128, NTILE, 256], BF16)
                bT = sb(ph, "bT", [128, 2, NT], BF16)
                cT = sb(ph, "cT", [128, 2, NT], BF16)
                zs = sb(ph, "zs", [128, NTILE, 512], BF16)
                dtr = sb(ph, "dtr", [128, NTILE, 16])
                alog = sb(ph, "alog", [128, 2, 8])
                dtb = sb(ph, "dtbb", [128, 2, 8])
                dsk = sb(ph, "dsk", [128, 8])
                ng = sb(ph, "ng", [128, 512])
                Abc = sb(ph, "Abc", [128, 2, 8])
                one = sb(ph, "one", [128, 1])
                SH4 = [128, 2, NTILE, 8]
                tmp = sb(ph, "s_tmp", SH4)
                dtv = sb(ph, "dtv", SH4)
                dtA = sb(ph, "dtA", SH4)
                negcum = sb(ph, "negcum", SH4)
                ecum = sb(ph, "ecum", SH4)
                etot = sb(ph, "etot", SH4)
                dec = sb(ph, "dec", SH4)
                wdec = sb(ph, "wdec", SH4)
                lnb = sb(ph, "lnb", SH4)
                IDd = sb(ph, "IDd", [128, 8, 128], BF16)
                sinb = sb(ph, "sinb", [128, NTILE, 512], BF16)
                ycs = sb(ph, "ycs", [128, 4, NT], BF16)
                fl = lambda t: t[:].rearrange("p a b c -> p (a b c)")

                tk.dma("sp", dtr[:], dt_s[:].rearrange("(c p) f -> p c f", p=128), reads=[dt_s.b], writes=[dtr.b])
                tk.dma("sp", alog[:].rearrange("p a b -> p (a b)"), alog_d[l].partition_broadcast(128), reads=[alog_d.b], writes=[alog.b])
                tk.dma("sp", dtb[:].rearrange("p a b -> p (a b)"), dtb_d[l].partition_broadcast(128), reads=[dtb_d.b], writes=[dtb.b])
                tk.dma("sp", dsk[:], dskip_d[l].partition_broadcast(128), reads=[dskip_d.b], writes=[dsk.b])
                tk.dma("sp", xs[:], xs_s[:].rearrange("(c p) f -> p c f", p=128), reads=[xs_s.b], writes=[xs.b])
                tk.dma("sp", btok[:], bt_s[:].rearrange("(c p) f -> p c f", p=128), reads=[bt_s.b], writes=[btok.b])
                tk.dma("sp", bT[:], bT_s[:].rearrange("(g p) t -> p g t", p=128), reads=[bT_s.b], writes=[bT.b])
                tk.dma("sp", cT[:], cT_s[:].rearrange("(g p) t -> p g t", p=128), reads=[cT_s.b], writes=[cT.b])
                tk.dma("sp", zs[:], zs_s[:].rearrange("(c p) f -> p c f", p=128), reads=[zs_s.b], writes=[zs.b])
                tk.dma("sp", ng[:], ssdg_d[l].partition_broadcast(128), reads=[ssdg_d.b], writes=[ng.b])
                tk.op("pool", lambda e: e.memset(one[:], 1.0), writes=[one.b])

                if S_LEVEL[0] == -1:
                    tk.barrier()
                    return
                with ExitStack() as s1:
                    pcum = ps(s1, "pcum", [128, 2, NTILE * 8])
                    ptot = ps(s1, "ptot", [128, 2 * NTILE * 8])
                    tk.op("act", lambda e: e.activation(out=Abc[:], in_=alog[:], func=AF.Exp), reads=[alog.b], writes=[Abc.b])
                    tk.op("dve", lambda e: e.tensor_scalar(out=Abc[:], in0=Abc[:], scalar1=-1.0, scalar2=None, op0=ALU.mult),
                          reads=[Abc.b], writes=[Abc.b])
                    for d in range(2):
                        tk.op("dve", lambda e, d=d: e.tensor_tensor(out=tmp[:, d], in0=dtr[:, :, d * 8:(d + 1) * 8],
                                                                    in1=bc(dtb[:, d:d + 1, :], [128, NTILE, 8]), op=ALU.add),
                              reads=[dtr.b, dtb.b], writes=[tmp.b])
                    tk.op("act", lambda e: e.activation(out=fl(tmp), in_=fl(tmp), func=AF.Exp), reads=[tmp.b], writes=[tmp.b])
                    tk.op("act", lambda e: e.activation(out=fl(dtv), in_=fl(tmp), func=AF.Ln, bias=one[:, 0:1]),
                          reads=[tmp.b, one.b], writes=[dtv.b])
                    tk.op("act", lambda e: e.activation(out=fl(lnb), in_=fl(dtv), func=AF.Ln), reads=[dtv.b], writes=[lnb.b])
                    for h in range(8):
                        tk.op("dve", lambda e, h=h: e.tensor_scalar(out=IDd[:, h, :], in0=identb[:], scalar1=dsk[:, h:h + 1], scalar2=None,
                                                                    op0=ALU.mult), reads=[identb.b, dsk.b], writes=[IDd.b])
                    for d in range(2):
                        tk.op("dve", lambda e, d=d: e.tensor_tensor(out=dtA[:, d], in0=dtv[:, d],
                                                                    in1=bc(Abc[:, d:d + 1, :], [128, NTILE, 8]), op=ALU.mult),
                              reads=[dtv.b, Abc.b], writes=[dtA.b])
                    if S_LEVEL[0] == -2:
                        tk.barrier()
                        return
                    tk.op("pe", lambda e: e.matmul(pcum[:, 0, :], lhsT=CM(K_UINC), rhs=dtA[:, 0].rearrange("p b c -> p (b c)"),
                                                   start=True, stop=True), reads=[cm.b, dtA.b], writes=[pcum.b])
                    tk.op("pe", lambda e: e.matmul(pcum[:, 1, :], lhsT=CM(K_LINC), rhs=dtA[:, 1].rearrange("p b c -> p (b c)"),
                                                   start=True, stop=True), reads=[cm.b, dtA.b], writes=[pcum.b])
                    tk.op("pe", lambda e: e.matmul(ptot[:, :], lhsT=CM(K_ONES), rhs=fl(dtA), start=True, stop=True),
                          reads=[cm.b, dtA.b], writes=[ptot.b])
                    if S_LEVEL[0] == -3:
                        tk.barrier()
                        return
                    pcf = pcum[:].rearrange("p a n -> p (a n)")
                    tk.op("dve", lambda e: e.tensor_scalar(out=fl(negcum), in0=pcf, scalar1=-1.0, scalar2=None, op0=ALU.mult),
                          reads=[pcum.b], writes=[negcum.b])
                    tk.op("dve", lambda e: e.tensor_copy(out=fl(wdec), in_=ptot[:, :]), reads=[ptot.b], writes=[wdec.b])
                    tk.op("dve", lambda e: e.tensor_tensor(out=fl(lnb), in0=fl(lnb), in1=fl(negcum), op=ALU.add),
                          reads=[lnb.b, negcum.b], writes=[lnb.b])
                    if S_LEVEL[0] == -4:
                        tk.barrier()
                        return
                    tk.op("act", lambda e: e.activation(out=fl(ecum), in_=fl(negcum), func=AF.Exp, scale=-1.0),
                          reads=[negcum.b], writes=[ecum.b])
                    if S_LEVEL[0] == -5:
                        tk.barrier()
                        return
                    tk.op("act", lambda e: e.activation(out=fl(etot), in_=fl(wdec), func=AF.Exp), reads=[wdec.b], writes=[etot.b])
                    tk.op("dve", lambda e: e.tensor_tensor(out=fl(tmp), in0=fl(wdec), in1=fl(negcum), op=ALU.add),
                          reads=[wdec.b, negcum.b], writes=[tmp.b])
                    if S_LEVEL[0] == -6:
                        tk.barrier()
                        return
                    tk.op("act", lambda e: e.activation(out=fl(dec), in_=fl(tmp), func=AF.Exp), reads=[tmp.b], writes=[dec.b])
                    tk.op("dve", lambda e: e.tensor_tensor(out=fl(wdec), in0=fl(dtv), in1=fl(dec), op=ALU.mult),
                          reads=[dtv.b, dec.b], writes=[wdec.b])
                    tk.barrier()

                with ExitStack() as s2:
                    S = [sb(s2, f"Sst{i}", [128, 512]) for i in range(2)]
                    Sfb = sb(s2, "Sfb", [128, 512], BF16)
                    stmp = sb(s2, "stmp", [128, 512])
                    xdd = sb(s2, "xdd", [128, 512], BF16)
                    LTD = [[sb(s2, f"LT{i}g{q}", [128, 4, 128], BF16) for q in range(4)] for i in range(2)]
                    MTD = [sb(s2, f"MT{i}", [128, 16, 128], BF16) for i in range(4)]
                    xddD = [sb(s2, f"xddD{i}", [128, 512], BF16) for i in range(4)]
                    ta = sb(s2, "ta", [128, 512])
                    tb_ = sb(s2, "tb", [128, 512])
                    yz = sb(s2, "yz", [128, 512])
                    yns = [sb(s2, f"yn{i}", [128, 512], BF16) for i in range(2)]
                    junk = sb(s2, "junkS", [128, 512], BF16)
                    ssq = sb(s2, "ssqS", [128, 1])
                    rstd = sb(s2, "rstdS", [128, 1])
                    pL = [ps(s2, f"pL{i}", [128, 4, 128]) for i in range(2)]
                    pcb = ps(s2, "pcb", [128, 2, 128])
                    cbs = [sb(s2, f"cbs{i}", [128, 2, 128]) for i in range(2)]
                    py = ps(s2, "py", [128, 512])
                    pyo = [ps(s2, f"pyo{i}", [128, 512]) for i in range(2)]
                    pst = ps(s2, "pst", [128, 512])
                    pT = ps(s2, "pTS", [128, 4, 128], BF16)
                    v3 = lambda ap: ap.rearrange("p (h d) -> p h d", h=8)

                    def scaled_x(dst, c, w4, d):
                        tk.op("dve", lambda e: e.tensor_tensor(out=v3(dst[:]), in0=v3(xs[:, c, :]),
                                                               in1=bc(w4[:, d, c, :].unsqueeze(2), [128, 8, 64]), op=ALU.mult),
                              reads=[xs.b, w4.b], writes=[dst.b])

                    def state_step(c, d, xsrc):
                        for g in range(2):
                            tk.op("pe", lambda e, g=g: e.matmul(pst[:, g * 256:(g + 1) * 256], lhsT=btok[:, c, g * 128:(g + 1) * 128],
                                                                rhs=xsrc[:, g * 256:(g + 1) * 256], start=True, stop=True),
                                  reads=[btok.b, xsrc.b], writes=[pst.b], sig=(g == 1))
                        tk.op("dve", lambda e: e.tensor_tensor(out=v3(stmp[:]), in0=v3(S[d][:]),
                                                               in1=bc(etot[:, d, c, :].unsqueeze(2), [128, 8, 64]), op=ALU.mult),
                              reads=[S[d].b, etot.b], writes=[stmp.b])
                        tk.op("dve", lambda e: e.tensor_tensor(out=S[d][:], in0=pst[:, :], in1=stmp[:], op=ALU.add),
                              reads=[pst.b, stmp.b], writes=[S[d].b])

                    border = [1, 0] + list(range(NTILE - 1, 1, -1))
                    if S_LEVEL[0] == 1:
                        tk.barrier()
                        return
                    tk.op("pool", lambda e: e.memset(S[1][:], 0.0), writes=[S[1].b])
                    tk.op("pool", lambda e: e.memset(S[0][:], 0.0), writes=[S[0].b])
                    tk.op("pool", lambda e: e.memset(Sfb[:], 0.0), writes=[Sfb.b])
                    for i, c in enumerate(border):
                        tk.op("act", lambda e, c=c: e.activation(out=sinb[:, c, :], in_=S[1][:], func=AF.Copy),
                              reads=[S[1].b], writes=[sinb.b])
                        if i == len(border) - 1:
                            break
                        scaled_x(xdd, c, wdec, 1)
                        state_step(c, 1, xdd)

                    def stageA(c):
                        t0 = c * 128
                        k = c % 2
                        k3 = c % 4
                        LT, MT, cb_ = LTD[k], MTD[k3], cbs[k]

                        def pre():
                            scaled_x(xddD[k3], c, wdec, 0)
                            for g in range(2):
                                tk.op("pe", lambda e, g=g: e.matmul(pcb[:, g, :], lhsT=bT[:, g, t0:t0 + 128], rhs=cT[:, g, t0:t0 + 128],
                                                                    start=True, stop=True), reads=[bT.b, cT.b], writes=[pcb.b], sig=(g == 1))
                            tk.op("dve", lambda e: e.tensor_copy(out=cb_[:], in_=pcb[:]), reads=[pcb.b], writes=[cb_.b])

                        def grp(q4):
                            p = pL[q4 % 2]
                            d = q4 // 2
                            for i in range(4):
                                hd = q4 * 4 + i
                                lh = bc(dtA[:, d, c, hd % 8:hd % 8 + 1], [128, 128])
                                tk.op("pe", lambda e, p=p, i=i, lh=lh, d=d: e.matmul(
                                    p[:, i, :], lhsT=lh, rhs=CM(K_UINC if d == 0 else K_LINC), start=True, stop=False),
                                    reads=[dtA.b, cm.b], writes=[p.b], sig=False)
                                tk.op("pe", lambda e, p=p, i=i, d=d: e.matmul(
                                    p[:, i, :], lhsT=CM(K_ID), rhs=CM(K_NEGF if d == 0 else K_NEGB), start=False, stop=True),
                                    reads=[cm.b], writes=[p.b], sig=(i == 3))
                            for i in range(4):
                                hd = q4 * 4 + i
                                tk.op("act", lambda e, p=p, i=i, hd=hd, d=d: e.activation(
                                    out=LT[q4][:, i, :], in_=p[:, i, :], func=AF.Exp, bias=lnb[:, d, c, hd % 8:hd % 8 + 1]),
                                    reads=[p.b, lnb.b], writes=[LT[q4].b])
                            g = q4 % 2
                            tk.op("dve", lambda e, q4=q4, g=g: e.tensor_tensor(out=MT[:, q4 * 4:(q4 + 1) * 4, :], in0=LT[q4][:],
                                                                               in1=bc(cb_[:, g:g + 1, :], [128, 4, 128]), op=ALU.mult),
                                  reads=[LT[q4].b, cb_.b], writes=[MT.b])
                        return [pre] + [(lambda q4=q4: grp(q4)) for q4 in range(4)]

                    def stageB(c):
                        t0 = c * 128
                        k = c % 4
                        MT, xdd3 = MTD[k], xddD[k]
                        yn = yns[c % 2]

                        def s1():
                            for h in range(8):
                                xh = xs[:, c, h * 64:(h + 1) * 64]
                                tk.op("pe", lambda e, h=h, xh=xh: e.matmul(py[:, h * 64:(h + 1) * 64], lhsT=IDd[:, h, :], rhs=xh,
                                                                           start=True, stop=False), reads=[IDd.b, xs.b], writes=[py.b], sig=False)
                                tk.op("pe", lambda e, h=h, xh=xh: e.matmul(py[:, h * 64:(h + 1) * 64], lhsT=MT[:, h, :], rhs=xh,
                                                                           start=False, stop=False), reads=[MT.b, xs.b], writes=[py.b], sig=False)
                                tk.op("pe", lambda e, h=h, xh=xh: e.matmul(py[:, h * 64:(h + 1) * 64], lhsT=MT[:, 8 + h, :], rhs=xh,
                                                                           start=False, stop=True), reads=[MT.b, xs.b], writes=[py.b], sig=(h == 7))
                            for g in range(2):
                                tk.op("pe", lambda e, g=g: e.matmul(pyo[0][:, g * 256:(g + 1) * 256], lhsT=cT[:, g, t0:t0 + 128],
                                                                    rhs=Sfb[:, g * 256:(g + 1) * 256], start=True, stop=True),
                                      reads=[cT.b, Sfb.b], writes=[pyo[0].b], sig=(g == 1))
                            for g in range(2):
                                tk.op("pe", lambda e, g=g: e.matmul(pyo[1][:, g * 256:(g + 1) * 256], lhsT=cT[:, g, t0:t0 + 128],
                                                                    rhs=sinb[:, c, g * 256:(g + 1) * 256], start=True, stop=True),
                                      reads=[cT.b, sinb.b], writes=[pyo[1].b], sig=(g == 1))

                        def s2():
                            if c < NTILE - 1:
                                state_step(c, 0, xdd3)
                                tk.op("act", lambda e: e.activation(out=Sfb[:], in_=S[0][:], func=AF.Copy), reads=[S[0].b], writes=[Sfb.b])

                        def s3():
                            tk.op("dve", lambda e: e.tensor_tensor(out=v3(ta[:]), in0=v3(pyo[0][:, :]),
                                                                   in1=bc(ecum[:, 0, c, :].unsqueeze(2), [128, 8, 64]), op=ALU.mult),
                                  reads=[pyo[0].b, ecum.b], writes=[ta.b])
                            tk.op("dve", lambda e: e.tensor_tensor(out=v3(tb_[:]), in0=v3(pyo[1][:, :]),
                                                                   in1=bc(ecum[:, 1, c, :].unsqueeze(2), [128, 8, 64]), op=ALU.mult),
                                  reads=[pyo[1].b, ecum.b], writes=[tb_.b])

                        def s4():
                            tk.op("pool", lambda e: e.tensor_tensor(out=ta[:], in0=ta[:], in1=tb_[:], op=ALU.add),
                                  reads=[ta.b, tb_.b], writes=[ta.b])

                        def s5():
                            tk.op("dve", lambda e: e.tensor_tensor(out=yz[:], in0=py[:, :], in1=ta[:], op=ALU.add),
                                  reads=[py.b, ta.b], writes=[yz.b])

                        def s6():
                            tk.op("pool", lambda e: e.tensor_tensor(out=yz[:], in0=yz[:], in1=zs[:, c, :], op=ALU.mult),
                                  reads=[yz.b, zs.b], writes=[yz.b])
                            tk.op("pool", lambda e: e.memset(ssq[:], 0.0), writes=[ssq.b])

                        def s7():
                            tk.op("act", lambda e: e.activation(out=junk[:], in_=yz[:], func=AF.Square, accum_out=ssq[:, 0:1]),
                                  reads=[yz.b], writes=[junk.b, ssq.b])
                            tk.op("act", lambda e: e.activation(out=rstd[:], in_=ssq[:], func=AF.Ln, scale=1.0 / 512, bias=epsc[:, 0:1]),
                                  reads=[ssq.b, epsc.b], writes=[rstd.b])
                            tk.op("act", lambda e: e.activation(out=rstd[:], in_=rstd[:], func=AF.Exp, scale=-0.5),
                                  reads=[rstd.b], writes=[rstd.b])

                        def s8():
                            tk.op("dve", lambda e: e.scalar_tensor_tensor(out=yn[:], in0=yz[:], scalar=rstd[:, 0:1], in1=ng[:],
                                                                          op0=ALU.mult, op1=ALU.mult),
                                  reads=[yz.b, rstd.b, ng.b], writes=[yn.b])
                        return [s1, s2, s3, s4, s5, s6, s7, s8]

                    def stageB2(c):
                        t0 = c * 128
                        yn = yns[c % 2]
                        for j in range(4):
                            tk.op("pe", lambda e, j=j: e.transpose(out=pT[:, j, :], in_=yn[:, j * 128:(j + 1) * 128], identity=identb[:]),
                                  reads=[yn.b, identb.b], writes=[pT.b], sig=(j == 3))
                        tk.op("act", lambda e: e.activation(out=ycs[:, :, t0:t0 + 128], in_=pT[:], func=AF.Copy),
                              reads=[pT.b], writes=[ycs.b])

                    lite = (lambda c: skip_ctx and c < 2)
                    for c0 in range(3):
                        if lite(c0):
                            scaled_x(xddD[c0 % 4], c0, wdec, 0)
                            continue
                        for f in stageA(c0):
                            f()
                    for c in range(NTILE):
                        A = stageA(c + 3) if c + 3 < NTILE else [lambda: None] * 5
                        B = stageB(c)
                        if lite(c):
                            for f in (A[0], A[1], B[1], A[2], A[3], A[4]):
                                f()
                        else:
                            for f in (A[0], B[0], A[1], B[1], B[2], A[2], B[3], B[4], A[3], B[5], B[6], A[4], B[7]):
                                f()
                        if c >= 1 and not lite(c - 1):
                            stageB2(c - 1)
                    stageB2(NTILE - 1)
                    tk.dma("sp", ycT_s[0:512, :].rearrange("(j p) t -> p j t", p=128), ycs[:], reads=[ycs.b], writes=[ycT_s.b])
                    tk.barrier()

        def attn_block(units, q0, nq, ktiles, pO, pS, PT, scale, stages=()):
            seq = [(kt, u) for kt in ktiles for u in range(len(units))]
            LAG = 2
            stages = list(stages)
            for i in range(len(seq) + LAG):
                if stages and i >= 3 and (i - 3) % 4 == 0:
                    stages.pop(0)()
                if i < len(seq):
                    kt, u = seq[i]
                    U = units[u]
                    sl = i % 3
                    tk.op("pe", lambda e, U=U, kt=kt, sl=sl: e.matmul(pS[sl][:, 0:nq], lhsT=U["k"][1](kt), rhs=U["q"][1](q0, nq),
                                                                      start=True, stop=True),
                          reads=[U["k"][0].b, U["q"][0].b], writes=[pS[sl].b])
                    tk.op("act", lambda e, sl=sl: e.activation(out=PT[sl][:, 0:nq], in_=pS[sl][:, 0:nq], func=AF.Exp, scale=scale),
                          reads=[pS[sl].b], writes=[PT[sl].b])
                j = i - LAG
                if j >= 0:
                    kt, u = seq[j]
                    U = units[u]
                    sl = j % 3
                    acc = pO[U["acc"]]
                    tk.op("pe", lambda e, U=U, kt=kt, sl=sl, acc=acc: e.matmul(acc[:, 0:nq], lhsT=U["v"][1](kt), rhs=PT[sl][:, 0:nq],
                                                                               start=(kt == ktiles[0]), stop=(kt == ktiles[-1])),
                          reads=[U["v"][0].b, PT[sl].b], writes=[acc.b])
            for st in stages:
                st()

        def qblocks(ctx_out):
            qb = [(256 + 512 * i, 512, list(range(NTILE))) for i in range(4)]
            if ctx_out:
                qb = [(0, 256, [0, 1])] + qb
            return qb

        def load_vaug(vaug, src, nh):
            sv = src[:].rearrange("(c p) g w -> p c (g w)", p=128)
            dv = vaug[:].rearrange("p c g w -> p c (g w)")
            for c0 in range(0, NTILE, 6):
                tk.dma("sp", dv[:, c0:c0 + 6, :], sv[:, c0:c0 + 6, :], reads=[src.b], writes=[vaug.b])

        def normalize_stages(pa, pb, nq, OS, SS, pR, rec):
            def s1():
                tk.op("dve", lambda e: e.tensor_copy(out=OS[0:64, 0:nq], in_=pa[0:64, 0:nq]), reads=[pa.b], writes=[OS.b])
                tk.op("dve", lambda e: e.tensor_copy(out=OS[64:128, 0:nq], in_=pb[64:128, 0:nq]), reads=[pb.b], writes=[OS.b])
                tk.op("dve", lambda e: e.tensor_copy(out=SS[0:64, 0:nq], in_=pb[0:64, 0:nq]), reads=[pb.b], writes=[SS.b])
                tk.op("dve", lambda e: e.tensor_copy(out=SS[64:128, 0:nq], in_=pa[64:128, 0:nq]), reads=[pa.b], writes=[SS.b])

            def s2():
                tk.op("pe", lambda e: e.matmul(pR[:, 0:nq], lhsT=CM(K_SWAP), rhs=SS[:, 0:nq], start=True, stop=True),
                      reads=[cm.b, SS.b], writes=[pR.b])

            def s3():
                tk.op("dve", lambda e: e.reciprocal(out=rec[:, 0:nq], in_=pR[:, 0:nq]), reads=[pR.b], writes=[rec.b])
                tk.op("dve", lambda e: e.tensor_tensor(out=OS[:, 0:nq], in0=OS[:, 0:nq], in1=rec[:, 0:nq], op=ALU.mult),
                      reads=[OS.b, rec.b], writes=[OS.b])
            return [s1, s2, s3]

        def phase_G(l, ctx_out, pS, pO, pR, PT, OS, SS, rec, yaccs):
            with ExitStack() as ph:
                qT = sb(ph, "qm", [128, 4, NT], BF16)
                kT = sb(ph, "kdup", [128, 2, NT], BF16)
                vaug = sb(ph, "vaugG", [128, NTILE, 2, 192], BF16)
                gg = sb(ph, "gg", [128, 2, NT], BF16)
                qz = qT[:].rearrange("p h t -> p (h t)").bitcast(F32)
                tk.op("dve", lambda e: e.memset(qz, 0.0), writes=[qT.b])
                for hf in range(2):
                    tk.dma("sp", kT[64 * hf:64 * hf + 64, :, :], gkT_s[:].rearrange("h d t -> d h t"), reads=[gkT_s.b], writes=[kT.b])
                load_vaug(vaug, gv_s, 2)
                for h in range(4):
                    tk.dma("sp", qT[64 * (h % 2):64 * (h % 2) + 64, h, :], gqT_s[h], reads=[gqT_s.b], writes=[qT.b])
                tk.dma("sp", gg[:], ggT_s[:].rearrange("(j p) t -> p j t", p=128), reads=[ggT_s.b], writes=[gg.b])
                yield
                pending = []
                nblk = 0
                for j in range(2):
                    yacc = yaccs[j]
                    if not ctx_out:
                        tk.op("pool", lambda e, yacc=yacc: e.memset(yacc[:, 0:256], 0.0), writes=[yacc.b])
                    qbs = qblocks(ctx_out)
                    for bi, (q0, nq, ktiles) in enumerate(qbs):
                        a0 = 2 * (nblk % 2)
                        nblk += 1
                        units = [
                            dict(q=(qT, lambda q0, nq, j=j: qT[:, 2 * j, q0:q0 + nq]), k=(kT, lambda kt, j=j: kT[:, j, kt * 128:(kt + 1) * 128]),
                                 v=(vaug, lambda kt, j=j: vaug[:, kt, j, 64:192]), acc=a0),
                            dict(q=(qT, lambda q0, nq, j=j: qT[:, 2 * j + 1, q0:q0 + nq]), k=(kT, lambda kt, j=j: kT[:, j, kt * 128:(kt + 1) * 128]),
                                 v=(vaug, lambda kt, j=j: vaug[:, kt, j, 0:128]), acc=a0 + 1),
                        ]
                        attn_block(units, q0, nq, ktiles, pO, pS, PT, 0.125, stages=pending)
                        pending = normalize_stages(pO[a0], pO[a0 + 1], nq, OS, SS, pR, rec)

                        def fin(q0=q0, nq=nq, j=j, yacc=yacc, last=(bi == len(qbs) - 1)):
                            tk.op("pool", lambda e: e.tensor_tensor(out=yacc[:, q0:q0 + nq], in0=OS[:, 0:nq],
                                                                    in1=gg[:, j, q0:q0 + nq], op=ALU.mult),
                                  reads=[OS.b, gg.b], writes=[yacc.b])
                            if last:
                                tk.dma("sp", ycT_s[512 + 128 * j:640 + 128 * j, :], yacc[:], reads=[yacc.b], writes=[ycT_s.b])
                        pending.append(fin)
                for st in pending:
                    st()
                yield

        def phase_D(l, ctx_out, pS, pO, pR, PT, OSp, SS, rec, yaccs):
            lam_init = 0.8 - 0.6 * float(np.exp(-0.3 * l))
            with ExitStack() as ph:
                dq = sb(ph, "dqm", [128, 8, NT], BF16)
                dk = sb(ph, "dkc", [128, 2, NT], BF16)
                vaug = sb(ph, "vaugD", [128, NTILE, 4, 192], BF16)
                dg = sb(ph, "dg", [128, 2, NT], BF16)
                OSn = sb(ph, "OSn", [128, 512])
                sq = SS
                lp = sb(ph, "lp", [128, 128])
                lpr = sb(ph, "lpr", [128, 2, 32])
                lsum = sb(ph, "lsum", [128, 2])
                neglam = sb(ph, "neglam", [128, 1])
                gsc = sb(ph, "gsc", [128, 1])
                for m in range(0, 8, 2):
                    dz = dq[:, m:m + 2, :].rearrange("p h t -> p (h t)").bitcast(F32)
                    tk.op("dve", lambda e, dz=dz: e.memset(dz, 0.0), writes=[dq.b])
                for m in range(8):
                    tk.dma("sp", dq[32 * (m % 4):32 * (m % 4) + 32, m, :], dqT_s[32 * m:32 * m + 32, :], reads=[dqT_s.b], writes=[dq.b])
                tk.dma("sp", dk[:], dkT_s[:].rearrange("(c p) t -> p c t", p=128), reads=[dkT_s.b], writes=[dk.b])
                tk.dma("sp", dg[:], dgT_s[:].rearrange("(j p) t -> p j t", p=128), reads=[dgT_s.b], writes=[dg.b])
                tk.dma("sp", lp[:], dlam_d[l].partition_broadcast(128), reads=[dlam_d.b], writes=[lp.b])
                tk.dma("sp", gsc[:], dng_d[l], reads=[dng_d.b], writes=[gsc.b])
                load_vaug(vaug, dv_s, 4)
                yield
                lp4 = lp[:].rearrange("p (a b c) -> p a b c", a=2, b=2)
                tk.op("dve", lambda e: e.tensor_tensor(out=lpr[:], in0=lp4[:, :, 0, :], in1=lp4[:, :, 1, :], op=ALU.mult),
                      reads=[lp.b], writes=[lpr.b])
                tk.op("dve", lambda e: e.reduce_sum(out=lsum[:], in_=lpr[:], axis=AX.X), reads=[lpr.b], writes=[lsum.b])
                tk.op("act", lambda e: e.activation(out=lsum[:], in_=lsum[:], func=AF.Exp), reads=[lsum.b], writes=[lsum.b])
                tk.op("dve", lambda e: e.tensor_tensor(out=neglam[:], in0=lsum[:, 1:2], in1=lsum[:, 0:1], op=ALU.subtract),
                      reads=[lsum.b], writes=[neglam.b])
                tk.op("dve", lambda e: e.tensor_scalar(out=neglam[:], in0=neglam[:], scalar1=-lam_init, scalar2=None, op0=ALU.add),
                      reads=[neglam.b], writes=[neglam.b])
                tk.op("dve", lambda e: e.tensor_scalar(out=gsc[:], in0=gsc[:], scalar1=1.0 - lam_init, scalar2=None, op0=ALU.mult),
                      reads=[gsc.b], writes=[gsc.b])
                pending = []
                nblk = 0
                for j in range(2):
                    yacc = yaccs[j]
                    if not ctx_out:
                        tk.op("pool", lambda e, yacc=yacc: e.memset(yacc[:, 0:256], 0.0), writes=[yacc.b])
                    qbs = qblocks(ctx_out)
                    for bi, (q0, nq, ktiles) in enumerate(qbs):
                        for sign in range(2):
                            a0 = 2 * (nblk % 2)
                            nblk += 1
                            mA = 4 * j + sign
                            mB = 4 * j + 2 + sign
                            units = [
                                dict(q=(dq, lambda q0, nq, m=mA: dq[:, m, q0:q0 + nq]), k=(dk, lambda kt, j=j: dk[:, j, kt * 128:(kt + 1) * 128]),
                                     v=(vaug, lambda kt, hh=2 * j: vaug[:, kt, hh, 64:192]), acc=a0),
                                dict(q=(dq, lambda q0, nq, m=mB: dq[:, m, q0:q0 + nq]), k=(dk, lambda kt, j=j: dk[:, j, kt * 128:(kt + 1) * 128]),
                                     v=(vaug, lambda kt, hh=2 * j + 1: vaug[:, kt, hh, 0:128]), acc=a0 + 1),
                            ]
                            attn_block(units, q0, nq, ktiles, pO, pS, PT, 32.0 ** -0.5, stages=pending)
                            if sign == 0:
                                pending = normalize_stages(pO[a0], pO[a0 + 1], nq, OSp, SS, pR, rec)
                                continue
                            pending = normalize_stages(pO[a0], pO[a0 + 1], nq, OSn, SS, pR, rec)

                            def c1(nq=nq):
                                tk.op("dve", lambda e: e.scalar_tensor_tensor(out=OSp[:, 0:nq], in0=OSn[:, 0:nq], scalar=neglam[:, 0:1],
                                                                              in1=OSp[:, 0:nq], op0=ALU.mult, op1=ALU.add),
                                      reads=[OSn.b, neglam.b, OSp.b], writes=[OSp.b])
                                tk.op("pool", lambda e: e.tensor_tensor(out=sq[:, 0:nq], in0=OSp[:, 0:nq], in1=OSp[:, 0:nq], op=ALU.mult),
                                      reads=[OSp.b], writes=[sq.b])

                            def c2(nq=nq):
                                tk.op("pe", lambda e: e.matmul(pR[:, 0:nq], lhsT=CM(K_BD64), rhs=sq[:, 0:nq], start=True, stop=True),
                                      reads=[cm.b, sq.b], writes=[pR.b])

                            def c3(nq=nq):
                                tk.op("act", lambda e: e.activation(out=rec[:, 0:nq], in_=pR[:, 0:nq], func=AF.Ln, bias=epsc[:, 0:1]),
                                      reads=[pR.b, epsc.b], writes=[rec.b])
                                tk.op("act", lambda e: e.activation(out=rec[:, 0:nq], in_=rec[:, 0:nq], func=AF.Exp, scale=-0.5),
                                      reads=[rec.b], writes=[rec.b])

                            def c4(nq=nq, q0=q0, j=j, yacc=yacc, last=(bi == len(qbs) - 1)):
                                tk.op("dve", lambda e: e.tensor_tensor(out=OSp[:, 0:nq], in0=OSp[:, 0:nq], in1=rec[:, 0:nq], op=ALU.mult),
                                      reads=[OSp.b, rec.b], writes=[OSp.b])
                                tk.op("dve", lambda e: e.scalar_tensor_tensor(
                                    out=yacc[:, q0:q0 + nq], in0=OSp[:, 0:nq], scalar=gsc[:, 0:1], in1=dg[:, j, q0:q0 + nq],
                                    op0=ALU.mult, op1=ALU.mult), reads=[OSp.b, gsc.b, dg.b], writes=[yacc.b])
                                if last:
                                    tk.dma("sp", ycT_s[768 + 128 * j:896 + 128 * j, :], yacc[:], reads=[yacc.b], writes=[ycT_s.b])
                            pending += [c1, c2, c3, c4]
                for st in pending:
                    st()
                yield

        def phase_GD(l, ctx_out, wo):
            wv = wout_d[l].rearrange("(kc p) n -> p kc n", p=128)
            for n in range(2):
                tk.dma("pool", wo[:, :, n * 512:(n + 1) * 512], wv[:, :, n * 512:(n + 1) * 512], reads=[wout_d.b], writes=[wo.b])
            with ExitStack() as ph:
                pS = [ps(ph, f"pSa{i}", [128, 512]) for i in range(3)]
                pO = [ps(ph, f"pOa{i}", [128, 512]) for i in range(4)]
                pR = ps(ph, "pRa", [128, 512])
                PT = [sb(ph, f"PTa{i}", [128, 512], BF16) for i in range(3)]
                OS = sb(ph, "OSa", [128, 512])
                SS = sb(ph, "SSa", [128, 512])
                rec = sb(ph, "reca", [128, 512])
                yaccs = [sb(ph, f"yacc{i}", [128, NT], BF16) for i in range(2)]
                g = phase_G(l, ctx_out, pS, pO, pR, PT, OS, SS, rec, yaccs)
                next(g)
                d = phase_D(l, ctx_out, pS, pO, pR, PT, OS, SS, rec, yaccs)
                next(d)
                next(g)
                next(d)
                tk.barrier()
                for gen in (d, g):
                    for _ in gen:
                        pass

        def phase_O(l, ctx_out, wo):
            with ExitStack() as ph:
                ycTs = [sb(ph, f"ycT{i}", [128, 8, 768], BF16) for i in range(3)]
                G2 = [sb(ph, f"G2_{i}", [128, D]) for i in range(2)]
                gpo = sb(ph, "gpo", [128, D])
                ht = [sb(ph, f"hto{i}", [128, D]) for i in range(2)]
                on = [sb(ph, f"on{i}", [128, D]) for i in range(2)]
                junk = sb(ph, "junkO", [128, 512], BF16)
                ssqs = [sb(ph, f"ssqO{i}", [128, 2]) for i in range(2)]
                rstds = [sb(ph, f"rstdO{i}", [128, 1]) for i in range(2)]
                po = [ps(ph, f"po{i}", [128, 512]) for i in range(4)]
                tk.dma("sp", gpo[:], gpost_d[l].partition_broadcast(128), reads=[gpost_d.b], writes=[gpo.b])
                for w in range(2):
                    tk.dma("sp", G2[w][:], gt_s[l, w].partition_broadcast(128), reads=[gt_s.b], writes=[G2[w].b])
                    tk.op("dve", lambda e, w=w: e.tensor_tensor(out=G2[w][:], in0=G2[w][:], in1=gpo[:], op=ALU.mult),
                          reads=[G2[w].b, gpo.b], writes=[G2[w].b])
                ysv = ycT_s[:].rearrange("(kc p) t -> p kc t", p=128)
                for i in range(3):
                    tk.dma("sp", ycTs[i][:], ysv[:, :, i * 768:(i + 1) * 768], reads=[ycT_s.b], writes=[ycTs[i].b])
                for it, tt in enumerate(range(0 if ctx_out else 2, NTILE)):
                    k = it % 2
                    w = 1 if tt < 2 else 0
                    if l == 0:
                        src = ctx_d if tt < 2 else x_d
                        sap = ctx_d[tt * 128:(tt + 1) * 128, :] if tt < 2 else x_d[(tt - 2) * 128:(tt - 1) * 128, :]
                    else:
                        src = h1_s
                        sap = h1_s[tt * 128:(tt + 1) * 128, :]
                    tk.dma("sp", ht[k][:], sap, reads=[src.b], writes=[ht[k].b])
                    pp = po[2 * k:2 * k + 2]
                    for n in range(2):
                        for kc in range(8):
                            tk.op("pe", lambda e, n=n, kc=kc, pp=pp, tt=tt: e.matmul(
                                pp[n][:, :], lhsT=ycTs[tt // 6][:, kc, (tt % 6) * 128:(tt % 6 + 1) * 128], rhs=wo[:, kc, n * 512:(n + 1) * 512],
                                start=(kc == 0), stop=(kc == 7)), reads=[ycTs[tt // 6].b, wo.b], writes=[pp[n].b], sig=(kc == 7))
                    ssq = ssqs[k]
                    rstd = rstds[k]
                    tk.op("pool", lambda e: e.memset(ssq[:], 0.0), writes=[ssq.b])
                    for n in range(2):
                        tk.op("act", lambda e, n=n, pp=pp: e.activation(out=junk[:], in_=pp[n][:, :], func=AF.Square,
                                                                        accum_out=ssq[:, n:n + 1]),
                              reads=[pp[n].b], writes=[junk.b, ssq.b])
                    tk.op("dve", lambda e: e.tensor_tensor(out=rstd[:], in0=ssq[:, 0:1], in1=ssq[:, 1:2], op=ALU.add),
                          reads=[ssq.b], writes=[rstd.b])
                    tk.op("act", lambda e: e.activation(out=rstd[:], in_=rstd[:], func=AF.Ln, scale=1.0 / D, bias=epsc[:, 0:1]),
                          reads=[rstd.b, epsc.b], writes=[rstd.b])
                    tk.op("act", lambda e: e.activation(out=rstd[:], in_=rstd[:], func=AF.Exp, scale=-0.5),
                          reads=[rstd.b], writes=[rstd.b])
                    for n in range(2):
                        tk.op("dve", lambda e, n=n, pp=pp, k=k, w=w: e.scalar_tensor_tensor(
                            out=on[k][:, n * 512:(n + 1) * 512], in0=pp[n][:, :], scalar=rstd[:, 0:1],
                            in1=G2[w][:, n * 512:(n + 1) * 512], op0=ALU.mult, op1=ALU.mult),
                            reads=[pp[n].b, rstd.b, G2[w].b], writes=[on[k].b])
                    tk.op("dve", lambda e, k=k: e.tensor_tensor(out=on[k][:], in0=on[k][:], in1=ht[k][:], op=ALU.add),
                          reads=[on[k].b, ht[k].b], writes=[on[k].b])
                    if l == 0:
                        tk.dma("sp", h1_s[tt * 128:(tt + 1) * 128, :], on[k][:], reads=[on[k].b], writes=[h1_s.b])
                    else:
                        tk.dma("sp", out_d[(tt - 2) * 128:(tt - 1) * 128, :], on[k][:], reads=[on[k].b], writes=[out_d.b])
                tk.barrier()

        m0 = phase_M(0, "sp")
        for _ in range(8):
            next(m0)
        tk.barrier()
        for _ in m0:
            pass
        for l in range(layers):
            if stop_after == ("M", l):
                break
            with ExitStack() as ni:
                uT = sb(ni, "uT", [128, 8, NT], BF16)
                wr = [sb(ni, f"wr{i}", [128, 8, 512], BF16) for i in range(4)]
                I_load = make_I_loader(l, wr)
                I_slots = {}

                def I_pref():
                    for i in range(3):
                        I_slots[i] = I_load(i)
                mnext = phase_M(l + 1, "pool") if l + 1 < layers else None
                phase_N(l, uT, mnext=mnext, mid_hook=I_pref)
                if mnext is not None:
                    raise_if = [x for x in mnext]
                if stop_after == ("N", l):
                    break
                phase_I(l, uT, wr, I_slots)
                if stop_after == ("I", l):
                    break
            ctx_out = l < DEPTH - 1
            phase_S(l, skip_ctx=not ctx_out)
            if stop_after == ("S", l):
                break
            with ExitStack() as go:
                wo = sb(go, "wo", [128, 8, D], BF16)
                phase_GD(l, ctx_out, wo)
                if stop_after == ("D", l):
                    break
                phase_O(l, ctx_out, wo)
                if stop_after == ("O", l):
                    break
        tk.barrier()
    return nc


def _rot_perm(base, nheads, hd):
    idx = []
    for h in range(nheads):
        o = base + h * hd
        idx += list(range(o + hd // 2, o + hd)) + list(range(o, o + hd // 2))
    return idx


def prep_inputs(inp):
    f = lambda a: np.ascontiguousarray(np.asarray(a), dtype=np.float32)
    x, c, ctx, c_ctx = f(inp["x"]), f(inp["c"]), f(inp["ctx"]), f(inp["c_ctx"])
    w_in = f(inp["w_in"])
    perm = (_rot_perm(C_GQ, 4, 64) + _rot_perm(C_GK, 2, 64) + _rot_perm(C_DQ, 8, 32) + _rot_perm(C_DK, 8, 32))
    w_rot = np.ascontiguousarray(w_in[:, :, perm])
    b_mod = f(inp["b_mod"])
    g_pre = f(inp["g_pre"])
    conv_w = f(inp["conv_w"])
    conv_b = f(inp["conv_b"])
    qg, kg = f(inp["q_norm_g"]), f(inp["k_norm_g"])
    pq = _rot_perm(0, 1, 64)
    cg, sg, cd, sd = _rope_tables()
    shared = {
        "w_mod": f(inp["w_mod"]),
        "bmodf": np.stack([_feat(b_mod[l, :2048]) for l in range(DEPTH)]),
        "b_mod": b_mod,
        "gpref": np.stack([_feat(g_pre[l]) for l in range(DEPTH)]),
        "g_post": f(inp["g_post"]),
        "w_in": w_in,
        "w_rot": w_rot,
        "w_out": f(inp["w_out"]),
        "convw_f": np.ascontiguousarray(conv_w.reshape(DEPTH, 5, 8, 128).transpose(0, 3, 2, 1)),
        "convb_f": np.stack([_feat(conv_b[l]) for l in range(DEPTH)]),
        "alog": np.concatenate([f(inp["a_log_fwd"]), f(inp["a_log_bwd"])], axis=1),
        "dtb": np.concatenate([f(inp["dt_bias_fwd"]), f(inp["dt_bias_bwd"])], axis=1),
        "d_skip": f(inp["d_skip"]),
        "ssd_norm_g": f(inp["ssd_norm_g"]),
        "qkg_f": np.ascontiguousarray(np.stack([qg, qg[:, pq], kg, kg[:, pq]], axis=2)),
        "diff_lambda": f(inp["diff_lambda"]).reshape(DEPTH, 128),
        "dng_f": np.ascontiguousarray(np.tile(f(inp["diff_norm_g"]), (1, 2))[:, :, None]),
        "cmat": _const_mats(),
        "identb": np.eye(128, dtype=np.float32).astype(ml_dtypes.bfloat16),
        "rope_g": np.stack([cg, sg]),
        "rope_d": np.stack([cd, sd]),
    }
    cc = _feat(c_ctx)
    maps = []
    for b in range(NCORES):
        m = dict(shared)
        m["x"] = x[b]
        m["ctx"] = ctx[b]
        m["cvec"] = np.ascontiguousarray(np.concatenate([_feat(c[b]), cc], axis=1))
        maps.append(m)
    return maps


_NC_CACHE = {}


def kernel(**inputs):
    if "nc" not in _NC_CACHE:
        _NC_CACHE["nc"] = build()
    nc = _NC_CACHE["nc"]
    maps = prep_inputs(inputs)
    res = run_bass_kernel_spmd(nc, maps, core_ids=list(range(NCORES)))
    return np.stack([np.asarray(r["out"], dtype=np.float32) for r in res.results], axis=0)
```

```python
import numpy as np
import ml_dtypes
from contextlib import ExitStack
import concourse.bass as bass
import concourse.mybir as mybir
from concourse.bass_utils import run_bass_kernel_spmd

F32 = mybir.dt.float32
BF16 = mybir.dt.bfloat16
AF = mybir.ActivationFunctionType
ALU = mybir.AluOpType
AX = mybir.AxisListType

NCORES = 8
D = 1024
SEQ = 2048
CTX = 256
NT = SEQ + CTX
NTILE = NT // 128
DEPTH = 2
EPS = 1e-6
INC = 3344
C_X, C_B, C_C, C_Z, C_DT = 0, 512, 768, 1024, 1536
C_GQ, C_GK, C_GV, C_GG = 1552, 1808, 1936, 2064
C_DQ, C_DK, C_DV, C_DG = 2320, 2576, 2832, 3088
R_GQ, R_GK, R_DQ, R_DK, NROT = 0, 256, 384, 640, 896
NEG = -30000.0
SAME_ENGINE_SYNC = True
S_LEVEL = [0]
BCAST_LHST = True


class Buf:
    __slots__ = ("w", "r", "name")

    def __init__(self, name=""):
        self.w = None
        self.r = {}
        self.name = name


class T:
    NDS = 8
    ROT = 20000

    def __init__(self, nc, es):
        self.nc = nc
        self.es = es
        self.E = {"pe": nc.tensor, "act": nc.scalar, "dve": nc.vector, "pool": nc.gpsimd, "sp": nc.sync}
        self.sem = {}
        self.cnt = {}
        self.nsem = 0
        for e in ("pe", "act", "dve", "pool"):
            self.sem[e] = self._newsem("s_" + e)
            self.cnt[e] = 0
        self.seen = {e: {} for e in self.E}
        self.pend = {e: ([], []) for e in self.E}
        self.dq = {}
        for q in ("sp", "pool", "act"):
            self.dq[q] = {"sems": [self._newsem(f"d_{q}{i}") for i in range(self.NDS)],
                          "vals": [0] * self.NDS, "n": 0}
        self.alltoks = {}
        self.ninstr = 0

    def _newsem(self, name):
        self.nsem += 1
        return self.es.enter_context(self.nc.semaphore(f"{name}_{self.nsem}"))

    def _wait(self, x, sem, val):
        if self.seen[x].get(sem, 0) >= val:
            return
        self.E[x].wait_ge(sem, val)
        self.seen[x][sem] = val

    def _deps(self, x, reads, writes):
        deps = {}
        for b in reads:
            if b.w is not None:
                s, v = b.w
                if deps.get(s, 0) < v:
                    deps[s] = v
        for b in writes:
            if b.w is not None:
                s, v = b.w
                if deps.get(s, 0) < v:
                    deps[s] = v
            for s, v in b.r.items():
                if deps.get(s, 0) < v:
                    deps[s] = v
        for e, (pr, pw) in self.pend.items():
            if e == x or (not pr and not pw):
                continue
            for b in writes:
                assert all(b is not o for o in pr) and all(b is not o for o in pw), f"pending conflict {b.name}"
            for b in reads:
                assert all(b is not o for o in pw), f"pending conflict {b.name}"
        own = self.sem.get(x)
        for s, v in deps.items():
            if s is own and (x == "pe" or not SAME_ENGINE_SYNC):
                continue
            self._wait(x, s, v)

    def op(self, x, fn, reads=(), writes=(), sig=True):
        self._deps(x, reads, writes)
        ins = fn(self.E[x])
        self.ninstr += 1
        pr, pw = self.pend[x]
        pr.extend(reads)
        pw.extend(writes)
        if sig:
            if self.cnt[x] >= self.ROT:
                self.sem[x] = self._newsem("s_" + x)
                self.cnt[x] = 0
            self.cnt[x] += 1
            s = self.sem[x]
            v = self.cnt[x]
            ins.then_inc(s, 1)
            self.alltoks[s] = v
            for b in pr:
                if b.r.get(s, 0) < v:
                    b.r[s] = v
            for b in pw:
                b.w = (s, v)
                b.r = {}
            self.pend[x] = ([], [])
        return ins

    def dma(self, q, out, in_, reads=(), writes=(), **kw):
        self._deps(q, reads, writes)
        st = self.dq[q]
        k = st["n"] % self.NDS
        st["n"] += 1
        sem = st["sems"][k]
        if st["vals"][k] > 0:
            self._wait(q, sem, st["vals"][k])
        ins = self.E[q].dma_start(out=out, in_=in_, **kw)
        self.ninstr += 1
        st["vals"][k] += 16
        v = st["vals"][k]
        ins.then_inc(sem, 16)
        self.alltoks[sem] = v
        for b in reads:
            if b.r.get(sem, 0) < v:
                b.r[sem] = v
        for b in writes:
            b.w = (sem, v)
            b.r = {}
        return ins

    def barrier(self):
        for e, (pr, pw) in self.pend.items():
            assert not pr and not pw, "pending unsignaled ops at barrier"
        for x in self.E:
            own = self.sem.get(x)
            for s, v in self.alltoks.items():
                if s is own and x == "pe":
                    continue
                self._wait(x, s, v)


def _rope_tables():
    rows = SEQ // 64
    row_idx = np.repeat(np.arange(rows), 64).astype(np.float32)
    col_idx = (np.arange(SEQ) % 64).astype(np.float32)

    def ang(dim):
        q = dim // 4
        inv = (np.float32(10000.0) ** (-np.arange(q, dtype=np.float32) / np.float32(q))).astype(np.float32)
        a = np.concatenate([row_idx[:, None] * inv, col_idx[:, None] * inv], axis=-1)
        return a.astype(np.float32)

    ag = ang(64)
    ad = ang(32)
    cg = np.concatenate([np.cos(ag), np.cos(ag)], axis=1).T
    sg = np.concatenate([-np.sin(ag), np.sin(ag)], axis=1).T
    cd = np.concatenate([np.cos(ad), np.cos(ad)], axis=1).T
    sd = np.concatenate([-np.sin(ad), np.sin(ad)], axis=1).T
    cd = np.tile(cd, (4, 1))
    sd = np.tile(sd, (4, 1))
    return (np.ascontiguousarray(cg, np.float32), np.ascontiguousarray(sg, np.float32),
            np.ascontiguousarray(cd, np.float32), np.ascontiguousarray(sd, np.float32))


K_ID, K_UINC, K_LINC, K_NEGF, K_NEGB, K_ONES, K_SWAP, K_BD64, K_SELW, NCONST = 0, 1, 2, 3, 4, 5, 6, 7, 8, 9


def _const_mats():
    i = np.arange(128)
    m = np.zeros((NCONST, 128, 128), np.float32)
    m[K_ID] = np.eye(128)
    m[K_UINC] = (i[:, None] <= i[None, :])
    m[K_LINC] = (i[:, None] >= i[None, :])
    m[K_NEGF] = np.where(i[None, :] < i[:, None], NEG, 0.0)
    m[K_NEGB] = np.where(i[None, :] > i[:, None], NEG, 0.0)
    m[K_ONES] = 1.0
    m[K_SWAP] = (i[:, None] == ((i[None, :] + 64) % 128))
    bd = np.zeros((128, 128), np.float32)
    bd[:64, :64] = 1.0 / 64
    bd[64:, 64:] = 1.0 / 64
    m[K_BD64] = bd
    sel = np.zeros((128, 128), np.float32)
    m[K_SELW] = sel
    return np.ascontiguousarray(m.transpose(1, 0, 2))


def _feat(v):
    v = np.asarray(v, np.float32)
    return np.ascontiguousarray(v.reshape(-1, 128).T)


class Tl:
    def __init__(self, h, name):
        self.h = h
        self.b = Buf(name)

    def __getitem__(self, k):
        return self.h[k]


def bc(ap, shape):
    return ap.broadcast_to(list(shape))


def build(dbg=(), stop_after=None, layers=DEPTH):
    nc = bass.Bass("TRN2", target_bir_lowering=False)
    dbg = set(dbg)

    def din(name, shape, dt=F32):
        return Tl(nc.dram_tensor(name, list(shape), dt, kind="ExternalInput").ap(), name)

    def dscr(name, shape, dt=BF16):
        kind = "ExternalOutput" if name in dbg else "Internal"
        return Tl(nc.dram_tensor(name, list(shape), dt, kind=kind).ap(), name)

    x_d = din("x", [SEQ, D])
    ctx_d = din("ctx", [CTX, D])
    cvec_d = din("cvec", [128, 16])
    wmod_d = din("w_mod", [DEPTH, D, 3 * D])
    bmodf_d = din("bmodf", [DEPTH, 128, 16])
    bmod_d = din("b_mod", [DEPTH, 3 * D])
    gpref_d = din("gpref", [DEPTH, 128, 8])
    gpost_d = din("g_post", [DEPTH, D])
    win_d = din("w_in", [DEPTH, D, INC])
    wrot_d = din("w_rot", [DEPTH, D, NROT])
    wout_d = din("w_out", [DEPTH, D, D])
    convw_d = din("convw_f", [DEPTH, 128, 8, 5])
    convb_d = din("convb_f", [DEPTH, 128, 8])
    alog_d = din("alog", [DEPTH, 16])
    dtb_d = din("dtb", [DEPTH, 16])
    dskip_d = din("d_skip", [DEPTH, 8])
    ssdg_d = din("ssd_norm_g", [DEPTH, 512])
    qkg_d = din("qkg_f", [DEPTH, 64, 4])
    dlam_d = din("diff_lambda", [DEPTH, 128])
    dng_d = din("dng_f", [DEPTH, 128, 1])
    cmat_d = din("cmat", [128, NCONST, 128])
    identb_d = din("identb", [128, 128], BF16)
    ropeg_d = din("rope_g", [2, 64, SEQ])
    roped_d = din("rope_d", [2, 128, SEQ])
    out_d = Tl(nc.dram_tensor("out", [SEQ, D], F32, kind="ExternalOutput").ap(), "out")

    gt_s = dscr("gt_s", [DEPTH, 2, D], F32)
    h1_s = dscr("h1_s", [NT, D], F32)
    xs_s = dscr("xs_s", [NT, 512])
    bt_s = dscr("bt_s", [NT, 256])
    bT_s = dscr("bT_s", [256, NT])
    cT_s = dscr("cT_s", [256, NT])
    zs_s = dscr("zs_s", [NT, 512])
    dt_s = dscr("dt_s", [NT, 16], F32)
    gqT_s = dscr("gqT_s", [4, 64, NT])
    gkT_s = dscr("gkT_s", [2, 64, NT])
    gv_s = dscr("gv_s", [NT, 2, 192])
    ggT_s = dscr("ggT_s", [256, NT])
    dqT_s = dscr("dqT_s", [256, NT])
    dkT_s = dscr("dkT_s", [256, NT])
    dv_s = dscr("dv_s", [NT, 4, 192])
    dgT_s = dscr("dgT_s", [256, NT])
    ycT_s = dscr("ycT_s", [D, NT])
    uT_dbg = dscr("uT_dbg", [D, NT]) if "uT_dbg" in dbg else None
    mod_dbg = dscr("mod_dbg", [128, 32], F32) if "mod_dbg" in dbg else None

    with ExitStack() as es:
        tk = T(nc, es)

        uniq = [0]

        def sb(st, name, shape, dt=F32):
            uniq[0] += 1
            name = f"{name}_s{uniq[0]}"
            return Tl(st.enter_context(nc.sbuf_tensor(name, list(shape), dt)), name)

        def ps(st, name, shape, dt=F32):
            uniq[0] += 1
            name = f"{name}_p{uniq[0]}"
            return Tl(st.enter_context(nc.psum_tensor(name, list(shape), dt)), name)

        cm = sb(es, "cm", [128, NCONST, 128])
        identb = sb(es, "identb_sb", [128, 128], BF16)
        A1L = [sb(es, f"A1_{i}", [128, 8, 2]) for i in range(DEPTH)]
        SHL = [sb(es, f"SH_{i}", [128, 8, 2]) for i in range(DEPTH)]
        Ssil = sb(es, "Ssil", [128, 8, 2])
        tk.dma("sp", cm[:], cmat_d[:], reads=[cmat_d.b], writes=[cm.b])
        tk.dma("sp", identb[:], identb_d[:], reads=[identb_d.b], writes=[identb.b])

        epsc = sb(es, "epsc", [128, 1])
        tk.op("pool", lambda e: e.memset(epsc[:], EPS), writes=[epsc.b])

        def CM(k):
            return cm[:, k, :]

        ones_t = sb(es, "ones_t", [128, 768], BF16)
        tk.op("dve", lambda e: e.memset(ones_t[:], 1.0), writes=[ones_t.b])
        for (dst, nh) in ((gv_s, 2), (dv_s, 4)):
            tk.dma("sp", dst[:].rearrange("(c p) g w -> p c (g w)", p=128),
                   bc(ones_t[:, 0:nh * 192].unsqueeze(1), [128, NTILE, nh * 192]), reads=[ones_t.b], writes=[dst.b])

        def phase_M(l, q):
            A1, SH, S = A1L[l], SHL[l], Ssil
            with ExitStack() as ph:
                bmf = sb(ph, "bmf", [128, 16])
                gpf = sb(ph, "gpf", [128, 8])
                bgt = sb(ph, "bgt", [2, D])
                modT = sb(ph, "modT", [128, 16, 2])
                gtr = sb(ph, "gtr", [2, D])
                wb = [sb(ph, f"wmb{i}", [128, 8, 512]) for i in range(2)]
                psM = ps(ph, "psM", [128, 16, 2])
                psG = [ps(ph, f"psG{i}", [2, 512]) for i in range(2)]
                if l == 0:
                    cv = sb(ph, "cv", [128, 16])
                    tk.dma(q, cv[:], cvec_d[:], reads=[cvec_d.b], writes=[cv.b])
                    for w in range(2):
                        tk.op("act", lambda e, w=w: e.activation(out=S[:, :, w], in_=cv[:, 8 * w:8 * w + 8], func=AF.Silu),
                              reads=[cv.b], writes=[S.b])
                tk.dma(q, bmf[:], bmodf_d[l], reads=[bmodf_d.b], writes=[bmf.b])
                tk.dma(q, gpf[:], gpref_d[l], reads=[gpref_d.b], writes=[gpf.b])
                tk.dma(q, bgt[:], bmod_d[l, 2 * D:3 * D].partition_broadcast(2), reads=[bmod_d.b], writes=[bgt.b])
                wv = wmod_d[l].rearrange("(kc p) n -> p kc n", p=128)
                yield
                for blk in range(6):
                    w_ = wb[blk % 2]
                    tk.dma(q, w_[:], wv[:, :, blk * 512:(blk + 1) * 512], reads=[wmod_d.b], writes=[w_.b])
                    if blk < 4:
                        for oc in range(4):
                            for kc in range(8):
                                tk.op("pe", lambda e, oc=oc, kc=kc, w_=w_, blk=blk: e.matmul(
                                    psM[:, blk * 4 + oc, :], lhsT=w_[:, kc, oc * 128:(oc + 1) * 128], rhs=S[:, kc, :],
                                    start=(kc == 0), stop=(kc == 7)),
                                    reads=[w_.b, S.b], writes=[psM.b], sig=(kc == 7 and oc == 3))
                    else:
                        g = psG[blk - 4]
                        for kc in range(8):
                            tk.op("pe", lambda e, kc=kc, w_=w_, g=g: e.matmul(
                                g[:, :], lhsT=S[:, kc, :], rhs=w_[:, kc, :], start=(kc == 0), stop=(kc == 7)),
                                reads=[w_.b, S.b], writes=[g.b], sig=(kc == 7))
                    yield
                tk.op("dve", lambda e: e.tensor_tensor(out=modT[:], in0=psM[:], in1=bc(bmf[:].unsqueeze(2), [128, 16, 2]),
                                                       op=ALU.add), reads=[psM.b, bmf.b], writes=[modT.b])
                tk.op("dve", lambda e: e.tensor_copy(out=SH[:], in_=modT[:, 0:8, :]), reads=[modT.b], writes=[SH.b])
                tk.op("dve", lambda e: e.scalar_tensor_tensor(out=A1[:], in0=modT[:, 8:16, :], scalar=1.0,
                                                              in1=bc(gpf[:].unsqueeze(2), [128, 8, 2]),
                                                              op0=ALU.add, op1=ALU.mult),
                      reads=[modT.b, gpf.b], writes=[A1.b])
                for i in range(2):
                    tk.op("dve", lambda e, i=i: e.tensor_tensor(out=gtr[:, i * 512:(i + 1) * 512], in0=psG[i][:, :],
                                                                in1=bgt[:, i * 512:(i + 1) * 512], op=ALU.add),
                          reads=[psG[i].b, bgt.b], writes=[gtr.b])
                tk.dma("sp", gt_s[l], gtr[:], reads=[gtr.b], writes=[gt_s.b])
                if mod_dbg is not None and l == 0:
                    tk.dma("sp", mod_dbg[:, 0:16], A1[:].rearrange("p a b -> p (a b)"), reads=[A1.b], writes=[mod_dbg.b])
                    tk.dma("sp", mod_dbg[:, 16:32], SH[:].rearrange("p a b -> p (a b)"), reads=[SH.b], writes=[mod_dbg.b])
                yield

        def phase_N(l, uT, mnext=None, mid_hook=None):
            A1, SH = A1L[l], SHL[l]
            with ExitStack() as ph:
                if mnext is not None:
                    next(mnext)
                ht = [sb(ph, f"ht{i}", [128, D]) for i in range(2)]
                hb = [sb(ph, f"hb{i}", [128, D], BF16) for i in range(2)]
                junk = sb(ph, "junkN", [128, D], BF16)
                ssqs = [sb(ph, f"ssq{i}", [128, 1]) for i in range(2)]
                rstds = [sb(ph, f"rstd{i}", [128, 1]) for i in range(2)]
                tmpf = [sb(ph, f"tmpf{i}", [128, 8, 128]) for i in range(2)]
                pT = [ps(ph, f"pT{i}", [128, 8, 128], BF16) for i in range(2)]
                for tt in range(NTILE):
                    k = tt % 2
                    w = 1 if tt < 2 else 0
                    if l == 0:
                        src = ctx_d if tt < 2 else x_d
                        sap = ctx_d[tt * 128:(tt + 1) * 128, :] if tt < 2 else x_d[(tt - 2) * 128:(tt - 1) * 128, :]
                    else:
                        src = h1_s
                        sap = h1_s[tt * 128:(tt + 1) * 128, :]
                    tk.dma("sp", ht[k][:], sap, reads=[src.b], writes=[ht[k].b])
                    ssq = ssqs[k]
                    rstd = rstds[k]
                    tk.op("pool", lambda e: e.memset(ssq[:], 0.0), writes=[ssq.b])
                    tk.op("act", lambda e, k=k: e.activation(out=junk[:], in_=ht[k][:], func=AF.Square, accum_out=ssq[:, 0:1]),
                          reads=[ht[k].b], writes=[junk.b, ssq.b])
                    tk.op("act", lambda e: e.activation(out=rstd[:], in_=ssq[:], func=AF.Ln, scale=1.0 / D, bias=epsc[:, 0:1]),
                          reads=[ssq.b, epsc.b], writes=[rstd.b])
                    tk.op("act", lambda e: e.activation(out=rstd[:], in_=rstd[:], func=AF.Exp, scale=-0.5),
                          reads=[rstd.b], writes=[rstd.b])
                    tk.op("act", lambda e, k=k: e.activation(out=hb[k][:], in_=ht[k][:], func=AF.Copy, scale=rstd[:, 0:1]),
                          reads=[ht[k].b, rstd.b], writes=[hb[k].b])
                    for j in range(8):
                        tk.op("pe", lambda e, k=k, j=j: e.transpose(out=pT[k][:, j, :], in_=hb[k][:, j * 128:(j + 1) * 128],
                                                                    identity=identb[:]),
                              reads=[hb[k].b, identb.b], writes=[pT[k].b], sig=(j == 7))
                    tk.op("dve", lambda e, k=k, w=w: e.tensor_tensor(out=tmpf[k][:], in0=pT[k][:],
                                                                     in1=bc(A1[:, :, w:w + 1], [128, 8, 128]), op=ALU.mult),
                          reads=[pT[k].b, A1.b], writes=[tmpf[k].b])
                    tk.op("dve", lambda e, k=k, w=w, tt=tt: e.tensor_tensor(
                        out=uT[:, :, tt * 128:(tt + 1) * 128], in0=tmpf[k][:],
                        in1=bc(SH[:, :, w:w + 1], [128, 8, 128]), op=ALU.add),
                        reads=[tmpf[k].b, SH.b], writes=[uT.b])
                    if mnext is not None and tt % 3 == 2:
                        next(mnext)
                    if mid_hook is not None and tt == 9:
                        mid_hook()
                if mnext is not None:
                    next(mnext)
                if uT_dbg is not None and l == 0:
                    tk.dma("sp", uT_dbg[:].rearrange("(kc p) t -> p kc t", p=128), uT[:], reads=[uT.b], writes=[uT_dbg.b])
                tk.barrier()

        def make_I_loader(l, wr):
            NS = 4
            winv = win_d[l].rearrange("(kc p) n -> p kc n", p=128)
            wrotv = wrot_d[l].rearrange("(kc p) n -> p kc n", p=128)
            blocks = [
                [(winv, 0, 512)],
                [(winv, 512, 512)],
                [(winv, C_Z, 512)],
                [(winv, C_DT, 16), (winv, C_GV, 128), (winv, C_DV, 256)],
                [(winv, C_GG, 256), (winv, C_DG, 256)],
                [(winv, C_GQ, 384)],
                [(wrotv, R_GQ, 384)],
                [(winv, C_DQ, 512)],
                [(wrotv, R_DQ, 512)],
            ]

            def load_block(i):
                slot = wr[i % NS]
                o = 0
                for (v, c0, n) in blocks[i]:
                    src = win_d if v is winv else wrot_d
                    tk.dma("pool", slot[:, :, o:o + n], v[:, :, c0:c0 + n], reads=[src.b], writes=[slot.b])
                    o += n
                return slot


            return load_block

        TB = [(0, 256)] + [(256 + 512 * i, 512) for i in range(4)]

        def phase_I(l, uT, wr, slots):
            last = (l == DEPTH - 1)
            with ExitStack() as ph:
                NS = 4
                cs = [sb(ph, f"cs{i}", [128, 2312]) for i in range(2)]
                accs = [sb(ph, f"acc{i}", [128, NT]) for i in range(2)]
                xb = [sb(ph, f"xb{i}", [128, NT], BF16) for i in range(2)]
                xtoks = [sb(ph, f"xtok{i}", [128, NTILE, 128], BF16) for i in range(2)]
                xti = [0]
                rg = sb(ph, "rg", [128, 2, SEQ])
                rd = sb(ph, "rd", [128, 2, SEQ])
                zb = [sb(ph, f"zb{i}", [128, 512], BF16) for i in range(2)]
                vb = [sb(ph, f"vb{i}", [128, 384], BF16) for i in range(2)]
                dts = sb(ph, "dts", [128, NTILE, 16])
                sqs = sb(ph, "sqs", [128, 512])
                rs = sb(ph, "rs", [128, 512])
                t1 = sb(ph, "t1", [128, 512])
                t2 = sb(ph, "t2", [128, 512])
                cw = sb(ph, "cw", [128, 8, 5])
                cb = sb(ph, "cb", [128, 8])
                qkg = sb(ph, "qkg", [128, 4])
                pf = [ps(ph, f"pf{i}", [128, 512]) for i in range(4)]
                pm = ps(ph, "pm", [128, 512])
                ptr = ps(ph, "ptr", [128, 4, 128], BF16)
                ptm = [ps(ph, f"ptm{i}", [128, 512]) for i in range(2)]
                pfi = [0]

                def nextpf():
                    p = pf[pfi[0] % 4]
                    pfi[0] += 1
                    return p

                for hf in range(2):
                    tk.dma("sp", rg[64 * hf:64 * hf + 64], ropeg_d[:].rearrange("a p t -> p a t"), reads=[ropeg_d.b], writes=[rg.b])
                tk.dma("sp", rd[:], roped_d[:].rearrange("a p t -> p a t"), reads=[roped_d.b], writes=[rd.b])
                tk.dma("sp", cw[:], convw_d[l], reads=[convw_d.b], writes=[cw.b])
                tk.dma("sp", cb[:], convb_d[l], reads=[convb_d.b], writes=[cb.b])
                for hf in range(2):
                    tk.dma("sp", qkg[64 * hf:64 * hf + 64], qkg_d[l], reads=[qkg_d.b], writes=[qkg.b])
                for c in cs:
                    tk.op("pool", lambda e, c=c: e.memset(c[:], 0.0), writes=[c.b])
                tk.op("pool", lambda e: e.memset(sqs[:], 0.0), writes=[sqs.b])

                load_block = make_I_loader(l, wr)

                def prefetch(i):
                    slots[i] = load_block(i)

                def fm_matmuls(psb, w_, off, m, t0, n):
                    for kc in range(8):
                        tk.op("pe", lambda e, kc=kc: e.matmul(psb[0:m, 0:n], lhsT=w_[:, kc, off:off + m],
                                                               rhs=uT[:, kc, t0:t0 + n], start=(kc == 0), stop=(kc == 7)),
                              reads=[w_.b, uT.b], writes=[psb.b], sig=(kc == 7))

                def transposes_to(xbt, dst_d, c0):
                    xtok = xtoks[xti[0] % 2]
                    xti[0] += 1
                    for g0 in range(0, NTILE, 4):
                        gn = min(4, NTILE - g0)
                        for i in range(gn):
                            tt = g0 + i
                            tk.op("pe", lambda e, i=i, tt=tt: e.transpose(out=ptr[:, i, :], in_=xbt[:, tt * 128:(tt + 1) * 128],
                                                                          identity=identb[:]),
                                  reads=[xbt.b, identb.b], writes=[ptr.b], sig=(i == gn - 1))
                        tk.op("act", lambda e, g0=g0, gn=gn: e.activation(out=xtok[:, g0:g0 + gn, :], in_=ptr[:, 0:gn, :], func=AF.Copy),
                              reads=[ptr.b], writes=[xtok.b])
                    tk.dma("sp", dst_d[:].rearrange("(tt p) f -> p tt f", p=128)[:, :, c0:c0 + 128], xtok[:],
                           reads=[xtok.b], writes=[dst_d.b])

                def A_P(j):
                    if j == 0:
                        prefetch(3)
                    if j == 4:
                        prefetch(4)
                    w_ = slots[j // 4]
                    c_ = cs[j % 2]
                    for (t0, n) in TB:
                        p = nextpf()
                        fm_matmuls(p, w_, (j % 4) * 128, 128, t0, n)
                        d0 = 2 + t0 if t0 < 256 else 262 + (t0 - 256)
                        tk.op("act", lambda e, p=p, d0=d0, n=n, c_=c_: e.activation(out=c_[:, d0:d0 + n], in_=p[:, 0:n], func=AF.Copy),
                              reads=[p.b], writes=[c_.b])

                def A_C(j):
                    c_ = cs[j % 2]
                    acc = accs[j % 2]
                    for (eng, s0, a0, n) in (("dve", 2, 0, 256), ("dve", 262, 256, 2048)):
                        tk.op(eng, lambda e, s0=s0, a0=a0, n=n: e.tensor_scalar(
                            out=acc[:, a0:a0 + n], in0=c_[:, s0 - 2:s0 - 2 + n], scalar1=cw[:, j, 0:1], scalar2=cb[:, j:j + 1],
                            op0=ALU.mult, op1=ALU.add), reads=[c_.b, cw.b, cb.b], writes=[acc.b])
                        for k in range(1, 5):
                            tk.op(eng, lambda e, s0=s0, a0=a0, n=n, k=k: e.scalar_tensor_tensor(
                                out=acc[:, a0:a0 + n], in0=c_[:, s0 - 2 + k:s0 - 2 + k + n], scalar=cw[:, j, k:k + 1],
                                in1=acc[:, a0:a0 + n], op0=ALU.mult, op1=ALU.add), reads=[c_.b, cw.b, acc.b], writes=[acc.b])

                def A_F(j):
                    acc = accs[j % 2]
                    xbt = xb[j % 2]
                    tk.op("act", lambda e: e.activation(out=xbt[:], in_=acc[:], func=AF.Silu), reads=[acc.b], writes=[xbt.b])
                    if j < 4:
                        transposes_to(xbt, xs_s, j * 128)
                    elif j < 6:
                        tk.dma("sp", bT_s[(j - 4) * 128:(j - 3) * 128, :], xbt[:], reads=[xbt.b], writes=[bT_s.b])
                        transposes_to(xbt, bt_s, (j - 4) * 128)
                    else:
                        tk.dma("sp", cT_s[(j - 6) * 128:(j - 5) * 128, :], xbt[:], reads=[xbt.b], writes=[cT_s.b])

                A_P(0)
                A_P(1)
                A_C(0)
                for j in range(8):
                    A_F(j)
                    if j + 2 < 8:
                        A_P(j + 2)
                    if j + 1 < 8:
                        A_C(j + 1)

                prefetch(5)
                wz = slots[2]
                wt = slots[3]
                for tt in range(NTILE):
                    k = tt % 2
                    for (w_, n, p) in ((wz, 512, ptm[0]), (wt, 400, ptm[1])):
                        for kc in range(8):
                            tk.op("pe", lambda e, kc=kc, w_=w_, n=n, p=p, tt=tt: e.matmul(
                                p[:, 0:n], lhsT=uT[:, kc, tt * 128:(tt + 1) * 128], rhs=w_[:, kc, 0:n],
                                start=(kc == 0), stop=(kc == 7)), reads=[w_.b, uT.b], writes=[p.b], sig=(kc == 7))
                    tk.op("act", lambda e, k=k: e.activation(out=zb[k][:], in_=ptm[0][:], func=AF.Silu),
                          reads=[ptm[0].b], writes=[zb[k].b])
                    tk.dma("sp", zs_s[tt * 128:(tt + 1) * 128, :], zb[k][:], reads=[zb[k].b], writes=[zs_s.b])
                    tk.op("dve", lambda e, tt=tt: e.tensor_copy(out=dts[:, tt, :], in_=ptm[1][:, 0:16]),
                          reads=[ptm[1].b], writes=[dts.b])
                    tk.op("dve", lambda e, k=k: e.tensor_copy(out=vb[k][:], in_=ptm[1][:, 16:400]),
                          reads=[ptm[1].b], writes=[vb[k].b])
                    tk.dma("sp", gv_s[tt * 128:(tt + 1) * 128, :, 64:128], vb[k][:, 0:128].rearrange("p (g d) -> p g d", g=2),
                           reads=[vb[k].b], writes=[gv_s.b])
                    tk.dma("sp", dv_s[tt * 128:(tt + 1) * 128, :, 64:128], vb[k][:, 128:384].rearrange("p (g d) -> p g d", g=4),
                           reads=[vb[k].b], writes=[dv_s.b])
                tk.dma("sp", dt_s[:].rearrange("(tt p) f -> p tt f", p=128), dts[:], reads=[dts.b], writes=[dt_s.b])

                prefetch(6)
                prefetch(7)
                wg = slots[4]
                for j in range(4):
                    xbt = xb[j % 2]
                    for (t0, n) in TB:
                        if last and t0 < 256:
                            continue
                        p = nextpf()
                        fm_matmuls(p, wg, j * 128, 128, t0, n)
                        tk.op("act", lambda e, p=p, t0=t0, n=n, xbt=xbt: e.activation(out=xbt[:, t0:t0 + n], in_=p[:, 0:n], func=AF.Silu),
                              reads=[p.b], writes=[xbt.b])
                    dst = ggT_s if j < 2 else dgT_s
                    tk.dma("sp", dst[(j % 2) * 128:(j % 2 + 1) * 128, :], xbt[:], reads=[xbt.b], writes=[dst.b])

                prefetch(8)
                wq = slots[5]
                wqr = slots[6]
                for pp_ in range(3):
                    gcol = 0 if pp_ < 2 else 2
                    xbt = xb[pp_ % 2]
                    for (t0, n) in TB:
                        if last and t0 < 256 and pp_ < 2:
                            continue
                        pa = nextpf()
                        fm_matmuls(pa, wq, pp_ * 128, 128, t0, n)
                        lat = t0 >= 256
                        if lat:
                            pb = nextpf()
                            fm_matmuls(pb, wqr, pp_ * 128, 128, t0, n)
                        tk.op("act", lambda e, pa=pa, n=n: e.activation(out=sqs[:, 0:n], in_=pa[:, 0:n], func=AF.Square),
                              reads=[pa.b], writes=[sqs.b])
                        tk.op("pe", lambda e, n=n: e.matmul(pm[:, 0:n], lhsT=CM(K_BD64), rhs=sqs[:, 0:n],
                                                            start=True, stop=True), reads=[cm.b, sqs.b], writes=[pm.b])
                        tk.op("act", lambda e, n=n: e.activation(out=rs[:, 0:n], in_=pm[:, 0:n], func=AF.Ln, bias=epsc[:, 0:1]),
                              reads=[pm.b, epsc.b], writes=[rs.b])
                        tk.op("act", lambda e, n=n: e.activation(out=rs[:, 0:n], in_=rs[:, 0:n], func=AF.Exp, scale=-0.5),
                              reads=[rs.b], writes=[rs.b])
                        if lat:
                            r0 = t0 - 256
                            tk.op("dve", lambda e, pa=pa, n=n, gcol=gcol: e.scalar_tensor_tensor(
                                out=t1[:, 0:n], in0=pa[:, 0:n], scalar=qkg[:, gcol:gcol + 1], in1=rs[:, 0:n],
                                op0=ALU.mult, op1=ALU.mult), reads=[pa.b, qkg.b, rs.b], writes=[t1.b])
                            tk.op("dve", lambda e, pb=pb, n=n, gcol=gcol: e.scalar_tensor_tensor(
                                out=t2[:, 0:n], in0=pb[:, 0:n], scalar=qkg[:, gcol + 1:gcol + 2], in1=rs[:, 0:n],
                                op0=ALU.mult, op1=ALU.mult), reads=[pb.b, qkg.b, rs.b], writes=[t2.b])
                            tk.op("pool", lambda e, n=n, r0=r0: e.tensor_tensor(out=t1[:, 0:n], in0=t1[:, 0:n],
                                                                                in1=rg[:, 0, r0:r0 + n], op=ALU.mult),
                                  reads=[t1.b, rg.b], writes=[t1.b])
                            tk.op("dve", lambda e, n=n, r0=r0: e.tensor_tensor(out=t2[:, 0:n], in0=t2[:, 0:n],
                                                                               in1=rg[:, 1, r0:r0 + n], op=ALU.mult),
                                  reads=[t2.b, rg.b], writes=[t2.b])
                            tk.op("dve", lambda e, n=n, t0=t0, xbt=xbt: e.tensor_tensor(out=xbt[:, t0:t0 + n], in0=t1[:, 0:n],
                                                                                        in1=t2[:, 0:n], op=ALU.add),
                                  reads=[t1.b, t2.b], writes=[xbt.b])
                        else:
                            tk.op("dve", lambda e, pa=pa, n=n, gcol=gcol, t0=t0, xbt=xbt: e.scalar_tensor_tensor(
                                out=xbt[:, t0:t0 + n], in0=pa[:, 0:n], scalar=qkg[:, gcol:gcol + 1], in1=rs[:, 0:n],
                                op0=ALU.mult, op1=ALU.mult), reads=[pa.b, qkg.b, rs.b], writes=[xbt.b])
                    if pp_ < 2:
                        dst = gqT_s[2 * pp_:2 * pp_ + 2].rearrange("h d t -> (h d) t")
                        dstb = gqT_s.b
                    else:
                        dst = gkT_s[:].rearrange("h d t -> (h d) t")
                        dstb = gkT_s.b
                    tk.dma("sp", dst, xbt[:], reads=[xbt.b], writes=[dstb])

                wd = slots[7]
                wdr = slots[8]
                for j in range(4):
                    xbt = xb[j % 2]
                    for (t0, n) in TB:
                        if last and t0 < 256 and j < 2:
                            continue
                        pa = nextpf()
                        fm_matmuls(pa, wd, j * 128, 128, t0, n)
                        if t0 >= 256:
                            pb = nextpf()
                            fm_matmuls(pb, wdr, j * 128, 128, t0, n)
                            r0 = t0 - 256
                            tk.op("dve", lambda e, pa=pa, n=n, r0=r0: e.tensor_tensor(out=t1[:, 0:n], in0=pa[:, 0:n],
                                                                                      in1=rd[:, 0, r0:r0 + n], op=ALU.mult),
                                  reads=[pa.b, rd.b], writes=[t1.b])
                            tk.op("dve", lambda e, pb=pb, n=n, r0=r0: e.tensor_tensor(out=t2[:, 0:n], in0=pb[:, 0:n],
                                                                                      in1=rd[:, 1, r0:r0 + n], op=ALU.mult),
                                  reads=[pb.b, rd.b], writes=[t2.b])
                            tk.op("pool", lambda e, n=n, t0=t0, xbt=xbt: e.tensor_tensor(out=xbt[:, t0:t0 + n], in0=t1[:, 0:n],
                                                                                         in1=t2[:, 0:n], op=ALU.add),
                                  reads=[t1.b, t2.b], writes=[xbt.b])
                        else:
                            tk.op("act", lambda e, pa=pa, n=n, t0=t0, xbt=xbt: e.activation(out=xbt[:, t0:t0 + n], in_=pa[:, 0:n], func=AF.Copy),
                                  reads=[pa.b], writes=[xbt.b])
                    dst = dqT_s if j < 2 else dkT_s
                    tk.dma("sp", dst[(j % 2) * 128:(j % 2 + 1) * 128, :], xbt[:], reads=[xbt.b], writes=[dst.b])
                tk.barrier()

        def phase_S(l, skip_ctx=False):
            with ExitStack() as ph:
                xs = sb(ph, "xs", [128, NTILE, 512], BF16)
                btok = sb(ph, "btok", [128, NTILE, 256], BF16)
                bT = sb(ph, "bT", [128, 2, NT], BF16)
                cT = sb(ph, "cT", [128, 2, NT], BF16)
                zs = sb(ph, "zs", [128, NTILE, 512], BF16)
                dtr = sb(ph, "dtr", [128, NTILE, 16])
                alog = sb(ph, "alog", [128, 2, 8])
                dtb = sb(ph, "dtbb", [128, 2, 8])
                dsk = sb(ph, "dsk", [128, 8])
                ng = sb(ph, "ng", [128, 512])
                Abc = sb(ph, "Abc", [128, 2, 8])
                one = sb(ph, "one", [128, 1])
                SH4 = [128, 2, NTILE, 8]
                tmp = sb(ph, "s_tmp", SH4)
                dtv = sb(ph, "dtv", SH4)
                dtA = sb(ph, "dtA", SH4)
                negcum = sb(ph, "negcum", SH4)
                ecum = sb(ph, "ecum", SH4)
                etot = sb(ph, "etot", SH4)
                dec = sb(ph, "dec", SH4)
                wdec = sb(ph, "wdec", SH4)
                lnb = sb(ph, "lnb", SH4)
                IDd = sb(ph, "IDd", [128, 8, 128], BF16)
                sinb = sb(ph, "sinb", [128, NTILE, 512], BF16)
                ycs = sb(ph, "ycs", [128, 4, NT], BF16)
                fl = lambda t: t[:].rearrange("p a b c -> p (a b c)")

                tk.dma("sp", dtr[:], dt_s[:].rearrange("(c p) f -> p c f", p=128), reads=[dt_s.b], writes=[dtr.b])
                tk.dma("sp", alog[:].rearrange("p a b -> p (a b)"), alog_d[l].partition_broadcast(128), reads=[alog_d.b], writes=[alog.b])
                tk.dma("sp", dtb[:].rearrange("p a b -> p (a b)"), dtb_d[l].partition_broadcast(128), reads=[dtb_d.b], writes=[dtb.b])
                tk.dma("sp", dsk[:], dskip_d[l].partition_broadcast(128), reads=[dskip_d.b], writes=[dsk.b])
                tk.dma("sp", xs[:], xs_s[:].rearrange("(c p) f -> p c f", p=128), reads=[xs_s.b], writes=[xs.b])
                tk.dma("sp", btok[:], bt_s[:].rearrange("(c p) f -> p c f", p=128), reads=[bt_s.b], writes=[btok.b])
                tk.dma("sp", bT[:], bT_s[:].rearrange("(g p) t -> p g t", p=128), reads=[bT_s.b], writes=[bT.b])
                tk.dma("sp", cT[:], cT_s[:].rearrange("(g p) t -> p g t", p=128), reads=[cT_s.b], writes=[cT.b])
                tk.dma("sp", zs[:], zs_s[:].rearrange("(c p) f -> p c f", p=128), reads=[zs_s.b], writes=[zs.b])
                tk.dma("sp", ng[:], ssdg_d[l].partition_broadcast(128), reads=[ssdg_d.b], writes=[ng.b])
                tk.op("pool", lambda e: e.memset(one[:], 1.0), writes=[one.b])

                if S_LEVEL[0] == -1:
                    tk.barrier()
                    return
                with ExitStack() as s1:
                    pcum = ps(s1, "pcum", [128, 2, NTILE * 8])
                    ptot = ps(s1, "ptot", [128, 2 * NTILE * 8])
                    tk.op("act", lambda e: e.activation(out=Abc[:], in_=alog[:], func=AF.Exp), reads=[alog.b], writes=[Abc.b])
                    tk.op("dve", lambda e: e.tensor_scalar(out=Abc[:], in0=Abc[:], scalar1=-1.0, scalar2=None, op0=ALU.mult),
                          reads=[Abc.b], writes=[Abc.b])
                    for d in range(2):
                        tk.op("dve", lambda e, d=d: e.tensor_tensor(out=tmp[:, d], in0=dtr[:, :, d * 8:(d + 1) * 8],
                                                                    in1=bc(dtb[:, d:d + 1, :], [128, NTILE, 8]), op=ALU.add),
                              reads=[dtr.b, dtb.b], writes=[tmp.b])
                    tk.op("act", lambda e: e.activation(out=fl(tmp), in_=fl(tmp), func=AF.Exp), reads=[tmp.b], writes=[tmp.b])
                    tk.op("act", lambda e: e.activation(out=fl(dtv), in_=fl(tmp), func=AF.Ln, bias=one[:, 0:1]),
                          reads=[tmp.b, one.b], writes=[dtv.b])
                    tk.op("act", lambda e: e.activation(out=fl(lnb), in_=fl(dtv), func=AF.Ln), reads=[dtv.b], writes=[lnb.b])
                    for h in range(8):
                        tk.op("dve", lambda e, h=h: e.tensor_scalar(out=IDd[:, h, :], in0=identb[:], scalar1=dsk[:, h:h + 1], scalar2=None,
                                                                    op0=ALU.mult), reads=[identb.b, dsk.b], writes=[IDd.b])
                    for d in range(2):
                        tk.op("dve", lambda e, d=d: e.tensor_tensor(out=dtA[:, d], in0=dtv[:, d],
                                                                    in1=bc(Abc[:, d:d + 1, :], [128, NTILE, 8]), op=ALU.mult),
                              reads=[dtv.b, Abc.b], writes=[dtA.b])
                    if S_LEVEL[0] == -2:
                        tk.barrier()
                        return
                    tk.op("pe", lambda e: e.matmul(pcum[:, 0, :], lhsT=CM(K_UINC), rhs=dtA[:, 0].rearrange("p b c -> p (b c)"),
                                                   start=True, stop=True), reads=[cm.b, dtA.b], writes=[pcum.b])
                    tk.op("pe", lambda e: e.matmul(pcum[:, 1, :], lhsT=CM(K_LINC), rhs=dtA[:, 1].rearrange("p b c -> p (b c)"),
                                                   start=True, stop=True), reads=[cm.b, dtA.b], writes=[pcum.b])
                    tk.op("pe", lambda e: e.matmul(ptot[:, :], lhsT=CM(K_ONES), rhs=fl(dtA), start=True, stop=True),
                          reads=[cm.b, dtA.b], writes=[ptot.b])
                    if S_LEVEL[0] == -3:
                        tk.barrier()
                        return
                    pcf = pcum[:].rearrange("p a n -> p (a n)")
                    tk.op("dve", lambda e: e.tensor_scalar(out=fl(negcum), in0=pcf, scalar1=-1.0, scalar2=None, op0=ALU.mult),
                          reads=[pcum.b], writes=[negcum.b])
                    tk.op("dve", lambda e: e.tensor_copy(out=fl(wdec), in_=ptot[:, :]), reads=[ptot.b], writes=[wdec.b])
                    tk.op("dve", lambda e: e.tensor_tensor(out=fl(lnb), in0=fl(lnb), in1=fl(negcum), op=ALU.add),
                          reads=[lnb.b, negcum.b], writes=[lnb.b])
                    if S_LEVEL[0] == -4:
                        tk.barrier()
                        return
                    tk.op("act", lambda e: e.activation(out=fl(ecum), in_=fl(negcum), func=AF.Exp, scale=-1.0),
                          reads=[negcum.b], writes=[ecum.b])
                    if S_LEVEL[0] == -5:
                        tk.barrier()
                        return
                    tk.op("act", lambda e: e.activation(out=fl(etot), in_=fl(wdec), func=AF.Exp), reads=[wdec.b], writes=[etot.b])
                    tk.op("dve", lambda e: e.tensor_tensor(out=fl(tmp), in0=fl(wdec), in1=fl(negcum), op=ALU.add),
                          reads=[wdec.b, negcum.b], writes=[tmp.b])
                    if S_LEVEL[0] == -6:
                        tk.barrier()
                        return
                    tk.op("act", lambda e: e.activation(out=fl(dec), in_=fl(tmp), func=AF.Exp), reads=[tmp.b], writes=[dec.b])
                    tk.op("dve", lambda e: e.tensor_tensor(out=fl(wdec), in0=fl(dtv), in1=fl(dec), op=ALU.mult),
                          reads=[dtv.b, dec.b], writes=[wdec.b])
                    tk.barrier()

                with ExitStack() as s2:
                    S = [sb(s2, f"Sst{i}", [128, 512]) for i in range(2)]
                    Sfb = sb(s2, "Sfb", [128, 512], BF16)
                    stmp = sb(s2, "stmp", [128, 512])
                    xdd = sb(s2, "xdd", [128, 512], BF16)
                    LTD = [[sb(s2, f"LT{i}g{q}", [128, 4, 128], BF16) for q in range(4)] for i in range(2)]
                    MTD = [sb(s2, f"MT{i}", [128, 16, 128], BF16) for i in range(4)]
                    xddD = [sb(s2, f"xddD{i}", [128, 512], BF16) for i in range(4)]
                    ta = sb(s2, "ta", [128, 512])
                    tb_ = sb(s2, "tb", [128, 512])
                    yz = sb(s2, "yz", [128, 512])
                    yns = [sb(s2, f"yn{i}", [128, 512], BF16) for i in range(2)]
                    junk = sb(s2, "junkS", [128, 512], BF16)
                    ssq = sb(s2, "ssqS", [128, 1])
                    rstd = sb(s2, "rstdS", [128, 1])
                    pL = [ps(s2, f"pL{i}", [128, 4, 128]) for i in range(2)]
                    pcb = ps(s2, "pcb", [128, 2, 128])
                    cbs = [sb(s2, f"cbs{i}", [128, 2, 128]) for i in range(2)]
                    py = ps(s2, "py", [128, 512])
                    pyo = [ps(s2, f"pyo{i}", [128, 512]) for i in range(2)]
                    pst = ps(s2, "pst", [128, 512])
                    pT = ps(s2, "pTS", [128, 4, 128], BF16)
                    v3 = lambda ap: ap.rearrange("p (h d) -> p h d", h=8)

                    def scaled_x(dst, c, w4, d):
                        tk.op("dve", lambda e: e.tensor_tensor(out=v3(dst[:]), in0=v3(xs[:, c, :]),
                                                               in1=bc(w4[:, d, c, :].unsqueeze(2), [128, 8, 64]), op=ALU.mult),
                              reads=[xs.b, w4.b], writes=[dst.b])

                    def state_step(c, d, xsrc):
                        for g in range(2):
                            tk.op("pe", lambda e, g=g: e.matmul(pst[:, g * 256:(g + 1) * 256], lhsT=btok[:, c, g * 128:(g + 1) * 128],
                                                                rhs=xsrc[:, g * 256:(g + 1) * 256], start=True, stop=True),
                                  reads=[btok.b, xsrc.b], writes=[pst.b], sig=(g == 1))
                        tk.op("dve", lambda e: e.tensor_tensor(out=v3(stmp[:]), in0=v3(S[d][:]),
                                                               in1=bc(etot[:, d, c, :].unsqueeze(2), [128, 8, 64]), op=ALU.mult),
                              reads=[S[d].b, etot.b], writes=[stmp.b])
                        tk.op("dve", lambda e: e.tensor_tensor(out=S[d][:], in0=pst[:, :], in1=stmp[:], op=ALU.add),
                              reads=[pst.b, stmp.b], writes=[S[d].b])

                    border = [1, 0] + list(range(NTILE - 1, 1, -1))
                    if S_LEVEL[0] == 1:
                        tk.barrier()
                        return
                    tk.op("pool", lambda e: e.memset(S[1][:], 0.0), writes=[S[1].b])
                    tk.op("pool", lambda e: e.memset(S[0][:], 0.0), writes=[S[0].b])
                    tk.op("pool", lambda e: e.memset(Sfb[:], 0.0), writes=[Sfb.b])
                    for i, c in enumerate(border):
                        tk.op("act", lambda e, c=c: e.activation(out=sinb[:, c, :], in_=S[1][:], func=AF.Copy),
                              reads=[S[1].b], writes=[sinb.b])
                        if i == len(border) - 1:
                            break
                        scaled_x(xdd, c, wdec, 1)
                        state_step(c, 1, xdd)

                    def stageA(c):
                        t0 = c * 128
                        k = c % 2
                        k3 = c % 4
                        LT, MT, cb_ = LTD[k], MTD[k3], cbs[k]

                        def pre():
                            scaled_x(xddD[k3], c, wdec, 0)
                            for g in range(2):
                                tk.op("pe", lambda e, g=g: e.matmul(pcb[:, g, :], lhsT=bT[:, g, t0:t0 + 128], rhs=cT[:, g, t0:t0 + 128],
                                                                    start=True, stop=True), reads=[bT.b, cT.b], writes=[pcb.b], sig=(g == 1))
                            tk.op("dve", lambda e: e.tensor_copy(out=cb_[:], in_=pcb[:]), reads=[pcb.b], writes=[cb_.b])

                        def grp(q4):
                            p = pL[q4 % 2]
                            d = q4 // 2
                            for i in range(4):
                                hd = q4 * 4 + i
                                lh = bc(dtA[:, d, c, hd % 8:hd % 8 + 1], [128, 128])
                                tk.op("pe", lambda e, p=p, i=i, lh=lh, d=d: e.matmul(
                                    p[:, i, :], lhsT=lh, rhs=CM(K_UINC if d == 0 else K_LINC), start=True, stop=False),
                                    reads=[dtA.b, cm.b], writes=[p.b], sig=False)
                                tk.op("pe", lambda e, p=p, i=i, d=d: e.matmul(
                                    p[:, i, :], lhsT=CM(K_ID), rhs=CM(K_NEGF if d == 0 else K_NEGB), start=False, stop=True),
                                    reads=[cm.b], writes=[p.b], sig=(i == 3))
                            for i in range(4):
                                hd = q4 * 4 + i
                                tk.op("act", lambda e, p=p, i=i, hd=hd, d=d: e.activation(
                                    out=LT[q4][:, i, :], in_=p[:, i, :], func=AF.Exp, bias=lnb[:, d, c, hd % 8:hd % 8 + 1]),
                                    reads=[p.b, lnb.b], writes=[LT[q4].b])
                            g = q4 % 2
                            tk.op("dve", lambda e, q4=q4, g=g: e.tensor_tensor(out=MT[:, q4 * 4:(q4 + 1) * 4, :], in0=LT[q4][:],
                                                                               in1=bc(cb_[:, g:g + 1, :], [128, 4, 128]), op=ALU.mult),
                                  reads=[LT[q4].b, cb_.b], writes=[MT.b])
                        return [pre] + [(lambda q4=q4: grp(q4)) for q4 in range(4)]

                    def stageB(c):
                        t0 = c * 128
                        k = c % 4
                        MT, xdd3 = MTD[k], xddD[k]
                        yn = yns[c % 2]

                        def s1():
                            for h in range(8):
                                xh = xs[:, c, h * 64:(h + 1) * 64]
                                tk.op("pe", lambda e, h=h, xh=xh: e.matmul(py[:, h * 64:(h + 1) * 64], lhsT=IDd[:, h, :], rhs=xh,
                                                                           start=True, stop=False), reads=[IDd.b, xs.b], writes=[py.b], sig=False)
                                tk.op("pe", lambda e, h=h, xh=xh: e.matmul(py[:, h * 64:(h + 1) * 64], lhsT=MT[:, h, :], rhs=xh,
                                                                           start=False, stop=False), reads=[MT.b, xs.b], writes=[py.b], sig=False)
                                tk.op("pe", lambda e, h=h, xh=xh: e.matmul(py[:, h * 64:(h + 1) * 64], lhsT=MT[:, 8 + h, :], rhs=xh,
                                                                           start=False, stop=True), reads=[MT.b, xs.b], writes=[py.b], sig=(h == 7))
                            for g in range(2):
                                tk.op("pe", lambda e, g=g: e.matmul(pyo[0][:, g * 256:(g + 1) * 256], lhsT=cT[:, g, t0:t0 + 128],
                                                                    rhs=Sfb[:, g * 256:(g + 1) * 256], start=True, stop=True),
                                      reads=[cT.b, Sfb.b], writes=[pyo[0].b], sig=(g == 1))
                            for g in range(2):
                                tk.op("pe", lambda e, g=g: e.matmul(pyo[1][:, g * 256:(g + 1) * 256], lhsT=cT[:, g, t0:t0 + 128],
                                                                    rhs=sinb[:, c, g * 256:(g + 1) * 256], start=True, stop=True),
                                      reads=[cT.b, sinb.b], writes=[pyo[1].b], sig=(g == 1))

                        def s2():
                            if c < NTILE - 1:
                                state_step(c, 0, xdd3)
                                tk.op("act", lambda e: e.activation(out=Sfb[:], in_=S[0][:], func=AF.Copy), reads=[S[0].b], writes=[Sfb.b])

                        def s3():
                            tk.op("dve", lambda e: e.tensor_tensor(out=v3(ta[:]), in0=v3(pyo[0][:, :]),
                                                                   in1=bc(ecum[:, 0, c, :].unsqueeze(2), [128, 8, 64]), op=ALU.mult),
                                  reads=[pyo[0].b, ecum.b], writes=[ta.b])
                            tk.op("dve", lambda e: e.tensor_tensor(out=v3(tb_[:]), in0=v3(pyo[1][:, :]),
                                                                   in1=bc(ecum[:, 1, c, :].unsqueeze(2), [128, 8, 64]), op=ALU.mult),
                                  reads=[pyo[1].b, ecum.b], writes=[tb_.b])

                        def s4():
                            tk.op("pool", lambda e: e.tensor_tensor(out=ta[:], in0=ta[:], in1=tb_[:], op=ALU.add),
                                  reads=[ta.b, tb_.b], writes=[ta.b])

                        def s5():
                            tk.op("dve", lambda e: e.tensor_tensor(out=yz[:], in0=py[:, :], in1=ta[:], op=ALU.add),
                                  reads=[py.b, ta.b], writes=[yz.b])

                        def s6():
                            tk.op("pool", lambda e: e.tensor_tensor(out=yz[:], in0=yz[:], in1=zs[:, c, :], op=ALU.mult),
                                  reads=[yz.b, zs.b], writes=[yz.b])
                            tk.op("pool", lambda e: e.memset(ssq[:], 0.0), writes=[ssq.b])

                        def s7():
                            tk.op("act", lambda e: e.activation(out=junk[:], in_=yz[:], func=AF.Square, accum_out=ssq[:, 0:1]),
                                  reads=[yz.b], writes=[junk.b, ssq.b])
                            tk.op("act", lambda e: e.activation(out=rstd[:], in_=ssq[:], func=AF.Ln, scale=1.0 / 512, bias=epsc[:, 0:1]),
                                  reads=[ssq.b, epsc.b], writes=[rstd.b])
                            tk.op("act", lambda e: e.activation(out=rstd[:], in_=rstd[:], func=AF.Exp, scale=-0.5),
                                  reads=[rstd.b], writes=[rstd.b])

                        def s8():
                            tk.op("dve", lambda e: e.scalar_tensor_tensor(out=yn[:], in0=yz[:], scalar=rstd[:, 0:1], in1=ng[:],
                                                                          op0=ALU.mult, op1=ALU.mult),
                                  reads=[yz.b, rstd.b, ng.b], writes=[yn.b])
                        return [s1, s2, s3, s4, s5, s6, s7, s8]

                    def stageB2(c):
                        t0 = c * 128
                        yn = yns[c % 2]
                        for j in range(4):
                            tk.op("pe", lambda e, j=j: e.transpose(out=pT[:, j, :], in_=yn[:, j * 128:(j + 1) * 128], identity=identb[:]),
                                  reads=[yn.b, identb.b], writes=[pT.b], sig=(j == 3))
                        tk.op("act", lambda e: e.activation(out=ycs[:, :, t0:t0 + 128], in_=pT[:], func=AF.Copy),
                              reads=[pT.b], writes=[ycs.b])

                    lite = (lambda c: skip_ctx and c < 2)
                    for c0 in range(3):
                        if lite(c0):
                            scaled_x(xddD[c0 % 4], c0, wdec, 0)
                            continue
                        for f in stageA(c0):
                            f()
                    for c in range(NTILE):
                        A = stageA(c + 3) if c + 3 < NTILE else [lambda: None] * 5
                        B = stageB(c)
                        if lite(c):
                            for f in (A[0], A[1], B[1], A[2], A[3], A[4]):
                                f()
                        else:
                            for f in (A[0], B[0], A[1], B[1], B[2], A[2], B[3], B[4], A[3], B[5], B[6], A[4], B[7]):
                                f()
                        if c >= 1 and not lite(c - 1):
                            stageB2(c - 1)
                    stageB2(NTILE - 1)
                    tk.dma("sp", ycT_s[0:512, :].rearrange("(j p) t -> p j t", p=128), ycs[:], reads=[ycs.b], writes=[ycT_s.b])
                    tk.barrier()

        def attn_block(units, q0, nq, ktiles, pO, pS, PT, scale, stages=()):
            seq = [(kt, u) for kt in ktiles for u in range(len(units))]
            LAG = 2
            stages = list(stages)
            for i in range(len(seq) + LAG):
                if stages and i >= 3 and (i - 3) % 4 == 0:
                    stages.pop(0)()
                if i < len(seq):
                    kt, u = seq[i]
                    U = units[u]
                    sl = i % 3
                    tk.op("pe", lambda e, U=U, kt=kt, sl=sl: e.matmul(pS[sl][:, 0:nq], lhsT=U["k"][1](kt), rhs=U["q"][1](q0, nq),
                                                                      start=True, stop=True),
                          reads=[U["k"][0].b, U["q"][0].b], writes=[pS[sl].b])
                    tk.op("act", lambda e, sl=sl: e.activation(out=PT[sl][:, 0:nq], in_=pS[sl][:, 0:nq], func=AF.Exp, scale=scale),
                          reads=[pS[sl].b], writes=[PT[sl].b])
                j = i - LAG
                if j >= 0:
                    kt, u = seq[j]
                    U = units[u]
                    sl = j % 3
                    acc = pO[U["acc"]]
                    tk.op("pe", lambda e, U=U, kt=kt, sl=sl, acc=acc: e.matmul(acc[:, 0:nq], lhsT=U["v"][1](kt), rhs=PT[sl][:, 0:nq],
                                                                               start=(kt == ktiles[0]), stop=(kt == ktiles[-1])),
                          reads=[U["v"][0].b, PT[sl].b], writes=[acc.b])
            for st in stages:
                st()

        def qblocks(ctx_out):
            qb = [(256 + 512 * i, 512, list(range(NTILE))) for i in range(4)]
            if ctx_out:
                qb = [(0, 256, [0, 1])] + qb
            return qb

        def load_vaug(vaug, src, nh):
            sv = src[:].rearrange("(c p) g w -> p c (g w)", p=128)
            dv = vaug[:].rearrange("p c g w -> p c (g w)")
            for c0 in range(0, NTILE, 6):
                tk.dma("sp", dv[:, c0:c0 + 6, :], sv[:, c0:c0 + 6, :], reads=[src.b], writes=[vaug.b])

        def normalize_stages(pa, pb, nq, OS, SS, pR, rec):
            def s1():
                tk.op("dve", lambda e: e.tensor_copy(out=OS[0:64, 0:nq], in_=pa[0:64, 0:nq]), reads=[pa.b], writes=[OS.b])
                tk.op("dve", lambda e: e.tensor_copy(out=OS[64:128, 0:nq], in_=pb[64:128, 0:nq]), reads=[pb.b], writes=[OS.b])
                tk.op("dve", lambda e: e.tensor_copy(out=SS[0:64, 0:nq], in_=pb[0:64, 0:nq]), reads=[pb.b], writes=[SS.b])
                tk.op("dve", lambda e: e.tensor_copy(out=SS[64:128, 0:nq], in_=pa[64:128, 0:nq]), reads=[pa.b], writes=[SS.b])

            def s2():
                tk.op("pe", lambda e: e.matmul(pR[:, 0:nq], lhsT=CM(K_SWAP), rhs=SS[:, 0:nq], start=True, stop=True),
                      reads=[cm.b, SS.b], writes=[pR.b])

            def s3():
                tk.op("dve", lambda e: e.reciprocal(out=rec[:, 0:nq], in_=pR[:, 0:nq]), reads=[pR.b], writes=[rec.b])
                tk.op("dve", lambda e: e.tensor_tensor(out=OS[:, 0:nq], in0=OS[:, 0:nq], in1=rec[:, 0:nq], op=ALU.mult),
                      reads=[OS.b, rec.b], writes=[OS.b])
            return [s1, s2, s3]

        def phase_G(l, ctx_out, pS, pO, pR, PT, OS, SS, rec, yaccs):
            with ExitStack() as ph:
                qT = sb(ph, "qm", [128, 4, NT], BF16)
                kT = sb(ph, "kdup", [128, 2, NT], BF16)
                vaug = sb(ph, "vaugG", [128, NTILE, 2, 192], BF16)
                gg = sb(ph, "gg", [128, 2, NT], BF16)
                qz = qT[:].rearrange("p h t -> p (h t)").bitcast(F32)
                tk.op("dve", lambda e: e.memset(qz, 0.0), writes=[qT.b])
                for hf in range(2):
                    tk.dma("sp", kT[64 * hf:64 * hf + 64, :, :], gkT_s[:].rearrange("h d t -> d h t"), reads=[gkT_s.b], writes=[kT.b])
                load_vaug(vaug, gv_s, 2)
                for h in range(4):
                    tk.dma("sp", qT[64 * (h % 2):64 * (h % 2) + 64, h, :], gqT_s[h], reads=[gqT_s.b], writes=[qT.b])
                tk.dma("sp", gg[:], ggT_s[:].rearrange("(j p) t -> p j t", p=128), reads=[ggT_s.b], writes=[gg.b])
                yield
                pending = []
                nblk = 0
                for j in range(2):
                    yacc = yaccs[j]
                    if not ctx_out:
                        tk.op("pool", lambda e, yacc=yacc: e.memset(yacc[:, 0:256], 0.0), writes=[yacc.b])
                    qbs = qblocks(ctx_out)
                    for bi, (q0, nq, ktiles) in enumerate(qbs):
                        a0 = 2 * (nblk % 2)
                        nblk += 1
                        units = [
                            dict(q=(qT, lambda q0, nq, j=j: qT[:, 2 * j, q0:q0 + nq]), k=(kT, lambda kt, j=j: kT[:, j, kt * 128:(kt + 1) * 128]),
                                 v=(vaug, lambda kt, j=j: vaug[:, kt, j, 64:192]), acc=a0),
                            dict(q=(qT, lambda q0, nq, j=j: qT[:, 2 * j + 1, q0:q0 + nq]), k=(kT, lambda kt, j=j: kT[:, j, kt * 128:(kt + 1) * 128]),
                                 v=(vaug, lambda kt, j=j: vaug[:, kt, j, 0:128]), acc=a0 + 1),
                        ]
                        attn_block(units, q0, nq, ktiles, pO, pS, PT, 0.125, stages=pending)
                        pending = normalize_stages(pO[a0], pO[a0 + 1], nq, OS, SS, pR, rec)

                        def fin(q0=q0, nq=nq, j=j, yacc=yacc, last=(bi == len(qbs) - 1)):
                            tk.op("pool", lambda e: e.tensor_tensor(out=yacc[:, q0:q0 + nq], in0=OS[:, 0:nq],
                                                                    in1=gg[:, j, q0:q0 + nq], op=ALU.mult),
                                  reads=[OS.b, gg.b], writes=[yacc.b])
                            if last:
                                tk.dma("sp", ycT_s[512 + 128 * j:640 + 128 * j, :], yacc[:], reads=[yacc.b], writes=[ycT_s.b])
                        pending.append(fin)
                for st in pending:
                    st()
                yield

        def phase_D(l, ctx_out, pS, pO, pR, PT, OSp, SS, rec, yaccs):
            lam_init = 0.8 - 0.6 * float(np.exp(-0.3 * l))
            with ExitStack() as ph:
                dq = sb(ph, "dqm", [128, 8, NT], BF16)
                dk = sb(ph, "dkc", [128, 2, NT], BF16)
                vaug = sb(ph, "vaugD", [128, NTILE, 4, 192], BF16)
                dg = sb(ph, "dg", [128, 2, NT], BF16)
                OSn = sb(ph, "OSn", [128, 512])
                sq = SS
                lp = sb(ph, "lp", [128, 128])
                lpr = sb(ph, "lpr", [128, 2, 32])
                lsum = sb(ph, "lsum", [128, 2])
                neglam = sb(ph, "neglam", [128, 1])
                gsc = sb(ph, "gsc", [128, 1])
                for m in range(0, 8, 2):
                    dz = dq[:, m:m + 2, :].rearrange("p h t -> p (h t)").bitcast(F32)
                    tk.op("dve", lambda e, dz=dz: e.memset(dz, 0.0), writes=[dq.b])
                for m in range(8):
                    tk.dma("sp", dq[32 * (m % 4):32 * (m % 4) + 32, m, :], dqT_s[32 * m:32 * m + 32, :], reads=[dqT_s.b], writes=[dq.b])
                tk.dma("sp", dk[:], dkT_s[:].rearrange("(c p) t -> p c t", p=128), reads=[dkT_s.b], writes=[dk.b])
                tk.dma("sp", dg[:], dgT_s[:].rearrange("(j p) t -> p j t", p=128), reads=[dgT_s.b], writes=[dg.b])
                tk.dma("sp", lp[:], dlam_d[l].partition_broadcast(128), reads=[dlam_d.b], writes=[lp.b])
                tk.dma("sp", gsc[:], dng_d[l], reads=[dng_d.b], writes=[gsc.b])
                load_vaug(vaug, dv_s, 4)
                yield
                lp4 = lp[:].rearrange("p (a b c) -> p a b c", a=2, b=2)
                tk.op("dve", lambda e: e.tensor_tensor(out=lpr[:], in0=lp4[:, :, 0, :], in1=lp4[:, :, 1, :], op=ALU.mult),
                      reads=[lp.b], writes=[lpr.b])
                tk.op("dve", lambda e: e.reduce_sum(out=lsum[:], in_=lpr[:], axis=AX.X), reads=[lpr.b], writes=[lsum.b])
                tk.op("act", lambda e: e.activation(out=lsum[:], in_=lsum[:], func=AF.Exp), reads=[lsum.b], writes=[lsum.b])
                tk.op("dve", lambda e: e.tensor_tensor(out=neglam[:], in0=lsum[:, 1:2], in1=lsum[:, 0:1], op=ALU.subtract),
                      reads=[lsum.b], writes=[neglam.b])
                tk.op("dve", lambda e: e.tensor_scalar(out=neglam[:], in0=neglam[:], scalar1=-lam_init, scalar2=None, op0=ALU.add),
                      reads=[neglam.b], writes=[neglam.b])
                tk.op("dve", lambda e: e.tensor_scalar(out=gsc[:], in0=gsc[:], scalar1=1.0 - lam_init, scalar2=None, op0=ALU.mult),
                      reads=[gsc.b], writes=[gsc.b])
                pending = []
                nblk = 0
                for j in range(2):
                    yacc = yaccs[j]
                    if not ctx_out:
                        tk.op("pool", lambda e, yacc=yacc: e.memset(yacc[:, 0:256], 0.0), writes=[yacc.b])
                    qbs = qblocks(ctx_out)
                    for bi, (q0, nq, ktiles) in enumerate(qbs):
                        for sign in range(2):
                            a0 = 2 * (nblk % 2)
                            nblk += 1
                            mA = 4 * j + sign
                            mB = 4 * j + 2 + sign
                            units = [
                                dict(q=(dq, lambda q0, nq, m=mA: dq[:, m, q0:q0 + nq]), k=(dk, lambda kt, j=j: dk[:, j, kt * 128:(kt + 1) * 128]),
                                     v=(vaug, lambda kt, hh=2 * j: vaug[:, kt, hh, 64:192]), acc=a0),
                                dict(q=(dq, lambda q0, nq, m=mB: dq[:, m, q0:q0 + nq]), k=(dk, lambda kt, j=j: dk[:, j, kt * 128:(kt + 1) * 128]),
                                     v=(vaug, lambda kt, hh=2 * j + 1: vaug[:, kt, hh, 0:128]), acc=a0 + 1),
                            ]
                            attn_block(units, q0, nq, ktiles, pO, pS, PT, 32.0 ** -0.5, stages=pending)
                            if sign == 0:
                                pending = normalize_stages(pO[a0], pO[a0 + 1], nq, OSp, SS, pR, rec)
                                continue
                            pending = normalize_stages(pO[a0], pO[a0 + 1], nq, OSn, SS, pR, rec)

                            def c1(nq=nq):
                                tk.op("dve", lambda e: e.scalar_tensor_tensor(out=OSp[:, 0:nq], in0=OSn[:, 0:nq], scalar=neglam[:, 0:1],
                                                                              in1=OSp[:, 0:nq], op0=ALU.mult, op1=ALU.add),
                                      reads=[OSn.b, neglam.b, OSp.b], writes=[OSp.b])
                                tk.op("pool", lambda e: e.tensor_tensor(out=sq[:, 0:nq], in0=OSp[:, 0:nq], in1=OSp[:, 0:nq], op=ALU.mult),
                                      reads=[OSp.b], writes=[sq.b])

                            def c2(nq=nq):
                                tk.op("pe", lambda e: e.matmul(pR[:, 0:nq], lhsT=CM(K_BD64), rhs=sq[:, 0:nq], start=True, stop=True),
                                      reads=[cm.b, sq.b], writes=[pR.b])

                            def c3(nq=nq):
                                tk.op("act", lambda e: e.activation(out=rec[:, 0:nq], in_=pR[:, 0:nq], func=AF.Ln, bias=epsc[:, 0:1]),
                                      reads=[pR.b, epsc.b], writes=[rec.b])
                                tk.op("act", lambda e: e.activation(out=rec[:, 0:nq], in_=rec[:, 0:nq], func=AF.Exp, scale=-0.5),
                                      reads=[rec.b], writes=[rec.b])

                            def c4(nq=nq, q0=q0, j=j, yacc=yacc, last=(bi == len(qbs) - 1)):
                                tk.op("dve", lambda e: e.tensor_tensor(out=OSp[:, 0:nq], in0=OSp[:, 0:nq], in1=rec[:, 0:nq], op=ALU.mult),
                                      reads=[OSp.b, rec.b], writes=[OSp.b])
                                tk.op("dve", lambda e: e.scalar_tensor_tensor(
                                    out=yacc[:, q0:q0 + nq], in0=OSp[:, 0:nq], scalar=gsc[:, 0:1], in1=dg[:, j, q0:q0 + nq],
                                    op0=ALU.mult, op1=ALU.mult), reads=[OSp.b, gsc.b, dg.b], writes=[yacc.b])
                                if last:
                                    tk.dma("sp", ycT_s[768 + 128 * j:896 + 128 * j, :], yacc[:], reads=[yacc.b], writes=[ycT_s.b])
                            pending += [c1, c2, c3, c4]
                for st in pending:
                    st()
                yield

        def phase_GD(l, ctx_out, wo):
            wv = wout_d[l].rearrange("(kc p) n -> p kc n", p=128)
            for n in range(2):
                tk.dma("pool", wo[:, :, n * 512:(n + 1) * 512], wv[:, :, n * 512:(n + 1) * 512], reads=[wout_d.b], writes=[wo.b])
            with ExitStack() as ph:
                pS = [ps(ph, f"pSa{i}", [128, 512]) for i in range(3)]
                pO = [ps(ph, f"pOa{i}", [128, 512]) for i in range(4)]
                pR = ps(ph, "pRa", [128, 512])
                PT = [sb(ph, f"PTa{i}", [128, 512], BF16) for i in range(3)]
                OS = sb(ph, "OSa", [128, 512])
                SS = sb(ph, "SSa", [128, 512])
                rec = sb(ph, "reca", [128, 512])
                yaccs = [sb(ph, f"yacc{i}", [128, NT], BF16) for i in range(2)]
                g = phase_G(l, ctx_out, pS, pO, pR, PT, OS, SS, rec, yaccs)
                next(g)
                d = phase_D(l, ctx_out, pS, pO, pR, PT, OS, SS, rec, yaccs)
                next(d)
                next(g)
                next(d)
                tk.barrier()
                for gen in (d, g):
                    for _ in gen:
                        pass

        def phase_O(l, ctx_out, wo):
            with ExitStack() as ph:
                ycTs = [sb(ph, f"ycT{i}", [128, 8, 768], BF16) for i in range(3)]
                G2 = [sb(ph, f"G2_{i}", [128, D]) for i in range(2)]
                gpo = sb(ph, "gpo", [128, D])
                ht = [sb(ph, f"hto{i}", [128, D]) for i in range(2)]
                on = [sb(ph, f"on{i}", [128, D]) for i in range(2)]
                junk = sb(ph, "junkO", [128, 512], BF16)
                ssqs = [sb(ph, f"ssqO{i}", [128, 2]) for i in range(2)]
                rstds = [sb(ph, f"rstdO{i}", [128, 1]) for i in range(2)]
                po = [ps(ph, f"po{i}", [128, 512]) for i in range(4)]
                tk.dma("sp", gpo[:], gpost_d[l].partition_broadcast(128), reads=[gpost_d.b], writes=[gpo.b])
                for w in range(2):
                    tk.dma("sp", G2[w][:], gt_s[l, w].partition_broadcast(128), reads=[gt_s.b], writes=[G2[w].b])
                    tk.op("dve", lambda e, w=w: e.tensor_tensor(out=G2[w][:], in0=G2[w][:], in1=gpo[:], op=ALU.mult),
                          reads=[G2[w].b, gpo.b], writes=[G2[w].b])
                ysv = ycT_s[:].rearrange("(kc p) t -> p kc t", p=128)
                for i in range(3):
                    tk.dma("sp", ycTs[i][:], ysv[:, :, i * 768:(i + 1) * 768], reads=[ycT_s.b], writes=[ycTs[i].b])
                for it, tt in enumerate(range(0 if ctx_out else 2, NTILE)):
                    k = it % 2
                    w = 1 if tt < 2 else 0
                    if l == 0:
                        src = ctx_d if tt < 2 else x_d
                        sap = ctx_d[tt * 128:(tt + 1) * 128, :] if tt < 2 else x_d[(tt - 2) * 128:(tt - 1) * 128, :]
                    else:
                        src = h1_s
                        sap = h1_s[tt * 128:(tt + 1) * 128, :]
                    tk.dma("sp", ht[k][:], sap, reads=[src.b], writes=[ht[k].b])
                    pp = po[2 * k:2 * k + 2]
                    for n in range(2):
                        for kc in range(8):
                            tk.op("pe", lambda e, n=n, kc=kc, pp=pp, tt=tt: e.matmul(
                                pp[n][:, :], lhsT=ycTs[tt // 6][:, kc, (tt % 6) * 128:(tt % 6 + 1) * 128], rhs=wo[:, kc, n * 512:(n + 1) * 512],
                                start=(kc == 0), stop=(kc == 7)), reads=[ycTs[tt // 6].b, wo.b], writes=[pp[n].b], sig=(kc == 7))
                    ssq = ssqs[k]
                    rstd = rstds[k]
                    tk.op("pool", lambda e: e.memset(ssq[:], 0.0), writes=[ssq.b])
                    for n in range(2):
                        tk.op("act", lambda e, n=n, pp=pp: e.activation(out=junk[:], in_=pp[n][:, :], func=AF.Square,
                                                                        accum_out=ssq[:, n:n + 1]),
                              reads=[pp[n].b], writes=[junk.b, ssq.b])
                    tk.op("dve", lambda e: e.tensor_tensor(out=rstd[:], in0=ssq[:, 0:1], in1=ssq[:, 1:2], op=ALU.add),
                          reads=[ssq.b], writes=[rstd.b])
                    tk.op("act", lambda e: e.activation(out=rstd[:], in_=rstd[:], func=AF.Ln, scale=1.0 / D, bias=epsc[:, 0:1]),
                          reads=[rstd.b, epsc.b], writes=[rstd.b])
                    tk.op("act", lambda e: e.activation(out=rstd[:], in_=rstd[:], func=AF.Exp, scale=-0.5),
                          reads=[rstd.b], writes=[rstd.b])
                    for n in range(2):
                        tk.op("dve", lambda e, n=n, pp=pp, k=k, w=w: e.scalar_tensor_tensor(
                            out=on[k][:, n * 512:(n + 1) * 512], in0=pp[n][:, :], scalar=rstd[:, 0:1],
                            in1=G2[w][:, n * 512:(n + 1) * 512], op0=ALU.mult, op1=ALU.mult),
                            reads=[pp[n].b, rstd.b, G2[w].b], writes=[on[k].b])
                    tk.op("dve", lambda e, k=k: e.tensor_tensor(out=on[k][:], in0=on[k][:], in1=ht[k][:], op=ALU.add),
                          reads=[on[k].b, ht[k].b], writes=[on[k].b])
                    if l == 0:
                        tk.dma("sp", h1_s[tt * 128:(tt + 1) * 128, :], on[k][:], reads=[on[k].b], writes=[h1_s.b])
                    else:
                        tk.dma("sp", out_d[(tt - 2) * 128:(tt - 1) * 128, :], on[k][:], reads=[on[k].b], writes=[out_d.b])
                tk.barrier()

        m0 = phase_M(0, "sp")
        for _ in range(8):
            next(m0)
        tk.barrier()
        for _ in m0:
            pass
        for l in range(layers):
            if stop_after == ("M", l):
                break
            with ExitStack() as ni:
                uT = sb(ni, "uT", [128, 8, NT], BF16)
                wr = [sb(ni, f"wr{i}", [128, 8, 512], BF16) for i in range(4)]
                I_load = make_I_loader(l, wr)
                I_slots = {}

                def I_pref():
                    for i in range(3):
                        I_slots[i] = I_load(i)
                mnext = phase_M(l + 1, "pool") if l + 1 < layers else None
                phase_N(l, uT, mnext=mnext, mid_hook=I_pref)
                if mnext is not None:
                    raise_if = [x for x in mnext]
                if stop_after == ("N", l):
                    break
                phase_I(l, uT, wr, I_slots)
                if stop_after == ("I", l):
                    break
            ctx_out = l < DEPTH - 1
            phase_S(l, skip_ctx=not ctx_out)
            if stop_after == ("S", l):
                break
            with ExitStack() as go:
                wo = sb(go, "wo", [128, 8, D], BF16)
                phase_GD(l, ctx_out, wo)
                if stop_after == ("D", l):
                    break
                phase_O(l, ctx_out, wo)
                if stop_after == ("O", l):
                    break
        tk.barrier()
    return nc


def _rot_perm(base, nheads, hd):
    idx = []
    for h in range(nheads):
        o = base + h * hd
        idx += list(range(o + hd // 2, o + hd)) + list(range(o, o + hd // 2))
    return idx


def prep_inputs(inp):
    f = lambda a: np.ascontiguousarray(np.asarray(a), dtype=np.float32)
    x, c, ctx, c_ctx = f(inp["x"]), f(inp["c"]), f(inp["ctx"]), f(inp["c_ctx"])
    w_in = f(inp["w_in"])
    perm = (_rot_perm(C_GQ, 4, 64) + _rot_perm(C_GK, 2, 64) + _rot_perm(C_DQ, 8, 32) + _rot_perm(C_DK, 8, 32))
    w_rot = np.ascontiguousarray(w_in[:, :, perm])
    b_mod = f(inp["b_mod"])
    g_pre = f(inp["g_pre"])
    conv_w = f(inp["conv_w"])
    conv_b = f(inp["conv_b"])
    qg, kg = f(inp["q_norm_g"]), f(inp["k_norm_g"])
    pq = _rot_perm(0, 1, 64)
    cg, sg, cd, sd = _rope_tables()
    shared = {
        "w_mod": f(inp["w_mod"]),
        "bmodf": np.stack([_feat(b_mod[l, :2048]) for l in range(DEPTH)]),
        "b_mod": b_mod,
        "gpref": np.stack([_feat(g_pre[l]) for l in range(DEPTH)]),
        "g_post": f(inp["g_post"]),
        "w_in": w_in,
        "w_rot": w_rot,
        "w_out": f(inp["w_out"]),
        "convw_f": np.ascontiguousarray(conv_w.reshape(DEPTH, 5, 8, 128).transpose(0, 3, 2, 1)),
        "convb_f": np.stack([_feat(conv_b[l]) for l in range(DEPTH)]),
        "alog": np.concatenate([f(inp["a_log_fwd"]), f(inp["a_log_bwd"])], axis=1),
        "dtb": np.concatenate([f(inp["dt_bias_fwd"]), f(inp["dt_bias_bwd"])], axis=1),
        "d_skip": f(inp["d_skip"]),
        "ssd_norm_g": f(inp["ssd_norm_g"]),
        "qkg_f": np.ascontiguousarray(np.stack([qg, qg[:, pq], kg, kg[:, pq]], axis=2)),
        "diff_lambda": f(inp["diff_lambda"]).reshape(DEPTH, 128),
        "dng_f": np.ascontiguousarray(np.tile(f(inp["diff_norm_g"]), (1, 2))[:, :, None]),
        "cmat": _const_mats(),
        "identb": np.eye(128, dtype=np.float32).astype(ml_dtypes.bfloat16),
        "rope_g": np.stack([cg, sg]),
        "rope_d": np.stack([cd, sd]),
    }
    cc = _feat(c_ctx)
    maps = []
    for b in range(NCORES):
        m = dict(shared)
        m["x"] = x[b]
        m["ctx"] = ctx[b]
        m["cvec"] = np.ascontiguousarray(np.concatenate([_feat(c[b]), cc], axis=1))
        maps.append(m)
    return maps


_NC_CACHE = {}


def kernel(**inputs):
    if "nc" not in _NC_CACHE:
        _NC_CACHE["nc"] = build()
    nc = _NC_CACHE["nc"]
    maps = prep_inputs(inputs)
    res = run_bass_kernel_spmd(nc, maps, core_ids=list(range(NCORES)))
    return np.stack([np.asarray(r["out"], dtype=np.float32) for r in res.results], axis=0)
```
